# Optimizing a Trainium2 kernel written in Bass

```python
import math
import jax, jax.numpy as jnp
from jax import lax
import numpy as np

D_MODEL = 2048
BATCH = 2
SEQ = 8192
DEPTH = 1
DEC_BATCH = 4
DEC_SEQ = 4096
PAST_LEN = 128

HEAD_DIM = 128
N_HEADS = D_MODEL // HEAD_DIM
HA = N_HEADS // 2
HN = N_HEADS - HA
DA = HEAD_DIM // 2
DV = HEAD_DIM
DN = HEAD_DIM
W_A = HA * DV
W_N = HN * DN
IN_COLS = 3 * W_A + 3 * W_N
D_FF = ((8 * D_MODEL // 3 + 127) // 128) * 128
CONV_W = 3
GRID_W = 64
NA_MAX_ROWS = 8
NA_COLS = 16
NUM_BUCKETS = 32
MAX_DISTANCE = 128
QBLK = 128
EPS = 1e-6

kernel_name = "hybrid_diffattn_natten_convffn_encoder"


def rms_norm(x, g):
    xf = x.astype(jnp.float32)
    y = xf * lax.rsqrt(jnp.mean(xf * xf, axis=-1, keepdims=True) + EPS)
    return (y * g.astype(jnp.float32)).astype(x.dtype)


def t5_bucket(rel):
    nb = NUM_BUCKETS // 2
    max_exact = nb // 2
    ret = jnp.where(rel > 0, nb, 0)
    n = jnp.abs(rel)
    nf = jnp.maximum(n, 1).astype(jnp.float32)
    large = max_exact + (jnp.log(nf / max_exact) / math.log(MAX_DISTANCE / max_exact)
                         * (nb - max_exact)).astype(jnp.int32)
    large = jnp.minimum(large, nb - 1)
    return ret + jnp.where(n < max_exact, n, large)


def diff_attention(q, k, v, lam, lam_init, rel_table, subln_g):
    B, L = q.shape[0], q.shape[1]
    nblk = L // QBLK
    scale = DA ** -0.5
    qb = q.reshape(B, nblk, QBLK, HA, 2, DA).transpose(1, 0, 2, 3, 4, 5)
    kpos = jnp.arange(L, dtype=jnp.int32)

    def block(args):
        qi, i = args
        qpos = i * QBLK + jnp.arange(QBLK, dtype=jnp.int32)
        bias = rel_table[t5_bucket(kpos[None, :] - qpos[:, None])]
        bias = bias.transpose(2, 0, 1).astype(jnp.float32)
        s = jnp.einsum('bqhcd,bkhcd->bhcqk', qi, k).astype(jnp.float32) * scale
        p = jax.nn.softmax(s + bias[None, :, None], axis=-1)
        a = p[:, :, 0] - lam * p[:, :, 1]
        return jnp.einsum('bhqk,bkhe->bqhe', a.astype(v.dtype), v)

    o = lax.map(block, (qb, jnp.arange(nblk, dtype=jnp.int32)))
    o = o.transpose(1, 0, 2, 3, 4).reshape(B, L, HA, DV)
    o = rms_norm(o, subln_g) * (1.0 - lam_init)
    return o.reshape(B, L, W_A)


def neighborhood_attention(q, k, v, rpb):
    B, L = q.shape[0], q.shape[1]
    rows = L // GRID_W
    kh = min(NA_MAX_ROWS, rows)
    scale = DN ** -0.5
    c = jnp.arange(GRID_W, dtype=jnp.int32)
    cs = jnp.clip(c - NA_COLS // 2, 0, GRID_W - NA_COLS)
    kcol = cs[:, None] + jnp.arange(NA_COLS, dtype=jnp.int32)
    dc = kcol - c[:, None] + (NA_COLS - 1)
    qr = q.reshape(B, rows, GRID_W, HN, DN).transpose(1, 0, 2, 3, 4)

    def row(args):
        qi, r = args
        rs = jnp.clip(r - kh // 2, 0, rows - kh)
        krow = rs + jnp.arange(kh, dtype=jnp.int32)
        idx = (krow[None, :, None] * GRID_W + kcol[:, None, :]).reshape(GRID_W, kh * NA_COLS)
        kg = k[:, idx]
        vg = v[:, idx]
        dr = krow - r + (NA_MAX_ROWS - 1)
        bias = rpb[:, dr[None, :, None], dc[:, None, :]]
        bias = bias.reshape(HN, GRID_W, kh * NA_COLS).astype(jnp.float32)
        s = jnp.einsum('bqhd,bqnhd->bhqn', qi, kg).astype(jnp.float32) * scale
        p = jax.nn.softmax(s + bias[None], axis=-1)
        return jnp.einsum('bhqn,bqnhd->bqhd', p.astype(v.dtype), vg)

    o = lax.map(row, (qr, jnp.arange(rows, dtype=jnp.int32)))
    return o.transpose(1, 0, 2, 3, 4).reshape(B, L, W_N)


def conv_glu_ffn(x, w_up, conv_w, conv_b, w_down):
    h = x @ w_up
    a, g = h[..., :D_FF], h[..., D_FF:]
    ap = jnp.pad(a, ((0, 0), (1, 1), (0, 0)))
    a = ap[:, :-2] * conv_w[0] + ap[:, 1:-1] * conv_w[1] + ap[:, 2:] * conv_w[2] + conv_b
    return (jax.nn.gelu(a) * g) @ w_down


def trunk(x, w_in, w_out, norm1_g, norm2_g, final_g, lambda_q1, lambda_k1, lambda_q2,
          lambda_k2, subln_g, rel_bias_table, na_rpb, w_up, conv_w, conv_b, w_down):
    B, L = x.shape[0], x.shape[1]
    for l in range(DEPTH):
        lam_init = 0.8 - 0.6 * math.exp(-0.3 * l)
        lam = (jnp.exp(jnp.sum(lambda_q1[l].astype(jnp.float32) * lambda_k1[l].astype(jnp.float32)))
               - jnp.exp(jnp.sum(lambda_q2[l].astype(jnp.float32) * lambda_k2[l].astype(jnp.float32)))
               + lam_init)
        h = rms_norm(x, norm1_g[l])
        proj = h @ w_in[l]
        qa = proj[..., 0:W_A].reshape(B, L, HA, 2, DA)
        ka = proj[..., W_A:2 * W_A].reshape(B, L, HA, 2, DA)
        va = proj[..., 2 * W_A:3 * W_A].reshape(B, L, HA, DV)
        o0 = 3 * W_A
        qn = proj[..., o0:o0 + W_N].reshape(B, L, HN, DN)
        kn = proj[..., o0 + W_N:o0 + 2 * W_N].reshape(B, L, HN, DN)
        vn = proj[..., o0 + 2 * W_N:o0 + 3 * W_N].reshape(B, L, HN, DN)
        oa = diff_attention(qa, ka, va, lam, lam_init, rel_bias_table, subln_g[l])
        on = neighborhood_attention(qn, kn, vn, na_rpb[l])
        x = x + jnp.concatenate([oa, on], axis=-1) @ w_out[l]
        x = x + conv_glu_ffn(rms_norm(x, norm2_g[l]), w_up[l], conv_w[l], conv_b[l], w_down[l])
    return rms_norm(x, final_g)


def setup_inputs(seed: int = 0) -> dict:
    key = jax.random.key(seed)
    ks = jax.random.split(key, 20)
    f32 = jnp.float32
    nrm = lambda k, s, sc: jax.random.normal(k, s, f32) * sc
    return {
        "x_prompt": nrm(ks[0], (BATCH, SEQ, D_MODEL), 1.0),
        "x_sample": nrm(ks[1], (DEC_BATCH, DEC_SEQ, D_MODEL), 1.0),
        "w_in": nrm(ks[2], (DEPTH, D_MODEL, IN_COLS), D_MODEL ** -0.5),
        "w_out": nrm(ks[3], (DEPTH, W_A + W_N, D_MODEL), (W_A + W_N) ** -0.5),
        "norm1_g": 1.0 + nrm(ks[4], (DEPTH, D_MODEL), 0.02),
        "norm2_g": 1.0 + nrm(ks[5], (DEPTH, D_MODEL), 0.02),
        "final_g": 1.0 + nrm(ks[6], (D_MODEL,), 0.02),
        "lambda_q1": nrm(ks[7], (DEPTH, DA), 0.1),
        "lambda_k1": nrm(ks[8], (DEPTH, DA), 0.1),
        "lambda_q2": nrm(ks[9], (DEPTH, DA), 0.1),
        "lambda_k2": nrm(ks[10], (DEPTH, DA), 0.1),
        "subln_g": 1.0 + nrm(ks[11], (DEPTH, DV), 0.02),
        "rel_bias_table": nrm(ks[12], (NUM_BUCKETS, HA), 0.1),
        "na_rpb": nrm(ks[13], (DEPTH, HN, 2 * NA_MAX_ROWS - 1, 2 * NA_COLS - 1), 0.1),
        "w_up": nrm(ks[14], (DEPTH, D_MODEL, 2 * D_FF), D_MODEL ** -0.5),
        "conv_w": nrm(ks[15], (DEPTH, CONV_W, D_FF), CONV_W ** -0.5),
        "conv_b": nrm(ks[16], (DEPTH, D_FF), 0.01),
        "w_down": nrm(ks[17], (DEPTH, D_FF, D_MODEL), D_FF ** -0.5),
    }


def reference(x_prompt, x_sample, w_in, w_out, norm1_g, norm2_g, final_g, lambda_q1, lambda_k1,
              lambda_q2, lambda_k2, subln_g, rel_bias_table, na_rpb, w_up, conv_w, conv_b, w_down):
    y_prompt = trunk(x_prompt, w_in, w_out, norm1_g, norm2_g, final_g, lambda_q1, lambda_k1,
                     lambda_q2, lambda_k2, subln_g, rel_bias_table, na_rpb, w_up, conv_w, conv_b, w_down)
    y_sample = trunk(x_sample, w_in, w_out, norm1_g, norm2_g, final_g, lambda_q1, lambda_k1,
                     lambda_q2, lambda_k2, subln_g, rel_bias_table, na_rpb, w_up, conv_w, conv_b, w_down)
    return (y_prompt, y_sample)
```

```python
import math
from contextlib import ExitStack

import numpy as np
import ml_dtypes

import concourse.bass as bass
import concourse.mybir as mybir
from concourse.bass_utils import run_bass_kernel_spmd

F32 = mybir.dt.float32
BF16 = mybir.dt.bfloat16
AF = mybir.ActivationFunctionType
ALU = mybir.AluOpType
AX = mybir.AxisListType

D = 2048
CK = 16
HA = 8
HN = 8
INC = 6144
DFF = 5504
FCH = 43
EPS = 1e-6
NCORES = 8
NEAR = 22
JOBS = ((64, 65), (32, 33))
SCALE_A = 0.125
SCALE_N = 128 ** -0.5
NEG = -3.0e5
LAM_INIT = 0.8 - 0.6 * math.exp(-0.3 * 0)

DEBUG = {"stop_after": None, "ext": False}


class Sem:
    def __init__(self, h, name):
        self.h = h
        self.v = 0
        self.name = name


class Res:
    __slots__ = ("name", "wr", "rd", "dsem")

    def __init__(self, name):
        self.name = name
        self.wr = None
        self.rd = []
        self.dsem = None


class Op:
    __slots__ = ("eng", "fn", "deps", "sig", "ev", "dma")


ENGS = ("pe", "act", "dve", "pool", "sp")


class Prog:
    def __init__(self, nc):
        self.nc = nc
        self.esem = {e: Sem(nc.alloc_semaphore(name="es_" + e), e) for e in ("pe", "act", "dve", "pool")}
        self.bar = Sem(nc.alloc_semaphore(name="bar"), "bar")
        self.free_dsems = []
        self.ndsem = 0
        self.reset()
        self.waited = {e: {} for e in ENGS}
        self.nphase = 0
        self.all_res = []

    def reset(self):
        self.ops = {e: [] for e in ENGS}
        self.order = []

    def res(self, name):
        r = Res(name)
        self.all_res.append(r)
        return r

    def _dsem(self, r):
        if r.dsem is None:
            if self.free_dsems:
                r.dsem = self.free_dsems.pop()
            else:
                r.dsem = Sem(self.nc.alloc_semaphore(name="ds%d" % self.ndsem), "ds%d" % self.ndsem)
                self.ndsem += 1
        return r.dsem

    def op(self, eng, fn, r=(), w=(), dma=None, after=()):
        o = Op()
        o.eng = eng
        o.fn = fn
        o.dma = dma
        o.sig = False
        o.ev = None
        deps = []
        seen = set()

        def add(d):
            if d is None or id(d) in seen:
                return
            seen.add(id(d))
            deps.append(d)

        for d in after:
            add(d)
        for x in r:
            add(x.wr)
        for x in w:
            add(x.wr)
            for d in x.rd:
                add(d)
        o.deps = [d for d in deps if not (d.eng == "pe" and eng == "pe" and d.dma is None)]
        for d in o.deps:
            d.sig = True
        for x in w:
            x.wr = o
            x.rd = []
        for x in r:
            x.rd.append(o)
        self.ops[eng].append(o)
        self.order.append(o)
        return o

    def emit_phase(self):
        nc = self.nc
        for o in self.order:
            if o.dma is not None:
                s = self._dsem(o.dma)
                s.v += 16
                o.ev = (s, s.v)
        for e in ("pe", "act", "dve", "pool"):
            ops = self.ops[e]
            lo = [o for o in ops if o.dma is None]
            if lo:
                lo[-1].sig = True
            for o in ops:
                if o.dma is None and o.sig:
                    s = self.esem[e]
                    s.v += 1
                    o.ev = (s, s.v)
        self.nphase += 1
        bar_target = self.nphase * len(ENGS)
        prog = self

        def run(e, eng):
            waited = prog.waited[e]

            def wait(ev):
                s, v = ev
                if waited.get(id(s), 0) < v:
                    eng.wait_ge(s.h, v)
                    waited[id(s)] = v

            last_dma = {}
            for o in prog.ops[e]:
                need = {}
                for d in o.deps:
                    sm, v = d.ev
                    if need.get(id(sm), (None, 0))[1] < v:
                        need[id(sm)] = (sm, v)
                for ev in need.values():
                    wait(ev)
                ins = o.fn(eng)
                if o.dma is not None:
                    ins.then_inc(o.ev[0].h, 16)
                    last_dma[id(o.ev[0])] = o.ev
                elif o.sig:
                    ins.then_inc(o.ev[0].h, 1)
            for ev in last_dma.values():
                wait(ev)
            if e in prog.esem and prog.ops[e]:
                lo = [o for o in prog.ops[e] if o.dma is None]
                if lo:
                    wait(lo[-1].ev)
            eng.sem_inc(prog.bar.h, 1)
            eng.wait_ge(prog.bar.h, bar_target)

        with nc.Block() as block:
            @block.tensor
            def _(eng):
                run("pe", eng)

            @block.scalar
            def _(eng):
                run("act", eng)

            @block.vector
            def _(eng):
                run("dve", eng)

            @block.gpsimd
            def _(eng):
                run("pool", eng)

            @block.sync
            def _(eng):
                run("sp", eng)

        for r in self.all_res:
            r.wr = None
            r.rd = []
            if r.dsem is not None:
                self.free_dsems.append(r.dsem)
                r.dsem = None
        self.all_res = []
        self.reset()


def mkap(t, off, dims):
    return bass.AP(t, off, [list(d) for d in dims])


def psz(t):
    return t[:].ap[0][0]


def t5_bucket_np(rel):
    nb = 16
    me = 8
    ret = np.where(rel > 0, nb, 0)
    n = np.abs(rel)
    nf = np.maximum(n, 1).astype(np.float32)
    large = me + (np.log(nf / np.float32(me)) / np.float32(math.log(128 / 8)) * np.float32(nb - me)).astype(np.int32)
    large = np.minimum(large, nb - 1)
    return ret + np.where(n < me, n, large)


def job_geometry(nblk, nslots, t):
    o = 16 * t
    blocks = [-1] * nslots
    near_true = [False] * NEAR
    used = set()
    for n in range(NEAR):
        gb = o + n - 3
        if 0 <= gb < nblk:
            blocks[n] = gb
            near_true[n] = True
            used.add(gb)
    rest = [b for b in range(nblk) if b not in used]
    for n in range(NEAR):
        if blocks[n] == -1 and n not in (2, 19) and rest:
            blocks[n] = rest.pop(0)
    for s in range(NEAR, nslots):
        if rest:
            blocks[s] = rest.pop(0)
    assert not rest
    return o, blocks, near_true


def nbr_units():
    units = []
    for i in range(16):
        if i == 0:
            dl = list(range(-2, 4))
        elif i == 15:
            dl = list(range(-3, 3))
        else:
            dl = list(range(-2, 3))
        units.append((i, dl))
    units.append((-1, list(range(-2, 3))))
    units.append((16, list(range(-2, 3))))
    return units


def mask_layout():
    off = {}
    col = 0
    for key, nd, w in (("int", 5, 128), (0, 6, 128), (1, 5, 128), (14, 5, 128), (15, 6, 128), (-1, 5, 1), (16, 5, 1)):
        off[key] = col
        col += nd * w
    return off, col


def build_masks(nblk, t, o, near_true):
    R = nblk * 2
    L = nblk * 128
    off, ncol = mask_layout()
    out = np.zeros((128, ncol), np.float32)

    def tile(tq, valid_q, nk):
        if not valid_q:
            return np.zeros((128, len(tq)), np.float32)
        if not near_true[nk]:
            return np.full((128, len(tq)), NEG, np.float32)
        tk = (o + nk - 3) * 128 + np.arange(128)
        r = tq // 64
        c = tq % 64
        rs = np.clip(r - 4, 0, R - 8)
        cs = np.clip(c - 8, 0, 64 - 16)
        rk = (tk // 64)[:, None]
        ckk = (tk % 64)[:, None]
        ok = (rk >= rs[None]) & (rk < rs[None] + 8) & (ckk >= cs[None]) & (ckk < cs[None] + 16)
        return np.where(ok, 0.0, NEG).astype(np.float32)

    units = dict(nbr_units())
    per_unit = {}
    for i, dl in units.items():
        if i == -1:
            tq = np.array([o * 128 - 1])
            vq = o > 0
        elif i == 16:
            tq = np.array([(o + 16) * 128])
            vq = (o + 16) < nblk
        else:
            tq = (o + i) * 128 + np.arange(128)
            vq = True
        per_unit[i] = [tile(tq, vq, i + dl_ + 3) for dl_ in dl]
    for i in range(2, 14):
        for a, b in zip(per_unit[i], per_unit[7]):
            assert np.array_equal(a, b)
    def put(key, tiles):
        c0 = off[key]
        for k, tl in enumerate(tiles):
            w = tl.shape[1]
            out[:, c0 + k * w:c0 + (k + 1) * w] = tl
    put("int", per_unit[7])
    for key in (0, 1, 14, 15, -1, 16):
        put(key, per_unit[key])
    return out.astype(ml_dtypes.bfloat16)


def bcast128(a):
    a = np.asarray(a, np.float32)
    return np.ascontiguousarray(np.broadcast_to(a.reshape(1, -1), (128, a.size)))


class K:
    pass


def build_program():
    nc = bass.Bass("TRN2", target_bir_lowering=False)
    P = Prog(nc)
    k = K()
    k.nc = nc
    k.P = P
    ext = DEBUG["ext"]

    def din(name, shape, dt=F32):
        return nc.dram_tensor(name, list(shape), dt, kind="ExternalInput")

    def dscr(name, shape, dt=BF16):
        return nc.dram_tensor(name, list(shape), dt, kind=("ExternalOutput" if (ext and name in ext) else "Internal"))

    k.xs = [din("xs%d" % j, (JOBS[j][1], 128, D)) for j in range(2)]
    k.w_in = din("w_in", (D, INC))
    k.w_out = din("w_out", (D, D))
    k.w_up = din("w_up", (D, 2 * DFF))
    k.w_down = din("w_down", (DFF, D))
    k.g12c = din("g12c", (128, 2, CK))
    k.gfb = din("gfb", (128, D))
    k.sgb = din("sgb", (128, 128))
    k.lamv = din("lamv", (128, 4, 64))
    k.tab = din("tab", (32, 8))
    k.tablr = din("tablr", (128, 2, 8))
    k.ohu = din("ohu", (32, 1536))
    k.rpbp = din("rpbp", (8, 15, 128))
    k.convc = din("convc", (128, FCH, 4))
    k.ident = din("ident", (128, 128), BF16)
    k.wsel = [din("wsel%d" % j, (128, 3, JOBS[j][1])) for j in range(2)]
    _, mcols = mask_layout()
    k.maskd = [din("maskd%d" % j, (128, mcols), BF16) for j in range(2)]
    k.hflag = din("hflag", (128, 4))
    k.y = [nc.dram_tensor("y%d" % j, [2048, D], F32, kind="ExternalOutput") for j in range(2)]
    k.wb_in = dscr("wb_in", (D, INC))
    k.wb_out = dscr("wb_out", (D, D))
    k.wb_up = dscr("wb_up", (D, 2 * DFF))
    k.wb_down = dscr("wb_down", (DFF, D))
    k.qaT = [dscr("qaT%d" % j, (HA, 128, NEAR * 128)) for j in range(2)]
    k.qnT = [dscr("qnT%d" % j, (HN, 128, NEAR * 128)) for j in range(2)]
    k.knT = [dscr("knT%d" % j, (HN, 128, NEAR * 128)) for j in range(2)]
    k.kaT = [dscr("kaT%d" % j, (HA, 128, JOBS[j][1] * 128)) for j in range(2)]
    k.va = [dscr("va%d" % j, (HA, 128, JOBS[j][1], 129)) for j in range(2)]
    k.vn = [dscr("vn%d" % j, (HN, 128, NEAR, 129)) for j in range(2)]
    k.aoT = [dscr("aoT%d" % j, (16, 128, 2050)) for j in range(2)]
    k.u2 = dscr("u2", (8, 1536), F32)
    k.hsc = dscr("hsc", (8, 128, 1408), F32)

    if not DEBUG.get("skip_a"):
        phase_a(k)
    if DEBUG["stop_after"] == "A":
        return nc
    phase_b(k)
    if DEBUG["stop_after"] == "B":
        return nc
    phase_c(k)
    return nc


def phase_a(k):
    nc, P = k.nc, k.P
    with ExitStack() as es:
        def sb(name, shape, dt=F32):
            return es.enter_context(nc.sbuf_tensor("A_" + name, list(shape), dt))

        def ps(name, shape, dt=F32):
            return es.enter_context(nc.psum_tensor("A_" + name, list(shape), dt))

        ident = sb("ident", (128, 128), BF16)
        g12c = sb("g12c", (128, 2, CK))
        tablr = sb("tablr", (128, 2, 8))
        elr = sb("elr", (128, 2, 8))
        epsc = sb("epsc", (128, 1))
        wsel = [sb("wsel%d" % j, (128, 3, JOBS[j][1])) for j in range(2)]
        wfull = [sb("wfull%d" % j, (128, JOBS[j][1], 8)) for j in range(2)]
        wtmp = sb("wtmp", (128, 65, 8))
        CW = 1024
        cin = [sb("cin%d" % i, (128, 1376)) for i in range(2)]
        cout = [sb("cout%d" % i, (128, 1376), BF16) for i in range(2)]
        xbuf = [sb("xbuf%d" % i, (128, D)) for i in range(3)]
        junk = sb("junk", (128, D), BF16)
        ssq = [sb("ssq%d" % i, (128, 1)) for i in range(3)]
        lnv = [sb("lnv%d" % i, (128, 1)) for i in range(3)]
        rstd = [sb("rstd%d" % i, (128, 1)) for i in range(3)]
        xn = [sb("xn%d" % i, (128, D), BF16) for i in range(2)]
        xnT = [sb("xnT%d" % i, (128, CK, 512), BF16) for i in range(2)]
        wring = [sb("wring%d" % i, (128, CK, 512), BF16) for i in range(3)]
        fmst = [sb("fmst%d" % i, (128, 4, 512), BF16) for i in range(2)]
        vst = [sb("vst%d" % i, (128, 8, 4, 129), BF16) for i in range(2)]
        pT = [ps("pT%d" % i, (128, 8 * 128), BF16) for i in range(2)]
        pO = [ps("pO%d" % i, (128, 512)) for i in range(4)]

        R = P.res
        r_ident, r_g, r_tablr, r_elr, r_eps = R("ident"), R("g12c"), R("tablr"), R("elr"), R("eps")
        r_wsel = [R("wsel0"), R("wsel1")]
        r_wfull = [R("wfull0"), R("wfull1")]
        r_wtmp = R("wtmp")

        P.op("sp", lambda e: e.dma_start(out=ident[:], in_=k.ident.ap()), w=[r_ident], dma=r_ident)
        P.op("sp", lambda e: e.dma_start(out=g12c[:], in_=k.g12c.ap()), w=[r_g], dma=r_g)
        P.op("sp", lambda e: e.dma_start(out=tablr[:], in_=k.tablr.ap()), w=[r_tablr], dma=r_tablr)
        for j in range(2):
            P.op("sp", lambda e, j=j: e.dma_start(out=wsel[j][:], in_=k.wsel[j].ap()), w=[r_wsel[j]], dma=r_wsel[j])
        P.op("dve", lambda e: e.memset(epsc[:], EPS), w=[r_eps])
        P.op("act", lambda e: e.activation(out=elr[:], in_=tablr[:], func=AF.Exp), r=[r_tablr], w=[r_elr])
        for j in range(2):
            S = JOBS[j][1]
            wf, ws = wfull[j], wsel[j]
            pw, pe_, pt = psz(wf), psz(ws), psz(wtmp)
            pl = psz(elr)

            def bc_s(c, ws=ws, pe_=pe_, S=S):
                return mkap(ws, c * S, [(pe_, 128), (1, S), (0, 8)])

            def bc_h(c, S=S, pl=pl):
                return mkap(elr, c * 8, [(pl, 128), (0, S), (1, 8)])

            wt = mkap(wtmp, 0, [(pt, 128), (8, S), (1, 8)])
            P.op("dve", lambda e, wf=wf, bc_s=bc_s, bc_h=bc_h: e.tensor_tensor(out=wf[:], in0=bc_s(1), in1=bc_h(0), op=ALU.mult),
                 r=[r_wsel[j], r_elr], w=[r_wfull[j]])
            P.op("dve", lambda e, wt=wt, bc_s=bc_s, bc_h=bc_h: e.tensor_tensor(out=wt, in0=bc_s(2), in1=bc_h(1), op=ALU.mult),
                 r=[r_wsel[j], r_elr], w=[r_wtmp])
            P.op("dve", lambda e, wf=wf, wt=wt: e.tensor_tensor(out=wf[:], in0=wf[:], in1=wt, op=ALU.add),
                 r=[r_wtmp], w=[r_wfull[j]])
            P.op("dve", lambda e, wf=wf, bc_s=bc_s: e.tensor_tensor(out=wf[:], in0=wf[:], in1=bc_s(0), op=ALU.add),
                 r=[r_wsel[j]], w=[r_wfull[j]])

        r_cin = [R("cin0"), R("cin1")]
        r_cout = [R("cout0"), R("cout1")]
        conv_state = {"n": 0}

        def convert(src, dst, nrows, ncols, tw, gidx):
            stores = []
            for rb in range(nrows // 128):
                for c0 in range(0, ncols, tw):
                    i = conv_state["n"] % 2
                    conv_state["n"] += 1
                    sv = src[rb * 128:(rb + 1) * 128, c0:c0 + tw]
                    dv = dst[rb * 128:(rb + 1) * 128, c0:c0 + tw]
                    P.op("pool", lambda e, i=i, sv=sv, tw=tw: e.dma_start(out=cin[i][:, 0:tw], in_=sv), w=[r_cin[i]], dma=r_cin[i])
                    if gidx is None:
                        P.op("pool", lambda e, i=i, tw=tw: e.tensor_copy(out=cout[i][:, 0:tw], in_=cin[i][:, 0:tw]),
                             r=[r_cin[i]], w=[r_cout[i]])
                    else:
                        P.op("pool", lambda e, i=i, tw=tw, rb=rb, gidx=gidx: e.tensor_scalar(
                            out=cout[i][:, 0:tw], in0=cin[i][:, 0:tw], scalar1=g12c[:, gidx, rb:rb + 1], scalar2=1.0, op0=ALU.mult, op1=ALU.mult),
                            r=[r_cin[i], r_g], w=[r_cout[i]])
                    st = P.op("pool", lambda e, i=i, dv=dv, tw=tw: e.dma_start(out=dv, in_=cout[i][:, 0:tw]), r=[r_cout[i]], dma=r_cout[i])
                    stores.append(st)
            return stores

        st_in = convert(k.w_in, k.wb_in, D, INC, 1024, 0)

        r_x = [R("xbuf%d" % i) for i in range(3)]
        r_junk = R("junk")
        r_ssq = [R("ssq%d" % i) for i in range(3)]
        r_lnv = [R("lnv%d" % i) for i in range(3)]
        r_rstd = [R("rstd%d" % i) for i in range(3)]
        r_xn = [R("xn%d" % i) for i in range(2)]
        r_xnT = [[R("xnT%d_%d" % (i, b)) for b in range(4)] for i in range(2)]
        r_w = [R("wring%d" % i) for i in range(3)]
        r_fm = [R("fmst%d" % i) for i in range(2)]
        r_vst = [R("vst%d" % i) for i in range(2)]
        r_pT = [R("pT%d" % i) for i in range(2)]
        r_pO = [R("pO%d" % i) for i in range(4)]
        cnt = {"x": 0, "xn": 0, "pT": 0, "w": 0, "pO": 0, "fm": 0, "vst": 0, "tile": 0}

        SUBS_NEAR = [("qa", 0), ("qa", 512), ("ka", 1024), ("ka", 1536), ("va", 2048), ("va", 2560),
                     ("qn", 3072), ("qn", 3584), ("kn", 4096), ("kn", 4608), ("vn", 5120), ("vn", 5632)]
        SUBS_FAR = [("ka", 1024), ("ka", 1536), ("va", 2048), ("va", 2560)]

        def load_x(j, s):
            i = cnt["x"] % 3
            cnt["x"] += 1
            P.op("sp", lambda e, i=i, j=j, s=s: e.dma_start(out=xbuf[i][:], in_=k.xs[j][s]), w=[r_x[i]], dma=r_x[i])
            return i

        def norm_block(xi, xnT_i, b):
            ni = cnt["xn"] % 2
            cnt["xn"] += 1
            P.op("act", lambda e: e.activation(out=junk[:], in_=xbuf[xi][:], func=AF.Square, accum_out=ssq[xi][:]),
                 r=[r_x[xi]], w=[r_junk, r_ssq[xi]])
            P.op("act", lambda e: e.activation(out=lnv[xi][:], in_=ssq[xi][:], func=AF.Ln, scale=1.0 / D, bias=epsc[:]),
                 r=[r_ssq[xi], r_eps], w=[r_lnv[xi]])
            P.op("act", lambda e: e.activation(out=rstd[xi][:], in_=lnv[xi][:], func=AF.Exp, scale=-0.5),
                 r=[r_lnv[xi]], w=[r_rstd[xi]])
            P.op("dve", lambda e: e.tensor_scalar(out=xn[ni][:], in0=xbuf[xi][:], scalar1=rstd[xi][:], scalar2=None, op0=ALU.mult),
                 r=[r_x[xi], r_rstd[xi]], w=[r_xn[ni]])
            for half in range(2):
                pi = cnt["pT"] % 2
                cnt["pT"] += 1
                for q in range(8):
                    ck = half * 8 + q
                    P.op("pe", lambda e, pi=pi, q=q, ck=ck: e.transpose(out=pT[pi][:, q * 128:(q + 1) * 128],
                                                                        in_=xn[ni][:, ck * 128:(ck + 1) * 128], identity=ident[:]),
                         r=[r_xn[ni], r_ident], w=[r_pT[pi]])
                dst = xnT[xnT_i][:, half * 8:(half + 1) * 8, b * 128:(b + 1) * 128]
                src = pT[pi][:].rearrange("p (a b) -> p a b", b=128)
                P.op("dve", lambda e, dst=dst, src=src: e.tensor_copy(out=dst, in_=src), r=[r_pT[pi]], w=[r_xnT[xnT_i][b]])

        def load_w(col0, first):
            i = cnt["w"] % 3
            cnt["w"] += 1
            src = k.wb_in[:, col0:col0 + 512].rearrange("(ck p) f -> p ck f", p=128)
            P.op("sp", lambda e, i=i, src=src: e.dma_start(out=wring[i][:], in_=src), w=[r_w[i]], dma=r_w[i],
                 after=(st_in if first else ()))
            return i

        def do_tile(j, s0, nb, near, xnT_i, prefetch):
            S = JOBS[j][1]
            N = nb * 128
            subs = SUBS_NEAR if near else SUBS_FAR
            wq = []
            state = {"first": cnt["w"] == 0}
            nxt = load_w(subs[0][1], state["first"])
            for si, (kind, col0) in enumerate(subs):
                wi = nxt
                if si + 1 < len(subs):
                    nxt = load_w(subs[si + 1][1], False)
                if si == 1 and prefetch is not None:
                    prefetch()
                hb = (col0 % 1024) // 128
                if kind in ("qa", "ka", "qn", "kn"):
                    fi = cnt["fm"] % 2
                    cnt["fm"] += 1
                    for fc in range(4):
                        oi = cnt["pO"] % 4
                        cnt["pO"] += 1
                        for ck in range(CK):
                            P.op("pe", lambda e, oi=oi, wi=wi, fc=fc, ck=ck: e.matmul(
                                pO[oi][:, 0:N], lhsT=wring[wi][:, ck, fc * 128:(fc + 1) * 128], rhs=xnT[xnT_i][:, ck, 0:N],
                                start=(ck == 0), stop=(ck == CK - 1)),
                                r=[r_w[wi]] + r_xnT[xnT_i][0:nb], w=[r_pO[oi]])
                        P.op("act", lambda e, oi=oi, fi=fi, fc=fc: e.activation(out=fmst[fi][:, fc, 0:N], in_=pO[oi][:, 0:N], func=AF.Copy),
                             r=[r_pO[oi]], w=[r_fm[fi]])
                    dstT = {"qa": k.qaT, "ka": k.kaT, "qn": k.qnT, "kn": k.knT}[kind][j]
                    dv = dstT[hb:hb + 4, :, s0 * 128:s0 * 128 + N].rearrange("h p n -> p h n")
                    P.op("act", lambda e, fi=fi, dv=dv: e.dma_start(out=dv, in_=fmst[fi][:, :, 0:N]), r=[r_fm[fi]], dma=r_fm[fi])
                else:
                    if hb == 0:
                        vi = cnt["vst"] % 2
                        cnt["vst"] += 1
                        state["vi"] = vi
                    vi = state["vi"]
                    for b in range(nb):
                        oi = cnt["pO"] % 4
                        cnt["pO"] += 1
                        for ck in range(CK):
                            P.op("pe", lambda e, oi=oi, wi=wi, b=b, ck=ck: e.matmul(
                                pO[oi][:, :], lhsT=xnT[xnT_i][:, ck, b * 128:(b + 1) * 128], rhs=wring[wi][:, ck, :],
                                start=(ck == 0), stop=(ck == CK - 1)),
                                r=[r_w[wi], r_xnT[xnT_i][b]], w=[r_pO[oi]])
                        src = pO[oi][:].rearrange("p (h e) -> p h e", e=128)
                        dst = vst[vi][:, hb:hb + 4, b, 0:128]
                        if kind == "va":
                            wf = wfull[j]
                            wb = mkap(wf, (s0 + b) * 8 + hb, [(psz(wf), 128), (1, 4), (0, 128)])
                            P.op("dve", lambda e, dst=dst, src=src, wb=wb: e.tensor_tensor(out=dst, in0=src, in1=wb, op=ALU.mult),
                                 r=[r_pO[oi], r_wfull[j]], w=[r_vst[vi]])
                            if hb == 4:
                                ones_dst = vst[vi][:, :, b, 128:129]
                                wsrc = mkap(wf, (s0 + b) * 8, [(psz(wf), 128), (1, 8), (1, 1)])
                                P.op("dve", lambda e, ones_dst=ones_dst, wsrc=wsrc: e.tensor_copy(out=ones_dst, in_=wsrc),
                                     r=[r_wfull[j]], w=[r_vst[vi]])
                        else:
                            P.op("act", lambda e, dst=dst, src=src: e.activation(out=dst, in_=src, func=AF.Copy),
                                 r=[r_pO[oi]], w=[r_vst[vi]])
                            if hb == 4:
                                ones_dst = vst[vi][:, :, b, 128:129]
                                P.op("dve", lambda e, ones_dst=ones_dst: e.memset(ones_dst, 1.0), w=[r_vst[vi]])
                    if hb == 4:
                        dstV = (k.va if kind == "va" else k.vn)[j]
                        dv = dstV[:, :, s0:s0 + nb, :].rearrange("h p s e -> p h s e")
                        P.op("act", lambda e, vi=vi, dv=dv: e.dma_start(out=dv, in_=vst[vi][:, :, 0:nb, :]), r=[r_vst[vi]], dma=r_vst[vi])

        tiles = []
        for j in range(2):
            S = JOBS[j][1]
            s = 0
            while s < NEAR:
                nb = min(4, NEAR - s)
                tiles.append((j, s, nb, True))
                s += nb
            while s < S:
                nb = min(4, S - s)
                tiles.append((j, s, nb, False))
                s += nb
        if DEBUG.get("max_tiles"):
            tiles = tiles[:DEBUG["max_tiles"]]

        def prep_tile(ti):
            j, s0, nb, near = tiles[ti]
            xi_list = [load_x(j, s0 + b) for b in range(nb)]
            for b in range(nb):
                norm_block(xi_list[b], ti % 2, b)

        def prep_tile_interleaved(ti):
            j, s0, nb, near = tiles[ti]
            pend = []
            for b in range(nb):
                pend.append(load_x(j, s0 + b))
                if len(pend) == 2:
                    norm_block(pend.pop(0), ti % 2, b - 1)
            bb = nb - len(pend)
            for xi in pend:
                norm_block(xi, ti % 2, bb)
                bb += 1

        prep_tile_interleaved(0)
        for ti in range(len(tiles)):
            j, s0, nb, near = tiles[ti]
            pf = (lambda ti=ti: prep_tile_interleaved(ti + 1)) if ti + 1 < len(tiles) else None
            do_tile(j, s0, nb, near, ti % 2, pf)

        k.st_out = convert(k.w_out, k.wb_out, D, D, 1024, None)
        k.st_up = convert(k.w_up, k.wb_up, D, 2 * DFF, 1376, 1)
        k.st_down = convert(k.w_down, k.wb_down, DFF, D, 1024, None)

        P.emit_phase()


def phase_b(k):
    nc, P = k.nc, k.P
    moff, mcols = mask_layout()
    with ExitStack() as es:
        def sb(name, shape, dt=F32):
            return es.enter_context(nc.sbuf_tensor("B_" + name, list(shape), dt))

        def ps(name, shape, dt=F32):
            return es.enter_context(nc.psum_tensor("B_" + name, list(shape), dt))

        R = P.res
        SMAX = JOBS[0][1]
        ident = sb("ident", (128, 128), BF16)
        tablr = sb("tablr", (128, 2, 8))
        lamv = sb("lamv", (128, 4, 64))
        lprod = sb("lprod", (128, 2, 64))
        lsum = sb("lsum", (128, 2))
        lexp = sb("lexp", (128, 2))
        nlam = sb("nlam", (128, 1))
        sg = sb("sg", (128, 128))
        epsc = sb("epsc", (128, 1))
        tabp = sb("tabp", (128, 128))
        ohup = sb("ohup", (128, 1536))
        u2s = sb("u2s", (8, 1536))
        KT = [sb("KT%d" % i, (128, SMAX * 128), BF16) for i in range(2)]
        VH = [sb("VH%d" % i, (128, SMAX, 129), BF16) for i in range(2)]
        QT = [sb("QT%d" % i, (128, 18 * 128), BF16) for i in range(2)]
        HH = [sb("HH%d" % i, (128, 1408)) for i in range(2)]
        PT = [sb("PT%d" % i, (128, 2, 512), BF16) for i in range(3)]
        PTm = sb("PTm", (128, SMAX * 4), BF16)
        Gmini = sb("Gmini", (128, 18, 2))
        osb = sb("osb", (128, 8, 129))
        rz = sb("rz", (128, 8))
        tt = sb("tt", (128, 8, 128))
        od = sb("od", (128, 4, 128))
        sqj = sb("sqj", (128, 128))
        ssq = sb("ssq", (128, 4))
        lnv = sb("lnv", (128, 4))
        rstd = sb("rstd", (128, 4))
        tmp2 = sb("tmp2", (128, 4, 128))
        onb = [sb("onb%d" % i, (128, 4, 128), BF16) for i in range(2)]
        AOh = [sb("AOh%d" % i, (128, 2050), BF16) for i in range(2)]
        Trt = [sb("Trt%d" % i, (128, 7, 2, 64)) for i in range(2)]
        Tfix = [sb("Tfix%d" % i, (128, 7, 128)) for i in range(2)]
        maskt = sb("maskt", (128, mcols), BF16)
        PTn = [sb("PTn%d" % i, (128, 7, 128), BF16) for i in range(2)]
        rzn = [sb("rzn%d" % i, (128, 1)) for i in range(3)]
        onbn = [sb("onbn%d" % i, (128, 128), BF16) for i in range(3)]

        psS = [ps("psS%d" % i, (128, 2, 512)) for i in range(2)]
        acc = ps("acc", (128, 3, 512))
        pTr = ps("pTr", (128, 1024), BF16)

        r_ident, r_tablr, r_lamv, r_lprod, r_lsum, r_lexp, r_nlam = (R(n) for n in ("ident", "tablr", "lamv", "lprod", "lsum", "lexp", "nlam"))
        r_sg, r_eps, r_tabp, r_ohup, r_u2s, r_u2d = (R(n) for n in ("sg", "eps", "tabp", "ohup", "u2s", "u2d"))
        r_KT = [R("KT0"), R("KT1")]
        r_VH = [R("VH0"), R("VH1")]
        r_QT = [R("QT0"), R("QT1")]
        r_HH = [R("HH0"), R("HH1")]
        r_PT = [R("PT%d" % i) for i in range(3)]
        r_PTm, r_Gm, r_osb, r_rz, r_tt, r_od, r_sqj, r_ssq, r_lnv, r_rstd, r_tmp2 = (
            R(n) for n in ("PTm", "Gm", "osb", "rz", "tt", "od", "sqj", "ssq", "lnv", "rstd", "tmp2"))
        r_onb = [R("onb0"), R("onb1")]
        r_AOh = [R("AOh0"), R("AOh1")]
        r_Trt = [R("Trt0"), R("Trt1")]
        r_Tfix = [R("Tfix0"), R("Tfix1")]
        r_mask = R("mask")
        r_PTn = [R("PTn0"), R("PTn1")]
        r_rzn = [R("rzn%d" % i) for i in range(3)]
        r_onbn = [R("onbn%d" % i) for i in range(3)]
        r_psS = [R("psS0"), R("psS1")]
        r_acc = [R("acc%d" % i) for i in range(3)]
        r_pTr = [R("pTr%d" % i) for i in range(4)]
        cnt = {"S": 0, "PT": 0, "onb": 0, "pTr": 0, "hb": 0, "PTn": 0, "accn": 0, "tb": 0}

        P.op("sp", lambda e: e.dma_start(out=ident[:], in_=k.ident.ap()), w=[r_ident], dma=r_ident)
        P.op("sp", lambda e: e.dma_start(out=tablr[:], in_=k.tablr.ap()), w=[r_tablr], dma=r_tablr)
        P.op("sp", lambda e: e.dma_start(out=lamv[:], in_=k.lamv.ap()), w=[r_lamv], dma=r_lamv)
        P.op("sp", lambda e: e.dma_start(out=sg[:], in_=k.sgb.ap()), w=[r_sg], dma=r_sg)
        P.op("dve", lambda e: e.memset(epsc[:], EPS), w=[r_eps])
        P.op("dve", lambda e: e.memset(tabp[:], 0.0), w=[r_tabp])
        P.op("dve", lambda e: e.memset(ohup[:], 0.0), w=[r_ohup])
        P.op("sp", lambda e: e.dma_start(out=tabp[0:32, 0:8], in_=k.tab.ap()), w=[r_tabp], dma=r_tabp)
        P.op("sp", lambda e: e.dma_start(out=ohup[0:32, :], in_=k.ohu.ap()), w=[r_ohup], dma=r_ohup)
        pl = psz(lamv)
        P.op("dve", lambda e: e.tensor_tensor(out=lprod[:], in0=mkap(lamv, 0, [(pl, 128), (128, 2), (1, 64)]),
                                              in1=mkap(lamv, 64, [(pl, 128), (128, 2), (1, 64)]), op=ALU.mult),
             r=[r_lamv], w=[r_lprod])
        P.op("dve", lambda e: e.tensor_reduce(out=lsum[:], in_=lprod[:], axis=AX.X, op=ALU.add), r=[r_lprod], w=[r_lsum])
        P.op("act", lambda e: e.activation(out=lexp[:], in_=lsum[:], func=AF.Exp), r=[r_lsum], w=[r_lexp])
        P.op("dve", lambda e: e.tensor_tensor(out=nlam[:], in0=lexp[:, 1:2], in1=lexp[:, 0:1], op=ALU.subtract), r=[r_lexp], w=[r_nlam])
        P.op("dve", lambda e: e.tensor_scalar(out=nlam[:], in0=nlam[:], scalar1=-LAM_INIT, scalar2=None, op0=ALU.add), r=[r_nlam], w=[r_nlam])
        P.op("dve", lambda e: e.tensor_scalar(out=sg[:], in0=sg[:], scalar1=1.0 - LAM_INIT, scalar2=None, op0=ALU.mult), r=[r_sg], w=[r_sg])
        P.op("dve", lambda e: e.tensor_scalar(out=tabp[:], in0=tabp[:], scalar1=1.0 / SCALE_A, scalar2=None, op0=ALU.mult), r=[r_tabp], w=[r_tabp])
        for q in range(3):
            P.op("pe", lambda e, q=q: e.matmul(psS[q % 2][:, q // 2, :], lhsT=tabp[:], rhs=ohup[:, q * 512:(q + 1) * 512], start=True, stop=True),
                 r=[r_tabp, r_ohup], w=[r_psS[q % 2]])
            P.op("dve", lambda e, q=q: e.tensor_copy(out=u2s[:, q * 512:(q + 1) * 512], in_=psS[q % 2][0:8, q // 2, :]),
                 r=[r_psS[q % 2]], w=[r_u2s])
        P.op("sp", lambda e: e.dma_start(out=k.u2.ap(), in_=u2s[:]), r=[r_u2s], w=[r_u2d], dma=r_u2s)

        def acc_ap(a, rows=128, cols=129):
            return acc[0:rows, a // 3, (a % 3) * 129:(a % 3) * 129 + cols]

        def load_head(j, h, nbr):
            S = JOBS[j][1]
            hb = cnt["hb"] % 2
            cnt["hb"] += 1
            if not nbr:
                P.op("sp", lambda e: e.dma_start(out=KT[hb][:, 0:S * 128], in_=k.kaT[j][h]), w=[r_KT[hb]], dma=r_KT[hb])
                P.op("sp", lambda e: e.dma_start(out=VH[hb][:, 0:S, :], in_=k.va[j][h]), w=[r_VH[hb]], dma=r_VH[hb])
                P.op("sp", lambda e: e.dma_start(out=QT[hb][:], in_=k.qaT[j][h][:, 2 * 128:20 * 128]), w=[r_QT[hb]], dma=r_QT[hb])
                P.op("sp", lambda e: e.dma_start(out=HH[hb][:], in_=mkap(k.u2, h * 1536, [(1, 128), (1, 1408)])),
                     r=[r_u2d], w=[r_HH[hb]], dma=r_HH[hb])
            else:
                P.op("sp", lambda e: e.dma_start(out=KT[hb][:, 0:NEAR * 128], in_=k.knT[j][h]), w=[r_KT[hb]], dma=r_KT[hb])
                P.op("sp", lambda e: e.dma_start(out=VH[hb][:, 0:NEAR, :], in_=k.vn[j][h]), w=[r_VH[hb]], dma=r_VH[hb])
                P.op("sp", lambda e: e.dma_start(out=QT[hb][:], in_=k.qnT[j][h][:, 2 * 128:20 * 128]), w=[r_QT[hb]], dma=r_QT[hb])
                tb = cnt["tb"] % 2
                cnt["tb"] += 1
                for dl in range(-3, 4):
                    for rk in range(2):
                        for rq in range(2):
                            dr = 2 * dl + rk - rq + 7
                            P.op("pool", lambda e, dl=dl, rk=rk, rq=rq, dr=dr: e.dma_start(
                                out=Trt[tb][rk * 64:(rk + 1) * 64, dl + 3, rq, :],
                                in_=mkap(k.rpbp, (h * 15 + dr) * 128, [(1, 64), (1, 64)])), w=[r_Trt[tb]], dma=r_Trt[tb])
                pt_ = psz(Trt[tb])
                for rq in range(2):
                    P.op("pool", lambda e, rq=rq: e.tensor_scalar(
                        out=Tfix[tb][:, :, rq * 64:(rq + 1) * 64],
                        in0=mkap(Trt[tb], rq * 64 + 63, [(pt_, 128), (128, 7), (-1, 64)]),
                        scalar1=1.0 / SCALE_N, scalar2=1.0, op0=ALU.mult, op1=ALU.mult),
                        r=[r_Trt[tb]], w=[r_Tfix[tb]])
                return hb, tb
            return hb, None

        def finish_diff(rows, nch, ao, ecols):
            na = 2 * nch
            nbanks = (na + 2) // 3
            for b in range(nbanks):
                n_in = min(3, na - 3 * b)
                P.op("act", lambda e, b=b, n_in=n_in: e.activation(
                    out=osb[0:rows, 3 * b:3 * b + n_in, :], in_=acc[0:rows, b, 0:n_in * 129].rearrange("p (a c) -> p a c", c=129), func=AF.Copy),
                    r=[r_acc[b]], w=[r_osb])
            po = psz(osb)
            P.op("dve", lambda e: e.reciprocal(out=rz[0:rows, 0:na], in_=mkap(osb, 128, [(po, rows), (129, na)])), r=[r_osb], w=[r_rz])
            P.op("dve", lambda e: e.tensor_tensor(out=tt[0:rows, 0:na, :], in0=osb[0:rows, 0:na, 0:128],
                                                  in1=mkap(rz, 0, [(psz(rz), rows), (1, na), (0, 128)]), op=ALU.mult),
                 r=[r_osb, r_rz], w=[r_tt])
            ptt = psz(tt)
            P.op("dve", lambda e: e.scalar_tensor_tensor(out=od[0:rows, 0:nch, :], in0=mkap(tt, 128, [(ptt, rows), (256, nch), (1, 128)]),
                                                         scalar=nlam[0:rows, :], in1=mkap(tt, 0, [(ptt, rows), (256, nch), (1, 128)]),
                                                         op0=ALU.mult, op1=ALU.add),
                 r=[r_tt, r_nlam], w=[r_od])
            P.op("dve", lambda e: e.tensor_tensor(out=tmp2[0:rows, 0:nch, :], in0=od[0:rows, 0:nch, :], in1=od[0:rows, 0:nch, :], op=ALU.mult),
                 r=[r_od], w=[r_tmp2])
            P.op("dve", lambda e: e.tensor_reduce(out=ssq[0:rows, 0:nch], in_=tmp2[0:rows, 0:nch, :], axis=AX.X, op=ALU.add),
                 r=[r_tmp2], w=[r_ssq])
            P.op("act", lambda e: e.activation(out=lnv[0:rows, 0:nch], in_=ssq[0:rows, 0:nch], func=AF.Ln, scale=1.0 / 128, bias=epsc[0:rows, :]),
                 r=[r_ssq, r_eps], w=[r_lnv])
            P.op("act", lambda e: e.activation(out=rstd[0:rows, 0:nch], in_=lnv[0:rows, 0:nch], func=AF.Exp, scale=-0.5), r=[r_lnv], w=[r_rstd])
            P.op("dve", lambda e: e.tensor_tensor(out=tmp2[0:rows, 0:nch, :], in0=od[0:rows, 0:nch, :],
                                                  in1=mkap(rstd, 0, [(psz(rstd), rows), (1, nch), (0, 128)]), op=ALU.mult),
                 r=[r_od, r_rstd], w=[r_tmp2])
            oi = cnt["onb"] % 2
            cnt["onb"] += 1
            P.op("dve", lambda e: e.tensor_tensor(out=onb[oi][0:rows, 0:nch, :], in0=tmp2[0:rows, 0:nch, :],
                                                  in1=mkap(sg, 0, [(psz(sg), rows), (0, nch), (1, 128)]), op=ALU.mult),
                 r=[r_tmp2, r_sg], w=[r_onb[oi]])
            for c in range(nch):
                P.op("pe", lambda e, c=c: e.transpose(out=pTr[:, c * 128:c * 128 + rows], in_=onb[oi][0:rows, c, :], identity=ident[0:rows, 0:rows]),
                     r=[r_onb[oi], r_ident], w=[r_pTr[0]])
            if rows == 128:
                dst = AOh[ao][:, ecols[0]:ecols[0] + 128 * nch]
                src = pTr[:, 0:128 * nch]
            else:
                dst = mkap(AOh[ao], 0, [(psz(AOh[ao]), 128), (2049, 2)])
                src = pTr[:, 0:2]
            P.op("dve", lambda e, dst=dst, src=src: e.tensor_copy(out=dst, in_=src), r=[r_pTr[0]], w=[r_AOh[ao]])

        def diff_head(j, h, hb, ao):
            S = JOBS[j][1]
            pq = psz(QT[hb])
            ph = psz(HH[hb])
            G = Gmini
            pg = psz(G)
            P.op("dve", lambda e: e.tensor_copy(out=G[:, 0:2, 0:1], in_=mkap(HH[hb], 640, [(ph, 128), (128, 2), (1, 1)])), r=[r_HH[hb]], w=[r_Gm])
            P.op("dve", lambda e: e.tensor_copy(out=G[:, 2:18, 0:1], in_=mkap(HH[hb], 896, [(ph, 128), (0, 16), (1, 1)])), r=[r_HH[hb]], w=[r_Gm])
            P.op("dve", lambda e: e.tensor_copy(out=G[:, 0:16, 1:2], in_=mkap(HH[hb], 511, [(ph, 128), (0, 16), (1, 1)])), r=[r_HH[hb]], w=[r_Gm])
            P.op("dve", lambda e: e.tensor_copy(out=G[:, 16:18, 1:2], in_=mkap(HH[hb], 639, [(ph, 128), (128, 2), (1, 1)])), r=[r_HH[hb]], w=[r_Gm])
            def slot_step(g, s):
                    qc0 = (4 * g + 1) * 128
                    ri = cnt["S"] % 2
                    cnt["S"] += 1
                    for m in range(2):
                        P.op("pe", lambda e, ri=ri, m=m, s=s: e.matmul(
                            psS[ri][:, m, :], lhsT=KT[hb][64 * m:64 * m + 64, s * 128:(s + 1) * 128],
                            rhs=QT[hb][64 * m:64 * m + 64, qc0:qc0 + 512], start=True, stop=True),
                            r=[r_KT[hb], r_QT[hb]], w=[r_psS[ri]])
                    bias = None
                    if 2 <= s <= 19:
                        d0 = (s - 3) - 4 * g
                        sw = 5 - d0
                        if sw <= 0:
                            bias = tablr[:, 1, h:h + 1]
                        elif sw >= 7:
                            bias = tablr[:, 0, h:h + 1]
                        else:
                            win = mkap(HH[hb], 1407 - 128 * sw, [(ph, 128), (0, 2), (-1, 512)])
                            P.op("dve", lambda e, ri=ri, win=win: e.tensor_tensor(out=psS[ri][:], in0=psS[ri][:], in1=win, op=ALU.add),
                                 r=[r_psS[ri], r_HH[hb]], w=[r_psS[ri]])
                    pi = cnt["PT"] % 3
                    cnt["PT"] += 1
                    if bias is None:
                        P.op("act", lambda e, ri=ri, pi=pi: e.activation(out=PT[pi][:], in_=psS[ri][:], func=AF.Exp, scale=SCALE_A),
                             r=[r_psS[ri]], w=[r_PT[pi]])
                    else:
                        P.op("act", lambda e, ri=ri, pi=pi, bias=bias: e.activation(out=PT[pi][:], in_=psS[ri][:], func=AF.Exp, scale=SCALE_A, bias=bias),
                             r=[r_psS[ri], r_tablr], w=[r_PT[pi]])
                    for c in range(4):
                        for m in range(2):
                            a = 2 * c + m
                            P.op("pe", lambda e, pi=pi, c=c, m=m, a=a, s=s: e.matmul(
                                acc_ap(a), lhsT=PT[pi][:, m, c * 128:(c + 1) * 128], rhs=VH[hb][:, s, :],
                                start=(s == 0 and a % 3 == 0), stop=(s == S - 1), skip_group_check=True),
                                r=[r_PT[pi], r_VH[hb]], w=[r_acc[a // 3]])

            parts = DEBUG.get("b_parts", ("main", "finish", "mini"))
            for g in range(4 if "main" in parts else 0):
                for s in range(S):
                    slot_step(g, s)
                if "finish" in parts:
                    finish_diff(128, 4, ao, [1 + 512 * g + 128 * c for c in range(4)])
            if "mini" not in parts:
                P.op("act", lambda e: e.dma_start(out=k.aoT[j][h], in_=AOh[ao][:]), r=[r_AOh[ao]], dma=r_AOh[ao])
                return
            ri = cnt["S"] % 2
            cnt["S"] += 1
            for s in range(S):
                for m in range(2):
                    P.op("pe", lambda e, ri=ri, m=m, s=s: e.matmul(
                        psS[ri][:, 0, s * 4 + 2 * m:s * 4 + 2 * m + 2], lhsT=KT[hb][64 * m:64 * m + 64, s * 128:(s + 1) * 128],
                        rhs=mkap(QT[hb], 64 * m * pq + 127, [(pq, 64), (2049, 2)]), start=True, stop=True),
                        r=[r_KT[hb], r_QT[hb]], w=[r_psS[ri]])
            pps = psz(psS[ri])
            reg = mkap(psS[ri], 8, [(pps, 128), (4, 18), (2, 2), (1, 2)])
            P.op("dve", lambda e, reg=reg: e.tensor_tensor(out=reg, in0=reg, in1=mkap(G, 0, [(pg, 128), (2, 18), (0, 2), (1, 2)]), op=ALU.add),
                 r=[r_psS[ri], r_Gm], w=[r_psS[ri]])
            P.op("act", lambda e, ri=ri: e.activation(out=PTm[:, 0:4 * S], in_=psS[ri][:, 0, 0:4 * S], func=AF.Exp, scale=SCALE_A),
                 r=[r_psS[ri]], w=[r_PTm])
            for s in range(S):
                for m in range(2):
                    P.op("pe", lambda e, m=m, s=s: e.matmul(
                        acc_ap(m, rows=2), lhsT=PTm[:, s * 4 + 2 * m:s * 4 + 2 * m + 2], rhs=VH[hb][:, s, :],
                        start=(s == 0 and m == 0), stop=(s == S - 1), skip_group_check=True),
                        r=[r_PTm, r_VH[hb]], w=[r_acc[0]])
            finish_diff(2, 1, ao, None)
            P.op("act", lambda e: e.dma_start(out=k.aoT[j][h], in_=AOh[ao][:]), r=[r_AOh[ao]], dma=r_AOh[ao])

        def nbr_head(j, h, hb, tb, ao):
            pq = psz(QT[hb])
            pm = psz(maskt)
            def unit(i, dl):
                nd = len(dl)
                if i == -1:
                    nq, qoff, key = 1, 127, -1
                elif i == 16:
                    nq, qoff, key = 1, 17 * 128, 16
                else:
                    nq, qoff = 128, (i + 1) * 128
                    key = i if i in (0, 1, 14, 15) else "int"
                ri = cnt["S"] % 2
                cnt["S"] += 1
                flat = psS[ri][:].rearrange("p a b -> p (a b)")
                for di, d_ in enumerate(dl):
                    nk = i + d_ + 3
                    P.op("pe", lambda e, di=di, nk=nk: e.matmul(flat[:, di * 128:di * 128 + nq], lhsT=KT[hb][:, nk * 128:(nk + 1) * 128],
                                                                rhs=QT[hb][:, qoff:qoff + nq], start=True, stop=False),
                         r=[r_KT[hb], r_QT[hb]], w=[r_psS[ri]])
                    mo = moff[key] + di * nq
                    P.op("pe", lambda e, di=di, mo=mo: e.matmul(flat[:, di * 128:di * 128 + nq], lhsT=ident[:], rhs=maskt[:, mo:mo + nq],
                                                                start=False, stop=True),
                         r=[r_ident, r_mask], w=[r_psS[ri]])
                pps = psz(psS[ri])
                reg = mkap(psS[ri], 0, [(pps, 128), (128, nd), (1, nq)])
                tfx = Tfix[tb][:, dl[0] + 3:dl[0] + 3 + nd, qoff % 128:qoff % 128 + nq]
                P.op("dve", lambda e, reg=reg, tfx=tfx: e.tensor_tensor(out=reg, in0=reg, in1=tfx, op=ALU.add),
                     r=[r_psS[ri], r_Tfix[tb]], w=[r_psS[ri]])
                pn = cnt["PTn"] % 2
                cnt["PTn"] += 1
                P.op("act", lambda e, reg=reg, pn=pn: e.activation(out=PTn[pn][:, 0:nd, 0:nq], in_=reg, func=AF.Exp, scale=SCALE_N),
                     r=[r_psS[ri]], w=[r_PTn[pn]])
                ai = cnt["accn"] % 3
                cnt["accn"] += 1
                for di, d_ in enumerate(dl):
                    nk = i + d_ + 3
                    P.op("pe", lambda e, di=di, nk=nk, ai=ai, pn=pn: e.matmul(acc[0:nq, ai, 0:129], lhsT=PTn[pn][:, di, 0:nq], rhs=VH[hb][:, nk, :],
                                                                        start=(di == 0), stop=(di == nd - 1)),
                         r=[r_PTn[pn], r_VH[hb]], w=[r_acc[ai]])
                P.op("dve", lambda e, ai=ai: e.reciprocal(out=rzn[ai][0:nq, :], in_=acc[0:nq, ai, 128:129]), r=[r_acc[ai]], w=[r_rzn[ai]])
                P.op("dve", lambda e, ai=ai: e.tensor_scalar(out=onbn[ai][0:nq, :], in0=acc[0:nq, ai, 0:128], scalar1=rzn[ai][0:nq, :], scalar2=None, op0=ALU.mult),
                     r=[r_acc[ai], r_rzn[ai]], w=[r_onbn[ai]])
                ti = 0
                P.op("pe", lambda e, ai=ai, ti=ti: e.transpose(out=pTr[:, ti * 128:ti * 128 + nq], in_=onbn[ai][0:nq, :], identity=ident[0:nq, 0:nq]),
                     r=[r_onbn[ai], r_ident], w=[r_pTr[0]])
                if nq == 128:
                    dst = AOh[ao][:, 1 + 128 * i:1 + 128 * i + 128]
                else:
                    ecol = 0 if i == -1 else 2049
                    dst = AOh[ao][:, ecol:ecol + 1]
                P.op("dve", lambda e, dst=dst, ti=ti: e.tensor_copy(out=dst, in_=pTr[:, ti * 128:ti * 128 + nq]), r=[r_pTr[0]], w=[r_AOh[ao]])

            for (i, dl) in nbr_units():
                unit(i, dl)
            P.op("act", lambda e: e.dma_start(out=k.aoT[j][8 + h], in_=AOh[ao][:]), r=[r_AOh[ao]], dma=r_AOh[ao])

        work = []
        for j in DEBUG.get("jobs", (0, 1)):
            for h in DEBUG.get("heads_a", range(HA)):
                work.append((j, h, False))
            for h in DEBUG.get("heads_n", range(HN)):
                work.append((j, h, True))
        cur_job = None
        if not work:
            P.emit_phase()
            return
        pre = load_head(*work[0])
        for wi, (j, h, nbr) in enumerate(work):
            hb, tb = pre
            if nbr and cur_job != j:
                cur_job = j
                P.op("sp", lambda e, j=j: e.dma_start(out=maskt[:], in_=k.maskd[j].ap()), w=[r_mask], dma=r_mask)
            if wi + 1 < len(work):
                pre = load_head(*work[wi + 1])
            ao = wi % 2
            if nbr:
                nbr_head(j, h, hb, tb, ao)
            else:
                diff_head(j, h, hb, ao)
        P.emit_phase()


def phase_c(k):
    nc, P = k.nc, k.P
    with ExitStack() as es:
        def sb(name, shape, dt=F32):
            return es.enter_context(nc.sbuf_tensor("C_" + name, list(shape), dt))

        R = P.res
        NW = 6
        ident = sb("ident", (128, 128), BF16)
        epsc = sb("epsc", (128, 1))
        gfb = sb("gfb", (128, D))
        convc = sb("convc", (128, FCH, 4))
        hflag = sb("hflag", (128, 4))
        cwe = sb("cwe", (128, 4, FCH))
        axT = sb("axT", (128, CK, 514), BF16)
        xmid = sb("xmid", (128, 4, D))
        xmh = sb("xmh", (2, D))
        hT = sb("hT", (128, FCH, 512), BF16)
        wr = [sb("wr%d" % i, (128, 4096), BF16) for i in range(NW)]
        xn2 = sb("xn2", (128, D), BF16)
        junk = sb("junk", (128, D), BF16)
        tb = [sb("tb%d" % i, (128, 2, 256)) for i in range(2)]
        ssq = sb("ssq", (128, 1))
        lnv = sb("lnv", (128, 1))
        rstd = sb("rstd", (128, 1))
        pb = es.enter_context(nc.psum_tensor("C_pb", [128, 8, 512], F32))

        r_ident, r_eps, r_gfb, r_convc, r_hflag, r_cwe = (R(n) for n in ("ident", "eps", "gfb", "convc", "hflag", "cwe"))
        r_axT = R("axT")
        r_xmid = [R("xmid%d" % i) for i in range(4)]
        r_xmh = R("xmh")
        r_hT = R("hT")
        r_wr = [R("wr%d" % i) for i in range(NW)]
        r_xn2, r_junk, r_ssq, r_lnv, r_rstd = (R(n) for n in ("xn2", "junk", "ssq", "lnv", "rstd"))
        r_tb = [R("tb0"), R("tb1")]
        r_pb = [R("pb%d" % i) for i in range(8)]
        cnt = {"w": 0, "pb": 0, "tb": 0, "u": 0}

        P.op("sp", lambda e: e.dma_start(out=ident[:], in_=k.ident.ap()), w=[r_ident], dma=r_ident)
        P.op("sp", lambda e: e.dma_start(out=gfb[:], in_=k.gfb.ap()), w=[r_gfb], dma=r_gfb)
        P.op("sp", lambda e: e.dma_start(out=convc[:], in_=k.convc.ap()), w=[r_convc], dma=r_convc)
        P.op("sp", lambda e: e.dma_start(out=hflag[:], in_=k.hflag.ap()), w=[r_hflag], dma=r_hflag)
        P.op("dve", lambda e: e.memset(epsc[:], EPS), w=[r_eps])
        pc = psz(convc)
        for q in range(4):
            ci = 0 if q % 2 == 0 else 2
            P.op("dve", lambda e, q=q, ci=ci: e.tensor_scalar(out=cwe[:, q, :], in0=mkap(convc, ci, [(pc, 128), (4, FCH)]),
                                                             scalar1=hflag[:, q:q + 1], scalar2=None, op0=ALU.mult),
                 r=[r_convc, r_hflag], w=[r_cwe])

        def wslot():
            i = cnt["w"] % NW
            cnt["w"] += 1
            return i

        def load_w_cols(src_dram, c0, ncols):
            i = wslot()
            src = src_dram[:, c0:c0 + ncols].rearrange("(ck p) f -> p ck f", p=128)
            dst = wr[i][:, 0:CK * ncols].rearrange("p (ck f) -> p ck f", f=ncols)
            P.op("sp", lambda e: e.dma_start(out=dst, in_=src), w=[r_wr[i]], dma=r_wr[i])
            return i

        def load_w_rows(src_dram, r0, nr):
            i = wslot()
            src = src_dram[r0 * 128:(r0 + nr) * 128, :].rearrange("(f p) n -> p f n", p=128)
            dst = wr[i][:, 0:nr * D].rearrange("p (f n) -> p f n", n=D)
            P.op("sp", lambda e: e.dma_start(out=dst, in_=src), w=[r_wr[i]], dma=r_wr[i])
            return i

        def wview_cols(i, ncols):
            return wr[i][:, 0:CK * ncols].rearrange("p (ck f) -> p ck f", f=ncols)

        def wview_rows(i, nr):
            return wr[i][:, 0:nr * D].rearrange("p (f n) -> p f n", n=D)

        def bank():
            b = cnt["pb"] % 6
            cnt["pb"] += 1
            return b

        pax = psz(axT)

        def rms_rows(rows, src_ap, r_src, dst_ap, r_dst):
            P.op("act", lambda e: e.activation(out=junk[0:rows, :], in_=src_ap, func=AF.Square, accum_out=ssq[0:rows, :]),
                 r=[r_src], w=[r_junk, r_ssq])
            P.op("act", lambda e: e.activation(out=lnv[0:rows, :], in_=ssq[0:rows, :], func=AF.Ln, scale=1.0 / D, bias=epsc[0:rows, :]),
                 r=[r_ssq, r_eps], w=[r_lnv])
            P.op("act", lambda e: e.activation(out=rstd[0:rows, :], in_=lnv[0:rows, :], func=AF.Exp, scale=-0.5), r=[r_lnv], w=[r_rstd])
            if dst_ap is not None:
                P.op("dve", lambda e: e.tensor_scalar(out=dst_ap, in0=src_ap, scalar1=rstd[0:rows, :], scalar2=None, op0=ALU.mult),
                     r=[r_src, r_rstd], w=[r_dst])

        def tile(j, T):
            e0 = 512 * T
            xs_flat = k.xs[j].ap().rearrange("s p d -> (s p) d")
            src = k.aoT[j][:, :, e0:e0 + 514].rearrange("c p n -> p c n")
            P.op("sp", lambda e: e.dma_start(out=axT[:], in_=src), w=[r_axT], dma=r_axT)
            for tc in range(4):
                r0 = 383 + e0 + 1 + 128 * tc
                P.op("sp", lambda e, tc=tc, r0=r0: e.dma_start(out=xmid[:, tc, :], in_=xs_flat[r0:r0 + 128, :]), w=[r_xmid[tc]], dma=r_xmid[tc])
            hsrc = mkap(k.xs[j], (383 + e0) * D, [(513 * D, 2), (1, D)])
            P.op("sp", lambda e: e.dma_start(out=xmh[:], in_=hsrc), w=[r_xmh], dma=r_xmh)
            NG = 256
            nxt = load_w_cols(k.wb_out, 0, NG)
            for cg in range(D // NG):
                wi = nxt
                if cg + 1 < D // NG:
                    nxt = load_w_cols(k.wb_out, (cg + 1) * NG, NG)
                wv = wview_cols(wi, NG)
                for tc in range(5):
                    b = bank()
                    if tc < 4:
                        rows = 128
                        def lhs(ck, tc=tc):
                            return axT[:, ck, 1 + 128 * tc:129 + 128 * tc]
                        dst = xmid[:, tc, cg * NG:(cg + 1) * NG]
                        rd = r_xmid[tc]
                    else:
                        rows = 2
                        def lhs(ck):
                            return mkap(axT, ck * 514, [(pax, 128), (513, 2)])
                        dst = xmh[0:2, cg * NG:(cg + 1) * NG]
                        rd = r_xmh
                    for ck in range(CK):
                        P.op("pe", lambda e, b=b, ck=ck, lhs=lhs, rows=rows, wv=wv: e.matmul(
                            pb[0:rows, b, 0:NG], lhsT=lhs(ck), rhs=wv[:, ck, :], start=(ck == 0), stop=(ck == CK - 1)),
                            r=[r_axT, r_wr[wi]], w=[r_pb[b]])
                    P.op("dve", lambda e, b=b, dst=dst, rows=rows: e.tensor_tensor(out=dst, in0=pb[0:rows, b, 0:NG], in1=dst, op=ALU.add),
                         r=[r_pb[b], rd], w=[rd])
            for tc in range(5):
                rows = 128 if tc < 4 else 2
                srcx = xmid[:, tc, :] if tc < 4 else xmh[0:2, :]
                rsrc = r_xmid[tc] if tc < 4 else r_xmh
                rms_rows(rows, srcx, rsrc, xn2[0:rows, :], r_xn2)
                if tc < 4:
                    for half in range(2):
                        pt = pb[:, 6 + half, :].bitcast(BF16)
                        for q in range(8):
                            ck = half * 8 + q
                            P.op("pe", lambda e, pt=pt, q=q, ck=ck: e.transpose(out=pt[:, q * 128:(q + 1) * 128], in_=xn2[:, ck * 128:(ck + 1) * 128], identity=ident[:]),
                                 r=[r_xn2, r_ident], w=[r_pb[6 + half]])
                        dst = axT[:, half * 8:(half + 1) * 8, 1 + 128 * tc:129 + 128 * tc]
                        srcp = pt[:, 0:1024].rearrange("p (a b) -> p a b", b=128)
                        P.op("dve", lambda e, dst=dst, srcp=srcp: e.tensor_copy(out=dst, in_=srcp), r=[r_pb[6 + half]], w=[r_axT])
                else:
                    pt = pb[:, 6, :].bitcast(BF16)
                    for ck in range(CK):
                        P.op("pe", lambda e, pt=pt, ck=ck: e.transpose(out=pt[:, 2 * ck:2 * ck + 2], in_=xn2[0:2, ck * 128:(ck + 1) * 128], identity=ident[0:2, 0:2]),
                             r=[r_xn2, r_ident], w=[r_pb[6]])
                    dst = mkap(axT, 0, [(pax, 128), (514, CK), (513, 2)])
                    srcp = pt[:, 0:2 * CK].rearrange("p (a b) -> p a b", b=2)
                    P.op("dve", lambda e, dst=dst, srcp=srcp: e.tensor_copy(out=dst, in_=srcp), r=[r_pb[6]], w=[r_axT])
            groups = [(f0, min(2, FCH - f0)) for f0 in range(0, FCH, 2)]

            def load_group(gi):
                f0, nf = groups[gi]
                return (load_w_cols(k.wb_up, f0 * 128, nf * 128), load_w_cols(k.wb_up, DFF + f0 * 128, nf * 128))

            pend = [load_group(0), load_group(1)]
            for gi, (f0, nf) in enumerate(groups):
                wa_i, wg_i = pend.pop(0)
                if gi + 2 < len(groups):
                    pend.append(load_group(gi + 2))
                wa = wview_cols(wa_i, nf * 128)
                wg = wview_cols(wg_i, nf * 128)
                for fl in range(nf):
                    fc = f0 + fl
                    u = cnt["u"] % 2
                    cnt["u"] += 1
                    ba = 3 * u
                    for ck in range(CK):
                        P.op("pe", lambda e, ba=ba, ck=ck, wa=wa, fl=fl: e.matmul(pb[:, ba, 0:258], lhsT=wa[:, ck, fl * 128:(fl + 1) * 128],
                                                                             rhs=axT[:, ck, 0:258], start=(ck == 0), stop=(ck == CK - 1)),
                             r=[r_axT, r_wr[wa_i]], w=[r_pb[ba]])
                    for ck in range(CK):
                        P.op("pe", lambda e, ba=ba, ck=ck, wa=wa, fl=fl: e.matmul(pb[:, ba + 1, 0:258], lhsT=wa[:, ck, fl * 128:(fl + 1) * 128],
                                                                             rhs=axT[:, ck, 256:514], start=(ck == 0), stop=(ck == CK - 1)),
                             r=[r_axT, r_wr[wa_i]], w=[r_pb[ba + 1]])
                    for ck in range(CK):
                        P.op("pe", lambda e, ba=ba, ck=ck, wg=wg, fl=fl: e.matmul(pb[:, ba + 2, :], lhsT=wg[:, ck, fl * 128:(fl + 1) * 128],
                                                                             rhs=axT[:, ck, 1:513], start=(ck == 0), stop=(ck == CK - 1)),
                             r=[r_axT, r_wr[wg_i]], w=[r_pb[ba + 2]])
                    ti = cnt["tb"] % 2
                    cnt["tb"] += 1
                    t_ = tb[ti]
                    rt = r_tb[ti]
                    ra = [r_pb[ba], r_pb[ba + 1]]
                    P.op("dve", lambda e, ba=ba, t_=t_, fc=fc: e.tensor_scalar(out=t_[:], in0=pb[:, ba:ba + 2, 0:256], scalar1=convc[:, fc, 0:1],
                                                                           scalar2=convc[:, fc, 3:4], op0=ALU.mult, op1=ALU.add),
                         r=ra + [r_convc], w=[rt])
                    if T == 0:
                        P.op("dve", lambda e, ba=ba, t_=t_, fc=fc: e.tensor_scalar(out=t_[:, 0, 0:1], in0=pb[:, ba, 0:1], scalar1=cwe[:, 2 * j, fc:fc + 1],
                                                                               scalar2=convc[:, fc, 3:4], op0=ALU.mult, op1=ALU.add),
                             r=ra + [r_convc, r_cwe], w=[rt])
                    P.op("dve", lambda e, ba=ba, t_=t_, fc=fc: e.scalar_tensor_tensor(out=t_[:], in0=pb[:, ba:ba + 2, 1:257], scalar=convc[:, fc, 1:2],
                                                                                  in1=t_[:], op0=ALU.mult, op1=ALU.add),
                         r=ra + [r_convc, rt], w=[rt])
                    if T == 3:
                        P.op("dve", lambda e, ba=ba, t_=t_, fc=fc: e.scalar_tensor_tensor(out=t_[:, 1, 255:256], in0=pb[:, ba + 1, 257:258], scalar=cwe[:, 2 * j + 1, fc:fc + 1],
                                                                                      in1=t_[:, 1, 255:256], op0=ALU.mult, op1=ALU.add),
                             r=ra + [r_cwe, rt], w=[rt])
                        P.op("dve", lambda e, ba=ba, t_=t_, fc=fc: e.scalar_tensor_tensor(out=t_[:, :, 0:255], in0=pb[:, ba:ba + 2, 2:257], scalar=convc[:, fc, 2:3],
                                                                                      in1=t_[:, :, 0:255], op0=ALU.mult, op1=ALU.add),
                             r=ra + [r_convc, rt], w=[rt])
                        P.op("dve", lambda e, ba=ba, t_=t_, fc=fc: e.scalar_tensor_tensor(out=t_[:, 0, 255:256], in0=pb[:, ba, 257:258], scalar=convc[:, fc, 2:3],
                                                                                      in1=t_[:, 0, 255:256], op0=ALU.mult, op1=ALU.add),
                             r=ra + [r_convc, rt], w=[rt])
                    else:
                        P.op("dve", lambda e, ba=ba, t_=t_, fc=fc: e.scalar_tensor_tensor(out=t_[:], in0=pb[:, ba:ba + 2, 2:258], scalar=convc[:, fc, 2:3],
                                                                                      in1=t_[:], op0=ALU.mult, op1=ALU.add),
                             r=ra + [r_convc, rt], w=[rt])
                    P.op("act", lambda e, t_=t_: e.activation(out=t_[:], in_=t_[:], func=AF.Gelu_apprx_tanh), r=[rt], w=[rt])
                    P.op("dve", lambda e, ba=ba, t_=t_, fc=fc: e.tensor_tensor(out=hT[:, fc, :], in0=t_[:].rearrange("p a b -> p (a b)"), in1=pb[:, ba + 2, :], op=ALU.mult),
                         r=[rt, r_pb[ba + 2]], w=[r_hT])
            dgroups = [(f0, min(2, FCH - f0)) for f0 in range(0, FCH, 2)]
            for hh in range(2):
                pend = [load_w_rows(k.wb_down, dgroups[0][0], dgroups[0][1]), load_w_rows(k.wb_down, dgroups[1][0], dgroups[1][1])]
                for gi, (f0, nf) in enumerate(dgroups):
                    wi = pend.pop(0)
                    if gi + 2 < len(dgroups):
                        pend.append(load_w_rows(k.wb_down, dgroups[gi + 2][0], dgroups[gi + 2][1]))
                    wv = wview_rows(wi, nf)
                    for fl in range(nf):
                        fc = f0 + fl
                        for tl in range(2):
                            tc = 2 * hh + tl
                            for cg in range(4):
                                b = tl * 4 + cg
                                P.op("pe", lambda e, b=b, fc=fc, tc=tc, fl=fl, cg=cg, wv=wv: e.matmul(
                                    pb[:, b, :], lhsT=hT[:, fc, tc * 128:(tc + 1) * 128], rhs=wv[:, fl, cg * 512:(cg + 1) * 512],
                                    start=(fc == 0), stop=(fc == FCH - 1)),
                                    r=[r_hT, r_wr[wi]], w=[r_pb[b]])
                for tl in range(2):
                    tc = 2 * hh + tl
                    for cg in range(4):
                        b = tl * 4 + cg
                        dst = xmid[:, tc, cg * 512:(cg + 1) * 512]
                        P.op("dve", lambda e, b=b, dst=dst: e.tensor_tensor(out=dst, in0=pb[:, b, :], in1=dst, op=ALU.add),
                             r=[r_pb[b], r_xmid[tc]], w=[r_xmid[tc]])
                    rms_rows(128, xmid[:, tc, :], r_xmid[tc], None, None)
                    P.op("dve", lambda e, tc=tc: e.scalar_tensor_tensor(out=xmid[:, tc, :], in0=xmid[:, tc, :], scalar=rstd[:, :], in1=gfb[:],
                                                                       op0=ALU.mult, op1=ALU.mult),
                         r=[r_xmid[tc], r_rstd, r_gfb], w=[r_xmid[tc]])
                    row0 = 512 * T + 128 * tc
                    P.op("sp", lambda e, tc=tc, row0=row0: e.dma_start(out=k.y[j][row0:row0 + 128, :], in_=xmid[:, tc, :]), r=[r_xmid[tc]], dma=r_xmid[tc])

        for j in DEBUG.get("jobs", (0, 1)):
            for T in DEBUG.get("tiles_c", range(4)):
                tile(j, T)
        P.emit_phase()


def prepare_inputs(inp):
    f32 = np.float32
    x_prompt = np.asarray(inp["x_prompt"], f32)
    x_sample = np.asarray(inp["x_sample"], f32)
    shared = {}
    shared["w_in"] = np.ascontiguousarray(np.asarray(inp["w_in"], f32)[0])
    shared["w_out"] = np.ascontiguousarray(np.asarray(inp["w_out"], f32)[0])
    shared["w_up"] = np.ascontiguousarray(np.asarray(inp["w_up"], f32)[0])
    shared["w_down"] = np.ascontiguousarray(np.asarray(inp["w_down"], f32)[0])
    g1 = np.asarray(inp["norm1_g"], f32)[0].reshape(CK, 128).T
    g2 = np.asarray(inp["norm2_g"], f32)[0].reshape(CK, 128).T
    shared["g12c"] = np.ascontiguousarray(np.stack([g1, g2], axis=1))
    shared["gfb"] = bcast128(inp["final_g"])
    shared["sgb"] = bcast128(np.asarray(inp["subln_g"], f32)[0])
    lam = np.concatenate([np.asarray(inp[n], f32)[0] for n in ("lambda_q1", "lambda_k1", "lambda_q2", "lambda_k2")])
    shared["lamv"] = bcast128(lam).reshape(128, 4, 64)
    tab = np.asarray(inp["rel_bias_table"], f32)
    shared["tab"] = np.ascontiguousarray(tab)
    shared["tablr"] = bcast128(np.concatenate([tab[15], tab[31]])).reshape(128, 2, 8)
    rel = np.arange(1536) - 767
    bk = t5_bucket_np(rel)
    ohu = np.zeros((32, 1536), f32)
    ohu[bk, np.arange(1536)] = 1.0
    shared["ohu"] = ohu
    rpbp = np.zeros((8, 15, 128), f32)
    rpbp[:, :, 48:79] = np.asarray(inp["na_rpb"], f32)[0]
    shared["rpbp"] = rpbp
    cw = np.asarray(inp["conv_w"], f32)[0]
    cb = np.asarray(inp["conv_b"], f32)[0]
    cc = np.stack([cw[0], cw[1], cw[2], cb], axis=1)
    shared["convc"] = np.ascontiguousarray(cc.reshape(FCH, 128, 4).transpose(1, 0, 2))
    shared["ident"] = np.eye(128, dtype=f32).astype(ml_dtypes.bfloat16)

    in_maps = []
    for c in range(NCORES):
        m = dict(shared)
        hflag = np.zeros((128, 4), f32)
        for j in range(2):
            nblk, S = JOBS[j]
            if j == 0:
                seq, t = x_prompt[c // 4], c % 4
            else:
                seq, t = x_sample[c // 2], c % 2
            o, blocks, near_true = job_geometry(nblk, S, t)
            xs = np.zeros((S, 128, D), f32)
            ws = np.zeros((3, S), f32)
            for s, gb in enumerate(blocks):
                if gb < 0:
                    continue
                xs[s] = seq[gb * 128:(gb + 1) * 128]
                if 2 <= s <= 19:
                    ws[0, s] = 1.0
                elif gb < o:
                    ws[1, s] = 1.0
                else:
                    ws[2, s] = 1.0
            m["xs%d" % j] = xs
            m["wsel%d" % j] = np.ascontiguousarray(np.broadcast_to(ws[None], (128, 3, S)))
            m["maskd%d" % j] = build_masks(nblk, t, o, near_true)
            hflag[:, 2 * j] = 1.0 if o > 0 else 0.0
            hflag[:, 2 * j + 1] = 1.0 if o + 16 < nblk else 0.0
        m["hflag"] = hflag
        in_maps.append(m)
    return in_maps


def kernel(**inputs):
    in_maps = prepare_inputs(inputs)
    nc = build_program()
    res = run_bass_kernel_spmd(nc, in_maps, core_ids=list(range(NCORES)))
    yp = np.zeros((2, 8192, D), np.float32)
    ysm = np.zeros((4, 4096, D), np.float32)
    for c in range(NCORES):
        r = res.results[c]
        yp[c // 4, (c % 4) * 2048:(c % 4 + 1) * 2048] = r["y0"]
        ysm[c // 2, (c % 2) * 2048:(c % 2 + 1) * 2048] = r["y1"]
    return (yp, ysm)
```

```python
import math
from contextlib import ExitStack

import numpy as np
import ml_dtypes

import concourse.bass as bass
import concourse.mybir as mybir
from concourse.bass_utils import run_bass_kernel_spmd

F32 = mybir.dt.float32
BF16 = mybir.dt.bfloat16
AF = mybir.ActivationFunctionType
ALU = mybir.AluOpType
AX = mybir.AxisListType

D = 2048
CK = 16
HA = 8
HN = 8
INC = 6144
DFF = 5504
FCH = 43
EPS = 1e-6
NCORES = 8
NEAR = 22
JOBS = ((64, 65), (32, 33))
SCALE_A = 0.125
SCALE_N = 128 ** -0.5
NEG = -3.0e5
LAM_INIT = 0.8 - 0.6 * math.exp(-0.3 * 0)

DEBUG = {"stop_after": None, "ext": False}


class Sem:
    def __init__(self, h, name):
        self.h = h
        self.v = 0
        self.name = name


class Res:
    __slots__ = ("name", "wr", "rd", "dsem")

    def __init__(self, name):
        self.name = name
        self.wr = None
        self.rd = []
        self.dsem = None


class Op:
    __slots__ = ("eng", "fn", "deps", "sig", "ev", "dma")


ENGS = ("pe", "act", "dve", "pool", "sp")


class Prog:
    def __init__(self, nc):
        self.nc = nc
        self.esem = {e: Sem(nc.alloc_semaphore(name="es_" + e), e) for e in ("pe", "act", "dve", "pool")}
        self.bar = Sem(nc.alloc_semaphore(name="bar"), "bar")
        self.free_dsems = []
        self.ndsem = 0
        self.reset()
        self.waited = {e: {} for e in ENGS}
        self.nphase = 0
        self.all_res = []

    def reset(self):
        self.ops = {e: [] for e in ENGS}
        self.order = []

    def res(self, name):
        r = Res(name)
        self.all_res.append(r)
        return r

    def _dsem(self, r):
        if r.dsem is None:
            if self.free_dsems:
                r.dsem = self.free_dsems.pop()
            else:
                r.dsem = Sem(self.nc.alloc_semaphore(name="ds%d" % self.ndsem), "ds%d" % self.ndsem)
                self.ndsem += 1
        return r.dsem

    def op(self, eng, fn, r=(), w=(), dma=None, after=()):
        o = Op()
        o.eng = eng
        o.fn = fn
        o.dma = dma
        o.sig = False
        o.ev = None
        deps = []
        seen = set()

        def add(d):
            if d is None or id(d) in seen:
                return
            seen.add(id(d))
            deps.append(d)

        for d in after:
            add(d)
        for x in r:
            add(x.wr)
        for x in w:
            add(x.wr)
            for d in x.rd:
                add(d)
        o.deps = [d for d in deps if not (d.eng == "pe" and eng == "pe" and d.dma is None)]
        for d in o.deps:
            d.sig = True
        for x in w:
            x.wr = o
            x.rd = []
        for x in r:
            x.rd.append(o)
        self.ops[eng].append(o)
        self.order.append(o)
        return o

    def emit_phase(self):
        nc = self.nc
        for o in self.order:
            if o.dma is not None:
                s = self._dsem(o.dma)
                s.v += 16
                o.ev = (s, s.v)
        for e in ("pe", "act", "dve", "pool"):
            ops = self.ops[e]
            lo = [o for o in ops if o.dma is None]
            if lo:
                lo[-1].sig = True
            for o in ops:
                if o.dma is None and o.sig:
                    s = self.esem[e]
                    s.v += 1
                    o.ev = (s, s.v)
        self.nphase += 1
        bar_target = self.nphase * len(ENGS)
        prog = self

        def run(e, eng):
            waited = prog.waited[e]

            def wait(ev):
                s, v = ev
                if waited.get(id(s), 0) < v:
                    eng.wait_ge(s.h, v)
                    waited[id(s)] = v

            last_dma = {}
            for o in prog.ops[e]:
                need = {}
                for d in o.deps:
                    sm, v = d.ev
                    if need.get(id(sm), (None, 0))[1] < v:
                        need[id(sm)] = (sm, v)
                for ev in need.values():
                    wait(ev)
                ins = o.fn(eng)
                if o.dma is not None:
                    ins.then_inc(o.ev[0].h, 16)
                    last_dma[id(o.ev[0])] = o.ev
                elif o.sig:
                    ins.then_inc(o.ev[0].h, 1)
            for ev in last_dma.values():
                wait(ev)
            if e in prog.esem and prog.ops[e]:
                lo = [o for o in prog.ops[e] if o.dma is None]
                if lo:
                    wait(lo[-1].ev)
            eng.sem_inc(prog.bar.h, 1)
            eng.wait_ge(prog.bar.h, bar_target)

        with nc.Block() as block:
            @block.tensor
            def _(eng):
                run("pe", eng)

            @block.scalar
            def _(eng):
                run("act", eng)

            @block.vector
            def _(eng):
                run("dve", eng)

            @block.gpsimd
            def _(eng):
                run("pool", eng)

            @block.sync
            def _(eng):
                run("sp", eng)

        for r in self.all_res:
            r.wr = None
            r.rd = []
            if r.dsem is not None:
                self.free_dsems.append(r.dsem)
                r.dsem = None
        self.all_res = []
        self.reset()


def mkap(t, off, dims):
    return bass.AP(t, off, [list(d) for d in dims])


def psz(t):
    return t[:].ap[0][0]


def t5_bucket_np(rel):
    nb = 16
    me = 8
    ret = np.where(rel > 0, nb, 0)
    n = np.abs(rel)
    nf = np.maximum(n, 1).astype(np.float32)
    large = me + (np.log(nf / np.float32(me)) / np.float32(math.log(128 / 8)) * np.float32(nb - me)).astype(np.int32)
    large = np.minimum(large, nb - 1)
    return ret + np.where(n < me, n, large)


def job_geometry(nblk, nslots, t):
    o = 16 * t
    blocks = [-1] * nslots
    near_true = [False] * NEAR
    used = set()
    for n in range(NEAR):
        gb = o + n - 3
        if 0 <= gb < nblk:
            blocks[n] = gb
            near_true[n] = True
            used.add(gb)
    rest = [b for b in range(nblk) if b not in used]
    for n in range(NEAR):
        if blocks[n] == -1 and n not in (2, 19) and rest:
            blocks[n] = rest.pop(0)
    for s in range(NEAR, nslots):
        if rest:
            blocks[s] = rest.pop(0)
    assert not rest
    return o, blocks, near_true


def nbr_units():
    units = []
    for i in range(16):
        if i == 0:
            dl = list(range(-2, 4))
        elif i == 15:
            dl = list(range(-3, 3))
        else:
            dl = list(range(-2, 3))
        units.append((i, dl))
    units.append((-1, list(range(-2, 3))))
    units.append((16, list(range(-2, 3))))
    return units


def mask_layout():
    off = {}
    col = 0
    for key, nd, w in (("int", 5, 128), (0, 6, 128), (1, 5, 128), (14, 5, 128), (15, 6, 128), (-1, 5, 1), (16, 5, 1)):
        off[key] = col
        col += nd * w
    return off, col


def build_masks(nblk, t, o, near_true):
    R = nblk * 2
    L = nblk * 128
    off, ncol = mask_layout()
    out = np.zeros((128, ncol), np.float32)

    def tile(tq, valid_q, nk):
        if not valid_q:
            return np.zeros((128, len(tq)), np.float32)
        if not near_true[nk]:
            return np.full((128, len(tq)), NEG, np.float32)
        tk = (o + nk - 3) * 128 + np.arange(128)
        r = tq // 64
        c = tq % 64
        rs = np.clip(r - 4, 0, R - 8)
        cs = np.clip(c - 8, 0, 64 - 16)
        rk = (tk // 64)[:, None]
        ckk = (tk % 64)[:, None]
        ok = (rk >= rs[None]) & (rk < rs[None] + 8) & (ckk >= cs[None]) & (ckk < cs[None] + 16)
        return np.where(ok, 0.0, NEG).astype(np.float32)

    units = dict(nbr_units())
    per_unit = {}
    for i, dl in units.items():
        if i == -1:
            tq = np.array([o * 128 - 1])
            vq = o > 0
        elif i == 16:
            tq = np.array([(o + 16) * 128])
            vq = (o + 16) < nblk
        else:
            tq = (o + i) * 128 + np.arange(128)
            vq = True
        per_unit[i] = [tile(tq, vq, i + dl_ + 3) for dl_ in dl]
    for i in range(2, 14):
        for a, b in zip(per_unit[i], per_unit[7]):
            assert np.array_equal(a, b)
    def put(key, tiles):
        c0 = off[key]
        for k, tl in enumerate(tiles):
            w = tl.shape[1]
            out[:, c0 + k * w:c0 + (k + 1) * w] = tl
    put("int", per_unit[7])
    for key in (0, 1, 14, 15, -1, 16):
        put(key, per_unit[key])
    return out.astype(ml_dtypes.bfloat16)


def bcast128(a):
    a = np.asarray(a, np.float32)
    return np.ascontiguousarray(np.broadcast_to(a.reshape(1, -1), (128, a.size)))


class K:
    pass


def build_program():
    nc = bass.Bass("TRN2", target_bir_lowering=False)
    P = Prog(nc)
    k = K()
    k.nc = nc
    k.P = P
    ext = DEBUG["ext"]

    def din(name, shape, dt=F32):
        return nc.dram_tensor(name, list(shape), dt, kind="ExternalInput")

    def dscr(name, shape, dt=BF16):
        return nc.dram_tensor(name, list(shape), dt, kind=("ExternalOutput" if (ext and name in ext) else "Internal"))

    k.xs = [din("xs%d" % j, (JOBS[j][1], 128, D)) for j in range(2)]
    k.w_in = din("w_in", (D, INC))
    k.w_out = din("w_out", (D, D))
    k.w_up = din("w_up", (D, 2 * DFF))
    k.w_down = din("w_down", (DFF, D))
    k.g12c = din("g12c", (128, 2, CK))
    k.gfb = din("gfb", (128, D))
    k.sgb = din("sgb", (128, 128))
    k.lamv = din("lamv", (128, 4, 64))
    k.tab = din("tab", (32, 8))
    k.tablr = din("tablr", (128, 2, 8))
    k.ohu = din("ohu", (32, 1536))
    k.rpbp = din("rpbp", (8, 15, 128))
    k.convc = din("convc", (128, FCH, 4))
    k.ident = din("ident", (128, 128), BF16)
    k.wsel = [din("wsel%d" % j, (128, 3, JOBS[j][1])) for j in range(2)]
    _, mcols = mask_layout()
    k.maskd = [din("maskd%d" % j, (128, mcols), BF16) for j in range(2)]
    k.hflag = din("hflag", (128, 4))
    k.y = [nc.dram_tensor("y%d" % j, [2048, D], F32, kind="ExternalOutput") for j in range(2)]
    k.wb_in = dscr("wb_in", (D, INC))
    k.wb_out = dscr("wb_out", (D, D))
    k.wb_up = dscr("wb_up", (D, 2 * DFF))
    k.wb_down = dscr("wb_down", (DFF, D))
    k.qaT = [dscr("qaT%d" % j, (HA, 128, NEAR * 128)) for j in range(2)]
    k.qnT = [dscr("qnT%d" % j, (HN, 128, NEAR * 128)) for j in range(2)]
    k.knT = [dscr("knT%d" % j, (HN, 128, NEAR * 128)) for j in range(2)]
    k.kaT = [dscr("kaT%d" % j, (HA, 128, JOBS[j][1] * 128)) for j in range(2)]
    k.va = [dscr("va%d" % j, (HA, 128, JOBS[j][1], 129)) for j in range(2)]
    k.vn = [dscr("vn%d" % j, (HN, 128, NEAR, 129)) for j in range(2)]
    k.aoT = [dscr("aoT%d" % j, (16, 128, 2050)) for j in range(2)]
    k.u2 = dscr("u2", (8, 1536), F32)
    k.hsc = dscr("hsc", (8, 128, 1408), F32)

    if not DEBUG.get("skip_a"):
        phase_a(k)
    if DEBUG["stop_after"] == "A":
        return nc
    phase_b(k)
    if DEBUG["stop_after"] == "B":
        return nc
    phase_c(k)
    return nc


def phase_a(k):
    nc, P = k.nc, k.P
    with ExitStack() as es:
        def sb(name, shape, dt=F32):
            return es.enter_context(nc.sbuf_tensor("A_" + name, list(shape), dt))

        def ps(name, shape, dt=F32):
            return es.enter_context(nc.psum_tensor("A_" + name, list(shape), dt))

        ident = sb("ident", (128, 128), BF16)
        g12c = sb("g12c", (128, 2, CK))
        tablr = sb("tablr", (128, 2, 8))
        elr = sb("elr", (128, 2, 8))
        epsc = sb("epsc", (128, 1))
        wsel = [sb("wsel%d" % j, (128, 3, JOBS[j][1])) for j in range(2)]
        wfull = [sb("wfull%d" % j, (128, JOBS[j][1], 8)) for j in range(2)]
        wtmp = sb("wtmp", (128, 65, 8))
        CW = 1024
        cin = [sb("cin%d" % i, (128, 1376)) for i in range(2)]
        cout = [sb("cout%d" % i, (128, 1376), BF16) for i in range(2)]
        xbuf = [sb("xbuf%d" % i, (128, D)) for i in range(3)]
        junk = sb("junk", (128, D), BF16)
        ssq = [sb("ssq%d" % i, (128, 1)) for i in range(3)]
        lnv = [sb("lnv%d" % i, (128, 1)) for i in range(3)]
        rstd = [sb("rstd%d" % i, (128, 1)) for i in range(3)]
        xn = [sb("xn%d" % i, (128, D), BF16) for i in range(2)]
        xnT = [sb("xnT%d" % i, (128, CK, 512), BF16) for i in range(2)]
        wring = [sb("wring%d" % i, (128, CK, 512), BF16) for i in range(3)]
        fmst = [sb("fmst%d" % i, (128, 4, 512), BF16) for i in range(2)]
        vst = [sb("vst%d" % i, (128, 8, 4, 129), BF16) for i in range(2)]
        pT = [ps("pT%d" % i, (128, 8 * 128), BF16) for i in range(2)]
        pO = [ps("pO%d" % i, (128, 512)) for i in range(4)]

        R = P.res
        r_ident, r_g, r_tablr, r_elr, r_eps = R("ident"), R("g12c"), R("tablr"), R("elr"), R("eps")
        r_wsel = [R("wsel0"), R("wsel1")]
        r_wfull = [R("wfull0"), R("wfull1")]
        r_wtmp = R("wtmp")

        P.op("sp", lambda e: e.dma_start(out=ident[:], in_=k.ident.ap()), w=[r_ident], dma=r_ident)
        P.op("sp", lambda e: e.dma_start(out=g12c[:], in_=k.g12c.ap()), w=[r_g], dma=r_g)
        P.op("sp", lambda e: e.dma_start(out=tablr[:], in_=k.tablr.ap()), w=[r_tablr], dma=r_tablr)
        for j in range(2):
            P.op("sp", lambda e, j=j: e.dma_start(out=wsel[j][:], in_=k.wsel[j].ap()), w=[r_wsel[j]], dma=r_wsel[j])
        P.op("dve", lambda e: e.memset(epsc[:], EPS), w=[r_eps])
        P.op("act", lambda e: e.activation(out=elr[:], in_=tablr[:], func=AF.Exp), r=[r_tablr], w=[r_elr])
        for j in range(2):
            S = JOBS[j][1]
            wf, ws = wfull[j], wsel[j]
            pw, pe_, pt = psz(wf), psz(ws), psz(wtmp)
            pl = psz(elr)

            def bc_s(c, ws=ws, pe_=pe_, S=S):
                return mkap(ws, c * S, [(pe_, 128), (1, S), (0, 8)])

            def bc_h(c, S=S, pl=pl):
                return mkap(elr, c * 8, [(pl, 128), (0, S), (1, 8)])

            wt = mkap(wtmp, 0, [(pt, 128), (8, S), (1, 8)])
            P.op("dve", lambda e, wf=wf, bc_s=bc_s, bc_h=bc_h: e.tensor_tensor(out=wf[:], in0=bc_s(1), in1=bc_h(0), op=ALU.mult),
                 r=[r_wsel[j], r_elr], w=[r_wfull[j]])
            P.op("dve", lambda e, wt=wt, bc_s=bc_s, bc_h=bc_h: e.tensor_tensor(out=wt, in0=bc_s(2), in1=bc_h(1), op=ALU.mult),
                 r=[r_wsel[j], r_elr], w=[r_wtmp])
            P.op("dve", lambda e, wf=wf, wt=wt: e.tensor_tensor(out=wf[:], in0=wf[:], in1=wt, op=ALU.add),
                 r=[r_wtmp], w=[r_wfull[j]])
            P.op("dve", lambda e, wf=wf, bc_s=bc_s: e.tensor_tensor(out=wf[:], in0=wf[:], in1=bc_s(0), op=ALU.add),
                 r=[r_wsel[j]], w=[r_wfull[j]])

        r_cin = [R("cin0"), R("cin1")]
        r_cout = [R("cout0"), R("cout1")]
        conv_state = {"n": 0}

        def convert(src, dst, nrows, ncols, tw, gidx):
            stores = []
            for rb in range(nrows // 128):
                for c0 in range(0, ncols, tw):
                    i = conv_state["n"] % 2
                    conv_state["n"] += 1
                    sv = src[rb * 128:(rb + 1) * 128, c0:c0 + tw]
                    dv = dst[rb * 128:(rb + 1) * 128, c0:c0 + tw]
                    P.op("pool", lambda e, i=i, sv=sv, tw=tw: e.dma_start(out=cin[i][:, 0:tw], in_=sv), w=[r_cin[i]], dma=r_cin[i])
                    if gidx is None:
                        P.op("pool", lambda e, i=i, tw=tw: e.tensor_copy(out=cout[i][:, 0:tw], in_=cin[i][:, 0:tw]),
                             r=[r_cin[i]], w=[r_cout[i]])
                    else:
                        P.op("pool", lambda e, i=i, tw=tw, rb=rb, gidx=gidx: e.tensor_scalar(
                            out=cout[i][:, 0:tw], in0=cin[i][:, 0:tw], scalar1=g12c[:, gidx, rb:rb + 1], scalar2=1.0, op0=ALU.mult, op1=ALU.mult),
                            r=[r_cin[i], r_g], w=[r_cout[i]])
                    st = P.op("pool", lambda e, i=i, dv=dv, tw=tw: e.dma_start(out=dv, in_=cout[i][:, 0:tw]), r=[r_cout[i]], dma=r_cout[i])
                    stores.append(st)
            return stores

        st_in = convert(k.w_in, k.wb_in, D, INC, 1024, 0)

        r_x = [R("xbuf%d" % i) for i in range(3)]
        r_junk = R("junk")
        r_ssq = [R("ssq%d" % i) for i in range(3)]
        r_lnv = [R("lnv%d" % i) for i in range(3)]
        r_rstd = [R("rstd%d" % i) for i in range(3)]
        r_xn = [R("xn%d" % i) for i in range(2)]
        r_xnT = [[R("xnT%d_%d" % (i, b)) for b in range(4)] for i in range(2)]
        r_w = [R("wring%d" % i) for i in range(3)]
        r_fm = [R("fmst%d" % i) for i in range(2)]
        r_vst = [R("vst%d" % i) for i in range(2)]
        r_pT = [R("pT%d" % i) for i in range(2)]
        r_pO = [R("pO%d" % i) for i in range(4)]
        cnt = {"x": 0, "xn": 0, "pT": 0, "w": 0, "pO": 0, "fm": 0, "vst": 0, "tile": 0}

        SUBS_NEAR = [("qa", 0), ("qa", 512), ("ka", 1024), ("ka", 1536), ("va", 2048), ("va", 2560),
                     ("qn", 3072), ("qn", 3584), ("kn", 4096), ("kn", 4608), ("vn", 5120), ("vn", 5632)]
        SUBS_FAR = [("ka", 1024), ("ka", 1536), ("va", 2048), ("va", 2560)]

        def load_x(j, s):
            i = cnt["x"] % 3
            cnt["x"] += 1
            P.op("sp", lambda e, i=i, j=j, s=s: e.dma_start(out=xbuf[i][:], in_=k.xs[j][s]), w=[r_x[i]], dma=r_x[i])
            return i

        def norm_block(xi, xnT_i, b):
            ni = cnt["xn"] % 2
            cnt["xn"] += 1
            P.op("act", lambda e: e.activation(out=junk[:], in_=xbuf[xi][:], func=AF.Square, accum_out=ssq[xi][:]),
                 r=[r_x[xi]], w=[r_junk, r_ssq[xi]])
            P.op("act", lambda e: e.activation(out=lnv[xi][:], in_=ssq[xi][:], func=AF.Ln, scale=1.0 / D, bias=epsc[:]),
                 r=[r_ssq[xi], r_eps], w=[r_lnv[xi]])
            P.op("act", lambda e: e.activation(out=rstd[xi][:], in_=lnv[xi][:], func=AF.Exp, scale=-0.5),
                 r=[r_lnv[xi]], w=[r_rstd[xi]])
            P.op("dve", lambda e: e.tensor_scalar(out=xn[ni][:], in0=xbuf[xi][:], scalar1=rstd[xi][:], scalar2=None, op0=ALU.mult),
                 r=[r_x[xi], r_rstd[xi]], w=[r_xn[ni]])
            for half in range(2):
                pi = cnt["pT"] % 2
                cnt["pT"] += 1
                for q in range(8):
                    ck = half * 8 + q
                    P.op("pe", lambda e, pi=pi, q=q, ck=ck: e.transpose(out=pT[pi][:, q * 128:(q + 1) * 128],
                                                                        in_=xn[ni][:, ck * 128:(ck + 1) * 128], identity=ident[:]),
                         r=[r_xn[ni], r_ident], w=[r_pT[pi]])
                dst = xnT[xnT_i][:, half * 8:(half + 1) * 8, b * 128:(b + 1) * 128]
                src = pT[pi][:].rearrange("p (a b) -> p a b", b=128)
                P.op("dve", lambda e, dst=dst, src=src: e.tensor_copy(out=dst, in_=src), r=[r_pT[pi]], w=[r_xnT[xnT_i][b]])

        def load_w(col0, first):
            i = cnt["w"] % 3
            cnt["w"] += 1
            src = k.wb_in[:, col0:col0 + 512].rearrange("(ck p) f -> p ck f", p=128)
            P.op("sp", lambda e, i=i, src=src: e.dma_start(out=wring[i][:], in_=src), w=[r_w[i]], dma=r_w[i],
                 after=(st_in if first else ()))
            return i

        def do_tile(j, s0, nb, near, xnT_i, prefetch):
            S = JOBS[j][1]
            N = nb * 128
            subs = SUBS_NEAR if near else SUBS_FAR
            wq = []
            state = {"first": cnt["w"] == 0}
            nxt = load_w(subs[0][1], state["first"])
            for si, (kind, col0) in enumerate(subs):
                wi = nxt
                if si + 1 < len(subs):
                    nxt = load_w(subs[si + 1][1], False)
                if si == 1 and prefetch is not None:
                    prefetch()
                hb = (col0 % 1024) // 128
                if kind in ("qa", "ka", "qn", "kn"):
                    fi = cnt["fm"] % 2
                    cnt["fm"] += 1
                    for fc in range(4):
                        oi = cnt["pO"] % 4
                        cnt["pO"] += 1
                        for ck in range(CK):
                            P.op("pe", lambda e, oi=oi, wi=wi, fc=fc, ck=ck: e.matmul(
                                pO[oi][:, 0:N], lhsT=wring[wi][:, ck, fc * 128:(fc + 1) * 128], rhs=xnT[xnT_i][:, ck, 0:N],
                                start=(ck == 0), stop=(ck == CK - 1)),
                                r=[r_w[wi]] + r_xnT[xnT_i][0:nb], w=[r_pO[oi]])
                        P.op("act", lambda e, oi=oi, fi=fi, fc=fc: e.activation(out=fmst[fi][:, fc, 0:N], in_=pO[oi][:, 0:N], func=AF.Copy),
                             r=[r_pO[oi]], w=[r_fm[fi]])
                    dstT = {"qa": k.qaT, "ka": k.kaT, "qn": k.qnT, "kn": k.knT}[kind][j]
                    dv = dstT[hb:hb + 4, :, s0 * 128:s0 * 128 + N].rearrange("h p n -> p h n")
                    P.op("act", lambda e, fi=fi, dv=dv: e.dma_start(out=dv, in_=fmst[fi][:, :, 0:N]), r=[r_fm[fi]], dma=r_fm[fi])
                else:
                    if hb == 0:
                        vi = cnt["vst"] % 2
                        cnt["vst"] += 1
                        state["vi"] = vi
                    vi = state["vi"]
                    for b in range(nb):
                        oi = cnt["pO"] % 4
                        cnt["pO"] += 1
                        for ck in range(CK):
                            P.op("pe", lambda e, oi=oi, wi=wi, b=b, ck=ck: e.matmul(
                                pO[oi][:, :], lhsT=xnT[xnT_i][:, ck, b * 128:(b + 1) * 128], rhs=wring[wi][:, ck, :],
                                start=(ck == 0), stop=(ck == CK - 1)),
                                r=[r_w[wi], r_xnT[xnT_i][b]], w=[r_pO[oi]])
                        src = pO[oi][:].rearrange("p (h e) -> p h e", e=128)
                        dst = vst[vi][:, hb:hb + 4, b, 0:128]
                        if kind == "va":
                            wf = wfull[j]
                            wb = mkap(wf, (s0 + b) * 8 + hb, [(psz(wf), 128), (1, 4), (0, 128)])
                            P.op("dve", lambda e, dst=dst, src=src, wb=wb: e.tensor_tensor(out=dst, in0=src, in1=wb, op=ALU.mult),
                                 r=[r_pO[oi], r_wfull[j]], w=[r_vst[vi]])
                            if hb == 4:
                                ones_dst = vst[vi][:, :, b, 128:129]
                                wsrc = mkap(wf, (s0 + b) * 8, [(psz(wf), 128), (1, 8), (1, 1)])
                                P.op("dve", lambda e, ones_dst=ones_dst, wsrc=wsrc: e.tensor_copy(out=ones_dst, in_=wsrc),
                                     r=[r_wfull[j]], w=[r_vst[vi]])
                        else:
                            P.op("act", lambda e, dst=dst, src=src: e.activation(out=dst, in_=src, func=AF.Copy),
                                 r=[r_pO[oi]], w=[r_vst[vi]])
                            if hb == 4:
                                ones_dst = vst[vi][:, :, b, 128:129]
                                P.op("dve", lambda e, ones_dst=ones_dst: e.memset(ones_dst, 1.0), w=[r_vst[vi]])
                    if hb == 4:
                        dstV = (k.va if kind == "va" else k.vn)[j]
                        dv = dstV[:, :, s0:s0 + nb, :].rearrange("h p s e -> p h s e")
                        P.op("act", lambda e, vi=vi, dv=dv: e.dma_start(out=dv, in_=vst[vi][:, :, 0:nb, :]), r=[r_vst[vi]], dma=r_vst[vi])

        tiles = []
        for j in range(2):
            S = JOBS[j][1]
            s = 0
            while s < NEAR:
                nb = min(4, NEAR - s)
                tiles.append((j, s, nb, True))
                s += nb
            while s < S:
                nb = min(4, S - s)
                tiles.append((j, s, nb, False))
                s += nb
        if DEBUG.get("max_tiles"):
            tiles = tiles[:DEBUG["max_tiles"]]

        def prep_tile(ti):
            j, s0, nb, near = tiles[ti]
            xi_list = [load_x(j, s0 + b) for b in range(nb)]
            for b in range(nb):
                norm_block(xi_list[b], ti % 2, b)

        def prep_tile_interleaved(ti):
            j, s0, nb, near = tiles[ti]
            pend = []
            for b in range(nb):
                pend.append(load_x(j, s0 + b))
                if len(pend) == 2:
                    norm_block(pend.pop(0), ti % 2, b - 1)
            bb = nb - len(pend)
            for xi in pend:
                norm_block(xi, ti % 2, bb)
                bb += 1

        prep_tile_interleaved(0)
        for ti in range(len(tiles)):
            j, s0, nb, near = tiles[ti]
            pf = (lambda ti=ti: prep_tile_interleaved(ti + 1)) if ti + 1 < len(tiles) else None
            do_tile(j, s0, nb, near, ti % 2, pf)

        P.emit_phase()


def phase_b(k):
    nc, P = k.nc, k.P
    moff, mcols = mask_layout()
    with ExitStack() as es:
        def sb(name, shape, dt=F32):
            return es.enter_context(nc.sbuf_tensor("B_" + name, list(shape), dt))

        def ps(name, shape, dt=F32):
            return es.enter_context(nc.psum_tensor("B_" + name, list(shape), dt))

        R = P.res
        SMAX = JOBS[0][1]
        ident = sb("ident", (128, 128), BF16)
        tablr = sb("tablr", (128, 2, 8))
        lamv = sb("lamv", (128, 4, 64))
        lprod = sb("lprod", (128, 2, 64))
        lsum = sb("lsum", (128, 2))
        lexp = sb("lexp", (128, 2))
        nlam = sb("nlam", (128, 1))
        sg = sb("sg", (128, 128))
        epsc = sb("epsc", (128, 1))
        tabp = sb("tabp", (128, 128))
        ohup = sb("ohup", (128, 1536))
        u2s = sb("u2s", (8, 1536))
        KT = [sb("KT%d" % i, (128, SMAX * 128), BF16) for i in range(2)]
        VH = [sb("VH%d" % i, (128, SMAX, 129), BF16) for i in range(2)]
        QT = [sb("QT%d" % i, (128, 18 * 128), BF16) for i in range(2)]
        QTm = [[sb("QTm%d_%d" % (m, i), (128, 18 * 128), BF16) for i in range(2)] for m in range(2)]
        HH = [sb("HH%d" % i, (128, 1408)) for i in range(2)]
        PT = [sb("PT%d" % i, (128, 2, 512), BF16) for i in range(3)]
        PTm = sb("PTm", (128, SMAX * 4), BF16)
        Gmini = sb("Gmini", (128, 18, 2))
        osb = sb("osb", (128, 8, 129))
        rz = sb("rz", (128, 8))
        tt = sb("tt", (128, 8, 128))
        od = sb("od", (128, 4, 128))
        sqj = sb("sqj", (128, 128))
        ssq = sb("ssq", (128, 4))
        lnv = sb("lnv", (128, 4))
        rstd = sb("rstd", (128, 4))
        tmp2 = sb("tmp2", (128, 4, 128))
        onb = [sb("onb%d" % i, (128, 4, 128), BF16) for i in range(2)]
        AOh = [sb("AOh%d" % i, (128, 2050), BF16) for i in range(2)]
        Trt = [sb("Trt%d" % i, (128, 7, 2, 64)) for i in range(2)]
        Tfix = [sb("Tfix%d" % i, (128, 7, 128)) for i in range(2)]
        maskt = sb("maskt", (128, mcols), BF16)
        PTn = [sb("PTn%d" % i, (128, 7, 128), BF16) for i in range(2)]
        rzn = [sb("rzn%d" % i, (128, 1)) for i in range(3)]
        onbn = [sb("onbn%d" % i, (128, 128), BF16) for i in range(3)]

        psS = [ps("psS%d" % i, (128, 2, 512)) for i in range(2)]
        acc = ps("acc", (128, 3, 512))
        pTr = ps("pTr", (128, 1024), BF16)

        r_ident, r_tablr, r_lamv, r_lprod, r_lsum, r_lexp, r_nlam = (R(n) for n in ("ident", "tablr", "lamv", "lprod", "lsum", "lexp", "nlam"))
        r_sg, r_eps, r_tabp, r_ohup, r_u2s, r_u2d = (R(n) for n in ("sg", "eps", "tabp", "ohup", "u2s", "u2d"))
        r_KT = [R("KT0"), R("KT1")]
        r_VH = [R("VH0"), R("VH1")]
        r_QT = [R("QT0"), R("QT1")]
        r_QTm = [[R("QTm%d_%d" % (m, i)) for i in range(2)] for m in range(2)]
        r_HH = [R("HH0"), R("HH1")]
        r_PT = [R("PT%d" % i) for i in range(3)]
        r_PTm, r_Gm, r_osb, r_rz, r_tt, r_od, r_sqj, r_ssq, r_lnv, r_rstd, r_tmp2 = (
            R(n) for n in ("PTm", "Gm", "osb", "rz", "tt", "od", "sqj", "ssq", "lnv", "rstd", "tmp2"))
        r_onb = [R("onb0"), R("onb1")]
        r_AOh = [R("AOh0"), R("AOh1")]
        r_Trt = [R("Trt0"), R("Trt1")]
        r_Tfix = [R("Tfix0"), R("Tfix1")]
        r_mask = R("mask")
        r_PTn = [R("PTn0"), R("PTn1")]
        r_rzn = [R("rzn%d" % i) for i in range(3)]
        r_onbn = [R("onbn%d" % i) for i in range(3)]
        r_psS = [R("psS0"), R("psS1")]
        r_acc = [R("acc%d" % i) for i in range(3)]
        r_pTr = [R("pTr%d" % i) for i in range(4)]
        cnt = {"S": 0, "PT": 0, "onb": 0, "pTr": 0, "hb": 0, "PTn": 0, "accn": 0, "tb": 0}

        P.op("sp", lambda e: e.dma_start(out=ident[:], in_=k.ident.ap()), w=[r_ident], dma=r_ident)
        P.op("sp", lambda e: e.dma_start(out=tablr[:], in_=k.tablr.ap()), w=[r_tablr], dma=r_tablr)
        P.op("sp", lambda e: e.dma_start(out=lamv[:], in_=k.lamv.ap()), w=[r_lamv], dma=r_lamv)
        P.op("sp", lambda e: e.dma_start(out=sg[:], in_=k.sgb.ap()), w=[r_sg], dma=r_sg)
        P.op("dve", lambda e: e.memset(epsc[:], EPS), w=[r_eps])
        for i in range(2):
            P.op("dve", lambda e, i=i: e.memset(QTm[0][i][64:128, :], 0.0), w=[r_QTm[0][i]])
            P.op("dve", lambda e, i=i: e.memset(QTm[1][i][0:64, :], 0.0), w=[r_QTm[1][i]])
        P.op("dve", lambda e: e.memset(tabp[:], 0.0), w=[r_tabp])
        P.op("dve", lambda e: e.memset(ohup[:], 0.0), w=[r_ohup])
        P.op("sp", lambda e: e.dma_start(out=tabp[0:32, 0:8], in_=k.tab.ap()), w=[r_tabp], dma=r_tabp)
        P.op("sp", lambda e: e.dma_start(out=ohup[0:32, :], in_=k.ohu.ap()), w=[r_ohup], dma=r_ohup)
        pl = psz(lamv)
        P.op("dve", lambda e: e.tensor_tensor(out=lprod[:], in0=mkap(lamv, 0, [(pl, 128), (128, 2), (1, 64)]),
                                              in1=mkap(lamv, 64, [(pl, 128), (128, 2), (1, 64)]), op=ALU.mult),
             r=[r_lamv], w=[r_lprod])
        P.op("dve", lambda e: e.tensor_reduce(out=lsum[:], in_=lprod[:], axis=AX.X, op=ALU.add), r=[r_lprod], w=[r_lsum])
        P.op("act", lambda e: e.activation(out=lexp[:], in_=lsum[:], func=AF.Exp), r=[r_lsum], w=[r_lexp])
        P.op("dve", lambda e: e.tensor_tensor(out=nlam[:], in0=lexp[:, 1:2], in1=lexp[:, 0:1], op=ALU.subtract), r=[r_lexp], w=[r_nlam])
        P.op("dve", lambda e: e.tensor_scalar(out=nlam[:], in0=nlam[:], scalar1=-LAM_INIT, scalar2=None, op0=ALU.add), r=[r_nlam], w=[r_nlam])
        P.op("dve", lambda e: e.tensor_scalar(out=sg[:], in0=sg[:], scalar1=1.0 - LAM_INIT, scalar2=None, op0=ALU.mult), r=[r_sg], w=[r_sg])
        P.op("dve", lambda e: e.tensor_scalar(out=tabp[:], in0=tabp[:], scalar1=1.0 / SCALE_A, scalar2=None, op0=ALU.mult), r=[r_tabp], w=[r_tabp])
        for q in range(3):
            P.op("pe", lambda e, q=q: e.matmul(psS[q % 2][:, q // 2, :], lhsT=tabp[:], rhs=ohup[:, q * 512:(q + 1) * 512], start=True, stop=True),
                 r=[r_tabp, r_ohup], w=[r_psS[q % 2]])
            P.op("dve", lambda e, q=q: e.tensor_copy(out=u2s[:, q * 512:(q + 1) * 512], in_=psS[q % 2][0:8, q // 2, :]),
                 r=[r_psS[q % 2]], w=[r_u2s])
        P.op("sp", lambda e: e.dma_start(out=k.u2.ap(), in_=u2s[:]), r=[r_u2s], w=[r_u2d], dma=r_u2s)

        def acc_ap(a, rows=128, cols=129):
            return acc[0:rows, a // 3, (a % 3) * 129:(a % 3) * 129 + cols]

        def load_head(j, h, nbr):
            S = JOBS[j][1]
            hb = cnt["hb"] % 2
            cnt["hb"] += 1
            if not nbr:
                P.op("sp", lambda e: e.dma_start(out=KT[hb][:, 0:S * 128], in_=k.kaT[j][h]), w=[r_KT[hb]], dma=r_KT[hb])
                P.op("sp", lambda e: e.dma_start(out=VH[hb][:, 0:S, :], in_=k.va[j][h]), w=[r_VH[hb]], dma=r_VH[hb])
                P.op("sp", lambda e: e.dma_start(out=QTm[0][hb][0:64, :], in_=k.qaT[j][h][0:64, 2 * 128:20 * 128]), w=[r_QTm[0][hb]], dma=r_QTm[0][hb])
                P.op("sp", lambda e: e.dma_start(out=QTm[1][hb][64:128, :], in_=k.qaT[j][h][64:128, 2 * 128:20 * 128]), w=[r_QTm[1][hb]], dma=r_QTm[1][hb])
                P.op("sp", lambda e: e.dma_start(out=HH[hb][:], in_=mkap(k.u2, h * 1536, [(1, 128), (1, 1408)])),
                     r=[r_u2d], w=[r_HH[hb]], dma=r_HH[hb])
            else:
                P.op("sp", lambda e: e.dma_start(out=KT[hb][:, 0:NEAR * 128], in_=k.knT[j][h]), w=[r_KT[hb]], dma=r_KT[hb])
                P.op("sp", lambda e: e.dma_start(out=VH[hb][:, 0:NEAR, :], in_=k.vn[j][h]), w=[r_VH[hb]], dma=r_VH[hb])
                P.op("sp", lambda e: e.dma_start(out=QT[hb][:], in_=k.qnT[j][h][:, 2 * 128:20 * 128]), w=[r_QT[hb]], dma=r_QT[hb])
                tb = cnt["tb"] % 2
                cnt["tb"] += 1
                for dl in range(-3, 4):
                    for rk in range(2):
                        for rq in range(2):
                            dr = 2 * dl + rk - rq + 7
                            P.op("pool", lambda e, dl=dl, rk=rk, rq=rq, dr=dr: e.dma_start(
                                out=Trt[tb][rk * 64:(rk + 1) * 64, dl + 3, rq, :],
                                in_=mkap(k.rpbp, (h * 15 + dr) * 128, [(1, 64), (1, 64)])), w=[r_Trt[tb]], dma=r_Trt[tb])
                pt_ = psz(Trt[tb])
                for rq in range(2):
                    P.op("pool", lambda e, rq=rq: e.tensor_scalar(
                        out=Tfix[tb][:, :, rq * 64:(rq + 1) * 64],
                        in0=mkap(Trt[tb], rq * 64 + 63, [(pt_, 128), (128, 7), (-1, 64)]),
                        scalar1=1.0 / SCALE_N, scalar2=1.0, op0=ALU.mult, op1=ALU.mult),
                        r=[r_Trt[tb]], w=[r_Tfix[tb]])
                return hb, tb
            return hb, None

        def finish_diff(rows, nch, ao, ecols):
            na = 2 * nch
            nbanks = (na + 2) // 3
            for b in range(nbanks):
                n_in = min(3, na - 3 * b)
                P.op("act", lambda e, b=b, n_in=n_in: e.activation(
                    out=osb[0:rows, 3 * b:3 * b + n_in, :], in_=acc[0:rows, b, 0:n_in * 129].rearrange("p (a c) -> p a c", c=129), func=AF.Copy),
                    r=[r_acc[b]], w=[r_osb])
            po = psz(osb)
            P.op("dve", lambda e: e.reciprocal(out=rz[0:rows, 0:na], in_=mkap(osb, 128, [(po, rows), (129, na)])), r=[r_osb], w=[r_rz])
            P.op("dve", lambda e: e.tensor_tensor(out=tt[0:rows, 0:na, :], in0=osb[0:rows, 0:na, 0:128],
                                                  in1=mkap(rz, 0, [(psz(rz), rows), (1, na), (0, 128)]), op=ALU.mult),
                 r=[r_osb, r_rz], w=[r_tt])
            ptt = psz(tt)
            P.op("dve", lambda e: e.scalar_tensor_tensor(out=od[0:rows, 0:nch, :], in0=mkap(tt, 128, [(ptt, rows), (256, nch), (1, 128)]),
                                                         scalar=nlam[0:rows, :], in1=mkap(tt, 0, [(ptt, rows), (256, nch), (1, 128)]),
                                                         op0=ALU.mult, op1=ALU.add),
                 r=[r_tt, r_nlam], w=[r_od])
            P.op("dve", lambda e: e.tensor_tensor(out=tmp2[0:rows, 0:nch, :], in0=od[0:rows, 0:nch, :], in1=od[0:rows, 0:nch, :], op=ALU.mult),
                 r=[r_od], w=[r_tmp2])
            P.op("dve", lambda e: e.tensor_reduce(out=ssq[0:rows, 0:nch], in_=tmp2[0:rows, 0:nch, :], axis=AX.X, op=ALU.add),
                 r=[r_tmp2], w=[r_ssq])
            P.op("act", lambda e: e.activation(out=lnv[0:rows, 0:nch], in_=ssq[0:rows, 0:nch], func=AF.Ln, scale=1.0 / 128, bias=epsc[0:rows, :]),
                 r=[r_ssq, r_eps], w=[r_lnv])
            P.op("act", lambda e: e.activation(out=rstd[0:rows, 0:nch], in_=lnv[0:rows, 0:nch], func=AF.Exp, scale=-0.5), r=[r_lnv], w=[r_rstd])
            P.op("dve", lambda e: e.tensor_tensor(out=tmp2[0:rows, 0:nch, :], in0=od[0:rows, 0:nch, :],
                                                  in1=mkap(rstd, 0, [(psz(rstd), rows), (1, nch), (0, 128)]), op=ALU.mult),
                 r=[r_od, r_rstd], w=[r_tmp2])
            oi = cnt["onb"] % 2
            cnt["onb"] += 1
            P.op("dve", lambda e: e.tensor_tensor(out=onb[oi][0:rows, 0:nch, :], in0=tmp2[0:rows, 0:nch, :],
                                                  in1=mkap(sg, 0, [(psz(sg), rows), (0, nch), (1, 128)]), op=ALU.mult),
                 r=[r_tmp2, r_sg], w=[r_onb[oi]])
            def part2():
                for c in range(nch):
                    P.op("pe", lambda e, c=c: e.transpose(out=pTr[:, c * 128:c * 128 + rows], in_=onb[oi][0:rows, c, :], identity=ident[0:rows, 0:rows]),
                         r=[r_onb[oi], r_ident], w=[r_pTr[0]])
                if rows == 128:
                    dst = AOh[ao][:, ecols[0]:ecols[0] + 128 * nch]
                    src = pTr[:, 0:128 * nch]
                else:
                    dst = mkap(AOh[ao], 0, [(psz(AOh[ao]), 128), (2049, 2)])
                    src = pTr[:, 0:2]
                P.op("dve", lambda e, dst=dst, src=src: e.tensor_copy(out=dst, in_=src), r=[r_pTr[0]], w=[r_AOh[ao]])
            return part2

        def diff_head(j, h, hb, ao):
            S = JOBS[j][1]
            pq = psz(QT[hb])
            ph = psz(HH[hb])
            G = Gmini
            pg = psz(G)
            P.op("dve", lambda e: e.tensor_copy(out=G[:, 0:2, 0:1], in_=mkap(HH[hb], 640, [(ph, 128), (128, 2), (1, 1)])), r=[r_HH[hb]], w=[r_Gm])
            P.op("dve", lambda e: e.tensor_copy(out=G[:, 2:18, 0:1], in_=mkap(HH[hb], 896, [(ph, 128), (0, 16), (1, 1)])), r=[r_HH[hb]], w=[r_Gm])
            P.op("dve", lambda e: e.tensor_copy(out=G[:, 0:16, 1:2], in_=mkap(HH[hb], 511, [(ph, 128), (0, 16), (1, 1)])), r=[r_HH[hb]], w=[r_Gm])
            P.op("dve", lambda e: e.tensor_copy(out=G[:, 16:18, 1:2], in_=mkap(HH[hb], 639, [(ph, 128), (128, 2), (1, 1)])), r=[r_HH[hb]], w=[r_Gm])
            def issue_S(g, s):
                    qc0 = (4 * g + 1) * 128
                    ri = cnt["S"] % 2
                    cnt["S"] += 1
                    for m in range(2):
                        P.op("pe", lambda e, ri=ri, m=m, s=s: e.matmul(
                            psS[ri][:, m, :], lhsT=KT[hb][:, s * 128:(s + 1) * 128],
                            rhs=QTm[m][hb][:, qc0:qc0 + 512], start=True, stop=True),
                            r=[r_KT[hb], r_QTm[m][hb]], w=[r_psS[ri]])
                    bias = None
                    if 2 <= s <= 19:
                        d0 = (s - 3) - 4 * g
                        sw = 5 - d0
                        if sw <= 0:
                            bias = tablr[:, 1, h:h + 1]
                        elif sw >= 7:
                            bias = tablr[:, 0, h:h + 1]
                        else:
                            win = mkap(HH[hb], 1407 - 128 * sw, [(ph, 128), (0, 2), (-1, 512)])
                            P.op("dve", lambda e, ri=ri, win=win: e.tensor_tensor(out=psS[ri][:], in0=psS[ri][:], in1=win, op=ALU.add),
                                 r=[r_psS[ri], r_HH[hb]], w=[r_psS[ri]])
                    pi = cnt["PT"] % 3
                    cnt["PT"] += 1
                    if bias is None:
                        P.op("act", lambda e, ri=ri, pi=pi: e.activation(out=PT[pi][:], in_=psS[ri][:], func=AF.Exp, scale=SCALE_A),
                             r=[r_psS[ri]], w=[r_PT[pi]])
                    else:
                        P.op("act", lambda e, ri=ri, pi=pi, bias=bias: e.activation(out=PT[pi][:], in_=psS[ri][:], func=AF.Exp, scale=SCALE_A, bias=bias),
                             r=[r_psS[ri], r_tablr], w=[r_PT[pi]])
                    return pi

            def issue_PV(s, pi):
                    for c in range(4):
                        for m in range(2):
                            a = 2 * c + m
                            P.op("pe", lambda e, pi=pi, c=c, m=m, a=a, s=s: e.matmul(
                                acc_ap(a), lhsT=PT[pi][:, m, c * 128:(c + 1) * 128], rhs=VH[hb][:, s, :],
                                start=(s == 0 and a % 3 == 0), stop=(s == S - 1), skip_group_check=True),
                                r=[r_PT[pi], r_VH[hb]], w=[r_acc[a // 3]])

            parts = DEBUG.get("b_parts", ("main", "finish", "mini"))
            steps = [(g, s) for g in range(4 if "main" in parts else 0) for s in range(S)]
            pending = []
            prev = None

            def retire(prev):
                g_, s_, pi_ = prev
                issue_PV(s_, pi_)
                if s_ == S - 1 and "finish" in parts:
                    pending.append([3, finish_diff(128, 4, ao, [1 + 512 * g_ + 128 * c for c in range(4)])])

            for (g, s) in steps:
                pi = issue_S(g, s)
                if prev is not None:
                    retire(prev)
                for pd in pending:
                    pd[0] -= 1
                while pending and pending[0][0] <= 0:
                    pending.pop(0)[1]()
                prev = (g, s, pi)
            if prev is not None:
                retire(prev)
            if "mini" not in parts:
                while pending:
                    pending.pop(0)[1]()
                P.op("act", lambda e: e.dma_start(out=k.aoT[j][h], in_=AOh[ao][:]), r=[r_AOh[ao]], dma=r_AOh[ao])
                return
            ri = cnt["S"] % 2
            cnt["S"] += 1
            for s in range(S):
                for m in range(2):
                    P.op("pe", lambda e, ri=ri, m=m, s=s: e.matmul(
                        psS[ri][:, 0, s * 4 + 2 * m:s * 4 + 2 * m + 2], lhsT=KT[hb][:, s * 128:(s + 1) * 128],
                        rhs=mkap(QTm[m][hb], 127, [(pq, 128), (2049, 2)]), start=True, stop=True),
                        r=[r_KT[hb], r_QTm[m][hb]], w=[r_psS[ri]])
            pps = psz(psS[ri])
            reg = mkap(psS[ri], 8, [(pps, 128), (4, 18), (2, 2), (1, 2)])
            P.op("dve", lambda e, reg=reg: e.tensor_tensor(out=reg, in0=reg, in1=mkap(G, 0, [(pg, 128), (2, 18), (0, 2), (1, 2)]), op=ALU.add),
                 r=[r_psS[ri], r_Gm], w=[r_psS[ri]])
            P.op("act", lambda e, ri=ri: e.activation(out=PTm[:, 0:4 * S], in_=psS[ri][:, 0, 0:4 * S], func=AF.Exp, scale=SCALE_A),
                 r=[r_psS[ri]], w=[r_PTm])
            while pending:
                pending.pop(0)[1]()
            for s in range(S):
                for m in range(2):
                    P.op("pe", lambda e, m=m, s=s: e.matmul(
                        acc_ap(m, rows=2), lhsT=PTm[:, s * 4 + 2 * m:s * 4 + 2 * m + 2], rhs=VH[hb][:, s, :],
                        start=(s == 0 and m == 0), stop=(s == S - 1), skip_group_check=True),
                        r=[r_PTm, r_VH[hb]], w=[r_acc[0]])
            finish_diff(2, 1, ao, None)()
            P.op("act", lambda e: e.dma_start(out=k.aoT[j][h], in_=AOh[ao][:]), r=[r_AOh[ao]], dma=r_AOh[ao])

        def nbr_head(j, h, hb, tb, ao):
            pq = psz(QT[hb])
            pm = psz(maskt)
            def stageA(i, dl):
                nd = len(dl)
                if i == -1:
                    nq, qoff, key = 1, 127, -1
                elif i == 16:
                    nq, qoff, key = 1, 17 * 128, 16
                else:
                    nq, qoff = 128, (i + 1) * 128
                    key = i if i in (0, 1, 14, 15) else "int"
                ri = cnt["S"] % 2
                cnt["S"] += 1
                flat = psS[ri][:].rearrange("p a b -> p (a b)")
                for di, d_ in enumerate(dl):
                    nk = i + d_ + 3
                    P.op("pe", lambda e, di=di, nk=nk: e.matmul(flat[:, di * 128:di * 128 + nq], lhsT=KT[hb][:, nk * 128:(nk + 1) * 128],
                                                                rhs=QT[hb][:, qoff:qoff + nq], start=True, stop=False),
                         r=[r_KT[hb], r_QT[hb]], w=[r_psS[ri]])
                    mo = moff[key] + di * nq
                    P.op("pe", lambda e, di=di, mo=mo: e.matmul(flat[:, di * 128:di * 128 + nq], lhsT=ident[:], rhs=maskt[:, mo:mo + nq],
                                                                start=False, stop=True),
                         r=[r_ident, r_mask], w=[r_psS[ri]])
                pps = psz(psS[ri])
                reg = mkap(psS[ri], 0, [(pps, 128), (128, nd), (1, nq)])
                tfx = Tfix[tb][:, dl[0] + 3:dl[0] + 3 + nd, qoff % 128:qoff % 128 + nq]
                P.op("dve", lambda e, reg=reg, tfx=tfx: e.tensor_tensor(out=reg, in0=reg, in1=tfx, op=ALU.add),
                     r=[r_psS[ri], r_Tfix[tb]], w=[r_psS[ri]])
                pn = cnt["PTn"] % 2
                cnt["PTn"] += 1
                P.op("act", lambda e, reg=reg, pn=pn: e.activation(out=PTn[pn][:, 0:nd, 0:nq], in_=reg, func=AF.Exp, scale=SCALE_N),
                     r=[r_psS[ri]], w=[r_PTn[pn]])
                return (i, dl, nq, pn)

            def stageB(st):
                i, dl, nq, pn = st
                nd = len(dl)
                ai = cnt["accn"] % 3
                cnt["accn"] += 1
                for di, d_ in enumerate(dl):
                    nk = i + d_ + 3
                    P.op("pe", lambda e, di=di, nk=nk, ai=ai, pn=pn: e.matmul(acc[0:nq, ai, 0:129], lhsT=PTn[pn][:, di, 0:nq], rhs=VH[hb][:, nk, :],
                                                                        start=(di == 0), stop=(di == nd - 1)),
                         r=[r_PTn[pn], r_VH[hb]], w=[r_acc[ai]])
                P.op("dve", lambda e, ai=ai: e.reciprocal(out=rzn[ai][0:nq, :], in_=acc[0:nq, ai, 128:129]), r=[r_acc[ai]], w=[r_rzn[ai]])
                P.op("dve", lambda e, ai=ai: e.tensor_scalar(out=onbn[ai][0:nq, :], in0=acc[0:nq, ai, 0:128], scalar1=rzn[ai][0:nq, :], scalar2=None, op0=ALU.mult),
                     r=[r_acc[ai], r_rzn[ai]], w=[r_onbn[ai]])
                return (i, nq, ai)

            def stageC(st):
                i, nq, ai = st
                P.op("pe", lambda e, ai=ai: e.transpose(out=pTr[:, 0:nq], in_=onbn[ai][0:nq, :], identity=ident[0:nq, 0:nq]),
                     r=[r_onbn[ai], r_ident], w=[r_pTr[0]])
                if nq == 128:
                    dst = AOh[ao][:, 1 + 128 * i:1 + 128 * i + 128]
                else:
                    ecol = 0 if i == -1 else 2049
                    dst = AOh[ao][:, ecol:ecol + 1]
                P.op("dve", lambda e, dst=dst: e.tensor_copy(out=dst, in_=pTr[:, 0:nq]), r=[r_pTr[0]], w=[r_AOh[ao]])

            units = nbr_units()
            sa, sbq = [], []
            for u in range(len(units) + 2):
                if u < len(units):
                    sa.append(stageA(*units[u]))
                if u >= 1 and u - 1 < len(units):
                    sbq.append(stageB(sa[u - 1]))
                if u >= 2:
                    stageC(sbq[u - 2])

            P.op("act", lambda e: e.dma_start(out=k.aoT[j][8 + h], in_=AOh[ao][:]), r=[r_AOh[ao]], dma=r_AOh[ao])

        g12c = sb("g12c", (128, 2, CK))
        cin = [sb("cin%d" % i, (128, 1024)) for i in range(2)]
        cout = [sb("cout%d" % i, (128, 1024), BF16) for i in range(2)]
        r_g = R("g12c")
        r_cin = [R("cin0"), R("cin1")]
        r_cout = [R("cout0"), R("cout1")]
        P.op("sp", lambda e: e.dma_start(out=g12c[:], in_=k.g12c.ap()), w=[r_g], dma=r_g)
        conv_state = {"n": 0}

        def convert(src, dst, nrows, ncols, tw, gidx):
            for rb in range(nrows // 128):
                for c0 in range(0, ncols, tw):
                    i = conv_state["n"] % 2
                    conv_state["n"] += 1
                    sv = src[rb * 128:(rb + 1) * 128, c0:c0 + tw]
                    dv = dst[rb * 128:(rb + 1) * 128, c0:c0 + tw]
                    P.op("pool", lambda e, i=i, sv=sv, tw=tw: e.dma_start(out=cin[i][:, 0:tw], in_=sv), w=[r_cin[i]], dma=r_cin[i])
                    if gidx is None:
                        P.op("pool", lambda e, i=i, tw=tw: e.tensor_copy(out=cout[i][:, 0:tw], in_=cin[i][:, 0:tw]),
                             r=[r_cin[i]], w=[r_cout[i]])
                    else:
                        P.op("pool", lambda e, i=i, tw=tw, rb=rb, gidx=gidx: e.tensor_scalar(
                            out=cout[i][:, 0:tw], in0=cin[i][:, 0:tw], scalar1=g12c[:, gidx, rb:rb + 1], scalar2=1.0, op0=ALU.mult, op1=ALU.mult),
                            r=[r_cin[i], r_g], w=[r_cout[i]])
                    P.op("pool", lambda e, i=i, dv=dv, tw=tw: e.dma_start(out=dv, in_=cout[i][:, 0:tw]), r=[r_cout[i]], dma=r_cout[i])

        if not DEBUG.get("skip_a") and not DEBUG.get("no_conv_b"):
            convert(k.w_out, k.wb_out, D, D, 1024, None)
            convert(k.w_up, k.wb_up, D, 2 * DFF, 688, 1)
            convert(k.w_down, k.wb_down, DFF, D, 1024, None)

        work = []
        for j in DEBUG.get("jobs", (0, 1)):
            for h in DEBUG.get("heads_a", range(HA)):
                work.append((j, h, False))
            for h in DEBUG.get("heads_n", range(HN)):
                work.append((j, h, True))
        cur_job = None
        if not work:
            P.emit_phase()
            return
        pre = load_head(*work[0])
        for wi, (j, h, nbr) in enumerate(work):
            hb, tb = pre
            if nbr and cur_job != j:
                cur_job = j
                P.op("sp", lambda e, j=j: e.dma_start(out=maskt[:], in_=k.maskd[j].ap()), w=[r_mask], dma=r_mask)
            if wi + 1 < len(work):
                pre = load_head(*work[wi + 1])
            ao = wi % 2
            if nbr:
                nbr_head(j, h, hb, tb, ao)
            else:
                diff_head(j, h, hb, ao)
        P.emit_phase()


def phase_c(k):
    nc, P = k.nc, k.P
    with ExitStack() as es:
        def sb(name, shape, dt=F32):
            return es.enter_context(nc.sbuf_tensor("C_" + name, list(shape), dt))

        R = P.res
        NW = 6
        ident = sb("ident", (128, 128), BF16)
        epsc = sb("epsc", (128, 1))
        gfb = sb("gfb", (128, D))
        convc = sb("convc", (128, FCH, 4))
        hflag = sb("hflag", (128, 4))
        cwe = sb("cwe", (128, 4, FCH))
        axT = sb("axT", (128, CK, 514), BF16)
        xmid = sb("xmid", (128, 4, D))
        xmh = sb("xmh", (2, D))
        hT = sb("hT", (128, FCH, 512), BF16)
        wr = [sb("wr%d" % i, (128, 4096), BF16) for i in range(NW)]
        xn2 = sb("xn2", (128, D), BF16)
        junk = sb("junk", (128, D), BF16)
        tb = [sb("tb%d" % i, (128, 2, 256)) for i in range(2)]
        ssq = sb("ssq", (128, 1))
        lnv = sb("lnv", (128, 1))
        rstd = sb("rstd", (128, 1))
        pb = es.enter_context(nc.psum_tensor("C_pb", [128, 8, 512], F32))

        r_ident, r_eps, r_gfb, r_convc, r_hflag, r_cwe = (R(n) for n in ("ident", "eps", "gfb", "convc", "hflag", "cwe"))
        r_axT = R("axT")
        r_xmid = [R("xmid%d" % i) for i in range(4)]
        r_xmh = R("xmh")
        r_hT = R("hT")
        r_wr = [R("wr%d" % i) for i in range(NW)]
        r_xn2, r_junk, r_ssq, r_lnv, r_rstd = (R(n) for n in ("xn2", "junk", "ssq", "lnv", "rstd"))
        r_tb = [R("tb0"), R("tb1")]
        r_pb = [R("pb%d" % i) for i in range(8)]
        cnt = {"w": 0, "pb": 0, "tb": 0, "u": 0}

        P.op("sp", lambda e: e.dma_start(out=ident[:], in_=k.ident.ap()), w=[r_ident], dma=r_ident)
        P.op("sp", lambda e: e.dma_start(out=gfb[:], in_=k.gfb.ap()), w=[r_gfb], dma=r_gfb)
        P.op("sp", lambda e: e.dma_start(out=convc[:], in_=k.convc.ap()), w=[r_convc], dma=r_convc)
        P.op("sp", lambda e: e.dma_start(out=hflag[:], in_=k.hflag.ap()), w=[r_hflag], dma=r_hflag)
        P.op("dve", lambda e: e.memset(epsc[:], EPS), w=[r_eps])
        pc = psz(convc)
        for q in range(4):
            ci = 0 if q % 2 == 0 else 2
            P.op("dve", lambda e, q=q, ci=ci: e.tensor_scalar(out=cwe[:, q, :], in0=mkap(convc, ci, [(pc, 128), (4, FCH)]),
                                                             scalar1=hflag[:, q:q + 1], scalar2=None, op0=ALU.mult),
                 r=[r_convc, r_hflag], w=[r_cwe])

        def wslot():
            i = cnt["w"] % NW
            cnt["w"] += 1
            return i

        def load_w_cols(src_dram, c0, ncols):
            i = wslot()
            src = src_dram[:, c0:c0 + ncols].rearrange("(ck p) f -> p ck f", p=128)
            dst = wr[i][:, 0:CK * ncols].rearrange("p (ck f) -> p ck f", f=ncols)
            P.op("sp", lambda e: e.dma_start(out=dst, in_=src), w=[r_wr[i]], dma=r_wr[i])
            return i

        def load_w_rows(src_dram, r0, nr):
            i = wslot()
            src = src_dram[r0 * 128:(r0 + nr) * 128, :].rearrange("(f p) n -> p f n", p=128)
            dst = wr[i][:, 0:nr * D].rearrange("p (f n) -> p f n", n=D)
            P.op("sp", lambda e: e.dma_start(out=dst, in_=src), w=[r_wr[i]], dma=r_wr[i])
            return i

        def wview_cols(i, ncols):
            return wr[i][:, 0:CK * ncols].rearrange("p (ck f) -> p ck f", f=ncols)

        def wview_rows(i, nr):
            return wr[i][:, 0:nr * D].rearrange("p (f n) -> p f n", n=D)

        def bank():
            b = cnt["pb"] % 6
            cnt["pb"] += 1
            return b

        pax = psz(axT)

        def rms_rows(rows, src_ap, r_src, dst_ap, r_dst):
            P.op("act", lambda e: e.activation(out=junk[0:rows, :], in_=src_ap, func=AF.Square, accum_out=ssq[0:rows, :]),
                 r=[r_src], w=[r_junk, r_ssq])
            P.op("act", lambda e: e.activation(out=lnv[0:rows, :], in_=ssq[0:rows, :], func=AF.Ln, scale=1.0 / D, bias=epsc[0:rows, :]),
                 r=[r_ssq, r_eps], w=[r_lnv])
            P.op("act", lambda e: e.activation(out=rstd[0:rows, :], in_=lnv[0:rows, :], func=AF.Exp, scale=-0.5), r=[r_lnv], w=[r_rstd])
            if dst_ap is not None:
                P.op("dve", lambda e: e.tensor_scalar(out=dst_ap, in0=src_ap, scalar1=rstd[0:rows, :], scalar2=None, op0=ALU.mult),
                     r=[r_src, r_rstd], w=[r_dst])

        def tile(j, T):
            e0 = 512 * T
            xs_flat = k.xs[j].ap().rearrange("s p d -> (s p) d")
            src = k.aoT[j][:, :, e0:e0 + 514].rearrange("c p n -> p c n")
            P.op("sp", lambda e: e.dma_start(out=axT[:], in_=src), w=[r_axT], dma=r_axT)
            for tc in range(4):
                r0 = 383 + e0 + 1 + 128 * tc
                P.op("sp", lambda e, tc=tc, r0=r0: e.dma_start(out=xmid[:, tc, :], in_=xs_flat[r0:r0 + 128, :]), w=[r_xmid[tc]], dma=r_xmid[tc])
            hsrc = mkap(k.xs[j], (383 + e0) * D, [(513 * D, 2), (1, D)])
            P.op("sp", lambda e: e.dma_start(out=xmh[:], in_=hsrc), w=[r_xmh], dma=r_xmh)
            NG = 256
            nxt = load_w_cols(k.wb_out, 0, NG)
            for cg in range(D // NG):
                wi = nxt
                if cg + 1 < D // NG:
                    nxt = load_w_cols(k.wb_out, (cg + 1) * NG, NG)
                wv = wview_cols(wi, NG)
                for tc in range(5):
                    b = bank()
                    if tc < 4:
                        rows = 128
                        def lhs(ck, tc=tc):
                            return axT[:, ck, 1 + 128 * tc:129 + 128 * tc]
                        dst = xmid[:, tc, cg * NG:(cg + 1) * NG]
                        rd = r_xmid[tc]
                    else:
                        rows = 2
                        def lhs(ck):
                            return mkap(axT, ck * 514, [(pax, 128), (513, 2)])
                        dst = xmh[0:2, cg * NG:(cg + 1) * NG]
                        rd = r_xmh
                    for ck in range(CK):
                        P.op("pe", lambda e, b=b, ck=ck, lhs=lhs, rows=rows, wv=wv: e.matmul(
                            pb[0:rows, b, 0:NG], lhsT=lhs(ck), rhs=wv[:, ck, :], start=(ck == 0), stop=(ck == CK - 1)),
                            r=[r_axT, r_wr[wi]], w=[r_pb[b]])
                    P.op("dve", lambda e, b=b, dst=dst, rows=rows: e.tensor_tensor(out=dst, in0=pb[0:rows, b, 0:NG], in1=dst, op=ALU.add),
                         r=[r_pb[b], rd], w=[rd])
            for tc in range(5):
                rows = 128 if tc < 4 else 2
                srcx = xmid[:, tc, :] if tc < 4 else xmh[0:2, :]
                rsrc = r_xmid[tc] if tc < 4 else r_xmh
                rms_rows(rows, srcx, rsrc, xn2[0:rows, :], r_xn2)
                if tc < 4:
                    for half in range(2):
                        pt = pb[:, 6 + half, :].bitcast(BF16)
                        for q in range(8):
                            ck = half * 8 + q
                            P.op("pe", lambda e, pt=pt, q=q, ck=ck: e.transpose(out=pt[:, q * 128:(q + 1) * 128], in_=xn2[:, ck * 128:(ck + 1) * 128], identity=ident[:]),
                                 r=[r_xn2, r_ident], w=[r_pb[6 + half]])
                        dst = axT[:, half * 8:(half + 1) * 8, 1 + 128 * tc:129 + 128 * tc]
                        srcp = pt[:, 0:1024].rearrange("p (a b) -> p a b", b=128)
                        P.op("dve", lambda e, dst=dst, srcp=srcp: e.tensor_copy(out=dst, in_=srcp), r=[r_pb[6 + half]], w=[r_axT])
                else:
                    pt = pb[:, 6, :].bitcast(BF16)
                    for ck in range(CK):
                        P.op("pe", lambda e, pt=pt, ck=ck: e.transpose(out=pt[:, 2 * ck:2 * ck + 2], in_=xn2[0:2, ck * 128:(ck + 1) * 128], identity=ident[0:2, 0:2]),
                             r=[r_xn2, r_ident], w=[r_pb[6]])
                    dst = mkap(axT, 0, [(pax, 128), (514, CK), (513, 2)])
                    srcp = pt[:, 0:2 * CK].rearrange("p (a b) -> p a b", b=2)
                    P.op("dve", lambda e, dst=dst, srcp=srcp: e.tensor_copy(out=dst, in_=srcp), r=[r_pb[6]], w=[r_axT])
            groups = [(f0, min(2, FCH - f0)) for f0 in range(0, FCH, 2)]

            def load_group(gi):
                f0, nf = groups[gi]
                return (load_w_cols(k.wb_up, f0 * 128, nf * 128), load_w_cols(k.wb_up, DFF + f0 * 128, nf * 128))

            pend = [load_group(0), load_group(1)]
            for gi, (f0, nf) in enumerate(groups):
                wa_i, wg_i = pend.pop(0)
                if gi + 2 < len(groups):
                    pend.append(load_group(gi + 2))
                wa = wview_cols(wa_i, nf * 128)
                wg = wview_cols(wg_i, nf * 128)
                for fl in range(nf):
                    fc = f0 + fl
                    u = cnt["u"] % 2
                    cnt["u"] += 1
                    ba = 3 * u
                    for ck in range(CK):
                        P.op("pe", lambda e, ba=ba, ck=ck, wa=wa, fl=fl: e.matmul(pb[:, ba, 0:258], lhsT=wa[:, ck, fl * 128:(fl + 1) * 128],
                                                                             rhs=axT[:, ck, 0:258], start=(ck == 0), stop=(ck == CK - 1)),
                             r=[r_axT, r_wr[wa_i]], w=[r_pb[ba]])
                    for ck in range(CK):
                        P.op("pe", lambda e, ba=ba, ck=ck, wa=wa, fl=fl: e.matmul(pb[:, ba + 1, 0:258], lhsT=wa[:, ck, fl * 128:(fl + 1) * 128],
                                                                             rhs=axT[:, ck, 256:514], start=(ck == 0), stop=(ck == CK - 1)),
                             r=[r_axT, r_wr[wa_i]], w=[r_pb[ba + 1]])
                    for ck in range(CK):
                        P.op("pe", lambda e, ba=ba, ck=ck, wg=wg, fl=fl: e.matmul(pb[:, ba + 2, :], lhsT=wg[:, ck, fl * 128:(fl + 1) * 128],
                                                                             rhs=axT[:, ck, 1:513], start=(ck == 0), stop=(ck == CK - 1)),
                             r=[r_axT, r_wr[wg_i]], w=[r_pb[ba + 2]])
                    ti = cnt["tb"] % 2
                    cnt["tb"] += 1
                    t_ = tb[ti]
                    rt = r_tb[ti]
                    ra = [r_pb[ba], r_pb[ba + 1]]
                    P.op("dve", lambda e, ba=ba, t_=t_, fc=fc: e.tensor_scalar(out=t_[:], in0=pb[:, ba:ba + 2, 0:256], scalar1=convc[:, fc, 0:1],
                                                                           scalar2=convc[:, fc, 3:4], op0=ALU.mult, op1=ALU.add),
                         r=ra + [r_convc], w=[rt])
                    if T == 0:
                        P.op("dve", lambda e, ba=ba, t_=t_, fc=fc: e.tensor_scalar(out=t_[:, 0, 0:1], in0=pb[:, ba, 0:1], scalar1=cwe[:, 2 * j, fc:fc + 1],
                                                                               scalar2=convc[:, fc, 3:4], op0=ALU.mult, op1=ALU.add),
                             r=ra + [r_convc, r_cwe], w=[rt])
                    P.op("dve", lambda e, ba=ba, t_=t_, fc=fc: e.scalar_tensor_tensor(out=t_[:], in0=pb[:, ba:ba + 2, 1:257], scalar=convc[:, fc, 1:2],
                                                                                  in1=t_[:], op0=ALU.mult, op1=ALU.add),
                         r=ra + [r_convc, rt], w=[rt])
                    if T == 3:
                        P.op("dve", lambda e, ba=ba, t_=t_, fc=fc: e.scalar_tensor_tensor(out=t_[:, 1, 255:256], in0=pb[:, ba + 1, 257:258], scalar=cwe[:, 2 * j + 1, fc:fc + 1],
                                                                                      in1=t_[:, 1, 255:256], op0=ALU.mult, op1=ALU.add),
                             r=ra + [r_cwe, rt], w=[rt])
                        P.op("dve", lambda e, ba=ba, t_=t_, fc=fc: e.scalar_tensor_tensor(out=t_[:, :, 0:255], in0=pb[:, ba:ba + 2, 2:257], scalar=convc[:, fc, 2:3],
                                                                                      in1=t_[:, :, 0:255], op0=ALU.mult, op1=ALU.add),
                             r=ra + [r_convc, rt], w=[rt])
                        P.op("dve", lambda e, ba=ba, t_=t_, fc=fc: e.scalar_tensor_tensor(out=t_[:, 0, 255:256], in0=pb[:, ba, 257:258], scalar=convc[:, fc, 2:3],
                                                                                      in1=t_[:, 0, 255:256], op0=ALU.mult, op1=ALU.add),
                             r=ra + [r_convc, rt], w=[rt])
                    else:
                        P.op("dve", lambda e, ba=ba, t_=t_, fc=fc: e.scalar_tensor_tensor(out=t_[:], in0=pb[:, ba:ba + 2, 2:258], scalar=convc[:, fc, 2:3],
                                                                                      in1=t_[:], op0=ALU.mult, op1=ALU.add),
                             r=ra + [r_convc, rt], w=[rt])
                    P.op("act", lambda e, t_=t_: e.activation(out=t_[:], in_=t_[:], func=AF.Gelu_apprx_tanh), r=[rt], w=[rt])
                    P.op("dve", lambda e, ba=ba, t_=t_, fc=fc: e.tensor_tensor(out=hT[:, fc, :], in0=t_[:].rearrange("p a b -> p (a b)"), in1=pb[:, ba + 2, :], op=ALU.mult),
                         r=[rt, r_pb[ba + 2]], w=[r_hT])
            dgroups = [(f0, min(2, FCH - f0)) for f0 in range(0, FCH, 2)]
            for hh in range(2):
                pend = [load_w_rows(k.wb_down, dgroups[0][0], dgroups[0][1]), load_w_rows(k.wb_down, dgroups[1][0], dgroups[1][1])]
                for gi, (f0, nf) in enumerate(dgroups):
                    wi = pend.pop(0)
                    if gi + 2 < len(dgroups):
                        pend.append(load_w_rows(k.wb_down, dgroups[gi + 2][0], dgroups[gi + 2][1]))
                    wv = wview_rows(wi, nf)
                    for fl in range(nf):
                        fc = f0 + fl
                        for tl in range(2):
                            tc = 2 * hh + tl
                            for cg in range(4):
                                b = tl * 4 + cg
                                P.op("pe", lambda e, b=b, fc=fc, tc=tc, fl=fl, cg=cg, wv=wv: e.matmul(
                                    pb[:, b, :], lhsT=hT[:, fc, tc * 128:(tc + 1) * 128], rhs=wv[:, fl, cg * 512:(cg + 1) * 512],
                                    start=(fc == 0), stop=(fc == FCH - 1)),
                                    r=[r_hT, r_wr[wi]], w=[r_pb[b]])
                for tl in range(2):
                    tc = 2 * hh + tl
                    for cg in range(4):
                        b = tl * 4 + cg
                        dst = xmid[:, tc, cg * 512:(cg + 1) * 512]
                        P.op("dve", lambda e, b=b, dst=dst: e.tensor_tensor(out=dst, in0=pb[:, b, :], in1=dst, op=ALU.add),
                             r=[r_pb[b], r_xmid[tc]], w=[r_xmid[tc]])
                    rms_rows(128, xmid[:, tc, :], r_xmid[tc], None, None)
                    P.op("dve", lambda e, tc=tc: e.scalar_tensor_tensor(out=xmid[:, tc, :], in0=xmid[:, tc, :], scalar=rstd[:, :], in1=gfb[:],
                                                                       op0=ALU.mult, op1=ALU.mult),
                         r=[r_xmid[tc], r_rstd, r_gfb], w=[r_xmid[tc]])
                    row0 = 512 * T + 128 * tc
                    P.op("sp", lambda e, tc=tc, row0=row0: e.dma_start(out=k.y[j][row0:row0 + 128, :], in_=xmid[:, tc, :]), r=[r_xmid[tc]], dma=r_xmid[tc])

        for j in DEBUG.get("jobs", (0, 1)):
            for T in DEBUG.get("tiles_c", range(4)):
                tile(j, T)
        P.emit_phase()


def prepare_inputs(inp):
    f32 = np.float32
    x_prompt = np.asarray(inp["x_prompt"], f32)
    x_sample = np.asarray(inp["x_sample"], f32)
    shared = {}
    shared["w_in"] = np.ascontiguousarray(np.asarray(inp["w_in"], f32)[0])
    shared["w_out"] = np.ascontiguousarray(np.asarray(inp["w_out"], f32)[0])
    shared["w_up"] = np.ascontiguousarray(np.asarray(inp["w_up"], f32)[0])
    shared["w_down"] = np.ascontiguousarray(np.asarray(inp["w_down"], f32)[0])
    g1 = np.asarray(inp["norm1_g"], f32)[0].reshape(CK, 128).T
    g2 = np.asarray(inp["norm2_g"], f32)[0].reshape(CK, 128).T
    shared["g12c"] = np.ascontiguousarray(np.stack([g1, g2], axis=1))
    shared["gfb"] = bcast128(inp["final_g"])
    shared["sgb"] = bcast128(np.asarray(inp["subln_g"], f32)[0])
    lam = np.concatenate([np.asarray(inp[n], f32)[0] for n in ("lambda_q1", "lambda_k1", "lambda_q2", "lambda_k2")])
    shared["lamv"] = bcast128(lam).reshape(128, 4, 64)
    tab = np.asarray(inp["rel_bias_table"], f32)
    shared["tab"] = np.ascontiguousarray(tab)
    shared["tablr"] = bcast128(np.concatenate([tab[15], tab[31]])).reshape(128, 2, 8)
    rel = np.arange(1536) - 767
    bk = t5_bucket_np(rel)
    ohu = np.zeros((32, 1536), f32)
    ohu[bk, np.arange(1536)] = 1.0
    shared["ohu"] = ohu
    rpbp = np.zeros((8, 15, 128), f32)
    rpbp[:, :, 48:79] = np.asarray(inp["na_rpb"], f32)[0]
    shared["rpbp"] = rpbp
    cw = np.asarray(inp["conv_w"], f32)[0]
    cb = np.asarray(inp["conv_b"], f32)[0]
    cc = np.stack([cw[0], cw[1], cw[2], cb], axis=1)
    shared["convc"] = np.ascontiguousarray(cc.reshape(FCH, 128, 4).transpose(1, 0, 2))
    shared["ident"] = np.eye(128, dtype=f32).astype(ml_dtypes.bfloat16)

    in_maps = []
    for c in range(NCORES):
        m = dict(shared)
        hflag = np.zeros((128, 4), f32)
        for j in range(2):
            nblk, S = JOBS[j]
            if j == 0:
                seq, t = x_prompt[c // 4], c % 4
            else:
                seq, t = x_sample[c // 2], c % 2
            o, blocks, near_true = job_geometry(nblk, S, t)
            xs = np.zeros((S, 128, D), f32)
            ws = np.zeros((3, S), f32)
            for s, gb in enumerate(blocks):
                if gb < 0:
                    continue
                xs[s] = seq[gb * 128:(gb + 1) * 128]
                if 2 <= s <= 19:
                    ws[0, s] = 1.0
                elif gb < o:
                    ws[1, s] = 1.0
                else:
                    ws[2, s] = 1.0
            m["xs%d" % j] = xs
            m["wsel%d" % j] = np.ascontiguousarray(np.broadcast_to(ws[None], (128, 3, S)))
            m["maskd%d" % j] = build_masks(nblk, t, o, near_true)
            hflag[:, 2 * j] = 1.0 if o > 0 else 0.0
            hflag[:, 2 * j + 1] = 1.0 if o + 16 < nblk else 0.0
        m["hflag"] = hflag
        in_maps.append(m)
    return in_maps


def kernel(**inputs):
    in_maps = prepare_inputs(inputs)
    nc = build_program()
    res = run_bass_kernel_spmd(nc, in_maps, core_ids=list(range(NCORES)))
    yp = np.zeros((2, 8192, D), np.float32)
    ysm = np.zeros((4, 4096, D), np.float32)
    for c in range(NCORES):
        r = res.results[c]
        yp[c // 4, (c % 4) * 2048:(c % 4 + 1) * 2048] = r["y0"]
        ysm[c // 2, (c % 2) * 2048:(c % 2 + 1) * 2048] = r["y1"]
    return (yp, ysm)
```

```python
import math
from contextlib import ExitStack

import numpy as np
import ml_dtypes

import concourse.bass as bass
import concourse.mybir as mybir
from concourse.bass_utils import run_bass_kernel_spmd

F32 = mybir.dt.float32
BF16 = mybir.dt.bfloat16
AF = mybir.ActivationFunctionType
ALU = mybir.AluOpType
AX = mybir.AxisListType

D = 2048
CK = 16
HA = 8
HN = 8
INC = 6144
DFF = 5504
FCH = 43
EPS = 1e-6
NCORES = 8
NEAR = 22
JOBS = ((64, 65), (32, 33))
SCALE_A = 0.125
SCALE_N = 128 ** -0.5
NEG = -3.0e5
LAM_INIT = 0.8 - 0.6 * math.exp(-0.3 * 0)

DEBUG = {"stop_after": None, "ext": False}


class Sem:
    def __init__(self, h, name):
        self.h = h
        self.v = 0
        self.name = name


class Res:
    __slots__ = ("name", "wr", "rd", "dsem")

    def __init__(self, name):
        self.name = name
        self.wr = None
        self.rd = []
        self.dsem = None


class Op:
    __slots__ = ("eng", "fn", "deps", "sig", "ev", "dma")


ENGS = ("pe", "act", "dve", "pool", "sp")


class Prog:
    def __init__(self, nc):
        self.nc = nc
        self.esem = {e: Sem(nc.alloc_semaphore(name="es_" + e), e) for e in ("pe", "act", "dve", "pool")}
        self.bar = Sem(nc.alloc_semaphore(name="bar"), "bar")
        self.free_dsems = []
        self.ndsem = 0
        self.reset()
        self.waited = {e: {} for e in ENGS}
        self.nphase = 0
        self.all_res = []

    def reset(self):
        self.ops = {e: [] for e in ENGS}
        self.order = []

    def res(self, name):
        r = Res(name)
        self.all_res.append(r)
        return r

    def _dsem(self, r):
        if r.dsem is None:
            if self.free_dsems:
                r.dsem = self.free_dsems.pop()
            else:
                r.dsem = Sem(self.nc.alloc_semaphore(name="ds%d" % self.ndsem), "ds%d" % self.ndsem)
                self.ndsem += 1
        return r.dsem

    def op(self, eng, fn, r=(), w=(), dma=None, after=()):
        o = Op()
        o.eng = eng
        o.fn = fn
        o.dma = dma
        o.sig = False
        o.ev = None
        deps = []
        seen = set()

        def add(d):
            if d is None or id(d) in seen:
                return
            seen.add(id(d))
            deps.append(d)

        for d in after:
            add(d)
        for x in r:
            add(x.wr)
        for x in w:
            add(x.wr)
            for d in x.rd:
                add(d)
        o.deps = [d for d in deps if not (d.eng == "pe" and eng == "pe" and d.dma is None)]
        for d in o.deps:
            d.sig = True
        for x in w:
            x.wr = o
            x.rd = []
        for x in r:
            x.rd.append(o)
        self.ops[eng].append(o)
        self.order.append(o)
        return o

    def emit_phase(self):
        nc = self.nc
        for o in self.order:
            if o.dma is not None:
                s = self._dsem(o.dma)
                s.v += 16
                o.ev = (s, s.v)
        for e in ("pe", "act", "dve", "pool"):
            ops = self.ops[e]
            lo = [o for o in ops if o.dma is None]
            if lo:
                lo[-1].sig = True
            for o in ops:
                if o.dma is None and o.sig:
                    s = self.esem[e]
                    s.v += 1
                    o.ev = (s, s.v)
        self.nphase += 1
        bar_target = self.nphase * len(ENGS)
        prog = self

        def run(e, eng):
            waited = prog.waited[e]

            def wait(ev):
                s, v = ev
                if waited.get(id(s), 0) < v:
                    eng.wait_ge(s.h, v)
                    waited[id(s)] = v

            last_dma = {}
            for o in prog.ops[e]:
                need = {}
                for d in o.deps:
                    sm, v = d.ev
                    if need.get(id(sm), (None, 0))[1] < v:
                        need[id(sm)] = (sm, v)
                for ev in need.values():
                    wait(ev)
                ins = o.fn(eng)
                if o.dma is not None:
                    ins.then_inc(o.ev[0].h, 16)
                    last_dma[id(o.ev[0])] = o.ev
                elif o.sig:
                    ins.then_inc(o.ev[0].h, 1)
            for ev in last_dma.values():
                wait(ev)
            if e in prog.esem and prog.ops[e]:
                lo = [o for o in prog.ops[e] if o.dma is None]
                if lo:
                    wait(lo[-1].ev)
            eng.sem_inc(prog.bar.h, 1)
            eng.wait_ge(prog.bar.h, bar_target)

        with nc.Block() as block:
            @block.tensor
            def _(eng):
                run("pe", eng)

            @block.scalar
            def _(eng):
                run("act", eng)

            @block.vector
            def _(eng):
                run("dve", eng)

            @block.gpsimd
            def _(eng):
                run("pool", eng)

            @block.sync
            def _(eng):
                run("sp", eng)

        for r in self.all_res:
            r.wr = None
            r.rd = []
            if r.dsem is not None:
                self.free_dsems.append(r.dsem)
                r.dsem = None
        self.all_res = []
        self.reset()


def mkap(t, off, dims):
    return bass.AP(t, off, [list(d) for d in dims])


def psz(t):
    return t[:].ap[0][0]


def t5_bucket_np(rel):
    nb = 16
    me = 8
    ret = np.where(rel > 0, nb, 0)
    n = np.abs(rel)
    nf = np.maximum(n, 1).astype(np.float32)
    large = me + (np.log(nf / np.float32(me)) / np.float32(math.log(128 / 8)) * np.float32(nb - me)).astype(np.int32)
    large = np.minimum(large, nb - 1)
    return ret + np.where(n < me, n, large)


def job_geometry(nblk, nslots, t):
    o = 16 * t
    blocks = [-1] * nslots
    near_true = [False] * NEAR
    used = set()
    for n in range(NEAR):
        gb = o + n - 3
        if 0 <= gb < nblk:
            blocks[n] = gb
            near_true[n] = True
            used.add(gb)
    rest = [b for b in range(nblk) if b not in used]
    for n in range(NEAR):
        if blocks[n] == -1 and n not in (2, 19) and rest:
            blocks[n] = rest.pop(0)
    for s in range(NEAR, nslots):
        if rest:
            blocks[s] = rest.pop(0)
    assert not rest
    return o, blocks, near_true


def nbr_units():
    units = []
    for i in range(16):
        if i == 0:
            dl = list(range(-2, 4))
        elif i == 15:
            dl = list(range(-3, 3))
        else:
            dl = list(range(-2, 3))
        units.append((i, dl))
    units.append((-1, list(range(-2, 3))))
    units.append((16, list(range(-2, 3))))
    return units


def mask_layout():
    off = {}
    col = 0
    for key, nd, w in (("int", 5, 128), (0, 6, 128), (1, 5, 128), (14, 5, 128), (15, 6, 128), (-1, 5, 1), (16, 5, 1)):
        off[key] = col
        col += nd * w
    return off, col


def build_masks(nblk, t, o, near_true):
    R = nblk * 2
    L = nblk * 128
    off, ncol = mask_layout()
    out = np.zeros((128, ncol), np.float32)

    def tile(tq, valid_q, nk):
        if not valid_q:
            return np.zeros((128, len(tq)), np.float32)
        if not near_true[nk]:
            return np.full((128, len(tq)), NEG, np.float32)
        tk = (o + nk - 3) * 128 + np.arange(128)
        r = tq // 64
        c = tq % 64
        rs = np.clip(r - 4, 0, R - 8)
        cs = np.clip(c - 8, 0, 64 - 16)
        rk = (tk // 64)[:, None]
        ckk = (tk % 64)[:, None]
        ok = (rk >= rs[None]) & (rk < rs[None] + 8) & (ckk >= cs[None]) & (ckk < cs[None] + 16)
        return np.where(ok, 0.0, NEG).astype(np.float32)

    units = dict(nbr_units())
    per_unit = {}
    for i, dl in units.items():
        if i == -1:
            tq = np.array([o * 128 - 1])
            vq = o > 0
        elif i == 16:
            tq = np.array([(o + 16) * 128])
            vq = (o + 16) < nblk
        else:
            tq = (o + i) * 128 + np.arange(128)
            vq = True
        per_unit[i] = [tile(tq, vq, i + dl_ + 3) for dl_ in dl]
    for i in range(2, 14):
        for a, b in zip(per_unit[i], per_unit[7]):
            assert np.array_equal(a, b)
    def put(key, tiles):
        c0 = off[key]
        for k, tl in enumerate(tiles):
            w = tl.shape[1]
            out[:, c0 + k * w:c0 + (k + 1) * w] = tl
    put("int", per_unit[7])
    for key in (0, 1, 14, 15, -1, 16):
        put(key, per_unit[key])
    return out.astype(ml_dtypes.bfloat16)


def bcast128(a):
    a = np.asarray(a, np.float32)
    return np.ascontiguousarray(np.broadcast_to(a.reshape(1, -1), (128, a.size)))


class K:
    pass


def build_program():
    nc = bass.Bass("TRN2", target_bir_lowering=False)
    P = Prog(nc)
    k = K()
    k.nc = nc
    k.P = P
    ext = DEBUG["ext"]

    def din(name, shape, dt=F32):
        return nc.dram_tensor(name, list(shape), dt, kind="ExternalInput")

    def dscr(name, shape, dt=BF16):
        return nc.dram_tensor(name, list(shape), dt, kind=("ExternalOutput" if (ext and name in ext) else "Internal"))

    k.xs = [din("xs%d" % j, (JOBS[j][1], 128, D)) for j in range(2)]
    k.w_in = din("w_in", (D, INC))
    k.w_out = din("w_out", (D, D))
    k.w_up = din("w_up", (D, 2 * DFF))
    k.w_down = din("w_down", (DFF, D))
    k.g12c = din("g12c", (128, 2, CK))
    k.gfb = din("gfb", (128, D))
    k.sgb = din("sgb", (128, 128))
    k.lamv = din("lamv", (128, 4, 64))
    k.tab = din("tab", (32, 8))
    k.tablr = din("tablr", (128, 2, 8))
    k.ohu = din("ohu", (32, 1536))
    k.rpbp = din("rpbp", (8, 15, 128))
    k.convc = din("convc", (128, FCH, 4))
    k.ident = din("ident", (128, 128), BF16)
    k.wsel = [din("wsel%d" % j, (128, 3, JOBS[j][1])) for j in range(2)]
    _, mcols = mask_layout()
    k.maskd = [din("maskd%d" % j, (128, mcols), BF16) for j in range(2)]
    k.hflag = din("hflag", (128, 4))
    k.y = [nc.dram_tensor("y%d" % j, [2048, D], F32, kind="ExternalOutput") for j in range(2)]
    k.wb_in = dscr("wb_in", (D, INC))
    k.wb_out = dscr("wb_out", (D, D))
    k.wb_up = dscr("wb_up", (D, 2 * DFF))
    k.wb_down = dscr("wb_down", (DFF, D))
    k.qaT = [dscr("qaT%d" % j, (HA, 128, NEAR * 128)) for j in range(2)]
    k.qnT = [dscr("qnT%d" % j, (HN, 128, NEAR * 128)) for j in range(2)]
    k.knT = [dscr("knT%d" % j, (HN, 128, NEAR * 128)) for j in range(2)]
    k.kaT = [dscr("kaT%d" % j, (HA, 128, JOBS[j][1] * 128)) for j in range(2)]
    k.va = [dscr("va%d" % j, (HA, 128, JOBS[j][1], 129)) for j in range(2)]
    k.vn = [dscr("vn%d" % j, (HN, 128, NEAR, 129)) for j in range(2)]
    k.aoT = [dscr("aoT%d" % j, (16, 128, 2050)) for j in range(2)]
    k.u2 = dscr("u2", (8, 1536), F32)
    k.hsc = dscr("hsc", (8, 128, 1408), F32)

    if not DEBUG.get("skip_a"):
        phase_a(k)
    if DEBUG["stop_after"] == "A":
        return nc
    phase_b(k)
    if DEBUG["stop_after"] == "B":
        return nc
    phase_c(k)
    return nc


def phase_a(k):
    nc, P = k.nc, k.P
    with ExitStack() as es:
        def sb(name, shape, dt=F32):
            return es.enter_context(nc.sbuf_tensor("A_" + name, list(shape), dt))

        def ps(name, shape, dt=F32):
            return es.enter_context(nc.psum_tensor("A_" + name, list(shape), dt))

        ident = sb("ident", (128, 128), BF16)
        g12c = sb("g12c", (128, 2, CK))
        tablr = sb("tablr", (128, 2, 8))
        elr = sb("elr", (128, 2, 8))
        epsc = sb("epsc", (128, 1))
        wsel = [sb("wsel%d" % j, (128, 3, JOBS[j][1])) for j in range(2)]
        wfull = [sb("wfull%d" % j, (128, JOBS[j][1], 8)) for j in range(2)]
        wtmp = sb("wtmp", (128, 65, 8))
        CW = 1024
        cin = [sb("cin%d" % i, (128, 512)) for i in range(2)]
        cout = [sb("cout%d" % i, (128, 512), BF16) for i in range(2)]
        cinA = [sb("cinA%d" % i, (128, 512)) for i in range(2)]
        coutA = [sb("coutA%d" % i, (128, 512), BF16) for i in range(2)]
        xbuf = [sb("xbuf%d" % i, (128, D)) for i in range(3)]
        junk = sb("junk", (128, D), BF16)
        ssq = [sb("ssq%d" % i, (128, 1)) for i in range(3)]
        lnv = [sb("lnv%d" % i, (128, 1)) for i in range(3)]
        rstd = [sb("rstd%d" % i, (128, 1)) for i in range(3)]
        xn = [sb("xn%d" % i, (128, D), BF16) for i in range(2)]
        xnT = [sb("xnT%d" % i, (128, CK, 512), BF16) for i in range(2)]
        wring = [sb("wring%d" % i, (128, CK, 512), BF16) for i in range(3)]
        fmst = [sb("fmst%d" % i, (128, 4, 512), BF16) for i in range(2)]
        vst = [sb("vst%d" % i, (128, 8, 4, 129), BF16) for i in range(2)]
        pT = [ps("pT%d" % i, (128, 8 * 128), BF16) for i in range(2)]
        pO = [ps("pO%d" % i, (128, 512)) for i in range(4)]

        R = P.res
        r_ident, r_g, r_tablr, r_elr, r_eps = R("ident"), R("g12c"), R("tablr"), R("elr"), R("eps")
        r_wsel = [R("wsel0"), R("wsel1")]
        r_wfull = [R("wfull0"), R("wfull1")]
        r_wtmp = R("wtmp")

        P.op("sp", lambda e: e.dma_start(out=ident[:], in_=k.ident.ap()), w=[r_ident], dma=r_ident)
        P.op("sp", lambda e: e.dma_start(out=g12c[:], in_=k.g12c.ap()), w=[r_g], dma=r_g)
        P.op("sp", lambda e: e.dma_start(out=tablr[:], in_=k.tablr.ap()), w=[r_tablr], dma=r_tablr)
        for j in range(2):
            P.op("sp", lambda e, j=j: e.dma_start(out=wsel[j][:], in_=k.wsel[j].ap()), w=[r_wsel[j]], dma=r_wsel[j])
        P.op("dve", lambda e: e.memset(epsc[:], EPS), w=[r_eps])
        P.op("act", lambda e: e.activation(out=elr[:], in_=tablr[:], func=AF.Exp), r=[r_tablr], w=[r_elr])
        for j in range(2):
            S = JOBS[j][1]
            wf, ws = wfull[j], wsel[j]
            pw, pe_, pt = psz(wf), psz(ws), psz(wtmp)
            pl = psz(elr)

            def bc_s(c, ws=ws, pe_=pe_, S=S):
                return mkap(ws, c * S, [(pe_, 128), (1, S), (0, 8)])

            def bc_h(c, S=S, pl=pl):
                return mkap(elr, c * 8, [(pl, 128), (0, S), (1, 8)])

            wt = mkap(wtmp, 0, [(pt, 128), (8, S), (1, 8)])
            P.op("dve", lambda e, wf=wf, bc_s=bc_s, bc_h=bc_h: e.tensor_tensor(out=wf[:], in0=bc_s(1), in1=bc_h(0), op=ALU.mult),
                 r=[r_wsel[j], r_elr], w=[r_wfull[j]])
            P.op("dve", lambda e, wt=wt, bc_s=bc_s, bc_h=bc_h: e.tensor_tensor(out=wt, in0=bc_s(2), in1=bc_h(1), op=ALU.mult),
                 r=[r_wsel[j], r_elr], w=[r_wtmp])
            P.op("dve", lambda e, wf=wf, wt=wt: e.tensor_tensor(out=wf[:], in0=wf[:], in1=wt, op=ALU.add),
                 r=[r_wtmp], w=[r_wfull[j]])
            P.op("dve", lambda e, wf=wf, bc_s=bc_s: e.tensor_tensor(out=wf[:], in0=wf[:], in1=bc_s(0), op=ALU.add),
                 r=[r_wsel[j]], w=[r_wfull[j]])

        r_cin = {"pool": [R("cin0"), R("cin1")], "act": [R("cinA0"), R("cinA1")]}
        r_cout = {"pool": [R("cout0"), R("cout1")], "act": [R("coutA0"), R("coutA1")]}
        cbuf_in = {"pool": cin, "act": cinA}
        cbuf_out = {"pool": cout, "act": coutA}
        conv_state = {"pool": 0, "act": 0, "n": 0}
        GROUP_ORDER = [1024, 1536, 2048, 2560, 0, 512, 3072, 3584, 4096, 4608, 5120, 5632]
        st_in = {}
        for col0 in GROUP_ORDER:
            st_in[col0] = []
            for rb in range(CK):
                eng = "pool" if conv_state["n"] % 2 == 0 else "act"
                conv_state["n"] += 1
                i = conv_state[eng] % 2
                conv_state[eng] += 1
                ci, co = cbuf_in[eng][i], cbuf_out[eng][i]
                rci, rco = r_cin[eng][i], r_cout[eng][i]
                sv = k.w_in[rb * 128:(rb + 1) * 128, col0:col0 + 512]
                dv = k.wb_in[rb * 128:(rb + 1) * 128, col0:col0 + 512]
                P.op(eng, lambda e, ci=ci, sv=sv: e.dma_start(out=ci[:, 0:512], in_=sv), w=[rci], dma=rci)
                if eng == "pool":
                    P.op("pool", lambda e, ci=ci, co=co, rb=rb: e.tensor_scalar(
                        out=co[:, 0:512], in0=ci[:, 0:512], scalar1=g12c[:, 0, rb:rb + 1], scalar2=1.0, op0=ALU.mult, op1=ALU.mult),
                        r=[rci, r_g], w=[rco])
                else:
                    P.op("act", lambda e, ci=ci, co=co, rb=rb: e.activation(out=co[:, 0:512], in_=ci[:, 0:512], func=AF.Copy, scale=g12c[:, 0, rb:rb + 1]),
                         r=[rci, r_g], w=[rco])
                st_in[col0].append(P.op(eng, lambda e, co=co, dv=dv: e.dma_start(out=dv, in_=co[:, 0:512]), r=[rco], dma=rco))
        groups_loaded = set()

        r_x = [R("xbuf%d" % i) for i in range(3)]
        r_junk = R("junk")
        r_ssq = [R("ssq%d" % i) for i in range(3)]
        r_lnv = [R("lnv%d" % i) for i in range(3)]
        r_rstd = [R("rstd%d" % i) for i in range(3)]
        r_xn = [R("xn%d" % i) for i in range(2)]
        r_xnT = [[R("xnT%d_%d" % (i, b)) for b in range(4)] for i in range(2)]
        r_w = [R("wring%d" % i) for i in range(3)]
        r_fm = [R("fmst%d" % i) for i in range(2)]
        r_vst = [R("vst%d" % i) for i in range(2)]
        r_pT = [R("pT%d" % i) for i in range(2)]
        r_pO = [R("pO%d" % i) for i in range(4)]
        cnt = {"x": 0, "xn": 0, "pT": 0, "w": 0, "pO": 0, "fm": 0, "vst": 0, "tile": 0}

        SUBS_NEAR = [("ka", 1024), ("ka", 1536), ("va", 2048), ("va", 2560), ("qa", 0), ("qa", 512),
                     ("qn", 3072), ("qn", 3584), ("kn", 4096), ("kn", 4608), ("vn", 5120), ("vn", 5632)]
        SUBS_FAR = [("ka", 1024), ("ka", 1536), ("va", 2048), ("va", 2560)]

        def load_x(j, s):
            i = cnt["x"] % 3
            cnt["x"] += 1
            P.op("sp", lambda e, i=i, j=j, s=s: e.dma_start(out=xbuf[i][:], in_=k.xs[j][s]), w=[r_x[i]], dma=r_x[i])
            return i

        def norm_block(xi, xnT_i, b):
            ni = cnt["xn"] % 2
            cnt["xn"] += 1
            P.op("act", lambda e: e.activation(out=junk[:], in_=xbuf[xi][:], func=AF.Square, accum_out=ssq[xi][:]),
                 r=[r_x[xi]], w=[r_junk, r_ssq[xi]])
            P.op("act", lambda e: e.activation(out=lnv[xi][:], in_=ssq[xi][:], func=AF.Ln, scale=1.0 / D, bias=epsc[:]),
                 r=[r_ssq[xi], r_eps], w=[r_lnv[xi]])
            P.op("act", lambda e: e.activation(out=rstd[xi][:], in_=lnv[xi][:], func=AF.Exp, scale=-0.5),
                 r=[r_lnv[xi]], w=[r_rstd[xi]])
            P.op("dve", lambda e: e.tensor_scalar(out=xn[ni][:], in0=xbuf[xi][:], scalar1=rstd[xi][:], scalar2=None, op0=ALU.mult),
                 r=[r_x[xi], r_rstd[xi]], w=[r_xn[ni]])
            for half in range(2):
                pi = cnt["pT"] % 2
                cnt["pT"] += 1
                for q in range(8):
                    ck = half * 8 + q
                    P.op("pe", lambda e, pi=pi, q=q, ck=ck: e.transpose(out=pT[pi][:, q * 128:(q + 1) * 128],
                                                                        in_=xn[ni][:, ck * 128:(ck + 1) * 128], identity=ident[:]),
                         r=[r_xn[ni], r_ident], w=[r_pT[pi]])
                dst = xnT[xnT_i][:, half * 8:(half + 1) * 8, b * 128:(b + 1) * 128]
                src = pT[pi][:].rearrange("p (a b) -> p a b", b=128)
                P.op("dve", lambda e, dst=dst, src=src: e.tensor_copy(out=dst, in_=src), r=[r_pT[pi]], w=[r_xnT[xnT_i][b]])

        def load_w(col0, first):
            i = cnt["w"] % 3
            cnt["w"] += 1
            src = k.wb_in[:, col0:col0 + 512].rearrange("(ck p) f -> p ck f", p=128)
            aft = ()
            if col0 not in groups_loaded:
                groups_loaded.add(col0)
                aft = st_in[col0]
            P.op("sp", lambda e, i=i, src=src: e.dma_start(out=wring[i][:], in_=src), w=[r_w[i]], dma=r_w[i], after=aft)
            return i

        def do_tile(j, s0, nb, near, xnT_i, prefetch):
            S = JOBS[j][1]
            N = nb * 128
            subs = SUBS_NEAR if near else SUBS_FAR
            wq = []
            state = {"first": cnt["w"] == 0}
            nxt = load_w(subs[0][1], state["first"])
            for si, (kind, col0) in enumerate(subs):
                wi = nxt
                if si + 1 < len(subs):
                    nxt = load_w(subs[si + 1][1], False)
                if si == 1 and prefetch is not None:
                    prefetch()
                hb = (col0 % 1024) // 128
                if kind in ("qa", "ka", "qn", "kn"):
                    fi = cnt["fm"] % 2
                    cnt["fm"] += 1
                    for fc in range(4):
                        oi = cnt["pO"] % 4
                        cnt["pO"] += 1
                        for ck in range(CK):
                            P.op("pe", lambda e, oi=oi, wi=wi, fc=fc, ck=ck: e.matmul(
                                pO[oi][:, 0:N], lhsT=wring[wi][:, ck, fc * 128:(fc + 1) * 128], rhs=xnT[xnT_i][:, ck, 0:N],
                                start=(ck == 0), stop=(ck == CK - 1)),
                                r=[r_w[wi]] + r_xnT[xnT_i][0:nb], w=[r_pO[oi]])
                        P.op("act", lambda e, oi=oi, fi=fi, fc=fc: e.activation(out=fmst[fi][:, fc, 0:N], in_=pO[oi][:, 0:N], func=AF.Copy),
                             r=[r_pO[oi]], w=[r_fm[fi]])
                    dstT = {"qa": k.qaT, "ka": k.kaT, "qn": k.qnT, "kn": k.knT}[kind][j]
                    dv = dstT[hb:hb + 4, :, s0 * 128:s0 * 128 + N].rearrange("h p n -> p h n")
                    P.op("act", lambda e, fi=fi, dv=dv: e.dma_start(out=dv, in_=fmst[fi][:, :, 0:N]), r=[r_fm[fi]], dma=r_fm[fi])
                else:
                    if hb == 0:
                        vi = cnt["vst"] % 2
                        cnt["vst"] += 1
                        state["vi"] = vi
                    vi = state["vi"]
                    for b in range(nb):
                        oi = cnt["pO"] % 4
                        cnt["pO"] += 1
                        for ck in range(CK):
                            P.op("pe", lambda e, oi=oi, wi=wi, b=b, ck=ck: e.matmul(
                                pO[oi][:, :], lhsT=xnT[xnT_i][:, ck, b * 128:(b + 1) * 128], rhs=wring[wi][:, ck, :],
                                start=(ck == 0), stop=(ck == CK - 1)),
                                r=[r_w[wi], r_xnT[xnT_i][b]], w=[r_pO[oi]])
                        src = pO[oi][:].rearrange("p (h e) -> p h e", e=128)
                        dst = vst[vi][:, hb:hb + 4, b, 0:128]
                        if kind == "va":
                            wf = wfull[j]
                            wb = mkap(wf, (s0 + b) * 8 + hb, [(psz(wf), 128), (1, 4), (0, 128)])
                            P.op("dve", lambda e, dst=dst, src=src, wb=wb: e.tensor_tensor(out=dst, in0=src, in1=wb, op=ALU.mult),
                                 r=[r_pO[oi], r_wfull[j]], w=[r_vst[vi]])
                            if hb == 4:
                                ones_dst = vst[vi][:, :, b, 128:129]
                                wsrc = mkap(wf, (s0 + b) * 8, [(psz(wf), 128), (1, 8), (1, 1)])
                                P.op("dve", lambda e, ones_dst=ones_dst, wsrc=wsrc: e.tensor_copy(out=ones_dst, in_=wsrc),
                                     r=[r_wfull[j]], w=[r_vst[vi]])
                        else:
                            P.op("act", lambda e, dst=dst, src=src: e.activation(out=dst, in_=src, func=AF.Copy),
                                 r=[r_pO[oi]], w=[r_vst[vi]])
                            if hb == 4:
                                ones_dst = vst[vi][:, :, b, 128:129]
                                P.op("dve", lambda e, ones_dst=ones_dst: e.memset(ones_dst, 1.0), w=[r_vst[vi]])
                    if hb == 4:
                        dstV = (k.va if kind == "va" else k.vn)[j]
                        dv = dstV[:, :, s0:s0 + nb, :].rearrange("h p s e -> p h s e")
                        P.op("act", lambda e, vi=vi, dv=dv: e.dma_start(out=dv, in_=vst[vi][:, :, 0:nb, :]), r=[r_vst[vi]], dma=r_vst[vi])

        tiles = []
        for j in range(2):
            S = JOBS[j][1]
            s = 0
            while s < NEAR:
                nb = min(4, NEAR - s)
                tiles.append((j, s, nb, True))
                s += nb
            while s < S:
                nb = min(4, S - s)
                tiles.append((j, s, nb, False))
                s += nb
        tiles = [t for t in tiles if not t[3]] + [t for t in tiles if t[3]]
        if DEBUG.get("max_tiles"):
            tiles = tiles[:DEBUG["max_tiles"]]

        def prep_tile(ti):
            j, s0, nb, near = tiles[ti]
            xi_list = [load_x(j, s0 + b) for b in range(nb)]
            for b in range(nb):
                norm_block(xi_list[b], ti % 2, b)

        def prep_tile_interleaved(ti):
            j, s0, nb, near = tiles[ti]
            pend = []
            for b in range(nb):
                pend.append(load_x(j, s0 + b))
                if len(pend) == 2:
                    norm_block(pend.pop(0), ti % 2, b - 1)
            bb = nb - len(pend)
            for xi in pend:
                norm_block(xi, ti % 2, bb)
                bb += 1

        prep_tile_interleaved(0)
        for ti in range(len(tiles)):
            j, s0, nb, near = tiles[ti]
            pf = (lambda ti=ti: prep_tile_interleaved(ti + 1)) if ti + 1 < len(tiles) else None
            do_tile(j, s0, nb, near, ti % 2, pf)

        P.emit_phase()


def phase_b(k):
    nc, P = k.nc, k.P
    moff, mcols = mask_layout()
    with ExitStack() as es:
        def sb(name, shape, dt=F32):
            return es.enter_context(nc.sbuf_tensor("B_" + name, list(shape), dt))

        def ps(name, shape, dt=F32):
            return es.enter_context(nc.psum_tensor("B_" + name, list(shape), dt))

        R = P.res
        SMAX = JOBS[0][1]
        ident = sb("ident", (128, 128), BF16)
        tablr = sb("tablr", (128, 2, 8))
        lamv = sb("lamv", (128, 4, 64))
        lprod = sb("lprod", (128, 2, 64))
        lsum = sb("lsum", (128, 2))
        lexp = sb("lexp", (128, 2))
        nlam = sb("nlam", (128, 1))
        sg = sb("sg", (128, 128))
        epsc = sb("epsc", (128, 1))
        tabp = sb("tabp", (128, 128))
        ohup = sb("ohup", (128, 1536))
        u2s = sb("u2s", (8, 1536))
        KT = [sb("KT%d" % i, (128, SMAX * 128), BF16) for i in range(2)]
        VH = [sb("VH%d" % i, (128, SMAX, 129), BF16) for i in range(2)]
        QT = [sb("QT%d" % i, (128, 18 * 128), BF16) for i in range(2)]
        QTm = [[sb("QTm%d_%d" % (m, i), (128, 18 * 128), BF16) for i in range(2)] for m in range(2)]
        HH = [sb("HH%d" % i, (128, 1408)) for i in range(2)]
        PT = [sb("PT%d" % i, (128, 2, 512), BF16) for i in range(3)]
        PTm = sb("PTm", (128, SMAX * 4), BF16)
        Gmini = sb("Gmini", (128, 18, 2))
        osb = sb("osb", (128, 8, 129))
        rz = sb("rz", (128, 8))
        tt = sb("tt", (128, 8, 128))
        od = sb("od", (128, 4, 128))
        sqj = sb("sqj", (128, 128))
        ssq = sb("ssq", (128, 4))
        lnv = sb("lnv", (128, 4))
        rstd = sb("rstd", (128, 4))
        tmp2 = sb("tmp2", (128, 4, 128))
        onb = [sb("onb%d" % i, (128, 4, 128), BF16) for i in range(2)]
        AOh = [sb("AOh%d" % i, (128, 2050), BF16) for i in range(2)]
        Trt = [sb("Trt%d" % i, (128, 7, 2, 64)) for i in range(2)]
        Tfix = [sb("Tfix%d" % i, (128, 7, 128)) for i in range(2)]
        maskt = sb("maskt", (128, mcols), BF16)
        PTn = [sb("PTn%d" % i, (128, 7, 128), BF16) for i in range(3)]
        rzn = [sb("rzn%d" % i, (128, 1)) for i in range(3)]
        onbn = [sb("onbn%d" % i, (128, 128), BF16) for i in range(3)]

        psS = [ps("psS%d" % i, (128, 2, 512)) for i in range(2)]
        acc = ps("acc", (128, 3, 512))
        pTr = ps("pTr", (128, 1024), BF16)

        r_ident, r_tablr, r_lamv, r_lprod, r_lsum, r_lexp, r_nlam = (R(n) for n in ("ident", "tablr", "lamv", "lprod", "lsum", "lexp", "nlam"))
        r_sg, r_eps, r_tabp, r_ohup, r_u2s, r_u2d = (R(n) for n in ("sg", "eps", "tabp", "ohup", "u2s", "u2d"))
        r_KT = [R("KT0"), R("KT1")]
        r_VH = [R("VH0"), R("VH1")]
        r_QT = [R("QT0"), R("QT1")]
        r_QTm = [[R("QTm%d_%d" % (m, i)) for i in range(2)] for m in range(2)]
        r_HH = [R("HH0"), R("HH1")]
        r_PT = [R("PT%d" % i) for i in range(3)]
        r_PTm, r_Gm, r_osb, r_rz, r_tt, r_od, r_sqj, r_ssq, r_lnv, r_rstd, r_tmp2 = (
            R(n) for n in ("PTm", "Gm", "osb", "rz", "tt", "od", "sqj", "ssq", "lnv", "rstd", "tmp2"))
        r_onb = [R("onb0"), R("onb1")]
        r_AOh = [R("AOh0"), R("AOh1")]
        r_Trt = [R("Trt0"), R("Trt1")]
        r_Tfix = [R("Tfix0"), R("Tfix1")]
        r_mask = R("mask")
        r_PTn = [R("PTn0"), R("PTn1"), R("PTn2")]
        r_rzn = [R("rzn%d" % i) for i in range(3)]
        r_onbn = [R("onbn%d" % i) for i in range(3)]
        r_psS = [R("psS0"), R("psS1")]
        r_acc = [R("acc%d" % i) for i in range(3)]
        r_pTr = [R("pTr%d" % i) for i in range(4)]
        cnt = {"S": 0, "PT": 0, "onb": 0, "pTr": 0, "hb": 0, "PTn": 0, "accn": 0, "tb": 0}

        P.op("sp", lambda e: e.dma_start(out=ident[:], in_=k.ident.ap()), w=[r_ident], dma=r_ident)
        P.op("sp", lambda e: e.dma_start(out=tablr[:], in_=k.tablr.ap()), w=[r_tablr], dma=r_tablr)
        P.op("sp", lambda e: e.dma_start(out=lamv[:], in_=k.lamv.ap()), w=[r_lamv], dma=r_lamv)
        P.op("sp", lambda e: e.dma_start(out=sg[:], in_=k.sgb.ap()), w=[r_sg], dma=r_sg)
        P.op("dve", lambda e: e.memset(epsc[:], EPS), w=[r_eps])
        for i in range(2):
            P.op("dve", lambda e, i=i: e.memset(QTm[0][i][64:128, :], 0.0), w=[r_QTm[0][i]])
            P.op("dve", lambda e, i=i: e.memset(QTm[1][i][0:64, :], 0.0), w=[r_QTm[1][i]])
        P.op("dve", lambda e: e.memset(tabp[:], 0.0), w=[r_tabp])
        P.op("dve", lambda e: e.memset(ohup[:], 0.0), w=[r_ohup])
        P.op("sp", lambda e: e.dma_start(out=tabp[0:32, 0:8], in_=k.tab.ap()), w=[r_tabp], dma=r_tabp)
        P.op("sp", lambda e: e.dma_start(out=ohup[0:32, :], in_=k.ohu.ap()), w=[r_ohup], dma=r_ohup)
        pl = psz(lamv)
        P.op("dve", lambda e: e.tensor_tensor(out=lprod[:], in0=mkap(lamv, 0, [(pl, 128), (128, 2), (1, 64)]),
                                              in1=mkap(lamv, 64, [(pl, 128), (128, 2), (1, 64)]), op=ALU.mult),
             r=[r_lamv], w=[r_lprod])
        P.op("dve", lambda e: e.tensor_reduce(out=lsum[:], in_=lprod[:], axis=AX.X, op=ALU.add), r=[r_lprod], w=[r_lsum])
        P.op("act", lambda e: e.activation(out=lexp[:], in_=lsum[:], func=AF.Exp), r=[r_lsum], w=[r_lexp])
        P.op("dve", lambda e: e.tensor_tensor(out=nlam[:], in0=lexp[:, 1:2], in1=lexp[:, 0:1], op=ALU.subtract), r=[r_lexp], w=[r_nlam])
        P.op("dve", lambda e: e.tensor_scalar(out=nlam[:], in0=nlam[:], scalar1=-LAM_INIT, scalar2=None, op0=ALU.add), r=[r_nlam], w=[r_nlam])
        P.op("dve", lambda e: e.tensor_scalar(out=sg[:], in0=sg[:], scalar1=1.0 - LAM_INIT, scalar2=None, op0=ALU.mult), r=[r_sg], w=[r_sg])
        P.op("dve", lambda e: e.tensor_scalar(out=tabp[:], in0=tabp[:], scalar1=1.0 / SCALE_A, scalar2=None, op0=ALU.mult), r=[r_tabp], w=[r_tabp])
        for q in range(3):
            P.op("pe", lambda e, q=q: e.matmul(psS[q % 2][:, q // 2, :], lhsT=tabp[:], rhs=ohup[:, q * 512:(q + 1) * 512], start=True, stop=True),
                 r=[r_tabp, r_ohup], w=[r_psS[q % 2]])
            P.op("dve", lambda e, q=q: e.tensor_copy(out=u2s[:, q * 512:(q + 1) * 512], in_=psS[q % 2][0:8, q // 2, :]),
                 r=[r_psS[q % 2]], w=[r_u2s])
        P.op("sp", lambda e: e.dma_start(out=k.u2.ap(), in_=u2s[:]), r=[r_u2s], w=[r_u2d], dma=r_u2s)

        def acc_ap(a, rows=128, cols=129):
            return acc[0:rows, a // 3, (a % 3) * 129:(a % 3) * 129 + cols]

        def load_head(j, h, nbr):
            S = JOBS[j][1]
            hb = cnt["hb"] % 2
            cnt["hb"] += 1
            if not nbr:
                P.op("sp", lambda e: e.dma_start(out=KT[hb][:, 0:S * 128], in_=k.kaT[j][h]), w=[r_KT[hb]], dma=r_KT[hb])
                P.op("sp", lambda e: e.dma_start(out=VH[hb][:, 0:S, :], in_=k.va[j][h]), w=[r_VH[hb]], dma=r_VH[hb])
                P.op("sp", lambda e: e.dma_start(out=QTm[0][hb][0:64, :], in_=k.qaT[j][h][0:64, 2 * 128:20 * 128]), w=[r_QTm[0][hb]], dma=r_QTm[0][hb])
                P.op("sp", lambda e: e.dma_start(out=QTm[1][hb][64:128, :], in_=k.qaT[j][h][64:128, 2 * 128:20 * 128]), w=[r_QTm[1][hb]], dma=r_QTm[1][hb])
                P.op("sp", lambda e: e.dma_start(out=HH[hb][:], in_=mkap(k.u2, h * 1536, [(1, 128), (1, 1408)])),
                     r=[r_u2d], w=[r_HH[hb]], dma=r_HH[hb])
            else:
                P.op("sp", lambda e: e.dma_start(out=KT[hb][:, 0:NEAR * 128], in_=k.knT[j][h]), w=[r_KT[hb]], dma=r_KT[hb])
                P.op("sp", lambda e: e.dma_start(out=VH[hb][:, 0:NEAR, :], in_=k.vn[j][h]), w=[r_VH[hb]], dma=r_VH[hb])
                P.op("sp", lambda e: e.dma_start(out=QT[hb][:], in_=k.qnT[j][h][:, 2 * 128:20 * 128]), w=[r_QT[hb]], dma=r_QT[hb])
                tb = cnt["tb"] % 2
                cnt["tb"] += 1
                for dl in range(-3, 4):
                    for rk in range(2):
                        for rq in range(2):
                            dr = 2 * dl + rk - rq + 7
                            P.op("pool", lambda e, dl=dl, rk=rk, rq=rq, dr=dr: e.dma_start(
                                out=Trt[tb][rk * 64:(rk + 1) * 64, dl + 3, rq, :],
                                in_=mkap(k.rpbp, (h * 15 + dr) * 128, [(1, 64), (1, 64)])), w=[r_Trt[tb]], dma=r_Trt[tb])
                pt_ = psz(Trt[tb])
                for rq in range(2):
                    P.op("pool", lambda e, rq=rq: e.tensor_scalar(
                        out=Tfix[tb][:, :, rq * 64:(rq + 1) * 64],
                        in0=mkap(Trt[tb], rq * 64 + 63, [(pt_, 128), (128, 7), (-1, 64)]),
                        scalar1=1.0 / SCALE_N, scalar2=1.0, op0=ALU.mult, op1=ALU.mult),
                        r=[r_Trt[tb]], w=[r_Tfix[tb]])
                return hb, tb
            return hb, None

        def finish_diff(rows, nch, ao, ecols):
            na = 2 * nch
            nbanks = (na + 2) // 3
            for b in range(nbanks):
                n_in = min(3, na - 3 * b)
                P.op("dve", lambda e, b=b, n_in=n_in: e.tensor_copy(
                    out=osb[0:rows, 3 * b:3 * b + n_in, :], in_=acc[0:rows, b, 0:n_in * 129].rearrange("p (a c) -> p a c", c=129)),
                    r=[r_acc[b]], w=[r_osb])
            po = psz(osb)
            P.op("dve", lambda e: e.reciprocal(out=rz[0:rows, 0:na], in_=mkap(osb, 128, [(po, rows), (129, na)])), r=[r_osb], w=[r_rz])
            P.op("dve", lambda e: e.tensor_tensor(out=tt[0:rows, 0:na, :], in0=osb[0:rows, 0:na, 0:128],
                                                  in1=mkap(rz, 0, [(psz(rz), rows), (1, na), (0, 128)]), op=ALU.mult),
                 r=[r_osb, r_rz], w=[r_tt])
            ptt = psz(tt)
            P.op("dve", lambda e: e.scalar_tensor_tensor(out=od[0:rows, 0:nch, :], in0=mkap(tt, 128, [(ptt, rows), (256, nch), (1, 128)]),
                                                         scalar=nlam[0:rows, :], in1=mkap(tt, 0, [(ptt, rows), (256, nch), (1, 128)]),
                                                         op0=ALU.mult, op1=ALU.add),
                 r=[r_tt, r_nlam], w=[r_od])
            P.op("dve", lambda e: e.tensor_tensor(out=tmp2[0:rows, 0:nch, :], in0=od[0:rows, 0:nch, :], in1=od[0:rows, 0:nch, :], op=ALU.mult),
                 r=[r_od], w=[r_tmp2])
            P.op("dve", lambda e: e.tensor_reduce(out=ssq[0:rows, 0:nch], in_=tmp2[0:rows, 0:nch, :], axis=AX.X, op=ALU.add),
                 r=[r_tmp2], w=[r_ssq])
            P.op("act", lambda e: e.activation(out=lnv[0:rows, 0:nch], in_=ssq[0:rows, 0:nch], func=AF.Ln, scale=1.0 / 128, bias=epsc[0:rows, :]),
                 r=[r_ssq, r_eps], w=[r_lnv])
            P.op("act", lambda e: e.activation(out=rstd[0:rows, 0:nch], in_=lnv[0:rows, 0:nch], func=AF.Exp, scale=-0.5), r=[r_lnv], w=[r_rstd])
            P.op("dve", lambda e: e.tensor_tensor(out=tmp2[0:rows, 0:nch, :], in0=od[0:rows, 0:nch, :],
                                                  in1=mkap(rstd, 0, [(psz(rstd), rows), (1, nch), (0, 128)]), op=ALU.mult),
                 r=[r_od, r_rstd], w=[r_tmp2])
            oi = cnt["onb"] % 2
            cnt["onb"] += 1
            P.op("dve", lambda e: e.tensor_tensor(out=onb[oi][0:rows, 0:nch, :], in0=tmp2[0:rows, 0:nch, :],
                                                  in1=mkap(sg, 0, [(psz(sg), rows), (0, nch), (1, 128)]), op=ALU.mult),
                 r=[r_tmp2, r_sg], w=[r_onb[oi]])
            def part2():
                for c in range(nch):
                    P.op("pe", lambda e, c=c: e.transpose(out=pTr[:, c * 128:c * 128 + rows], in_=onb[oi][0:rows, c, :], identity=ident[0:rows, 0:rows]),
                         r=[r_onb[oi], r_ident], w=[r_pTr[0]])
                if rows == 128:
                    dst = AOh[ao][:, ecols[0]:ecols[0] + 128 * nch]
                    src = pTr[:, 0:128 * nch]
                else:
                    dst = mkap(AOh[ao], 0, [(psz(AOh[ao]), 128), (2049, 2)])
                    src = pTr[:, 0:2]
                P.op("dve", lambda e, dst=dst, src=src: e.tensor_copy(out=dst, in_=src), r=[r_pTr[0]], w=[r_AOh[ao]])
            return part2

        def diff_head(j, h, hb, ao):
            S = JOBS[j][1]
            pq = psz(QT[hb])
            ph = psz(HH[hb])
            G = Gmini
            pg = psz(G)
            P.op("dve", lambda e: e.tensor_copy(out=G[:, 0:2, 0:1], in_=mkap(HH[hb], 640, [(ph, 128), (128, 2), (1, 1)])), r=[r_HH[hb]], w=[r_Gm])
            P.op("dve", lambda e: e.tensor_copy(out=G[:, 2:18, 0:1], in_=mkap(HH[hb], 896, [(ph, 128), (0, 16), (1, 1)])), r=[r_HH[hb]], w=[r_Gm])
            P.op("dve", lambda e: e.tensor_copy(out=G[:, 0:16, 1:2], in_=mkap(HH[hb], 511, [(ph, 128), (0, 16), (1, 1)])), r=[r_HH[hb]], w=[r_Gm])
            P.op("dve", lambda e: e.tensor_copy(out=G[:, 16:18, 1:2], in_=mkap(HH[hb], 639, [(ph, 128), (128, 2), (1, 1)])), r=[r_HH[hb]], w=[r_Gm])
            def issue_S(g, s):
                    qc0 = (4 * g + 1) * 128
                    ri = cnt["S"] % 2
                    cnt["S"] += 1
                    for m in range(2):
                        P.op("pe", lambda e, ri=ri, m=m, s=s: e.matmul(
                            psS[ri][:, m, :], lhsT=KT[hb][:, s * 128:(s + 1) * 128],
                            rhs=QTm[m][hb][:, qc0:qc0 + 512], start=True, stop=True),
                            r=[r_KT[hb], r_QTm[m][hb]], w=[r_psS[ri]])
                    bias = None
                    if 2 <= s <= 19:
                        d0 = (s - 3) - 4 * g
                        sw = 5 - d0
                        if sw <= 0:
                            bias = tablr[:, 1, h:h + 1]
                        elif sw >= 7:
                            bias = tablr[:, 0, h:h + 1]
                        else:
                            win = mkap(HH[hb], 1407 - 128 * sw, [(ph, 128), (0, 2), (-1, 512)])
                            P.op("dve", lambda e, ri=ri, win=win: e.tensor_tensor(out=psS[ri][:], in0=psS[ri][:], in1=win, op=ALU.add),
                                 r=[r_psS[ri], r_HH[hb]], w=[r_psS[ri]])
                    pi = cnt["PT"] % 3
                    cnt["PT"] += 1
                    if bias is None:
                        P.op("act", lambda e, ri=ri, pi=pi: e.activation(out=PT[pi][:], in_=psS[ri][:], func=AF.Exp, scale=SCALE_A),
                             r=[r_psS[ri]], w=[r_PT[pi]])
                    else:
                        P.op("act", lambda e, ri=ri, pi=pi, bias=bias: e.activation(out=PT[pi][:], in_=psS[ri][:], func=AF.Exp, scale=SCALE_A, bias=bias),
                             r=[r_psS[ri], r_tablr], w=[r_PT[pi]])
                    return pi

            def issue_PV(s, pi):
                    for c in range(4):
                        for m in range(2):
                            a = 2 * c + m
                            P.op("pe", lambda e, pi=pi, c=c, m=m, a=a, s=s: e.matmul(
                                acc_ap(a), lhsT=PT[pi][:, m, c * 128:(c + 1) * 128], rhs=VH[hb][:, s, :],
                                start=(s == 0 and a % 3 == 0), stop=(s == S - 1), skip_group_check=True),
                                r=[r_PT[pi], r_VH[hb]], w=[r_acc[a // 3]])

            parts = DEBUG.get("b_parts", ("main", "finish", "mini"))
            steps = [(g, s) for g in range(4 if "main" in parts else 0) for s in range(S)]
            pending = []
            prev = None

            def retire(prev):
                g_, s_, pi_ = prev
                issue_PV(s_, pi_)
                if s_ == S - 1 and "finish" in parts:
                    pending.append([3, finish_diff(128, 4, ao, [1 + 512 * g_ + 128 * c for c in range(4)])])

            inflight = []
            for (g, s) in steps:
                pi = issue_S(g, s)
                inflight.append((g, s, pi))
                if len(inflight) >= 3:
                    retire(inflight.pop(0))
                for pd in pending:
                    pd[0] -= 1
                while pending and pending[0][0] <= 0:
                    pending.pop(0)[1]()
            while inflight:
                retire(inflight.pop(0))
            if "mini" not in parts:
                while pending:
                    pending.pop(0)[1]()
                P.op("act", lambda e: e.dma_start(out=k.aoT[j][h], in_=AOh[ao][:]), r=[r_AOh[ao]], dma=r_AOh[ao])
                return
            ri = cnt["S"] % 2
            cnt["S"] += 1
            for s in range(S):
                for m in range(2):
                    P.op("pe", lambda e, ri=ri, m=m, s=s: e.matmul(
                        psS[ri][:, 0, s * 4 + 2 * m:s * 4 + 2 * m + 2], lhsT=KT[hb][:, s * 128:(s + 1) * 128],
                        rhs=mkap(QTm[m][hb], 127, [(pq, 128), (2049, 2)]), start=True, stop=True),
                        r=[r_KT[hb], r_QTm[m][hb]], w=[r_psS[ri]])
            pps = psz(psS[ri])
            reg = mkap(psS[ri], 8, [(pps, 128), (4, 18), (2, 2), (1, 2)])
            P.op("dve", lambda e, reg=reg: e.tensor_tensor(out=reg, in0=reg, in1=mkap(G, 0, [(pg, 128), (2, 18), (0, 2), (1, 2)]), op=ALU.add),
                 r=[r_psS[ri], r_Gm], w=[r_psS[ri]])
            P.op("act", lambda e, ri=ri: e.activation(out=PTm[:, 0:4 * S], in_=psS[ri][:, 0, 0:4 * S], func=AF.Exp, scale=SCALE_A),
                 r=[r_psS[ri]], w=[r_PTm])
            while pending:
                pending.pop(0)[1]()
            for s in range(S):
                for m in range(2):
                    P.op("pe", lambda e, m=m, s=s: e.matmul(
                        acc_ap(m, rows=2), lhsT=PTm[:, s * 4 + 2 * m:s * 4 + 2 * m + 2], rhs=VH[hb][:, s, :],
                        start=(s == 0 and m == 0), stop=(s == S - 1), skip_group_check=True),
                        r=[r_PTm, r_VH[hb]], w=[r_acc[0]])
            finish_diff(2, 1, ao, None)()
            P.op("act", lambda e: e.dma_start(out=k.aoT[j][h], in_=AOh[ao][:]), r=[r_AOh[ao]], dma=r_AOh[ao])

        def nbr_head(j, h, hb, tb, ao):
            pq = psz(QT[hb])
            pm = psz(maskt)
            def stageA(i, dl):
                nd = len(dl)
                if i == -1:
                    nq, qoff, key = 1, 127, -1
                elif i == 16:
                    nq, qoff, key = 1, 17 * 128, 16
                else:
                    nq, qoff = 128, (i + 1) * 128
                    key = i if i in (0, 1, 14, 15) else "int"
                ri = cnt["S"] % 2
                cnt["S"] += 1
                flat = psS[ri][:].rearrange("p a b -> p (a b)")
                for di, d_ in enumerate(dl):
                    nk = i + d_ + 3
                    P.op("pe", lambda e, di=di, nk=nk: e.matmul(flat[:, di * 128:di * 128 + nq], lhsT=KT[hb][:, nk * 128:(nk + 1) * 128],
                                                                rhs=QT[hb][:, qoff:qoff + nq], start=True, stop=False),
                         r=[r_KT[hb], r_QT[hb]], w=[r_psS[ri]])
                    mo = moff[key] + di * nq
                    P.op("pe", lambda e, di=di, mo=mo: e.matmul(flat[:, di * 128:di * 128 + nq], lhsT=ident[:], rhs=maskt[:, mo:mo + nq],
                                                                start=False, stop=True),
                         r=[r_ident, r_mask], w=[r_psS[ri]])
                pps = psz(psS[ri])
                reg = mkap(psS[ri], 0, [(pps, 128), (128, nd), (1, nq)])
                tfx = Tfix[tb][:, dl[0] + 3:dl[0] + 3 + nd, qoff % 128:qoff % 128 + nq]
                P.op("dve", lambda e, reg=reg, tfx=tfx: e.tensor_tensor(out=reg, in0=reg, in1=tfx, op=ALU.add),
                     r=[r_psS[ri], r_Tfix[tb]], w=[r_psS[ri]])
                pn = cnt["PTn"] % 3
                cnt["PTn"] += 1
                P.op("act", lambda e, reg=reg, pn=pn: e.activation(out=PTn[pn][:, 0:nd, 0:nq], in_=reg, func=AF.Exp, scale=SCALE_N),
                     r=[r_psS[ri]], w=[r_PTn[pn]])
                return (i, dl, nq, pn)

            def stageB(st):
                i, dl, nq, pn = st
                nd = len(dl)
                ai = cnt["accn"] % 3
                cnt["accn"] += 1
                for di, d_ in enumerate(dl):
                    nk = i + d_ + 3
                    P.op("pe", lambda e, di=di, nk=nk, ai=ai, pn=pn: e.matmul(acc[0:nq, ai, 0:129], lhsT=PTn[pn][:, di, 0:nq], rhs=VH[hb][:, nk, :],
                                                                        start=(di == 0), stop=(di == nd - 1)),
                         r=[r_PTn[pn], r_VH[hb]], w=[r_acc[ai]])
                P.op("dve", lambda e, ai=ai: e.reciprocal(out=rzn[ai][0:nq, :], in_=acc[0:nq, ai, 128:129]), r=[r_acc[ai]], w=[r_rzn[ai]])
                P.op("dve", lambda e, ai=ai: e.tensor_scalar(out=onbn[ai][0:nq, :], in0=acc[0:nq, ai, 0:128], scalar1=rzn[ai][0:nq, :], scalar2=None, op0=ALU.mult),
                     r=[r_acc[ai], r_rzn[ai]], w=[r_onbn[ai]])
                return (i, nq, ai)

            def stageC(st):
                i, nq, ai = st
                P.op("pe", lambda e, ai=ai: e.transpose(out=pTr[:, 0:nq], in_=onbn[ai][0:nq, :], identity=ident[0:nq, 0:nq]),
                     r=[r_onbn[ai], r_ident], w=[r_pTr[0]])
                if nq == 128:
                    dst = AOh[ao][:, 1 + 128 * i:1 + 128 * i + 128]
                else:
                    ecol = 0 if i == -1 else 2049
                    dst = AOh[ao][:, ecol:ecol + 1]
                P.op("dve", lambda e, dst=dst: e.tensor_copy(out=dst, in_=pTr[:, 0:nq]), r=[r_pTr[0]], w=[r_AOh[ao]])

            units = nbr_units()
            nu = len(units)
            sa, sbq = [], []
            for u in range(nu + 4):
                if u < nu:
                    sa.append(stageA(*units[u]))
                if 2 <= u < nu + 2:
                    sbq.append(stageB(sa[u - 2]))
                if 4 <= u:
                    stageC(sbq[u - 4])

            P.op("act", lambda e: e.dma_start(out=k.aoT[j][8 + h], in_=AOh[ao][:]), r=[r_AOh[ao]], dma=r_AOh[ao])

        g12c = sb("g12c", (128, 2, CK))
        cin = [sb("cin%d" % i, (128, 1024)) for i in range(2)]
        cout = [sb("cout%d" % i, (128, 1024), BF16) for i in range(2)]
        r_g = R("g12c")
        r_cin = [R("cin0"), R("cin1")]
        r_cout = [R("cout0"), R("cout1")]
        P.op("sp", lambda e: e.dma_start(out=g12c[:], in_=k.g12c.ap()), w=[r_g], dma=r_g)
        conv_state = {"n": 0}

        def convert(src, dst, nrows, ncols, tw, gidx):
            for rb in range(nrows // 128):
                for c0 in range(0, ncols, tw):
                    i = conv_state["n"] % 2
                    conv_state["n"] += 1
                    sv = src[rb * 128:(rb + 1) * 128, c0:c0 + tw]
                    dv = dst[rb * 128:(rb + 1) * 128, c0:c0 + tw]
                    P.op("pool", lambda e, i=i, sv=sv, tw=tw: e.dma_start(out=cin[i][:, 0:tw], in_=sv), w=[r_cin[i]], dma=r_cin[i])
                    if gidx is None:
                        P.op("pool", lambda e, i=i, tw=tw: e.tensor_copy(out=cout[i][:, 0:tw], in_=cin[i][:, 0:tw]),
                             r=[r_cin[i]], w=[r_cout[i]])
                    else:
                        P.op("pool", lambda e, i=i, tw=tw, rb=rb, gidx=gidx: e.tensor_scalar(
                            out=cout[i][:, 0:tw], in0=cin[i][:, 0:tw], scalar1=g12c[:, gidx, rb:rb + 1], scalar2=1.0, op0=ALU.mult, op1=ALU.mult),
                            r=[r_cin[i], r_g], w=[r_cout[i]])
                    P.op("pool", lambda e, i=i, dv=dv, tw=tw: e.dma_start(out=dv, in_=cout[i][:, 0:tw]), r=[r_cout[i]], dma=r_cout[i])

        if not DEBUG.get("skip_a") and not DEBUG.get("no_conv_b"):
            convert(k.w_out, k.wb_out, D, D, 1024, None)
            convert(k.w_up, k.wb_up, D, 2 * DFF, 688, 1)
            convert(k.w_down, k.wb_down, DFF, D, 1024, None)

        work = []
        for j in DEBUG.get("jobs", (0, 1)):
            for h in DEBUG.get("heads_a", range(HA)):
                work.append((j, h, False))
            for h in DEBUG.get("heads_n", range(HN)):
                work.append((j, h, True))
        cur_job = None
        if not work:
            P.emit_phase()
            return
        pre = load_head(*work[0])
        for wi, (j, h, nbr) in enumerate(work):
            hb, tb = pre
            if nbr and cur_job != j:
                cur_job = j
                P.op("sp", lambda e, j=j: e.dma_start(out=maskt[:], in_=k.maskd[j].ap()), w=[r_mask], dma=r_mask)
            if wi + 1 < len(work):
                pre = load_head(*work[wi + 1])
            ao = wi % 2
            if nbr:
                nbr_head(j, h, hb, tb, ao)
            else:
                diff_head(j, h, hb, ao)
        P.emit_phase()


def phase_c(k):
    nc, P = k.nc, k.P
    with ExitStack() as es:
        def sb(name, shape, dt=F32):
            return es.enter_context(nc.sbuf_tensor("C_" + name, list(shape), dt))

        R = P.res
        NW = 6
        ident = sb("ident", (128, 128), BF16)
        epsc = sb("epsc", (128, 1))
        gfb = sb("gfb", (128, D))
        convc = sb("convc", (128, FCH, 4))
        hflag = sb("hflag", (128, 4))
        cwe = sb("cwe", (128, 4, FCH))
        axT = sb("axT", (128, CK, 514), BF16)
        xmid = sb("xmid", (128, 4, D))
        xmh = sb("xmh", (2, D))
        hT = sb("hT", (128, FCH, 512), BF16)
        wr = [sb("wr%d" % i, (128, 4096), BF16) for i in range(NW)]
        xn2 = sb("xn2", (128, D), BF16)
        junk = sb("junk", (128, D), BF16)
        tb = [sb("tb%d" % i, (128, 2, 256)) for i in range(2)]
        ssq = sb("ssq", (128, 1))
        lnv = sb("lnv", (128, 1))
        rstd = sb("rstd", (128, 1))
        pb = es.enter_context(nc.psum_tensor("C_pb", [128, 8, 512], F32))

        r_ident, r_eps, r_gfb, r_convc, r_hflag, r_cwe = (R(n) for n in ("ident", "eps", "gfb", "convc", "hflag", "cwe"))
        r_axT = R("axT")
        r_xmid = [R("xmid%d" % i) for i in range(4)]
        r_xmh = R("xmh")
        r_hT = R("hT")
        r_wr = [R("wr%d" % i) for i in range(NW)]
        r_xn2, r_junk, r_ssq, r_lnv, r_rstd = (R(n) for n in ("xn2", "junk", "ssq", "lnv", "rstd"))
        r_tb = [R("tb0"), R("tb1")]
        r_pb = [R("pb%d" % i) for i in range(8)]
        cnt = {"w": 0, "pb": 0, "tb": 0, "u": 0}

        P.op("sp", lambda e: e.dma_start(out=ident[:], in_=k.ident.ap()), w=[r_ident], dma=r_ident)
        P.op("sp", lambda e: e.dma_start(out=gfb[:], in_=k.gfb.ap()), w=[r_gfb], dma=r_gfb)
        P.op("sp", lambda e: e.dma_start(out=convc[:], in_=k.convc.ap()), w=[r_convc], dma=r_convc)
        P.op("sp", lambda e: e.dma_start(out=hflag[:], in_=k.hflag.ap()), w=[r_hflag], dma=r_hflag)
        P.op("dve", lambda e: e.memset(epsc[:], EPS), w=[r_eps])
        pc = psz(convc)
        for q in range(4):
            ci = 0 if q % 2 == 0 else 2
            P.op("dve", lambda e, q=q, ci=ci: e.tensor_scalar(out=cwe[:, q, :], in0=mkap(convc, ci, [(pc, 128), (4, FCH)]),
                                                             scalar1=hflag[:, q:q + 1], scalar2=None, op0=ALU.mult),
                 r=[r_convc, r_hflag], w=[r_cwe])

        def wslot():
            i = cnt["w"] % NW
            cnt["w"] += 1
            return i

        def load_w_cols(src_dram, c0, ncols):
            i = wslot()
            src = src_dram[:, c0:c0 + ncols].rearrange("(ck p) f -> p ck f", p=128)
            dst = wr[i][:, 0:CK * ncols].rearrange("p (ck f) -> p ck f", f=ncols)
            P.op("sp", lambda e: e.dma_start(out=dst, in_=src), w=[r_wr[i]], dma=r_wr[i])
            return i

        def load_w_rows(src_dram, r0, nr):
            i = wslot()
            src = src_dram[r0 * 128:(r0 + nr) * 128, :].rearrange("(f p) n -> p f n", p=128)
            dst = wr[i][:, 0:nr * D].rearrange("p (f n) -> p f n", n=D)
            P.op("sp", lambda e: e.dma_start(out=dst, in_=src), w=[r_wr[i]], dma=r_wr[i])
            return i

        def wview_cols(i, ncols):
            return wr[i][:, 0:CK * ncols].rearrange("p (ck f) -> p ck f", f=ncols)

        def wview_rows(i, nr):
            return wr[i][:, 0:nr * D].rearrange("p (f n) -> p f n", n=D)

        def bank():
            b = cnt["pb"] % 6
            cnt["pb"] += 1
            return b

        pax = psz(axT)

        def rms_rows(rows, src_ap, r_src, dst_ap, r_dst):
            P.op("act", lambda e: e.activation(out=junk[0:rows, :], in_=src_ap, func=AF.Square, accum_out=ssq[0:rows, :]),
                 r=[r_src], w=[r_junk, r_ssq])
            P.op("act", lambda e: e.activation(out=lnv[0:rows, :], in_=ssq[0:rows, :], func=AF.Ln, scale=1.0 / D, bias=epsc[0:rows, :]),
                 r=[r_ssq, r_eps], w=[r_lnv])
            P.op("act", lambda e: e.activation(out=rstd[0:rows, :], in_=lnv[0:rows, :], func=AF.Exp, scale=-0.5), r=[r_lnv], w=[r_rstd])
            if dst_ap is not None:
                P.op("dve", lambda e: e.tensor_scalar(out=dst_ap, in0=src_ap, scalar1=rstd[0:rows, :], scalar2=None, op0=ALU.mult),
                     r=[r_src, r_rstd], w=[r_dst])

        def tile(j, T):
            e0 = 512 * T
            xs_flat = k.xs[j].ap().rearrange("s p d -> (s p) d")
            src = k.aoT[j][:, :, e0:e0 + 514].rearrange("c p n -> p c n")
            P.op("sp", lambda e: e.dma_start(out=axT[:], in_=src), w=[r_axT], dma=r_axT)
            for tc in range(4):
                r0 = 383 + e0 + 1 + 128 * tc
                P.op("sp", lambda e, tc=tc, r0=r0: e.dma_start(out=xmid[:, tc, :], in_=xs_flat[r0:r0 + 128, :]), w=[r_xmid[tc]], dma=r_xmid[tc])
            hsrc = mkap(k.xs[j], (383 + e0) * D, [(513 * D, 2), (1, D)])
            P.op("sp", lambda e: e.dma_start(out=xmh[:], in_=hsrc), w=[r_xmh], dma=r_xmh)
            NG = 256
            nxt = load_w_cols(k.wb_out, 0, NG)
            for cg in range(D // NG):
                wi = nxt
                if cg + 1 < D // NG:
                    nxt = load_w_cols(k.wb_out, (cg + 1) * NG, NG)
                wv = wview_cols(wi, NG)
                for tc in range(5):
                    b = bank()
                    if tc < 4:
                        rows = 128
                        def lhs(ck, tc=tc):
                            return axT[:, ck, 1 + 128 * tc:129 + 128 * tc]
                        dst = xmid[:, tc, cg * NG:(cg + 1) * NG]
                        rd = r_xmid[tc]
                    else:
                        rows = 2
                        def lhs(ck):
                            return mkap(axT, ck * 514, [(pax, 128), (513, 2)])
                        dst = xmh[0:2, cg * NG:(cg + 1) * NG]
                        rd = r_xmh
                    for ck in range(CK):
                        P.op("pe", lambda e, b=b, ck=ck, lhs=lhs, rows=rows, wv=wv: e.matmul(
                            pb[0:rows, b, 0:NG], lhsT=lhs(ck), rhs=wv[:, ck, :], start=(ck == 0), stop=(ck == CK - 1)),
                            r=[r_axT, r_wr[wi]], w=[r_pb[b]])
                    P.op("dve", lambda e, b=b, dst=dst, rows=rows: e.tensor_tensor(out=dst, in0=pb[0:rows, b, 0:NG], in1=dst, op=ALU.add),
                         r=[r_pb[b], rd], w=[rd])
            for tc in range(5):
                rows = 128 if tc < 4 else 2
                srcx = xmid[:, tc, :] if tc < 4 else xmh[0:2, :]
                rsrc = r_xmid[tc] if tc < 4 else r_xmh
                rms_rows(rows, srcx, rsrc, xn2[0:rows, :], r_xn2)
                if tc < 4:
                    for half in range(2):
                        pt = pb[:, 6 + half, :].bitcast(BF16)
                        for q in range(8):
                            ck = half * 8 + q
                            P.op("pe", lambda e, pt=pt, q=q, ck=ck: e.transpose(out=pt[:, q * 128:(q + 1) * 128], in_=xn2[:, ck * 128:(ck + 1) * 128], identity=ident[:]),
                                 r=[r_xn2, r_ident], w=[r_pb[6 + half]])
                        dst = axT[:, half * 8:(half + 1) * 8, 1 + 128 * tc:129 + 128 * tc]
                        srcp = pt[:, 0:1024].rearrange("p (a b) -> p a b", b=128)
                        P.op("dve", lambda e, dst=dst, srcp=srcp: e.tensor_copy(out=dst, in_=srcp), r=[r_pb[6 + half]], w=[r_axT])
                else:
                    pt = pb[:, 6, :].bitcast(BF16)
                    for ck in range(CK):
                        P.op("pe", lambda e, pt=pt, ck=ck: e.transpose(out=pt[:, 2 * ck:2 * ck + 2], in_=xn2[0:2, ck * 128:(ck + 1) * 128], identity=ident[0:2, 0:2]),
                             r=[r_xn2, r_ident], w=[r_pb[6]])
                    dst = mkap(axT, 0, [(pax, 128), (514, CK), (513, 2)])
                    srcp = pt[:, 0:2 * CK].rearrange("p (a b) -> p a b", b=2)
                    P.op("dve", lambda e, dst=dst, srcp=srcp: e.tensor_copy(out=dst, in_=srcp), r=[r_pb[6]], w=[r_axT])
            groups = [(f0, min(2, FCH - f0)) for f0 in range(0, FCH, 2)]

            def load_group(gi):
                f0, nf = groups[gi]
                return (load_w_cols(k.wb_up, f0 * 128, nf * 128), load_w_cols(k.wb_up, DFF + f0 * 128, nf * 128))

            pend = [load_group(0), load_group(1)]
            for gi, (f0, nf) in enumerate(groups):
                wa_i, wg_i = pend.pop(0)
                if gi + 2 < len(groups):
                    pend.append(load_group(gi + 2))
                wa = wview_cols(wa_i, nf * 128)
                wg = wview_cols(wg_i, nf * 128)
                for fl in range(nf):
                    fc = f0 + fl
                    u = cnt["u"] % 2
                    cnt["u"] += 1
                    ba = 3 * u
                    for ck in range(CK):
                        P.op("pe", lambda e, ba=ba, ck=ck, wa=wa, fl=fl: e.matmul(pb[:, ba, 0:258], lhsT=wa[:, ck, fl * 128:(fl + 1) * 128],
                                                                             rhs=axT[:, ck, 0:258], start=(ck == 0), stop=(ck == CK - 1)),
                             r=[r_axT, r_wr[wa_i]], w=[r_pb[ba]])
                    for ck in range(CK):
                        P.op("pe", lambda e, ba=ba, ck=ck, wa=wa, fl=fl: e.matmul(pb[:, ba + 1, 0:258], lhsT=wa[:, ck, fl * 128:(fl + 1) * 128],
                                                                             rhs=axT[:, ck, 256:514], start=(ck == 0), stop=(ck == CK - 1)),
                             r=[r_axT, r_wr[wa_i]], w=[r_pb[ba + 1]])
                    for ck in range(CK):
                        P.op("pe", lambda e, ba=ba, ck=ck, wg=wg, fl=fl: e.matmul(pb[:, ba + 2, :], lhsT=wg[:, ck, fl * 128:(fl + 1) * 128],
                                                                             rhs=axT[:, ck, 1:513], start=(ck == 0), stop=(ck == CK - 1)),
                             r=[r_axT, r_wr[wg_i]], w=[r_pb[ba + 2]])
                    ti = cnt["tb"] % 2
                    cnt["tb"] += 1
                    t_ = tb[ti]
                    rt = r_tb[ti]
                    ra = [r_pb[ba], r_pb[ba + 1]]
                    P.op("dve", lambda e, ba=ba, t_=t_, fc=fc: e.tensor_scalar(out=t_[:], in0=pb[:, ba:ba + 2, 0:256], scalar1=convc[:, fc, 0:1],
                                                                           scalar2=convc[:, fc, 3:4], op0=ALU.mult, op1=ALU.add),
                         r=ra + [r_convc], w=[rt])
                    if T == 0:
                        P.op("dve", lambda e, ba=ba, t_=t_, fc=fc: e.tensor_scalar(out=t_[:, 0, 0:1], in0=pb[:, ba, 0:1], scalar1=cwe[:, 2 * j, fc:fc + 1],
                                                                               scalar2=convc[:, fc, 3:4], op0=ALU.mult, op1=ALU.add),
                             r=ra + [r_convc, r_cwe], w=[rt])
                    P.op("dve", lambda e, ba=ba, t_=t_, fc=fc: e.scalar_tensor_tensor(out=t_[:], in0=pb[:, ba:ba + 2, 1:257], scalar=convc[:, fc, 1:2],
                                                                                  in1=t_[:], op0=ALU.mult, op1=ALU.add),
                         r=ra + [r_convc, rt], w=[rt])
                    if T == 3:
                        P.op("dve", lambda e, ba=ba, t_=t_, fc=fc: e.scalar_tensor_tensor(out=t_[:, 1, 255:256], in0=pb[:, ba + 1, 257:258], scalar=cwe[:, 2 * j + 1, fc:fc + 1],
                                                                                      in1=t_[:, 1, 255:256], op0=ALU.mult, op1=ALU.add),
                             r=ra + [r_cwe, rt], w=[rt])
                        P.op("dve", lambda e, ba=ba, t_=t_, fc=fc: e.scalar_tensor_tensor(out=t_[:, :, 0:255], in0=pb[:, ba:ba + 2, 2:257], scalar=convc[:, fc, 2:3],
                                                                                      in1=t_[:, :, 0:255], op0=ALU.mult, op1=ALU.add),
                             r=ra + [r_convc, rt], w=[rt])
                        P.op("dve", lambda e, ba=ba, t_=t_, fc=fc: e.scalar_tensor_tensor(out=t_[:, 0, 255:256], in0=pb[:, ba, 257:258], scalar=convc[:, fc, 2:3],
                                                                                      in1=t_[:, 0, 255:256], op0=ALU.mult, op1=ALU.add),
                             r=ra + [r_convc, rt], w=[rt])
                    else:
                        P.op("dve", lambda e, ba=ba, t_=t_, fc=fc: e.scalar_tensor_tensor(out=t_[:], in0=pb[:, ba:ba + 2, 2:258], scalar=convc[:, fc, 2:3],
                                                                                      in1=t_[:], op0=ALU.mult, op1=ALU.add),
                             r=ra + [r_convc, rt], w=[rt])
                    P.op("act", lambda e, t_=t_: e.activation(out=t_[:], in_=t_[:], func=AF.Gelu_apprx_tanh), r=[rt], w=[rt])
                    P.op("dve", lambda e, ba=ba, t_=t_, fc=fc: e.tensor_tensor(out=hT[:, fc, :], in0=t_[:].rearrange("p a b -> p (a b)"), in1=pb[:, ba + 2, :], op=ALU.mult),
                         r=[rt, r_pb[ba + 2]], w=[r_hT])
            dgroups = [(f0, min(2, FCH - f0)) for f0 in range(0, FCH, 2)]
            for hh in range(2):
                pend = [load_w_rows(k.wb_down, dgroups[0][0], dgroups[0][1]), load_w_rows(k.wb_down, dgroups[1][0], dgroups[1][1])]
                for gi, (f0, nf) in enumerate(dgroups):
                    wi = pend.pop(0)
                    if gi + 2 < len(dgroups):
                        pend.append(load_w_rows(k.wb_down, dgroups[gi + 2][0], dgroups[gi + 2][1]))
                    wv = wview_rows(wi, nf)
                    for fl in range(nf):
                        fc = f0 + fl
                        for tl in range(2):
                            tc = 2 * hh + tl
                            for cg in range(4):
                                b = tl * 4 + cg
                                P.op("pe", lambda e, b=b, fc=fc, tc=tc, fl=fl, cg=cg, wv=wv: e.matmul(
                                    pb[:, b, :], lhsT=hT[:, fc, tc * 128:(tc + 1) * 128], rhs=wv[:, fl, cg * 512:(cg + 1) * 512],
                                    start=(fc == 0), stop=(fc == FCH - 1)),
                                    r=[r_hT, r_wr[wi]], w=[r_pb[b]])
                for tl in range(2):
                    tc = 2 * hh + tl
                    for cg in range(4):
                        b = tl * 4 + cg
                        dst = xmid[:, tc, cg * 512:(cg + 1) * 512]
                        P.op("dve", lambda e, b=b, dst=dst: e.tensor_tensor(out=dst, in0=pb[:, b, :], in1=dst, op=ALU.add),
                             r=[r_pb[b], r_xmid[tc]], w=[r_xmid[tc]])
                    rms_rows(128, xmid[:, tc, :], r_xmid[tc], None, None)
                    P.op("dve", lambda e, tc=tc: e.scalar_tensor_tensor(out=xmid[:, tc, :], in0=xmid[:, tc, :], scalar=rstd[:, :], in1=gfb[:],
                                                                       op0=ALU.mult, op1=ALU.mult),
                         r=[r_xmid[tc], r_rstd, r_gfb], w=[r_xmid[tc]])
                    row0 = 512 * T + 128 * tc
                    P.op("sp", lambda e, tc=tc, row0=row0: e.dma_start(out=k.y[j][row0:row0 + 128, :], in_=xmid[:, tc, :]), r=[r_xmid[tc]], dma=r_xmid[tc])

        for j in DEBUG.get("jobs", (0, 1)):
            for T in DEBUG.get("tiles_c", range(4)):
                tile(j, T)
        P.emit_phase()


def prepare_inputs(inp):
    f32 = np.float32
    x_prompt = np.asarray(inp["x_prompt"], f32)
    x_sample = np.asarray(inp["x_sample"], f32)
    shared = {}
    shared["w_in"] = np.ascontiguousarray(np.asarray(inp["w_in"], f32)[0])
    shared["w_out"] = np.ascontiguousarray(np.asarray(inp["w_out"], f32)[0])
    shared["w_up"] = np.ascontiguousarray(np.asarray(inp["w_up"], f32)[0])
    shared["w_down"] = np.ascontiguousarray(np.asarray(inp["w_down"], f32)[0])
    g1 = np.asarray(inp["norm1_g"], f32)[0].reshape(CK, 128).T
    g2 = np.asarray(inp["norm2_g"], f32)[0].reshape(CK, 128).T
    shared["g12c"] = np.ascontiguousarray(np.stack([g1, g2], axis=1))
    shared["gfb"] = bcast128(inp["final_g"])
    shared["sgb"] = bcast128(np.asarray(inp["subln_g"], f32)[0])
    lam = np.concatenate([np.asarray(inp[n], f32)[0] for n in ("lambda_q1", "lambda_k1", "lambda_q2", "lambda_k2")])
    shared["lamv"] = bcast128(lam).reshape(128, 4, 64)
    tab = np.asarray(inp["rel_bias_table"], f32)
    shared["tab"] = np.ascontiguousarray(tab)
    shared["tablr"] = bcast128(np.concatenate([tab[15], tab[31]])).reshape(128, 2, 8)
    rel = np.arange(1536) - 767
    bk = t5_bucket_np(rel)
    ohu = np.zeros((32, 1536), f32)
    ohu[bk, np.arange(1536)] = 1.0
    shared["ohu"] = ohu
    rpbp = np.zeros((8, 15, 128), f32)
    rpbp[:, :, 48:79] = np.asarray(inp["na_rpb"], f32)[0]
    shared["rpbp"] = rpbp
    cw = np.asarray(inp["conv_w"], f32)[0]
    cb = np.asarray(inp["conv_b"], f32)[0]
    cc = np.stack([cw[0], cw[1], cw[2], cb], axis=1)
    shared["convc"] = np.ascontiguousarray(cc.reshape(FCH, 128, 4).transpose(1, 0, 2))
    shared["ident"] = np.eye(128, dtype=f32).astype(ml_dtypes.bfloat16)

    in_maps = []
    for c in range(NCORES):
        m = dict(shared)
        hflag = np.zeros((128, 4), f32)
        for j in range(2):
            nblk, S = JOBS[j]
            if j == 0:
                seq, t = x_prompt[c // 4], c % 4
            else:
                seq, t = x_sample[c // 2], c % 2
            o, blocks, near_true = job_geometry(nblk, S, t)
            xs = np.zeros((S, 128, D), f32)
            ws = np.zeros((3, S), f32)
            for s, gb in enumerate(blocks):
                if gb < 0:
                    continue
                xs[s] = seq[gb * 128:(gb + 1) * 128]
                if 2 <= s <= 19:
                    ws[0, s] = 1.0
                elif gb < o:
                    ws[1, s] = 1.0
                else:
                    ws[2, s] = 1.0
            m["xs%d" % j] = xs
            m["wsel%d" % j] = np.ascontiguousarray(np.broadcast_to(ws[None], (128, 3, S)))
            m["maskd%d" % j] = build_masks(nblk, t, o, near_true)
            hflag[:, 2 * j] = 1.0 if o > 0 else 0.0
            hflag[:, 2 * j + 1] = 1.0 if o + 16 < nblk else 0.0
        m["hflag"] = hflag
        in_maps.append(m)
    return in_maps


def kernel(**inputs):
    in_maps = prepare_inputs(inputs)
    nc = build_program()
    res = run_bass_kernel_spmd(nc, in_maps, core_ids=list(range(NCORES)))
    yp = np.zeros((2, 8192, D), np.float32)
    ysm = np.zeros((4, 4096, D), np.float32)
    for c in range(NCORES):
        r = res.results[c]
        yp[c // 4, (c % 4) * 2048:(c % 4 + 1) * 2048] = r["y0"]
        ysm[c // 2, (c % 2) * 2048:(c % 2 + 1) * 2048] = r["y1"]
    return (yp, ysm)
```

```python
import math
from contextlib import ExitStack

import numpy as np
import ml_dtypes

import concourse.bass as bass
import concourse.mybir as mybir
from concourse.bass_utils import run_bass_kernel_spmd

F32 = mybir.dt.float32
BF16 = mybir.dt.bfloat16
AF = mybir.ActivationFunctionType
ALU = mybir.AluOpType
AX = mybir.AxisListType

D = 2048
CK = 16
HA = 8
HN = 8
INC = 6144
DFF = 5504
FCH = 43
EPS = 1e-6
NCORES = 8
NEAR = 22
JOBS = ((64, 65), (32, 33))
SCALE_A = 0.125
SCALE_N = 128 ** -0.5
NEG = -3.0e5
LAM_INIT = 0.8 - 0.6 * math.exp(-0.3 * 0)

DEBUG = {"stop_after": None, "ext": False}


class Sem:
    def __init__(self, h, name):
        self.h = h
        self.v = 0
        self.name = name


class Res:
    __slots__ = ("name", "wr", "rd", "dsem")

    def __init__(self, name):
        self.name = name
        self.wr = None
        self.rd = []
        self.dsem = None


class Op:
    __slots__ = ("eng", "fn", "deps", "sig", "ev", "dma")


ENGS = ("pe", "act", "dve", "pool", "sp")


class Prog:
    def __init__(self, nc):
        self.nc = nc
        self.esem = {e: Sem(nc.alloc_semaphore(name="es_" + e), e) for e in ("pe", "act", "dve", "pool")}
        self.bar = Sem(nc.alloc_semaphore(name="bar"), "bar")
        self.free_dsems = []
        self.ndsem = 0
        self.reset()
        self.waited = {e: {} for e in ENGS}
        self.nphase = 0
        self.all_res = []

    def reset(self):
        self.ops = {e: [] for e in ENGS}
        self.order = []

    def res(self, name):
        r = Res(name)
        self.all_res.append(r)
        return r

    def _dsem(self, r):
        if r.dsem is None:
            if self.free_dsems:
                r.dsem = self.free_dsems.pop()
            else:
                r.dsem = Sem(self.nc.alloc_semaphore(name="ds%d" % self.ndsem), "ds%d" % self.ndsem)
                self.ndsem += 1
        return r.dsem

    def op(self, eng, fn, r=(), w=(), dma=None, after=()):
        o = Op()
        o.eng = eng
        o.fn = fn
        o.dma = dma
        o.sig = False
        o.ev = None
        deps = []
        seen = set()

        def add(d):
            if d is None or id(d) in seen:
                return
            seen.add(id(d))
            deps.append(d)

        for d in after:
            add(d)
        for x in r:
            add(x.wr)
        for x in w:
            add(x.wr)
            for d in x.rd:
                add(d)
        o.deps = [d for d in deps if not (d.eng == "pe" and eng == "pe" and d.dma is None)]
        for d in o.deps:
            d.sig = True
        for x in w:
            x.wr = o
            x.rd = []
        for x in r:
            x.rd.append(o)
        self.ops[eng].append(o)
        self.order.append(o)
        return o

    def emit_phase(self):
        nc = self.nc
        for o in self.order:
            if o.dma is not None:
                s = self._dsem(o.dma)
                s.v += 16
                o.ev = (s, s.v)
        for e in ("pe", "act", "dve", "pool"):
            ops = self.ops[e]
            lo = [o for o in ops if o.dma is None]
            if lo:
                lo[-1].sig = True
            for o in ops:
                if o.dma is None and o.sig:
                    s = self.esem[e]
                    s.v += 1
                    o.ev = (s, s.v)
        self.nphase += 1
        bar_target = self.nphase * len(ENGS)
        prog = self

        def run(e, eng):
            waited = prog.waited[e]

            def wait(ev):
                s, v = ev
                if waited.get(id(s), 0) < v:
                    eng.wait_ge(s.h, v)
                    waited[id(s)] = v

            last_dma = {}
            for o in prog.ops[e]:
                need = {}
                for d in o.deps:
                    sm, v = d.ev
                    if need.get(id(sm), (None, 0))[1] < v:
                        need[id(sm)] = (sm, v)
                for ev in need.values():
                    wait(ev)
                ins = o.fn(eng)
                if o.dma is not None:
                    ins.then_inc(o.ev[0].h, 16)
                    last_dma[id(o.ev[0])] = o.ev
                elif o.sig:
                    ins.then_inc(o.ev[0].h, 1)
            for ev in last_dma.values():
                wait(ev)
            if e in prog.esem and prog.ops[e]:
                lo = [o for o in prog.ops[e] if o.dma is None]
                if lo:
                    wait(lo[-1].ev)
            eng.sem_inc(prog.bar.h, 1)
            eng.wait_ge(prog.bar.h, bar_target)

        with nc.Block() as block:
            @block.tensor
            def _(eng):
                run("pe", eng)

            @block.scalar
            def _(eng):
                run("act", eng)

            @block.vector
            def _(eng):
                run("dve", eng)

            @block.gpsimd
            def _(eng):
                run("pool", eng)

            @block.sync
            def _(eng):
                run("sp", eng)

        for r in self.all_res:
            r.wr = None
            r.rd = []
            if r.dsem is not None:
                self.free_dsems.append(r.dsem)
                r.dsem = None
        self.all_res = []
        self.reset()


def mkap(t, off, dims):
    return bass.AP(t, off, [list(d) for d in dims])


def psz(t):
    return t[:].ap[0][0]


def t5_bucket_np(rel):
    nb = 16
    me = 8
    ret = np.where(rel > 0, nb, 0)
    n = np.abs(rel)
    nf = np.maximum(n, 1).astype(np.float32)
    large = me + (np.log(nf / np.float32(me)) / np.float32(math.log(128 / 8)) * np.float32(nb - me)).astype(np.int32)
    large = np.minimum(large, nb - 1)
    return ret + np.where(n < me, n, large)


def job_geometry(nblk, nslots, t):
    o = 16 * t
    blocks = [-1] * nslots
    near_true = [False] * NEAR
    used = set()
    for n in range(NEAR):
        gb = o + n - 3
        if 0 <= gb < nblk:
            blocks[n] = gb
            near_true[n] = True
            used.add(gb)
    rest = [b for b in range(nblk) if b not in used]
    for n in range(NEAR):
        if blocks[n] == -1 and n not in (2, 19) and rest:
            blocks[n] = rest.pop(0)
    for s in range(NEAR, nslots):
        if rest:
            blocks[s] = rest.pop(0)
    assert not rest
    return o, blocks, near_true


def nbr_units():
    units = []
    for i in range(16):
        if i == 0:
            dl = list(range(-2, 4))
        elif i == 15:
            dl = list(range(-3, 3))
        else:
            dl = list(range(-2, 3))
        units.append((i, dl))
    units.append((-1, list(range(-2, 3))))
    units.append((16, list(range(-2, 3))))
    return units


def mask_layout():
    off = {}
    col = 0
    for key, nd, w in (("int", 5, 128), (0, 6, 128), (1, 5, 128), (14, 5, 128), (15, 6, 128), (-1, 5, 1), (16, 5, 1)):
        off[key] = col
        col += nd * w
    return off, col


def build_masks(nblk, t, o, near_true):
    R = nblk * 2
    L = nblk * 128
    off, ncol = mask_layout()
    out = np.zeros((128, ncol), np.float32)

    def tile(tq, valid_q, nk):
        if not valid_q:
            return np.zeros((128, len(tq)), np.float32)
        if not near_true[nk]:
            return np.full((128, len(tq)), NEG, np.float32)
        tk = (o + nk - 3) * 128 + np.arange(128)
        r = tq // 64
        c = tq % 64
        rs = np.clip(r - 4, 0, R - 8)
        cs = np.clip(c - 8, 0, 64 - 16)
        rk = (tk // 64)[:, None]
        ckk = (tk % 64)[:, None]
        ok = (rk >= rs[None]) & (rk < rs[None] + 8) & (ckk >= cs[None]) & (ckk < cs[None] + 16)
        return np.where(ok, 0.0, NEG).astype(np.float32)

    units = dict(nbr_units())
    per_unit = {}
    for i, dl in units.items():
        if i == -1:
            tq = np.array([o * 128 - 1])
            vq = o > 0
        elif i == 16:
            tq = np.array([(o + 16) * 128])
            vq = (o + 16) < nblk
        else:
            tq = (o + i) * 128 + np.arange(128)
            vq = True
        per_unit[i] = [tile(tq, vq, i + dl_ + 3) for dl_ in dl]
    for i in range(2, 14):
        for a, b in zip(per_unit[i], per_unit[7]):
            assert np.array_equal(a, b)
    def put(key, tiles):
        c0 = off[key]
        for k, tl in enumerate(tiles):
            w = tl.shape[1]
            out[:, c0 + k * w:c0 + (k + 1) * w] = tl
    put("int", per_unit[7])
    for key in (0, 1, 14, 15, -1, 16):
        put(key, per_unit[key])
    return out.astype(ml_dtypes.bfloat16)


def bcast128(a):
    a = np.asarray(a, np.float32)
    return np.ascontiguousarray(np.broadcast_to(a.reshape(1, -1), (128, a.size)))


class K:
    pass


def build_program():
    nc = bass.Bass("TRN2", target_bir_lowering=False)
    P = Prog(nc)
    k = K()
    k.nc = nc
    k.P = P
    ext = DEBUG["ext"]

    def din(name, shape, dt=F32):
        return nc.dram_tensor(name, list(shape), dt, kind="ExternalInput")

    def dscr(name, shape, dt=BF16):
        return nc.dram_tensor(name, list(shape), dt, kind=("ExternalOutput" if (ext and name in ext) else "Internal"))

    k.xs = [din("xs%d" % j, (JOBS[j][1], 128, D)) for j in range(2)]
    k.w_in = din("w_in", (D, INC))
    k.w_out = din("w_out", (D, D))
    k.w_up = din("w_up", (D, 2 * DFF))
    k.w_down = din("w_down", (DFF, D))
    k.g12c = din("g12c", (128, 2, CK))
    k.gfb = din("gfb", (128, D))
    k.sgb = din("sgb", (128, 128))
    k.lamv = din("lamv", (128, 4, 64))
    k.tab = din("tab", (32, 8))
    k.tablr = din("tablr", (128, 2, 8))
    k.ohu = din("ohu", (32, 1536))
    k.rpbp = din("rpbp", (8, 15, 128))
    k.convc = din("convc", (128, FCH, 4))
    k.ident = din("ident", (128, 128), BF16)
    k.wsel = [din("wsel%d" % j, (128, 3, JOBS[j][1])) for j in range(2)]
    _, mcols = mask_layout()
    k.maskd = [din("maskd%d" % j, (128, mcols), BF16) for j in range(2)]
    k.hflag = din("hflag", (128, 4))
    k.y = [nc.dram_tensor("y%d" % j, [2048, D], F32, kind="ExternalOutput") for j in range(2)]
    k.wb_in = dscr("wb_in", (D, INC))
    k.wb_out = dscr("wb_out", (D, D))
    k.wb_up = dscr("wb_up", (D, 2 * DFF))
    k.wb_down = dscr("wb_down", (DFF, D))
    k.qaT = [dscr("qaT%d" % j, (HA, 128, NEAR * 128)) for j in range(2)]
    k.qnT = [dscr("qnT%d" % j, (HN, 128, NEAR * 128)) for j in range(2)]
    k.knT = [dscr("knT%d" % j, (HN, 128, NEAR * 128)) for j in range(2)]
    k.kaT = [dscr("kaT%d" % j, (HA, 128, JOBS[j][1] * 128)) for j in range(2)]
    k.va = [dscr("va%d" % j, (HA, 128, JOBS[j][1], 129)) for j in range(2)]
    k.vn = [dscr("vn%d" % j, (HN, 128, NEAR, 129)) for j in range(2)]
    k.aoT = [dscr("aoT%d" % j, (16, 128, 2050)) for j in range(2)]
    k.u2 = dscr("u2", (8, 1536), F32)
    k.tfd = dscr("tfd", (8, 128, 896), F32)

    if not DEBUG.get("skip_a"):
        phase_a(k)
    if DEBUG["stop_after"] == "A":
        return nc
    phase_b(k)
    if DEBUG["stop_after"] == "B":
        return nc
    phase_c(k)
    return nc


def phase_a(k):
    nc, P = k.nc, k.P
    with ExitStack() as es:
        def sb(name, shape, dt=F32):
            return es.enter_context(nc.sbuf_tensor("A_" + name, list(shape), dt))

        def ps(name, shape, dt=F32):
            return es.enter_context(nc.psum_tensor("A_" + name, list(shape), dt))

        ident = sb("ident", (128, 128), BF16)
        g12c = sb("g12c", (128, 2, CK))
        tablr = sb("tablr", (128, 2, 8))
        elr = sb("elr", (128, 2, 8))
        epsc = sb("epsc", (128, 1))
        wsel = [sb("wsel%d" % j, (128, 3, JOBS[j][1])) for j in range(2)]
        wfull = [sb("wfull%d" % j, (128, JOBS[j][1], 8)) for j in range(2)]
        wtmp = sb("wtmp", (128, 65, 8))
        CW = 1024
        cin = [sb("cin%d" % i, (128, 512)) for i in range(4)]
        cout = [sb("cout%d" % i, (128, 512), BF16) for i in range(4)]
        xbuf = [sb("xbuf%d" % i, (128, D)) for i in range(3)]
        junk = sb("junk", (128, D), BF16)
        ssq = [sb("ssq%d" % i, (128, 1)) for i in range(3)]
        lnv = [sb("lnv%d" % i, (128, 1)) for i in range(3)]
        rstd = [sb("rstd%d" % i, (128, 1)) for i in range(3)]
        xn = [sb("xn%d" % i, (128, D), BF16) for i in range(2)]
        xnT = [sb("xnT%d" % i, (128, CK, 512), BF16) for i in range(2)]
        wring = [sb("wring%d" % i, (128, CK, 512), BF16) for i in range(3)]
        fmst = [sb("fmst%d" % i, (128, 4, 512), BF16) for i in range(2)]
        vst = [sb("vst%d" % i, (128, 8, 4, 129), BF16) for i in range(2)]
        pT = [ps("pT%d" % i, (128, 8 * 128), BF16) for i in range(2)]
        pO = [ps("pO%d" % i, (128, 512)) for i in range(4)]

        R = P.res
        r_ident, r_g, r_tablr, r_elr, r_eps = R("ident"), R("g12c"), R("tablr"), R("elr"), R("eps")
        r_wsel = [R("wsel0"), R("wsel1")]
        r_wfull = [R("wfull0"), R("wfull1")]
        r_wtmp = R("wtmp")

        P.op("sp", lambda e: e.dma_start(out=ident[:], in_=k.ident.ap()), w=[r_ident], dma=r_ident)
        P.op("sp", lambda e: e.dma_start(out=g12c[:], in_=k.g12c.ap()), w=[r_g], dma=r_g)
        P.op("sp", lambda e: e.dma_start(out=tablr[:], in_=k.tablr.ap()), w=[r_tablr], dma=r_tablr)
        for j in range(2):
            P.op("sp", lambda e, j=j: e.dma_start(out=wsel[j][:], in_=k.wsel[j].ap()), w=[r_wsel[j]], dma=r_wsel[j])
        P.op("dve", lambda e: e.memset(epsc[:], EPS), w=[r_eps])
        P.op("act", lambda e: e.activation(out=elr[:], in_=tablr[:], func=AF.Exp), r=[r_tablr], w=[r_elr])
        for j in range(2):
            S = JOBS[j][1]
            wf, ws = wfull[j], wsel[j]
            pw, pe_, pt = psz(wf), psz(ws), psz(wtmp)
            pl = psz(elr)

            def bc_s(c, ws=ws, pe_=pe_, S=S):
                return mkap(ws, c * S, [(pe_, 128), (1, S), (0, 8)])

            def bc_h(c, S=S, pl=pl):
                return mkap(elr, c * 8, [(pl, 128), (0, S), (1, 8)])

            wt = mkap(wtmp, 0, [(pt, 128), (8, S), (1, 8)])
            P.op("dve", lambda e, wf=wf, bc_s=bc_s, bc_h=bc_h: e.tensor_tensor(out=wf[:], in0=bc_s(1), in1=bc_h(0), op=ALU.mult),
                 r=[r_wsel[j], r_elr], w=[r_wfull[j]])
            P.op("dve", lambda e, wt=wt, bc_s=bc_s, bc_h=bc_h: e.tensor_tensor(out=wt, in0=bc_s(2), in1=bc_h(1), op=ALU.mult),
                 r=[r_wsel[j], r_elr], w=[r_wtmp])
            P.op("dve", lambda e, wf=wf, wt=wt: e.tensor_tensor(out=wf[:], in0=wf[:], in1=wt, op=ALU.add),
                 r=[r_wtmp], w=[r_wfull[j]])
            P.op("dve", lambda e, wf=wf, bc_s=bc_s: e.tensor_tensor(out=wf[:], in0=wf[:], in1=bc_s(0), op=ALU.add),
                 r=[r_wsel[j]], w=[r_wfull[j]])

        r_cin = [R("cin%d" % i) for i in range(4)]
        r_cout = [R("cout%d" % i) for i in range(4)]
        conv_state = {"n": 0}
        GROUP_ORDER = [1024, 1536, 2048, 2560, 0, 512, 3072, 3584, 4096, 4608, 5120, 5632]
        st_in = {}
        for col0 in GROUP_ORDER:
            st_in[col0] = []
            for rb in range(CK):
                i = conv_state["n"] % 4
                conv_state["n"] += 1
                ci, co = cin[i], cout[i]
                rci, rco = r_cin[i], r_cout[i]
                sv = k.w_in[rb * 128:(rb + 1) * 128, col0:col0 + 512]
                dv = k.wb_in[rb * 128:(rb + 1) * 128, col0:col0 + 512]
                P.op("pool", lambda e, ci=ci, sv=sv: e.dma_start(out=ci[:, 0:512], in_=sv), w=[rci], dma=rci)
                P.op("pool", lambda e, ci=ci, co=co, rb=rb: e.tensor_scalar(
                    out=co[:, 0:512], in0=ci[:, 0:512], scalar1=g12c[:, 0, rb:rb + 1], scalar2=1.0, op0=ALU.mult, op1=ALU.mult),
                    r=[rci, r_g], w=[rco])
                st_in[col0].append(P.op("pool", lambda e, co=co, dv=dv: e.dma_start(out=dv, in_=co[:, 0:512]), r=[rco], dma=rco))
        groups_loaded = set()

        r_x = [R("xbuf%d" % i) for i in range(3)]
        r_junk = R("junk")
        r_ssq = [R("ssq%d" % i) for i in range(3)]
        r_lnv = [R("lnv%d" % i) for i in range(3)]
        r_rstd = [R("rstd%d" % i) for i in range(3)]
        r_xn = [R("xn%d" % i) for i in range(2)]
        r_xnT = [[R("xnT%d_%d" % (i, b)) for b in range(4)] for i in range(2)]
        r_w = [R("wring%d" % i) for i in range(3)]
        r_fm = [R("fmst%d" % i) for i in range(2)]
        r_vst = [R("vst%d" % i) for i in range(2)]
        r_pT = [R("pT%d" % i) for i in range(2)]
        r_pO = [R("pO%d" % i) for i in range(4)]
        cnt = {"x": 0, "xn": 0, "pT": 0, "w": 0, "pO": 0, "fm": 0, "vst": 0, "tile": 0}

        SUBS_NEAR = [("ka", 1024), ("ka", 1536), ("va", 2048), ("va", 2560), ("qa", 0), ("qa", 512),
                     ("qn", 3072), ("qn", 3584), ("kn", 4096), ("kn", 4608), ("vn", 5120), ("vn", 5632)]
        SUBS_FAR = [("ka", 1024), ("ka", 1536), ("va", 2048), ("va", 2560)]

        def load_x(j, s):
            i = cnt["x"] % 3
            cnt["x"] += 1
            P.op("sp", lambda e, i=i, j=j, s=s: e.dma_start(out=xbuf[i][:], in_=k.xs[j][s]), w=[r_x[i]], dma=r_x[i])
            return i

        def norm_block(xi, xnT_i, b):
            ni = cnt["xn"] % 2
            cnt["xn"] += 1
            P.op("act", lambda e: e.activation(out=junk[:], in_=xbuf[xi][:], func=AF.Square, accum_out=ssq[xi][:]),
                 r=[r_x[xi]], w=[r_junk, r_ssq[xi]])
            P.op("act", lambda e: e.activation(out=lnv[xi][:], in_=ssq[xi][:], func=AF.Ln, scale=1.0 / D, bias=epsc[:]),
                 r=[r_ssq[xi], r_eps], w=[r_lnv[xi]])
            P.op("act", lambda e: e.activation(out=rstd[xi][:], in_=lnv[xi][:], func=AF.Exp, scale=-0.5),
                 r=[r_lnv[xi]], w=[r_rstd[xi]])
            P.op("dve", lambda e: e.tensor_scalar(out=xn[ni][:], in0=xbuf[xi][:], scalar1=rstd[xi][:], scalar2=None, op0=ALU.mult),
                 r=[r_x[xi], r_rstd[xi]], w=[r_xn[ni]])
            for half in range(2):
                pi = cnt["pT"] % 2
                cnt["pT"] += 1
                for q in range(8):
                    ck = half * 8 + q
                    P.op("pe", lambda e, pi=pi, q=q, ck=ck: e.transpose(out=pT[pi][:, q * 128:(q + 1) * 128],
                                                                        in_=xn[ni][:, ck * 128:(ck + 1) * 128], identity=ident[:]),
                         r=[r_xn[ni], r_ident], w=[r_pT[pi]])
                dst = xnT[xnT_i][:, half * 8:(half + 1) * 8, b * 128:(b + 1) * 128]
                src = pT[pi][:].rearrange("p (a b) -> p a b", b=128)
                P.op("dve", lambda e, dst=dst, src=src: e.tensor_copy(out=dst, in_=src), r=[r_pT[pi]], w=[r_xnT[xnT_i][b]])

        def load_w(col0, first):
            i = cnt["w"] % 3
            cnt["w"] += 1
            src = k.wb_in[:, col0:col0 + 512].rearrange("(ck p) f -> p ck f", p=128)
            aft = ()
            if col0 not in groups_loaded:
                groups_loaded.add(col0)
                aft = st_in[col0]
            P.op("sp", lambda e, i=i, src=src: e.dma_start(out=wring[i][:], in_=src), w=[r_w[i]], dma=r_w[i], after=aft)
            return i

        def do_tile(j, s0, nb, near, xnT_i, prefetch):
            S = JOBS[j][1]
            N = nb * 128
            subs = SUBS_NEAR if near else SUBS_FAR
            wq = []
            state = {"first": cnt["w"] == 0}
            nxt = load_w(subs[0][1], state["first"])
            for si, (kind, col0) in enumerate(subs):
                wi = nxt
                if si + 1 < len(subs):
                    nxt = load_w(subs[si + 1][1], False)
                if si == 1 and prefetch is not None:
                    prefetch()
                hb = (col0 % 1024) // 128
                if kind in ("qa", "ka", "qn", "kn"):
                    fi = cnt["fm"] % 2
                    cnt["fm"] += 1
                    for fc in range(4):
                        oi = cnt["pO"] % 4
                        cnt["pO"] += 1
                        for ck in range(CK):
                            P.op("pe", lambda e, oi=oi, wi=wi, fc=fc, ck=ck: e.matmul(
                                pO[oi][:, 0:N], lhsT=wring[wi][:, ck, fc * 128:(fc + 1) * 128], rhs=xnT[xnT_i][:, ck, 0:N],
                                start=(ck == 0), stop=(ck == CK - 1)),
                                r=[r_w[wi]] + r_xnT[xnT_i][0:nb], w=[r_pO[oi]])
                        P.op("act", lambda e, oi=oi, fi=fi, fc=fc: e.activation(out=fmst[fi][:, fc, 0:N], in_=pO[oi][:, 0:N], func=AF.Copy),
                             r=[r_pO[oi]], w=[r_fm[fi]])
                    dstT = {"qa": k.qaT, "ka": k.kaT, "qn": k.qnT, "kn": k.knT}[kind][j]
                    dv = dstT[hb:hb + 4, :, s0 * 128:s0 * 128 + N].rearrange("h p n -> p h n")
                    P.op("act", lambda e, fi=fi, dv=dv: e.dma_start(out=dv, in_=fmst[fi][:, :, 0:N]), r=[r_fm[fi]], dma=r_fm[fi])
                else:
                    if hb == 0:
                        vi = cnt["vst"] % 2
                        cnt["vst"] += 1
                        state["vi"] = vi
                    vi = state["vi"]
                    for b in range(nb):
                        oi = cnt["pO"] % 4
                        cnt["pO"] += 1
                        for ck in range(CK):
                            P.op("pe", lambda e, oi=oi, wi=wi, b=b, ck=ck: e.matmul(
                                pO[oi][:, :], lhsT=xnT[xnT_i][:, ck, b * 128:(b + 1) * 128], rhs=wring[wi][:, ck, :],
                                start=(ck == 0), stop=(ck == CK - 1)),
                                r=[r_w[wi], r_xnT[xnT_i][b]], w=[r_pO[oi]])
                        src = pO[oi][:].rearrange("p (h e) -> p h e", e=128)
                        dst = vst[vi][:, hb:hb + 4, b, 0:128]
                        if kind == "va":
                            wf = wfull[j]
                            wb = mkap(wf, (s0 + b) * 8 + hb, [(psz(wf), 128), (1, 4), (0, 128)])
                            P.op("dve", lambda e, dst=dst, src=src, wb=wb: e.tensor_tensor(out=dst, in0=src, in1=wb, op=ALU.mult),
                                 r=[r_pO[oi], r_wfull[j]], w=[r_vst[vi]])
                            if hb == 4:
                                ones_dst = vst[vi][:, :, b, 128:129]
                                wsrc = mkap(wf, (s0 + b) * 8, [(psz(wf), 128), (1, 8), (1, 1)])
                                P.op("dve", lambda e, ones_dst=ones_dst, wsrc=wsrc: e.tensor_copy(out=ones_dst, in_=wsrc),
                                     r=[r_wfull[j]], w=[r_vst[vi]])
                        else:
                            P.op("act", lambda e, dst=dst, src=src: e.activation(out=dst, in_=src, func=AF.Copy),
                                 r=[r_pO[oi]], w=[r_vst[vi]])
                            if hb == 4:
                                ones_dst = vst[vi][:, :, b, 128:129]
                                P.op("dve", lambda e, ones_dst=ones_dst: e.memset(ones_dst, 1.0), w=[r_vst[vi]])
                    if hb == 4:
                        dstV = (k.va if kind == "va" else k.vn)[j]
                        dv = dstV[:, :, s0:s0 + nb, :].rearrange("h p s e -> p h s e")
                        P.op("act", lambda e, vi=vi, dv=dv: e.dma_start(out=dv, in_=vst[vi][:, :, 0:nb, :]), r=[r_vst[vi]], dma=r_vst[vi])

        tiles = []
        for j in range(2):
            S = JOBS[j][1]
            s = 0
            while s < NEAR:
                nb = min(4, NEAR - s)
                tiles.append((j, s, nb, True))
                s += nb
            while s < S:
                nb = min(4, S - s)
                tiles.append((j, s, nb, False))
                s += nb
        tiles = [t for t in tiles if not t[3]] + [t for t in tiles if t[3]]
        if DEBUG.get("max_tiles"):
            tiles = tiles[:DEBUG["max_tiles"]]

        def prep_tile(ti):
            j, s0, nb, near = tiles[ti]
            xi_list = [load_x(j, s0 + b) for b in range(nb)]
            for b in range(nb):
                norm_block(xi_list[b], ti % 2, b)

        def prep_tile_interleaved(ti):
            j, s0, nb, near = tiles[ti]
            pend = []
            for b in range(nb):
                pend.append(load_x(j, s0 + b))
                if len(pend) == 2:
                    norm_block(pend.pop(0), ti % 2, b - 1)
            bb = nb - len(pend)
            for xi in pend:
                norm_block(xi, ti % 2, bb)
                bb += 1

        prep_tile_interleaved(0)
        for ti in range(len(tiles)):
            j, s0, nb, near = tiles[ti]
            pf = (lambda ti=ti: prep_tile_interleaved(ti + 1)) if ti + 1 < len(tiles) else None
            do_tile(j, s0, nb, near, ti % 2, pf)

        P.emit_phase()


def phase_b(k):
    nc, P = k.nc, k.P
    moff, mcols = mask_layout()
    with ExitStack() as es:
        def sb(name, shape, dt=F32):
            return es.enter_context(nc.sbuf_tensor("B_" + name, list(shape), dt))

        def ps(name, shape, dt=F32):
            return es.enter_context(nc.psum_tensor("B_" + name, list(shape), dt))

        R = P.res
        SMAX = JOBS[0][1]
        ident = sb("ident", (128, 128), BF16)
        tablr = sb("tablr", (128, 2, 8))
        lamv = sb("lamv", (128, 4, 64))
        lprod = sb("lprod", (128, 2, 64))
        lsum = sb("lsum", (128, 2))
        lexp = sb("lexp", (128, 2))
        nlam = sb("nlam", (128, 1))
        sg = sb("sg", (128, 128))
        epsc = sb("epsc", (128, 1))
        tabp = sb("tabp", (128, 128))
        ohup = sb("ohup", (128, 1536))
        u2s = sb("u2s", (8, 1536))
        KT = [sb("KT%d" % i, (128, SMAX * 128), BF16) for i in range(2)]
        VH = [sb("VH%d" % i, (128, SMAX, 129), BF16) for i in range(2)]
        QT = [sb("QT%d" % i, (128, 18 * 128), BF16) for i in range(2)]
        QTm = [[sb("QTm%d_%d" % (m, i), (128, 18 * 128), BF16) for i in range(2)] for m in range(2)]
        HH = [sb("HH%d" % i, (128, 1408)) for i in range(2)]
        PT = [sb("PT%d" % i, (128, 2, 512), BF16) for i in range(3)]
        PTm = sb("PTm", (128, SMAX * 4), BF16)
        Gmini = sb("Gmini", (128, 18, 2))
        osb = sb("osb", (128, 8, 129))
        rz = sb("rz", (128, 8))
        tt = sb("tt", (128, 8, 128))
        od = sb("od", (128, 4, 128))
        sqj = sb("sqj", (128, 128))
        ssq = sb("ssq", (128, 4))
        lnv = sb("lnv", (128, 4))
        rstd = sb("rstd", (128, 4))
        tmp2 = sb("tmp2", (128, 4, 128))
        onb = [sb("onb%d" % i, (128, 4, 128), BF16) for i in range(2)]
        AOh = [sb("AOh%d" % i, (128, 2050), BF16) for i in range(2)]
        Trt = [sb("Trt%d" % i, (128, 7, 2, 64)) for i in range(2)]
        TfixB = [sb("TfixB%d" % i, (128, 7, 128)) for i in range(2)]
        Tfix = [sb("Tfix%d" % i, (128, 7, 128)) for i in range(2)]
        maskt = sb("maskt", (128, mcols), BF16)
        PTn = [sb("PTn%d" % i, (128, 7, 128), BF16) for i in range(3)]
        rzn = [sb("rzn%d" % i, (128, 1)) for i in range(3)]
        onbn = [sb("onbn%d" % i, (128, 128), BF16) for i in range(3)]

        psS = [ps("psS%d" % i, (128, 2, 512)) for i in range(2)]
        acc = ps("acc", (128, 3, 512))
        pTr = ps("pTr", (128, 1024), BF16)

        r_ident, r_tablr, r_lamv, r_lprod, r_lsum, r_lexp, r_nlam = (R(n) for n in ("ident", "tablr", "lamv", "lprod", "lsum", "lexp", "nlam"))
        r_sg, r_eps, r_tabp, r_ohup, r_u2s, r_u2d = (R(n) for n in ("sg", "eps", "tabp", "ohup", "u2s", "u2d"))
        r_KT = [R("KT0"), R("KT1")]
        r_VH = [R("VH0"), R("VH1")]
        r_QT = [R("QT0"), R("QT1")]
        r_QTm = [[R("QTm%d_%d" % (m, i)) for i in range(2)] for m in range(2)]
        r_HH = [R("HH0"), R("HH1")]
        r_PT = [R("PT%d" % i) for i in range(3)]
        r_PTm, r_Gm, r_osb, r_rz, r_tt, r_od, r_sqj, r_ssq, r_lnv, r_rstd, r_tmp2 = (
            R(n) for n in ("PTm", "Gm", "osb", "rz", "tt", "od", "sqj", "ssq", "lnv", "rstd", "tmp2"))
        r_onb = [R("onb0"), R("onb1")]
        r_AOh = [R("AOh0"), R("AOh1")]
        r_Trt = [[R("Trt%d_%d" % (i, q)) for q in range(4)] for i in range(2)]
        r_TfixB = [R("TfixB0"), R("TfixB1")]
        r_Tfix = [R("Tfix0"), R("Tfix1")]
        r_mask = R("mask")
        r_PTn = [R("PTn0"), R("PTn1"), R("PTn2")]
        r_rzn = [R("rzn%d" % i) for i in range(3)]
        r_onbn = [R("onbn%d" % i) for i in range(3)]
        r_psS = [R("psS0"), R("psS1")]
        r_acc = [R("acc%d" % i) for i in range(3)]
        r_pTr = [R("pTr%d" % i) for i in range(4)]
        cnt = {"S": 0, "PT": 0, "onb": 0, "pTr": 0, "hb": 0, "PTn": 0, "accn": 0, "tb": 0}

        P.op("sp", lambda e: e.dma_start(out=ident[:], in_=k.ident.ap()), w=[r_ident], dma=r_ident)
        P.op("sp", lambda e: e.dma_start(out=tablr[:], in_=k.tablr.ap()), w=[r_tablr], dma=r_tablr)
        P.op("sp", lambda e: e.dma_start(out=lamv[:], in_=k.lamv.ap()), w=[r_lamv], dma=r_lamv)
        P.op("sp", lambda e: e.dma_start(out=sg[:], in_=k.sgb.ap()), w=[r_sg], dma=r_sg)
        P.op("dve", lambda e: e.memset(epsc[:], EPS), w=[r_eps])
        for i in range(2):
            P.op("dve", lambda e, i=i: e.memset(QTm[0][i][64:128, :], 0.0), w=[r_QTm[0][i]])
            P.op("dve", lambda e, i=i: e.memset(QTm[1][i][0:64, :], 0.0), w=[r_QTm[1][i]])
        P.op("dve", lambda e: e.memset(tabp[:], 0.0), w=[r_tabp])
        P.op("dve", lambda e: e.memset(ohup[:], 0.0), w=[r_ohup])
        P.op("sp", lambda e: e.dma_start(out=tabp[0:32, 0:8], in_=k.tab.ap()), w=[r_tabp], dma=r_tabp)
        P.op("sp", lambda e: e.dma_start(out=ohup[0:32, :], in_=k.ohu.ap()), w=[r_ohup], dma=r_ohup)
        pl = psz(lamv)
        P.op("dve", lambda e: e.tensor_tensor(out=lprod[:], in0=mkap(lamv, 0, [(pl, 128), (128, 2), (1, 64)]),
                                              in1=mkap(lamv, 64, [(pl, 128), (128, 2), (1, 64)]), op=ALU.mult),
             r=[r_lamv], w=[r_lprod])
        P.op("dve", lambda e: e.tensor_reduce(out=lsum[:], in_=lprod[:], axis=AX.X, op=ALU.add), r=[r_lprod], w=[r_lsum])
        P.op("act", lambda e: e.activation(out=lexp[:], in_=lsum[:], func=AF.Exp), r=[r_lsum], w=[r_lexp])
        P.op("dve", lambda e: e.tensor_tensor(out=nlam[:], in0=lexp[:, 1:2], in1=lexp[:, 0:1], op=ALU.subtract), r=[r_lexp], w=[r_nlam])
        P.op("dve", lambda e: e.tensor_scalar(out=nlam[:], in0=nlam[:], scalar1=-LAM_INIT, scalar2=None, op0=ALU.add), r=[r_nlam], w=[r_nlam])
        P.op("dve", lambda e: e.tensor_scalar(out=sg[:], in0=sg[:], scalar1=1.0 - LAM_INIT, scalar2=None, op0=ALU.mult), r=[r_sg], w=[r_sg])
        P.op("dve", lambda e: e.tensor_scalar(out=tabp[:], in0=tabp[:], scalar1=1.0 / SCALE_A, scalar2=None, op0=ALU.mult), r=[r_tabp], w=[r_tabp])
        for q in range(3):
            P.op("pe", lambda e, q=q: e.matmul(psS[q % 2][:, q // 2, :], lhsT=tabp[:], rhs=ohup[:, q * 512:(q + 1) * 512], start=True, stop=True),
                 r=[r_tabp, r_ohup], w=[r_psS[q % 2]])
            P.op("dve", lambda e, q=q: e.tensor_copy(out=u2s[:, q * 512:(q + 1) * 512], in_=psS[q % 2][0:8, q // 2, :]),
                 r=[r_psS[q % 2]], w=[r_u2s])
        P.op("sp", lambda e: e.dma_start(out=k.u2.ap(), in_=u2s[:]), r=[r_u2s], w=[r_u2d], dma=r_u2s)

        def acc_ap(a, rows=128, cols=129):
            return acc[0:rows, a // 3, (a % 3) * 129:(a % 3) * 129 + cols]

        def load_head(j, h, nbr):
            S = JOBS[j][1]
            hb = cnt["hb"] % 2
            cnt["hb"] += 1
            if not nbr:
                P.op("sp", lambda e: e.dma_start(out=KT[hb][:, 0:S * 128], in_=k.kaT[j][h]), w=[r_KT[hb]], dma=r_KT[hb])
                P.op("sp", lambda e: e.dma_start(out=VH[hb][:, 0:S, :], in_=k.va[j][h]), w=[r_VH[hb]], dma=r_VH[hb])
                P.op("sp", lambda e: e.dma_start(out=QTm[0][hb][0:64, :], in_=k.qaT[j][h][0:64, 2 * 128:20 * 128]), w=[r_QTm[0][hb]], dma=r_QTm[0][hb])
                P.op("sp", lambda e: e.dma_start(out=QTm[1][hb][64:128, :], in_=k.qaT[j][h][64:128, 2 * 128:20 * 128]), w=[r_QTm[1][hb]], dma=r_QTm[1][hb])
                P.op("sp", lambda e: e.dma_start(out=HH[hb][:], in_=mkap(k.u2, h * 1536, [(1, 128), (1, 1408)])),
                     r=[r_u2d], w=[r_HH[hb]], dma=r_HH[hb])
            else:
                P.op("sp", lambda e: e.dma_start(out=KT[hb][:, 0:NEAR * 128], in_=k.knT[j][h]), w=[r_KT[hb]], dma=r_KT[hb])
                P.op("sp", lambda e: e.dma_start(out=VH[hb][:, 0:NEAR, :], in_=k.vn[j][h]), w=[r_VH[hb]], dma=r_VH[hb])
                P.op("sp", lambda e: e.dma_start(out=QT[hb][:], in_=k.qnT[j][h][:, 2 * 128:20 * 128]), w=[r_QT[hb]], dma=r_QT[hb])
                tb = cnt["tb"] % 2
                cnt["tb"] += 1
                P.op("sp", lambda e: e.dma_start(out=Tfix[tb][:], in_=k.tfd[h]), w=[r_Tfix[tb]], dma=r_Tfix[tb], after=[tf_store[h]])
                return hb, tb
            return hb, None

        def finish_diff(rows, nch, ao, ecols):
            na = 2 * nch
            nbanks = (na + 2) // 3
            for b in range(nbanks):
                n_in = min(3, na - 3 * b)
                P.op("dve", lambda e, b=b, n_in=n_in: e.tensor_copy(
                    out=osb[0:rows, 3 * b:3 * b + n_in, :], in_=acc[0:rows, b, 0:n_in * 129].rearrange("p (a c) -> p a c", c=129)),
                    r=[r_acc[b]], w=[r_osb])
            po = psz(osb)
            P.op("dve", lambda e: e.reciprocal(out=rz[0:rows, 0:na], in_=mkap(osb, 128, [(po, rows), (129, na)])), r=[r_osb], w=[r_rz])
            P.op("dve", lambda e: e.tensor_tensor(out=tt[0:rows, 0:na, :], in0=osb[0:rows, 0:na, 0:128],
                                                  in1=mkap(rz, 0, [(psz(rz), rows), (1, na), (0, 128)]), op=ALU.mult),
                 r=[r_osb, r_rz], w=[r_tt])
            ptt = psz(tt)
            P.op("dve", lambda e: e.scalar_tensor_tensor(out=od[0:rows, 0:nch, :], in0=mkap(tt, 128, [(ptt, rows), (256, nch), (1, 128)]),
                                                         scalar=nlam[0:rows, :], in1=mkap(tt, 0, [(ptt, rows), (256, nch), (1, 128)]),
                                                         op0=ALU.mult, op1=ALU.add),
                 r=[r_tt, r_nlam], w=[r_od])
            P.op("dve", lambda e: e.tensor_tensor(out=tmp2[0:rows, 0:nch, :], in0=od[0:rows, 0:nch, :], in1=od[0:rows, 0:nch, :], op=ALU.mult),
                 r=[r_od], w=[r_tmp2])
            P.op("dve", lambda e: e.tensor_reduce(out=ssq[0:rows, 0:nch], in_=tmp2[0:rows, 0:nch, :], axis=AX.X, op=ALU.add),
                 r=[r_tmp2], w=[r_ssq])
            P.op("act", lambda e: e.activation(out=lnv[0:rows, 0:nch], in_=ssq[0:rows, 0:nch], func=AF.Ln, scale=1.0 / 128, bias=epsc[0:rows, :]),
                 r=[r_ssq, r_eps], w=[r_lnv])
            P.op("act", lambda e: e.activation(out=rstd[0:rows, 0:nch], in_=lnv[0:rows, 0:nch], func=AF.Exp, scale=-0.5), r=[r_lnv], w=[r_rstd])
            P.op("dve", lambda e: e.tensor_tensor(out=tmp2[0:rows, 0:nch, :], in0=od[0:rows, 0:nch, :],
                                                  in1=mkap(rstd, 0, [(psz(rstd), rows), (1, nch), (0, 128)]), op=ALU.mult),
                 r=[r_od, r_rstd], w=[r_tmp2])
            oi = cnt["onb"] % 2
            cnt["onb"] += 1
            P.op("dve", lambda e: e.tensor_tensor(out=onb[oi][0:rows, 0:nch, :], in0=tmp2[0:rows, 0:nch, :],
                                                  in1=mkap(sg, 0, [(psz(sg), rows), (0, nch), (1, 128)]), op=ALU.mult),
                 r=[r_tmp2, r_sg], w=[r_onb[oi]])
            def part2():
                for c in range(nch):
                    P.op("pe", lambda e, c=c: e.transpose(out=pTr[:, c * 128:c * 128 + rows], in_=onb[oi][0:rows, c, :], identity=ident[0:rows, 0:rows]),
                         r=[r_onb[oi], r_ident], w=[r_pTr[0]])
                if rows == 128:
                    dst = AOh[ao][:, ecols[0]:ecols[0] + 128 * nch]
                    src = pTr[:, 0:128 * nch]
                else:
                    dst = mkap(AOh[ao], 0, [(psz(AOh[ao]), 128), (2049, 2)])
                    src = pTr[:, 0:2]
                P.op("dve", lambda e, dst=dst, src=src: e.tensor_copy(out=dst, in_=src), r=[r_pTr[0]], w=[r_AOh[ao]])
            return part2

        def diff_head(j, h, hb, ao):
            S = JOBS[j][1]
            pq = psz(QT[hb])
            ph = psz(HH[hb])
            G = Gmini
            pg = psz(G)
            P.op("dve", lambda e: e.tensor_copy(out=G[:, 0:2, 0:1], in_=mkap(HH[hb], 640, [(ph, 128), (128, 2), (1, 1)])), r=[r_HH[hb]], w=[r_Gm])
            P.op("dve", lambda e: e.tensor_copy(out=G[:, 2:18, 0:1], in_=mkap(HH[hb], 896, [(ph, 128), (0, 16), (1, 1)])), r=[r_HH[hb]], w=[r_Gm])
            P.op("dve", lambda e: e.tensor_copy(out=G[:, 0:16, 1:2], in_=mkap(HH[hb], 511, [(ph, 128), (0, 16), (1, 1)])), r=[r_HH[hb]], w=[r_Gm])
            P.op("dve", lambda e: e.tensor_copy(out=G[:, 16:18, 1:2], in_=mkap(HH[hb], 639, [(ph, 128), (128, 2), (1, 1)])), r=[r_HH[hb]], w=[r_Gm])
            def issue_S(g, s):
                    qc0 = (4 * g + 1) * 128
                    ri = cnt["S"] % 2
                    cnt["S"] += 1
                    for m in range(2):
                        P.op("pe", lambda e, ri=ri, m=m, s=s: e.matmul(
                            psS[ri][:, m, :], lhsT=KT[hb][:, s * 128:(s + 1) * 128],
                            rhs=QTm[m][hb][:, qc0:qc0 + 512], start=True, stop=True),
                            r=[r_KT[hb], r_QTm[m][hb]], w=[r_psS[ri]])
                    bias = None
                    if 2 <= s <= 19:
                        d0 = (s - 3) - 4 * g
                        sw = 5 - d0
                        if sw <= 0:
                            bias = tablr[:, 1, h:h + 1]
                        elif sw >= 7:
                            bias = tablr[:, 0, h:h + 1]
                        else:
                            win = mkap(HH[hb], 1407 - 128 * sw, [(ph, 128), (0, 2), (-1, 512)])
                            P.op("dve", lambda e, ri=ri, win=win: e.tensor_tensor(out=psS[ri][:], in0=psS[ri][:], in1=win, op=ALU.add),
                                 r=[r_psS[ri], r_HH[hb]], w=[r_psS[ri]])
                    pi = cnt["PT"] % 3
                    cnt["PT"] += 1
                    if bias is None:
                        P.op("act", lambda e, ri=ri, pi=pi: e.activation(out=PT[pi][:], in_=psS[ri][:], func=AF.Exp, scale=SCALE_A),
                             r=[r_psS[ri]], w=[r_PT[pi]])
                    else:
                        P.op("act", lambda e, ri=ri, pi=pi, bias=bias: e.activation(out=PT[pi][:], in_=psS[ri][:], func=AF.Exp, scale=SCALE_A, bias=bias),
                             r=[r_psS[ri], r_tablr], w=[r_PT[pi]])
                    return pi

            def issue_PV(s, pi):
                    for c in range(4):
                        for m in range(2):
                            a = 2 * c + m
                            P.op("pe", lambda e, pi=pi, c=c, m=m, a=a, s=s: e.matmul(
                                acc_ap(a), lhsT=PT[pi][:, m, c * 128:(c + 1) * 128], rhs=VH[hb][:, s, :],
                                start=(s == 0 and a % 3 == 0), stop=(s == S - 1), skip_group_check=True),
                                r=[r_PT[pi], r_VH[hb]], w=[r_acc[a // 3]])

            parts = DEBUG.get("b_parts", ("main", "finish", "mini"))
            steps = [(g, s) for g in range(4 if "main" in parts else 0) for s in range(S)]
            pending = []
            prev = None

            def retire(prev):
                g_, s_, pi_ = prev
                issue_PV(s_, pi_)
                if s_ == S - 1 and "finish" in parts:
                    pending.append([3, finish_diff(128, 4, ao, [1 + 512 * g_ + 128 * c for c in range(4)])])

            inflight = []
            for (g, s) in steps:
                pi = issue_S(g, s)
                inflight.append((g, s, pi))
                if len(inflight) >= 3:
                    retire(inflight.pop(0))
                for pd in pending:
                    pd[0] -= 1
                while pending and pending[0][0] <= 0:
                    pending.pop(0)[1]()
            while inflight:
                retire(inflight.pop(0))
            if "mini" not in parts:
                while pending:
                    pending.pop(0)[1]()
                P.op("act", lambda e: e.dma_start(out=k.aoT[j][h], in_=AOh[ao][:]), r=[r_AOh[ao]], dma=r_AOh[ao])
                return
            ri = cnt["S"] % 2
            cnt["S"] += 1
            for s in range(S):
                for m in range(2):
                    P.op("pe", lambda e, ri=ri, m=m, s=s: e.matmul(
                        psS[ri][:, 0, s * 4 + 2 * m:s * 4 + 2 * m + 2], lhsT=KT[hb][:, s * 128:(s + 1) * 128],
                        rhs=mkap(QTm[m][hb], 127, [(pq, 128), (2049, 2)]), start=True, stop=True),
                        r=[r_KT[hb], r_QTm[m][hb]], w=[r_psS[ri]])
            pps = psz(psS[ri])
            reg = mkap(psS[ri], 8, [(pps, 128), (4, 18), (2, 2), (1, 2)])
            P.op("dve", lambda e, reg=reg: e.tensor_tensor(out=reg, in0=reg, in1=mkap(G, 0, [(pg, 128), (2, 18), (0, 2), (1, 2)]), op=ALU.add),
                 r=[r_psS[ri], r_Gm], w=[r_psS[ri]])
            P.op("act", lambda e, ri=ri: e.activation(out=PTm[:, 0:4 * S], in_=psS[ri][:, 0, 0:4 * S], func=AF.Exp, scale=SCALE_A),
                 r=[r_psS[ri]], w=[r_PTm])
            while pending:
                pending.pop(0)[1]()
            for s in range(S):
                for m in range(2):
                    P.op("pe", lambda e, m=m, s=s: e.matmul(
                        acc_ap(m, rows=2), lhsT=PTm[:, s * 4 + 2 * m:s * 4 + 2 * m + 2], rhs=VH[hb][:, s, :],
                        start=(s == 0 and m == 0), stop=(s == S - 1), skip_group_check=True),
                        r=[r_PTm, r_VH[hb]], w=[r_acc[0]])
            finish_diff(2, 1, ao, None)()
            P.op("act", lambda e: e.dma_start(out=k.aoT[j][h], in_=AOh[ao][:]), r=[r_AOh[ao]], dma=r_AOh[ao])

        def nbr_head(j, h, hb, tb, ao):
            pq = psz(QT[hb])
            pm = psz(maskt)
            def stageA(i, dl):
                nd = len(dl)
                if i == -1:
                    nq, qoff, key = 1, 127, -1
                elif i == 16:
                    nq, qoff, key = 1, 17 * 128, 16
                else:
                    nq, qoff = 128, (i + 1) * 128
                    key = i if i in (0, 1, 14, 15) else "int"
                ri = cnt["S"] % 2
                cnt["S"] += 1
                flat = psS[ri][:].rearrange("p a b -> p (a b)")
                for di, d_ in enumerate(dl):
                    nk = i + d_ + 3
                    P.op("pe", lambda e, di=di, nk=nk: e.matmul(flat[:, di * 128:di * 128 + nq], lhsT=KT[hb][:, nk * 128:(nk + 1) * 128],
                                                                rhs=QT[hb][:, qoff:qoff + nq], start=True, stop=False),
                         r=[r_KT[hb], r_QT[hb]], w=[r_psS[ri]])
                    mo = moff[key] + di * nq
                    P.op("pe", lambda e, di=di, mo=mo: e.matmul(flat[:, di * 128:di * 128 + nq], lhsT=ident[:], rhs=maskt[:, mo:mo + nq],
                                                                start=False, stop=True),
                         r=[r_ident, r_mask], w=[r_psS[ri]])
                pps = psz(psS[ri])
                reg = mkap(psS[ri], 0, [(pps, 128), (128, nd), (1, nq)])
                tfx = Tfix[tb][:, dl[0] + 3:dl[0] + 3 + nd, qoff % 128:qoff % 128 + nq]
                P.op("dve", lambda e, reg=reg, tfx=tfx: e.tensor_tensor(out=reg, in0=reg, in1=tfx, op=ALU.add),
                     r=[r_psS[ri], r_Tfix[tb]], w=[r_psS[ri]])
                pn = cnt["PTn"] % 3
                cnt["PTn"] += 1
                P.op("act", lambda e, reg=reg, pn=pn: e.activation(out=PTn[pn][:, 0:nd, 0:nq], in_=reg, func=AF.Exp, scale=SCALE_N),
                     r=[r_psS[ri]], w=[r_PTn[pn]])
                return (i, dl, nq, pn)

            def stageB(st):
                i, dl, nq, pn = st
                nd = len(dl)
                ai = cnt["accn"] % 3
                cnt["accn"] += 1
                for di, d_ in enumerate(dl):
                    nk = i + d_ + 3
                    P.op("pe", lambda e, di=di, nk=nk, ai=ai, pn=pn: e.matmul(acc[0:nq, ai, 0:129], lhsT=PTn[pn][:, di, 0:nq], rhs=VH[hb][:, nk, :],
                                                                        start=(di == 0), stop=(di == nd - 1)),
                         r=[r_PTn[pn], r_VH[hb]], w=[r_acc[ai]])
                P.op("dve", lambda e, ai=ai: e.reciprocal(out=rzn[ai][0:nq, :], in_=acc[0:nq, ai, 128:129]), r=[r_acc[ai]], w=[r_rzn[ai]])
                P.op("dve", lambda e, ai=ai: e.tensor_scalar(out=onbn[ai][0:nq, :], in0=acc[0:nq, ai, 0:128], scalar1=rzn[ai][0:nq, :], scalar2=None, op0=ALU.mult),
                     r=[r_acc[ai], r_rzn[ai]], w=[r_onbn[ai]])
                return (i, nq, ai)

            def stageC(st):
                i, nq, ai = st
                P.op("pe", lambda e, ai=ai: e.transpose(out=pTr[:, 0:nq], in_=onbn[ai][0:nq, :], identity=ident[0:nq, 0:nq]),
                     r=[r_onbn[ai], r_ident], w=[r_pTr[0]])
                if nq == 128:
                    dst = AOh[ao][:, 1 + 128 * i:1 + 128 * i + 128]
                else:
                    ecol = 0 if i == -1 else 2049
                    dst = AOh[ao][:, ecol:ecol + 1]
                P.op("dve", lambda e, dst=dst: e.tensor_copy(out=dst, in_=pTr[:, 0:nq]), r=[r_pTr[0]], w=[r_AOh[ao]])

            units = nbr_units()
            nu = len(units)
            sa, sbq = [], []
            for u in range(nu + 4):
                if u < nu:
                    sa.append(stageA(*units[u]))
                if 2 <= u < nu + 2:
                    sbq.append(stageB(sa[u - 2]))
                if 4 <= u:
                    stageC(sbq[u - 4])

            P.op("act", lambda e: e.dma_start(out=k.aoT[j][8 + h], in_=AOh[ao][:]), r=[r_AOh[ao]], dma=r_AOh[ao])

        tf_store = {}
        for h in range(HN):
            tb_ = h % 2
            for rk in range(2):
                for rq in range(2):
                    for dl in range(-3, 4):
                        dr = 2 * dl + rk - rq + 7
                        P.op("pool", lambda e, tb_=tb_, dl=dl, rk=rk, rq=rq, dr=dr, h=h: e.dma_start(
                            out=Trt[tb_][rk * 64:(rk + 1) * 64, dl + 3, rq, :],
                            in_=mkap(k.rpbp, (h * 15 + dr) * 128, [(1, 64), (1, 64)])), w=[r_Trt[tb_][2 * rk + rq]], dma=r_Trt[tb_][2 * rk + rq])
            pt_ = psz(Trt[tb_])
            for rq in range(2):
                P.op("pool", lambda e, tb_=tb_, rq=rq, pt_=pt_: e.tensor_scalar(
                    out=TfixB[tb_][:, :, rq * 64:(rq + 1) * 64],
                    in0=mkap(Trt[tb_], rq * 64 + 63, [(pt_, 128), (128, 7), (-1, 64)]),
                    scalar1=1.0 / SCALE_N, scalar2=1.0, op0=ALU.mult, op1=ALU.mult),
                    r=[r_Trt[tb_][rq], r_Trt[tb_][2 + rq]], w=[r_TfixB[tb_]])
            tf_store[h] = P.op("pool", lambda e, tb_=tb_, h=h: e.dma_start(out=k.tfd[h], in_=TfixB[tb_][:]), r=[r_TfixB[tb_]], dma=r_TfixB[tb_])

        g12c = sb("g12c", (128, 2, CK))
        cin = [sb("cin%d" % i, (128, 1024)) for i in range(2)]
        cout = [sb("cout%d" % i, (128, 1024), BF16) for i in range(2)]
        r_g = R("g12c")
        r_cin = [R("cin0"), R("cin1")]
        r_cout = [R("cout0"), R("cout1")]
        P.op("sp", lambda e: e.dma_start(out=g12c[:], in_=k.g12c.ap()), w=[r_g], dma=r_g)
        conv_state = {"n": 0}

        def convert(src, dst, nrows, ncols, tw, gidx):
            for rb in range(nrows // 128):
                for c0 in range(0, ncols, tw):
                    i = conv_state["n"] % 2
                    conv_state["n"] += 1
                    sv = src[rb * 128:(rb + 1) * 128, c0:c0 + tw]
                    dv = dst[rb * 128:(rb + 1) * 128, c0:c0 + tw]
                    P.op("pool", lambda e, i=i, sv=sv, tw=tw: e.dma_start(out=cin[i][:, 0:tw], in_=sv), w=[r_cin[i]], dma=r_cin[i])
                    if gidx is None:
                        P.op("pool", lambda e, i=i, tw=tw: e.tensor_copy(out=cout[i][:, 0:tw], in_=cin[i][:, 0:tw]),
                             r=[r_cin[i]], w=[r_cout[i]])
                    else:
                        P.op("pool", lambda e, i=i, tw=tw, rb=rb, gidx=gidx: e.tensor_scalar(
                            out=cout[i][:, 0:tw], in0=cin[i][:, 0:tw], scalar1=g12c[:, gidx, rb:rb + 1], scalar2=1.0, op0=ALU.mult, op1=ALU.mult),
                            r=[r_cin[i], r_g], w=[r_cout[i]])
                    P.op("pool", lambda e, i=i, dv=dv, tw=tw: e.dma_start(out=dv, in_=cout[i][:, 0:tw]), r=[r_cout[i]], dma=r_cout[i])

        if not DEBUG.get("skip_a") and not DEBUG.get("no_conv_b"):
            convert(k.w_out, k.wb_out, D, D, 1024, None)
            convert(k.w_up, k.wb_up, D, 2 * DFF, 688, 1)
            convert(k.w_down, k.wb_down, DFF, D, 1024, None)

        work = []
        for j in DEBUG.get("jobs", (0, 1)):
            for h in DEBUG.get("heads_a", range(HA)):
                work.append((j, h, False))
            for h in DEBUG.get("heads_n", range(HN)):
                work.append((j, h, True))
        cur_job = None
        if not work:
            P.emit_phase()
            return
        pre = load_head(*work[0])
        for wi, (j, h, nbr) in enumerate(work):
            hb, tb = pre
            if nbr and cur_job != j:
                cur_job = j
                P.op("sp", lambda e, j=j: e.dma_start(out=maskt[:], in_=k.maskd[j].ap()), w=[r_mask], dma=r_mask)
            if wi + 1 < len(work):
                pre = load_head(*work[wi + 1])
            ao = wi % 2
            if nbr:
                nbr_head(j, h, hb, tb, ao)
            else:
                diff_head(j, h, hb, ao)
        P.emit_phase()


def phase_c(k):
    nc, P = k.nc, k.P
    with ExitStack() as es:
        def sb(name, shape, dt=F32):
            return es.enter_context(nc.sbuf_tensor("C_" + name, list(shape), dt))

        R = P.res
        NW = 6
        ident = sb("ident", (128, 128), BF16)
        epsc = sb("epsc", (128, 1))
        gfb = sb("gfb", (128, D))
        convc = sb("convc", (128, FCH, 4))
        hflag = sb("hflag", (128, 4))
        cwe = sb("cwe", (128, 4, FCH))
        axT = sb("axT", (128, CK, 514), BF16)
        xmid = sb("xmid", (128, 4, D))
        xmh = sb("xmh", (2, D))
        hT = sb("hT", (128, FCH, 512), BF16)
        wr = [sb("wr%d" % i, (128, 4096), BF16) for i in range(NW)]
        xn2 = sb("xn2", (128, D), BF16)
        junk = sb("junk", (128, D), BF16)
        tb = [sb("tb%d" % i, (128, 2, 256)) for i in range(2)]
        ssq = sb("ssq", (128, 1))
        lnv = sb("lnv", (128, 1))
        rstd = sb("rstd", (128, 1))
        pb = es.enter_context(nc.psum_tensor("C_pb", [128, 8, 512], F32))

        r_ident, r_eps, r_gfb, r_convc, r_hflag, r_cwe = (R(n) for n in ("ident", "eps", "gfb", "convc", "hflag", "cwe"))
        r_axT = R("axT")
        r_xmid = [R("xmid%d" % i) for i in range(4)]
        r_xmh = R("xmh")
        r_hT = R("hT")
        r_wr = [R("wr%d" % i) for i in range(NW)]
        r_xn2, r_junk, r_ssq, r_lnv, r_rstd = (R(n) for n in ("xn2", "junk", "ssq", "lnv", "rstd"))
        r_tb = [R("tb0"), R("tb1")]
        r_pb = [R("pb%d" % i) for i in range(8)]
        cnt = {"w": 0, "pb": 0, "tb": 0, "u": 0}

        P.op("sp", lambda e: e.dma_start(out=ident[:], in_=k.ident.ap()), w=[r_ident], dma=r_ident)
        P.op("sp", lambda e: e.dma_start(out=gfb[:], in_=k.gfb.ap()), w=[r_gfb], dma=r_gfb)
        P.op("sp", lambda e: e.dma_start(out=convc[:], in_=k.convc.ap()), w=[r_convc], dma=r_convc)
        P.op("sp", lambda e: e.dma_start(out=hflag[:], in_=k.hflag.ap()), w=[r_hflag], dma=r_hflag)
        P.op("dve", lambda e: e.memset(epsc[:], EPS), w=[r_eps])
        pc = psz(convc)
        for q in range(4):
            ci = 0 if q % 2 == 0 else 2
            P.op("dve", lambda e, q=q, ci=ci: e.tensor_scalar(out=cwe[:, q, :], in0=mkap(convc, ci, [(pc, 128), (4, FCH)]),
                                                             scalar1=hflag[:, q:q + 1], scalar2=None, op0=ALU.mult),
                 r=[r_convc, r_hflag], w=[r_cwe])

        def wslot():
            i = cnt["w"] % NW
            cnt["w"] += 1
            return i

        def load_w_cols(src_dram, c0, ncols):
            i = wslot()
            src = src_dram[:, c0:c0 + ncols].rearrange("(ck p) f -> p ck f", p=128)
            dst = wr[i][:, 0:CK * ncols].rearrange("p (ck f) -> p ck f", f=ncols)
            P.op("sp", lambda e: e.dma_start(out=dst, in_=src), w=[r_wr[i]], dma=r_wr[i])
            return i

        def load_w_rows(src_dram, r0, nr):
            i = wslot()
            src = src_dram[r0 * 128:(r0 + nr) * 128, :].rearrange("(f p) n -> p f n", p=128)
            dst = wr[i][:, 0:nr * D].rearrange("p (f n) -> p f n", n=D)
            P.op("sp", lambda e: e.dma_start(out=dst, in_=src), w=[r_wr[i]], dma=r_wr[i])
            return i

        def wview_cols(i, ncols):
            return wr[i][:, 0:CK * ncols].rearrange("p (ck f) -> p ck f", f=ncols)

        def wview_rows(i, nr):
            return wr[i][:, 0:nr * D].rearrange("p (f n) -> p f n", n=D)

        def bank():
            b = cnt["pb"] % 6
            cnt["pb"] += 1
            return b

        pax = psz(axT)

        def rms_rows(rows, src_ap, r_src, dst_ap, r_dst):
            P.op("act", lambda e: e.activation(out=junk[0:rows, :], in_=src_ap, func=AF.Square, accum_out=ssq[0:rows, :]),
                 r=[r_src], w=[r_junk, r_ssq])
            P.op("act", lambda e: e.activation(out=lnv[0:rows, :], in_=ssq[0:rows, :], func=AF.Ln, scale=1.0 / D, bias=epsc[0:rows, :]),
                 r=[r_ssq, r_eps], w=[r_lnv])
            P.op("act", lambda e: e.activation(out=rstd[0:rows, :], in_=lnv[0:rows, :], func=AF.Exp, scale=-0.5), r=[r_lnv], w=[r_rstd])
            if dst_ap is not None:
                P.op("dve", lambda e: e.tensor_scalar(out=dst_ap, in0=src_ap, scalar1=rstd[0:rows, :], scalar2=None, op0=ALU.mult),
                     r=[r_src, r_rstd], w=[r_dst])

        def tile(j, T):
            e0 = 512 * T
            xs_flat = k.xs[j].ap().rearrange("s p d -> (s p) d")
            src = k.aoT[j][:, :, e0:e0 + 514].rearrange("c p n -> p c n")
            P.op("sp", lambda e: e.dma_start(out=axT[:], in_=src), w=[r_axT], dma=r_axT)
            for tc in range(4):
                r0 = 383 + e0 + 1 + 128 * tc
                P.op("sp", lambda e, tc=tc, r0=r0: e.dma_start(out=xmid[:, tc, :], in_=xs_flat[r0:r0 + 128, :]), w=[r_xmid[tc]], dma=r_xmid[tc])
            hsrc = mkap(k.xs[j], (383 + e0) * D, [(513 * D, 2), (1, D)])
            P.op("sp", lambda e: e.dma_start(out=xmh[:], in_=hsrc), w=[r_xmh], dma=r_xmh)
            NG = 256
            nxt = load_w_cols(k.wb_out, 0, NG)
            for cg in range(D // NG):
                wi = nxt
                if cg + 1 < D // NG:
                    nxt = load_w_cols(k.wb_out, (cg + 1) * NG, NG)
                wv = wview_cols(wi, NG)
                for tc in range(5):
                    b = bank()
                    if tc < 4:
                        rows = 128
                        def lhs(ck, tc=tc):
                            return axT[:, ck, 1 + 128 * tc:129 + 128 * tc]
                        dst = xmid[:, tc, cg * NG:(cg + 1) * NG]
                        rd = r_xmid[tc]
                    else:
                        rows = 2
                        def lhs(ck):
                            return mkap(axT, ck * 514, [(pax, 128), (513, 2)])
                        dst = xmh[0:2, cg * NG:(cg + 1) * NG]
                        rd = r_xmh
                    for ck in range(CK):
                        P.op("pe", lambda e, b=b, ck=ck, lhs=lhs, rows=rows, wv=wv: e.matmul(
                            pb[0:rows, b, 0:NG], lhsT=lhs(ck), rhs=wv[:, ck, :], start=(ck == 0), stop=(ck == CK - 1)),
                            r=[r_axT, r_wr[wi]], w=[r_pb[b]])
                    P.op("dve", lambda e, b=b, dst=dst, rows=rows: e.tensor_tensor(out=dst, in0=pb[0:rows, b, 0:NG], in1=dst, op=ALU.add),
                         r=[r_pb[b], rd], w=[rd])
            for tc in range(5):
                rows = 128 if tc < 4 else 2
                srcx = xmid[:, tc, :] if tc < 4 else xmh[0:2, :]
                rsrc = r_xmid[tc] if tc < 4 else r_xmh
                rms_rows(rows, srcx, rsrc, xn2[0:rows, :], r_xn2)
                if tc < 4:
                    for half in range(2):
                        pt = pb[:, 6 + half, :].bitcast(BF16)
                        for q in range(8):
                            ck = half * 8 + q
                            P.op("pe", lambda e, pt=pt, q=q, ck=ck: e.transpose(out=pt[:, q * 128:(q + 1) * 128], in_=xn2[:, ck * 128:(ck + 1) * 128], identity=ident[:]),
                                 r=[r_xn2, r_ident], w=[r_pb[6 + half]])
                        dst = axT[:, half * 8:(half + 1) * 8, 1 + 128 * tc:129 + 128 * tc]
                        srcp = pt[:, 0:1024].rearrange("p (a b) -> p a b", b=128)
                        P.op("dve", lambda e, dst=dst, srcp=srcp: e.tensor_copy(out=dst, in_=srcp), r=[r_pb[6 + half]], w=[r_axT])
                else:
                    pt = pb[:, 6, :].bitcast(BF16)
                    for ck in range(CK):
                        P.op("pe", lambda e, pt=pt, ck=ck: e.transpose(out=pt[:, 2 * ck:2 * ck + 2], in_=xn2[0:2, ck * 128:(ck + 1) * 128], identity=ident[0:2, 0:2]),
                             r=[r_xn2, r_ident], w=[r_pb[6]])
                    dst = mkap(axT, 0, [(pax, 128), (514, CK), (513, 2)])
                    srcp = pt[:, 0:2 * CK].rearrange("p (a b) -> p a b", b=2)
                    P.op("dve", lambda e, dst=dst, srcp=srcp: e.tensor_copy(out=dst, in_=srcp), r=[r_pb[6]], w=[r_axT])
            groups = [(f0, min(2, FCH - f0)) for f0 in range(0, FCH, 2)]

            def load_group(gi):
                f0, nf = groups[gi]
                return (load_w_cols(k.wb_up, f0 * 128, nf * 128), load_w_cols(k.wb_up, DFF + f0 * 128, nf * 128))

            pend = [load_group(0), load_group(1)]
            for gi, (f0, nf) in enumerate(groups):
                wa_i, wg_i = pend.pop(0)
                if gi + 2 < len(groups):
                    pend.append(load_group(gi + 2))
                wa = wview_cols(wa_i, nf * 128)
                wg = wview_cols(wg_i, nf * 128)
                for fl in range(nf):
                    fc = f0 + fl
                    u = cnt["u"] % 2
                    cnt["u"] += 1
                    ba = 3 * u
                    for ck in range(CK):
                        P.op("pe", lambda e, ba=ba, ck=ck, wa=wa, fl=fl: e.matmul(pb[:, ba, 0:258], lhsT=wa[:, ck, fl * 128:(fl + 1) * 128],
                                                                             rhs=axT[:, ck, 0:258], start=(ck == 0), stop=(ck == CK - 1)),
                             r=[r_axT, r_wr[wa_i]], w=[r_pb[ba]])
                    for ck in range(CK):
                        P.op("pe", lambda e, ba=ba, ck=ck, wa=wa, fl=fl: e.matmul(pb[:, ba + 1, 0:258], lhsT=wa[:, ck, fl * 128:(fl + 1) * 128],
                                                                             rhs=axT[:, ck, 256:514], start=(ck == 0), stop=(ck == CK - 1)),
                             r=[r_axT, r_wr[wa_i]], w=[r_pb[ba + 1]])
                    for ck in range(CK):
                        P.op("pe", lambda e, ba=ba, ck=ck, wg=wg, fl=fl: e.matmul(pb[:, ba + 2, :], lhsT=wg[:, ck, fl * 128:(fl + 1) * 128],
                                                                             rhs=axT[:, ck, 1:513], start=(ck == 0), stop=(ck == CK - 1)),
                             r=[r_axT, r_wr[wg_i]], w=[r_pb[ba + 2]])
                    ti = cnt["tb"] % 2
                    cnt["tb"] += 1
                    t_ = tb[ti]
                    rt = r_tb[ti]
                    ra = [r_pb[ba], r_pb[ba + 1]]
                    P.op("dve", lambda e, ba=ba, t_=t_, fc=fc: e.tensor_scalar(out=t_[:], in0=pb[:, ba:ba + 2, 0:256], scalar1=convc[:, fc, 0:1],
                                                                           scalar2=convc[:, fc, 3:4], op0=ALU.mult, op1=ALU.add),
                         r=ra + [r_convc], w=[rt])
                    if T == 0:
                        P.op("dve", lambda e, ba=ba, t_=t_, fc=fc: e.tensor_scalar(out=t_[:, 0, 0:1], in0=pb[:, ba, 0:1], scalar1=cwe[:, 2 * j, fc:fc + 1],
                                                                               scalar2=convc[:, fc, 3:4], op0=ALU.mult, op1=ALU.add),
                             r=ra + [r_convc, r_cwe], w=[rt])
                    P.op("dve", lambda e, ba=ba, t_=t_, fc=fc: e.scalar_tensor_tensor(out=t_[:], in0=pb[:, ba:ba + 2, 1:257], scalar=convc[:, fc, 1:2],
                                                                                  in1=t_[:], op0=ALU.mult, op1=ALU.add),
                         r=ra + [r_convc, rt], w=[rt])
                    if T == 3:
                        P.op("dve", lambda e, ba=ba, t_=t_, fc=fc: e.scalar_tensor_tensor(out=t_[:, 1, 255:256], in0=pb[:, ba + 1, 257:258], scalar=cwe[:, 2 * j + 1, fc:fc + 1],
                                                                                      in1=t_[:, 1, 255:256], op0=ALU.mult, op1=ALU.add),
                             r=ra + [r_cwe, rt], w=[rt])
                        P.op("dve", lambda e, ba=ba, t_=t_, fc=fc: e.scalar_tensor_tensor(out=t_[:, :, 0:255], in0=pb[:, ba:ba + 2, 2:257], scalar=convc[:, fc, 2:3],
                                                                                      in1=t_[:, :, 0:255], op0=ALU.mult, op1=ALU.add),
                             r=ra + [r_convc, rt], w=[rt])
                        P.op("dve", lambda e, ba=ba, t_=t_, fc=fc: e.scalar_tensor_tensor(out=t_[:, 0, 255:256], in0=pb[:, ba, 257:258], scalar=convc[:, fc, 2:3],
                                                                                      in1=t_[:, 0, 255:256], op0=ALU.mult, op1=ALU.add),
                             r=ra + [r_convc, rt], w=[rt])
                    else:
                        P.op("dve", lambda e, ba=ba, t_=t_, fc=fc: e.scalar_tensor_tensor(out=t_[:], in0=pb[:, ba:ba + 2, 2:258], scalar=convc[:, fc, 2:3],
                                                                                      in1=t_[:], op0=ALU.mult, op1=ALU.add),
                             r=ra + [r_convc, rt], w=[rt])
                    P.op("act", lambda e, t_=t_: e.activation(out=t_[:], in_=t_[:], func=AF.Gelu_apprx_tanh), r=[rt], w=[rt])
                    P.op("dve", lambda e, ba=ba, t_=t_, fc=fc: e.tensor_tensor(out=hT[:, fc, :], in0=t_[:].rearrange("p a b -> p (a b)"), in1=pb[:, ba + 2, :], op=ALU.mult),
                         r=[rt, r_pb[ba + 2]], w=[r_hT])
            dgroups = [(f0, min(2, FCH - f0)) for f0 in range(0, FCH, 2)]
            for hh in range(2):
                pend = [load_w_rows(k.wb_down, dgroups[0][0], dgroups[0][1]), load_w_rows(k.wb_down, dgroups[1][0], dgroups[1][1])]
                for gi, (f0, nf) in enumerate(dgroups):
                    wi = pend.pop(0)
                    if gi + 2 < len(dgroups):
                        pend.append(load_w_rows(k.wb_down, dgroups[gi + 2][0], dgroups[gi + 2][1]))
                    wv = wview_rows(wi, nf)
                    for fl in range(nf):
                        fc = f0 + fl
                        for tl in range(2):
                            tc = 2 * hh + tl
                            for cg in range(4):
                                b = tl * 4 + cg
                                P.op("pe", lambda e, b=b, fc=fc, tc=tc, fl=fl, cg=cg, wv=wv: e.matmul(
                                    pb[:, b, :], lhsT=hT[:, fc, tc * 128:(tc + 1) * 128], rhs=wv[:, fl, cg * 512:(cg + 1) * 512],
                                    start=(fc == 0), stop=(fc == FCH - 1)),
                                    r=[r_hT, r_wr[wi]], w=[r_pb[b]])
                for tl in range(2):
                    tc = 2 * hh + tl
                    for cg in range(4):
                        b = tl * 4 + cg
                        dst = xmid[:, tc, cg * 512:(cg + 1) * 512]
                        P.op("dve", lambda e, b=b, dst=dst: e.tensor_tensor(out=dst, in0=pb[:, b, :], in1=dst, op=ALU.add),
                             r=[r_pb[b], r_xmid[tc]], w=[r_xmid[tc]])
                    rms_rows(128, xmid[:, tc, :], r_xmid[tc], None, None)
                    P.op("dve", lambda e, tc=tc: e.scalar_tensor_tensor(out=xmid[:, tc, :], in0=xmid[:, tc, :], scalar=rstd[:, :], in1=gfb[:],
                                                                       op0=ALU.mult, op1=ALU.mult),
                         r=[r_xmid[tc], r_rstd, r_gfb], w=[r_xmid[tc]])
                    row0 = 512 * T + 128 * tc
                    P.op("sp", lambda e, tc=tc, row0=row0: e.dma_start(out=k.y[j][row0:row0 + 128, :], in_=xmid[:, tc, :]), r=[r_xmid[tc]], dma=r_xmid[tc])

        for j in DEBUG.get("jobs", (0, 1)):
            for T in DEBUG.get("tiles_c", range(4)):
                tile(j, T)
        P.emit_phase()


def prepare_inputs(inp):
    f32 = np.float32
    x_prompt = np.asarray(inp["x_prompt"], f32)
    x_sample = np.asarray(inp["x_sample"], f32)
    shared = {}
    shared["w_in"] = np.ascontiguousarray(np.asarray(inp["w_in"], f32)[0])
    shared["w_out"] = np.ascontiguousarray(np.asarray(inp["w_out"], f32)[0])
    shared["w_up"] = np.ascontiguousarray(np.asarray(inp["w_up"], f32)[0])
    shared["w_down"] = np.ascontiguousarray(np.asarray(inp["w_down"], f32)[0])
    g1 = np.asarray(inp["norm1_g"], f32)[0].reshape(CK, 128).T
    g2 = np.asarray(inp["norm2_g"], f32)[0].reshape(CK, 128).T
    shared["g12c"] = np.ascontiguousarray(np.stack([g1, g2], axis=1))
    shared["gfb"] = bcast128(inp["final_g"])
    shared["sgb"] = bcast128(np.asarray(inp["subln_g"], f32)[0])
    lam = np.concatenate([np.asarray(inp[n], f32)[0] for n in ("lambda_q1", "lambda_k1", "lambda_q2", "lambda_k2")])
    shared["lamv"] = bcast128(lam).reshape(128, 4, 64)
    tab = np.asarray(inp["rel_bias_table"], f32)
    shared["tab"] = np.ascontiguousarray(tab)
    shared["tablr"] = bcast128(np.concatenate([tab[15], tab[31]])).reshape(128, 2, 8)
    rel = np.arange(1536) - 767
    bk = t5_bucket_np(rel)
    ohu = np.zeros((32, 1536), f32)
    ohu[bk, np.arange(1536)] = 1.0
    shared["ohu"] = ohu
    rpbp = np.zeros((8, 15, 128), f32)
    rpbp[:, :, 48:79] = np.asarray(inp["na_rpb"], f32)[0]
    shared["rpbp"] = rpbp
    cw = np.asarray(inp["conv_w"], f32)[0]
    cb = np.asarray(inp["conv_b"], f32)[0]
    cc = np.stack([cw[0], cw[1], cw[2], cb], axis=1)
    shared["convc"] = np.ascontiguousarray(cc.reshape(FCH, 128, 4).transpose(1, 0, 2))
    shared["ident"] = np.eye(128, dtype=f32).astype(ml_dtypes.bfloat16)

    in_maps = []
    for c in range(NCORES):
        m = dict(shared)
        hflag = np.zeros((128, 4), f32)
        for j in range(2):
            nblk, S = JOBS[j]
            if j == 0:
                seq, t = x_prompt[c // 4], c % 4
            else:
                seq, t = x_sample[c // 2], c % 2
            o, blocks, near_true = job_geometry(nblk, S, t)
            xs = np.zeros((S, 128, D), f32)
            ws = np.zeros((3, S), f32)
            for s, gb in enumerate(blocks):
                if gb < 0:
                    continue
                xs[s] = seq[gb * 128:(gb + 1) * 128]
                if 2 <= s <= 19:
                    ws[0, s] = 1.0
                elif gb < o:
                    ws[1, s] = 1.0
                else:
                    ws[2, s] = 1.0
            m["xs%d" % j] = xs
            m["wsel%d" % j] = np.ascontiguousarray(np.broadcast_to(ws[None], (128, 3, S)))
            m["maskd%d" % j] = build_masks(nblk, t, o, near_true)
            hflag[:, 2 * j] = 1.0 if o > 0 else 0.0
            hflag[:, 2 * j + 1] = 1.0 if o + 16 < nblk else 0.0
        m["hflag"] = hflag
        in_maps.append(m)
    return in_maps


def kernel(**inputs):
    in_maps = prepare_inputs(inputs)
    nc = build_program()
    res = run_bass_kernel_spmd(nc, in_maps, core_ids=list(range(NCORES)))
    yp = np.zeros((2, 8192, D), np.float32)
    ysm = np.zeros((4, 4096, D), np.float32)
    for c in range(NCORES):
        r = res.results[c]
        yp[c // 4, (c % 4) * 2048:(c % 4 + 1) * 2048] = r["y0"]
        ysm[c // 2, (c % 2) * 2048:(c % 2 + 1) * 2048] = r["y1"]
    return (yp, ysm)
```

```python
import math
from contextlib import ExitStack

import numpy as np
import ml_dtypes

import concourse.bass as bass
import concourse.mybir as mybir
from concourse.bass_utils import run_bass_kernel_spmd

F32 = mybir.dt.float32
BF16 = mybir.dt.bfloat16
AF = mybir.ActivationFunctionType
ALU = mybir.AluOpType
AX = mybir.AxisListType

D = 2048
CK = 16
HA = 8
HN = 8
INC = 6144
DFF = 5504
FCH = 43
EPS = 1e-6
NCORES = 8
NEAR = 22
JOBS = ((64, 65), (32, 33))
SCALE_A = 0.125
SCALE_N = 128 ** -0.5
NEG = -3.0e5
LAM_INIT = 0.8 - 0.6 * math.exp(-0.3 * 0)

DEBUG = {"stop_after": None, "ext": False}


class Sem:
    def __init__(self, h, name):
        self.h = h
        self.v = 0
        self.name = name


class Res:
    __slots__ = ("name", "wr", "rd", "dsem")

    def __init__(self, name):
        self.name = name
        self.wr = None
        self.rd = []
        self.dsem = None


class Op:
    __slots__ = ("eng", "fn", "deps", "sig", "ev", "dma")


ENGS = ("pe", "act", "dve", "pool", "sp")


class Prog:
    def __init__(self, nc):
        self.nc = nc
        self.esem = {e: Sem(nc.alloc_semaphore(name="es_" + e), e) for e in ("pe", "act", "dve", "pool")}
        self.bar = Sem(nc.alloc_semaphore(name="bar"), "bar")
        self.free_dsems = []
        self.free_dsems_sw = []
        self.ndsem = 0
        self.reset()
        self.waited = {e: {} for e in ENGS}
        self.nphase = 0
        self.all_res = []

    def reset(self):
        self.ops = {e: [] for e in ENGS}
        self.order = []

    def res(self, name):
        r = Res(name)
        self.all_res.append(r)
        return r

    def _dsem(self, r, eng):
        if r.dsem is None:
            sw = eng == "pool"
            free = self.free_dsems_sw if sw else self.free_dsems
            if free:
                r.dsem = free.pop()
            else:
                r.dsem = Sem(self.nc.alloc_semaphore(name="ds%d" % self.ndsem), "ds%d" % self.ndsem)
                r.dsem.sw = sw
                self.ndsem += 1
        return r.dsem

    def op(self, eng, fn, r=(), w=(), dma=None, after=()):
        o = Op()
        o.eng = eng
        o.fn = fn
        o.dma = dma
        o.sig = False
        o.ev = None
        deps = []
        seen = set()

        def add(d):
            if d is None or id(d) in seen:
                return
            seen.add(id(d))
            deps.append(d)

        for d in after:
            add(d)
        for x in r:
            add(x.wr)
        for x in w:
            add(x.wr)
            for d in x.rd:
                add(d)
        o.deps = [d for d in deps if not (d.eng == "pe" and eng == "pe" and d.dma is None)]
        for d in o.deps:
            d.sig = True
        for x in w:
            x.wr = o
            x.rd = []
        for x in r:
            x.rd.append(o)
        self.ops[eng].append(o)
        self.order.append(o)
        return o

    def emit_phase(self):
        nc = self.nc
        for o in self.order:
            if o.dma is not None:
                s = self._dsem(o.dma, o.eng)
                s.v += 16
                o.ev = (s, s.v)
        for e in ("pe", "act", "dve", "pool"):
            ops = self.ops[e]
            lo = [o for o in ops if o.dma is None]
            if lo:
                lo[-1].sig = True
            for o in ops:
                if o.dma is None and o.sig:
                    s = self.esem[e]
                    s.v += 1
                    o.ev = (s, s.v)
        self.nphase += 1
        bar_target = self.nphase * len(ENGS)
        prog = self

        def run(e, eng):
            waited = prog.waited[e]

            def wait(ev):
                s, v = ev
                if waited.get(id(s), 0) < v:
                    eng.wait_ge(s.h, v)
                    waited[id(s)] = v

            last_dma = {}
            for o in prog.ops[e]:
                need = {}
                for d in o.deps:
                    sm, v = d.ev
                    if need.get(id(sm), (None, 0))[1] < v:
                        need[id(sm)] = (sm, v)
                for ev in need.values():
                    wait(ev)
                ins = o.fn(eng)
                if o.dma is not None:
                    ins.then_inc(o.ev[0].h, 16)
                    last_dma[id(o.ev[0])] = o.ev
                elif o.sig:
                    ins.then_inc(o.ev[0].h, 1)
            for ev in last_dma.values():
                wait(ev)
            if e in prog.esem and prog.ops[e]:
                lo = [o for o in prog.ops[e] if o.dma is None]
                if lo:
                    wait(lo[-1].ev)
            eng.sem_inc(prog.bar.h, 1)
            eng.wait_ge(prog.bar.h, bar_target)

        with nc.Block() as block:
            @block.tensor
            def _(eng):
                run("pe", eng)

            @block.scalar
            def _(eng):
                run("act", eng)

            @block.vector
            def _(eng):
                run("dve", eng)

            @block.gpsimd
            def _(eng):
                run("pool", eng)

            @block.sync
            def _(eng):
                run("sp", eng)

        for r in self.all_res:
            r.wr = None
            r.rd = []
            if r.dsem is not None:
                (self.free_dsems_sw if getattr(r.dsem, "sw", False) else self.free_dsems).append(r.dsem)
                r.dsem = None
        self.all_res = []
        self.reset()


def mkap(t, off, dims):
    return bass.AP(t, off, [list(d) for d in dims])


def psz(t):
    return t[:].ap[0][0]


def t5_bucket_np(rel):
    nb = 16
    me = 8
    ret = np.where(rel > 0, nb, 0)
    n = np.abs(rel)
    nf = np.maximum(n, 1).astype(np.float32)
    large = me + (np.log(nf / np.float32(me)) / np.float32(math.log(128 / 8)) * np.float32(nb - me)).astype(np.int32)
    large = np.minimum(large, nb - 1)
    return ret + np.where(n < me, n, large)


def job_geometry(nblk, nslots, t):
    o = 16 * t
    blocks = [-1] * nslots
    near_true = [False] * NEAR
    used = set()
    for n in range(NEAR):
        gb = o + n - 3
        if 0 <= gb < nblk:
            blocks[n] = gb
            near_true[n] = True
            used.add(gb)
    rest = [b for b in range(nblk) if b not in used]
    for n in range(NEAR):
        if blocks[n] == -1 and n not in (2, 19) and rest:
            blocks[n] = rest.pop(0)
    for s in range(NEAR, nslots):
        if rest:
            blocks[s] = rest.pop(0)
    assert not rest
    return o, blocks, near_true


def nbr_units():
    units = []
    for i in range(16):
        if i == 0:
            dl = list(range(-2, 4))
        elif i == 15:
            dl = list(range(-3, 3))
        else:
            dl = list(range(-2, 3))
        units.append((i, dl))
    units.append((-1, list(range(-2, 3))))
    units.append((16, list(range(-2, 3))))
    return units


def mask_layout():
    off = {}
    col = 0
    for key, nd, w in (("int", 5, 128), (0, 6, 128), (1, 5, 128), (14, 5, 128), (15, 6, 128), (-1, 5, 1), (16, 5, 1)):
        off[key] = col
        col += nd * w
    return off, col


def build_masks(nblk, t, o, near_true):
    R = nblk * 2
    L = nblk * 128
    off, ncol = mask_layout()
    out = np.zeros((128, ncol), np.float32)

    def tile(tq, valid_q, nk):
        if not valid_q:
            return np.zeros((128, len(tq)), np.float32)
        if not near_true[nk]:
            return np.full((128, len(tq)), NEG, np.float32)
        tk = (o + nk - 3) * 128 + np.arange(128)
        r = tq // 64
        c = tq % 64
        rs = np.clip(r - 4, 0, R - 8)
        cs = np.clip(c - 8, 0, 64 - 16)
        rk = (tk // 64)[:, None]
        ckk = (tk % 64)[:, None]
        ok = (rk >= rs[None]) & (rk < rs[None] + 8) & (ckk >= cs[None]) & (ckk < cs[None] + 16)
        return np.where(ok, 0.0, NEG).astype(np.float32)

    units = dict(nbr_units())
    per_unit = {}
    for i, dl in units.items():
        if i == -1:
            tq = np.array([o * 128 - 1])
            vq = o > 0
        elif i == 16:
            tq = np.array([(o + 16) * 128])
            vq = (o + 16) < nblk
        else:
            tq = (o + i) * 128 + np.arange(128)
            vq = True
        per_unit[i] = [tile(tq, vq, i + dl_ + 3) for dl_ in dl]
    for i in range(2, 14):
        for a, b in zip(per_unit[i], per_unit[7]):
            assert np.array_equal(a, b)
    def put(key, tiles):
        c0 = off[key]
        for k, tl in enumerate(tiles):
            w = tl.shape[1]
            out[:, c0 + k * w:c0 + (k + 1) * w] = tl
    put("int", per_unit[7])
    for key in (0, 1, 14, 15, -1, 16):
        put(key, per_unit[key])
    return out.astype(ml_dtypes.bfloat16)


def bcast128(a):
    a = np.asarray(a, np.float32)
    return np.ascontiguousarray(np.broadcast_to(a.reshape(1, -1), (128, a.size)))


class K:
    pass


def build_program():
    nc = bass.Bass("TRN2", target_bir_lowering=False)
    P = Prog(nc)
    k = K()
    k.nc = nc
    k.P = P
    ext = DEBUG["ext"]

    def din(name, shape, dt=F32):
        return nc.dram_tensor(name, list(shape), dt, kind="ExternalInput")

    def dscr(name, shape, dt=BF16):
        return nc.dram_tensor(name, list(shape), dt, kind=("ExternalOutput" if (ext and name in ext) else "Internal"))

    k.xs = [din("xs%d" % j, (JOBS[j][1], 128, D)) for j in range(2)]
    k.w_in = din("w_in", (D, INC))
    k.w_out = din("w_out", (D, D))
    k.w_up = din("w_up", (D, 2 * DFF))
    k.w_down = din("w_down", (DFF, D))
    k.g12c = din("g12c", (128, 2, CK))
    k.gfb = din("gfb", (128, D))
    k.sgb = din("sgb", (128, 128))
    k.lamv = din("lamv", (128, 4, 64))
    k.tab = din("tab", (32, 8))
    k.tablr = din("tablr", (128, 2, 8))
    k.ohu = din("ohu", (32, 1536))
    k.rpbp = din("rpbp", (8, 15, 128))
    k.convc = din("convc", (128, FCH, 4))
    k.ident = din("ident", (128, 128), BF16)
    k.wsel = [din("wsel%d" % j, (128, 3, JOBS[j][1])) for j in range(2)]
    _, mcols = mask_layout()
    k.maskd = [din("maskd%d" % j, (128, mcols), BF16) for j in range(2)]
    k.hflag = din("hflag", (128, 4))
    k.y = [nc.dram_tensor("y%d" % j, [2048, D], F32, kind="ExternalOutput") for j in range(2)]
    k.wb_in = dscr("wb_in", (D, INC))
    k.wb_out = dscr("wb_out", (D, D))
    k.wb_up = dscr("wb_up", (D, 2 * DFF))
    k.wb_down = dscr("wb_down", (DFF, D))
    k.qaT = [dscr("qaT%d" % j, (HA, 128, NEAR * 128)) for j in range(2)]
    k.qnT = [dscr("qnT%d" % j, (HN, 128, NEAR * 128)) for j in range(2)]
    k.knT = [dscr("knT%d" % j, (HN, 128, NEAR * 128)) for j in range(2)]
    k.kaT = [dscr("kaT%d" % j, (HA, 128, JOBS[j][1] * 128)) for j in range(2)]
    k.va = [dscr("va%d" % j, (HA, 128, JOBS[j][1], 129)) for j in range(2)]
    k.vn = [dscr("vn%d" % j, (HN, 128, NEAR, 129)) for j in range(2)]
    k.aoT = [dscr("aoT%d" % j, (16, 128, 2050)) for j in range(2)]
    k.u2 = dscr("u2", (8, 1536), F32)
    k.tfd = dscr("tfd", (8, 128, 896), F32)

    if not DEBUG.get("skip_a"):
        phase_a(k)
    if DEBUG["stop_after"] == "A":
        return nc
    phase_b(k)
    if DEBUG["stop_after"] == "B":
        return nc
    phase_c(k)
    return nc


def phase_a(k):
    nc, P = k.nc, k.P
    with ExitStack() as es:
        def sb(name, shape, dt=F32):
            return es.enter_context(nc.sbuf_tensor("A_" + name, list(shape), dt))

        def ps(name, shape, dt=F32):
            return es.enter_context(nc.psum_tensor("A_" + name, list(shape), dt))

        ident = sb("ident", (128, 128), BF16)
        g12c = sb("g12c", (128, 2, CK))
        tablr = sb("tablr", (128, 2, 8))
        elr = sb("elr", (128, 2, 8))
        epsc = sb("epsc", (128, 1))
        wsel = [sb("wsel%d" % j, (128, 3, JOBS[j][1])) for j in range(2)]
        wfull = [sb("wfull%d" % j, (128, JOBS[j][1], 8)) for j in range(2)]
        wtmp = sb("wtmp", (128, 65, 8))
        CW = 1024
        cin = [sb("cin%d" % i, (128, 512)) for i in range(4)]
        cout = [sb("cout%d" % i, (128, 512), BF16) for i in range(4)]
        xbuf = [sb("xbuf%d" % i, (128, D)) for i in range(3)]
        junk = sb("junk", (128, D), BF16)
        ssq = [sb("ssq%d" % i, (128, 1)) for i in range(3)]
        lnv = [sb("lnv%d" % i, (128, 1)) for i in range(3)]
        rstd = [sb("rstd%d" % i, (128, 1)) for i in range(3)]
        xn = [sb("xn%d" % i, (128, D), BF16) for i in range(2)]
        xnT = [sb("xnT%d" % i, (128, CK, 512), BF16) for i in range(2)]
        wring = [sb("wring%d" % i, (128, CK, 512), BF16) for i in range(3)]
        fmst = [sb("fmst%d" % i, (128, 4, 512), BF16) for i in range(2)]
        vst = [sb("vst%d" % i, (128, 8, 4, 129), BF16) for i in range(2)]
        pT = [ps("pT%d" % i, (128, 8 * 128), BF16) for i in range(2)]
        pO = [ps("pO%d" % i, (128, 512)) for i in range(4)]

        R = P.res
        r_ident, r_g, r_tablr, r_elr, r_eps = R("ident"), R("g12c"), R("tablr"), R("elr"), R("eps")
        r_wsel = [R("wsel0"), R("wsel1")]
        r_wfull = [R("wfull0"), R("wfull1")]
        r_wtmp = R("wtmp")

        P.op("sp", lambda e: e.dma_start(out=ident[:], in_=k.ident.ap()), w=[r_ident], dma=r_ident)
        P.op("sp", lambda e: e.dma_start(out=g12c[:], in_=k.g12c.ap()), w=[r_g], dma=r_g)
        P.op("sp", lambda e: e.dma_start(out=tablr[:], in_=k.tablr.ap()), w=[r_tablr], dma=r_tablr)
        for j in range(2):
            P.op("sp", lambda e, j=j: e.dma_start(out=wsel[j][:], in_=k.wsel[j].ap()), w=[r_wsel[j]], dma=r_wsel[j])
        P.op("dve", lambda e: e.memset(epsc[:], EPS), w=[r_eps])
        P.op("act", lambda e: e.activation(out=elr[:], in_=tablr[:], func=AF.Exp), r=[r_tablr], w=[r_elr])
        for j in range(2):
            S = JOBS[j][1]
            wf, ws = wfull[j], wsel[j]
            pw, pe_, pt = psz(wf), psz(ws), psz(wtmp)
            pl = psz(elr)

            def bc_s(c, ws=ws, pe_=pe_, S=S):
                return mkap(ws, c * S, [(pe_, 128), (1, S), (0, 8)])

            def bc_h(c, S=S, pl=pl):
                return mkap(elr, c * 8, [(pl, 128), (0, S), (1, 8)])

            wt = mkap(wtmp, 0, [(pt, 128), (8, S), (1, 8)])
            P.op("dve", lambda e, wf=wf, bc_s=bc_s, bc_h=bc_h: e.tensor_tensor(out=wf[:], in0=bc_s(1), in1=bc_h(0), op=ALU.mult),
                 r=[r_wsel[j], r_elr], w=[r_wfull[j]])
            P.op("dve", lambda e, wt=wt, bc_s=bc_s, bc_h=bc_h: e.tensor_tensor(out=wt, in0=bc_s(2), in1=bc_h(1), op=ALU.mult),
                 r=[r_wsel[j], r_elr], w=[r_wtmp])
            P.op("dve", lambda e, wf=wf, wt=wt: e.tensor_tensor(out=wf[:], in0=wf[:], in1=wt, op=ALU.add),
                 r=[r_wtmp], w=[r_wfull[j]])
            P.op("dve", lambda e, wf=wf, bc_s=bc_s: e.tensor_tensor(out=wf[:], in0=wf[:], in1=bc_s(0), op=ALU.add),
                 r=[r_wsel[j]], w=[r_wfull[j]])

        r_cin = [R("cin%d" % i) for i in range(4)]
        r_cout = [R("cout%d" % i) for i in range(4)]
        conv_state = {"n": 0}
        GROUP_ORDER = [1024, 1536, 2048, 2560, 0, 512, 3072, 3584, 4096, 4608, 5120, 5632]
        st_in = {}
        for col0 in GROUP_ORDER:
            st_in[col0] = []
            for rb in range(CK):
                i = conv_state["n"] % 4
                conv_state["n"] += 1
                ci, co = cin[i], cout[i]
                rci, rco = r_cin[i], r_cout[i]
                sv = k.w_in[rb * 128:(rb + 1) * 128, col0:col0 + 512]
                dv = k.wb_in[rb * 128:(rb + 1) * 128, col0:col0 + 512]
                P.op("pool", lambda e, ci=ci, sv=sv: e.dma_start(out=ci[:, 0:512], in_=sv), w=[rci], dma=rci)
                P.op("pool", lambda e, ci=ci, co=co, rb=rb: e.tensor_scalar(
                    out=co[:, 0:512], in0=ci[:, 0:512], scalar1=g12c[:, 0, rb:rb + 1], scalar2=1.0, op0=ALU.mult, op1=ALU.mult),
                    r=[rci, r_g], w=[rco])
                st_in[col0].append(P.op("pool", lambda e, co=co, dv=dv: e.dma_start(out=dv, in_=co[:, 0:512]), r=[rco], dma=rco))
        groups_loaded = set()

        r_x = [R("xbuf%d" % i) for i in range(3)]
        r_junk = R("junk")
        r_ssq = [R("ssq%d" % i) for i in range(3)]
        r_lnv = [R("lnv%d" % i) for i in range(3)]
        r_rstd = [R("rstd%d" % i) for i in range(3)]
        r_xn = [R("xn%d" % i) for i in range(2)]
        r_xnT = [[R("xnT%d_%d" % (i, b)) for b in range(4)] for i in range(2)]
        r_w = [R("wring%d" % i) for i in range(3)]
        r_fm = [R("fmst%d" % i) for i in range(2)]
        r_vst = [R("vst%d" % i) for i in range(2)]
        r_pT = [R("pT%d" % i) for i in range(2)]
        r_pO = [R("pO%d" % i) for i in range(4)]
        cnt = {"x": 0, "xn": 0, "pT": 0, "w": 0, "pO": 0, "fm": 0, "vst": 0, "tile": 0}

        SUBS_NEAR = [("ka", 1024), ("ka", 1536), ("va", 2048), ("va", 2560), ("qa", 0), ("qa", 512),
                     ("qn", 3072), ("qn", 3584), ("kn", 4096), ("kn", 4608), ("vn", 5120), ("vn", 5632)]
        SUBS_FAR = [("ka", 1024), ("ka", 1536), ("va", 2048), ("va", 2560)]

        def load_x(j, s):
            i = cnt["x"] % 3
            cnt["x"] += 1
            P.op("sp", lambda e, i=i, j=j, s=s: e.dma_start(out=xbuf[i][:], in_=k.xs[j][s]), w=[r_x[i]], dma=r_x[i])
            return i

        def norm_block(xi, xnT_i, b):
            ni = cnt["xn"] % 2
            cnt["xn"] += 1
            P.op("act", lambda e: e.activation(out=junk[:], in_=xbuf[xi][:], func=AF.Square, accum_out=ssq[xi][:]),
                 r=[r_x[xi]], w=[r_junk, r_ssq[xi]])
            P.op("act", lambda e: e.activation(out=lnv[xi][:], in_=ssq[xi][:], func=AF.Ln, scale=1.0 / D, bias=epsc[:]),
                 r=[r_ssq[xi], r_eps], w=[r_lnv[xi]])
            P.op("act", lambda e: e.activation(out=rstd[xi][:], in_=lnv[xi][:], func=AF.Exp, scale=-0.5),
                 r=[r_lnv[xi]], w=[r_rstd[xi]])
            P.op("dve", lambda e: e.tensor_scalar(out=xn[ni][:], in0=xbuf[xi][:], scalar1=rstd[xi][:], scalar2=None, op0=ALU.mult),
                 r=[r_x[xi], r_rstd[xi]], w=[r_xn[ni]])
            for half in range(2):
                pi = cnt["pT"] % 2
                cnt["pT"] += 1
                for q in range(8):
                    ck = half * 8 + q
                    P.op("pe", lambda e, pi=pi, q=q, ck=ck: e.transpose(out=pT[pi][:, q * 128:(q + 1) * 128],
                                                                        in_=xn[ni][:, ck * 128:(ck + 1) * 128], identity=ident[:]),
                         r=[r_xn[ni], r_ident], w=[r_pT[pi]])
                dst = xnT[xnT_i][:, half * 8:(half + 1) * 8, b * 128:(b + 1) * 128]
                src = pT[pi][:].rearrange("p (a b) -> p a b", b=128)
                P.op("dve", lambda e, dst=dst, src=src: e.tensor_copy(out=dst, in_=src), r=[r_pT[pi]], w=[r_xnT[xnT_i][b]])

        def load_w(col0, first):
            i = cnt["w"] % 3
            cnt["w"] += 1
            src = k.wb_in[:, col0:col0 + 512].rearrange("(ck p) f -> p ck f", p=128)
            aft = ()
            if col0 not in groups_loaded:
                groups_loaded.add(col0)
                aft = st_in[col0]
            P.op("sp", lambda e, i=i, src=src: e.dma_start(out=wring[i][:], in_=src), w=[r_w[i]], dma=r_w[i], after=aft)
            return i

        def do_tile(j, s0, nb, near, xnT_i, prefetch):
            S = JOBS[j][1]
            N = nb * 128
            subs = SUBS_NEAR if near else SUBS_FAR
            wq = []
            state = {"first": cnt["w"] == 0}
            nxt = load_w(subs[0][1], state["first"])
            for si, (kind, col0) in enumerate(subs):
                wi = nxt
                if si + 1 < len(subs):
                    nxt = load_w(subs[si + 1][1], False)
                if si == 1 and prefetch is not None:
                    prefetch()
                hb = (col0 % 1024) // 128
                if kind in ("qa", "ka", "qn", "kn"):
                    fi = cnt["fm"] % 2
                    cnt["fm"] += 1
                    for fc in range(4):
                        oi = cnt["pO"] % 4
                        cnt["pO"] += 1
                        for ck in range(CK):
                            P.op("pe", lambda e, oi=oi, wi=wi, fc=fc, ck=ck: e.matmul(
                                pO[oi][:, 0:N], lhsT=wring[wi][:, ck, fc * 128:(fc + 1) * 128], rhs=xnT[xnT_i][:, ck, 0:N],
                                start=(ck == 0), stop=(ck == CK - 1)),
                                r=[r_w[wi]] + r_xnT[xnT_i][0:nb], w=[r_pO[oi]])
                        P.op("act", lambda e, oi=oi, fi=fi, fc=fc: e.activation(out=fmst[fi][:, fc, 0:N], in_=pO[oi][:, 0:N], func=AF.Copy),
                             r=[r_pO[oi]], w=[r_fm[fi]])
                    dstT = {"qa": k.qaT, "ka": k.kaT, "qn": k.qnT, "kn": k.knT}[kind][j]
                    dv = dstT[hb:hb + 4, :, s0 * 128:s0 * 128 + N].rearrange("h p n -> p h n")
                    P.op("act", lambda e, fi=fi, dv=dv: e.dma_start(out=dv, in_=fmst[fi][:, :, 0:N]), r=[r_fm[fi]], dma=r_fm[fi])
                else:
                    if hb == 0:
                        vi = cnt["vst"] % 2
                        cnt["vst"] += 1
                        state["vi"] = vi
                    vi = state["vi"]
                    for b in range(nb):
                        oi = cnt["pO"] % 4
                        cnt["pO"] += 1
                        for ck in range(CK):
                            P.op("pe", lambda e, oi=oi, wi=wi, b=b, ck=ck: e.matmul(
                                pO[oi][:, :], lhsT=xnT[xnT_i][:, ck, b * 128:(b + 1) * 128], rhs=wring[wi][:, ck, :],
                                start=(ck == 0), stop=(ck == CK - 1)),
                                r=[r_w[wi], r_xnT[xnT_i][b]], w=[r_pO[oi]])
                        src = pO[oi][:].rearrange("p (h e) -> p h e", e=128)
                        dst = vst[vi][:, hb:hb + 4, b, 0:128]
                        if kind == "va":
                            wf = wfull[j]
                            wb = mkap(wf, (s0 + b) * 8 + hb, [(psz(wf), 128), (1, 4), (0, 128)])
                            P.op("dve", lambda e, dst=dst, src=src, wb=wb: e.tensor_tensor(out=dst, in0=src, in1=wb, op=ALU.mult),
                                 r=[r_pO[oi], r_wfull[j]], w=[r_vst[vi]])
                            if hb == 4:
                                ones_dst = vst[vi][:, :, b, 128:129]
                                wsrc = mkap(wf, (s0 + b) * 8, [(psz(wf), 128), (1, 8), (1, 1)])
                                P.op("dve", lambda e, ones_dst=ones_dst, wsrc=wsrc: e.tensor_copy(out=ones_dst, in_=wsrc),
                                     r=[r_wfull[j]], w=[r_vst[vi]])
                        else:
                            P.op("act", lambda e, dst=dst, src=src: e.activation(out=dst, in_=src, func=AF.Copy),
                                 r=[r_pO[oi]], w=[r_vst[vi]])
                            if hb == 4:
                                ones_dst = vst[vi][:, :, b, 128:129]
                                P.op("dve", lambda e, ones_dst=ones_dst: e.memset(ones_dst, 1.0), w=[r_vst[vi]])
                    if hb == 4:
                        dstV = (k.va if kind == "va" else k.vn)[j]
                        dv = dstV[:, :, s0:s0 + nb, :].rearrange("h p s e -> p h s e")
                        P.op("act", lambda e, vi=vi, dv=dv: e.dma_start(out=dv, in_=vst[vi][:, :, 0:nb, :]), r=[r_vst[vi]], dma=r_vst[vi])

        tiles = []
        for j in range(2):
            S = JOBS[j][1]
            s = 0
            while s < NEAR:
                nb = min(4, NEAR - s)
                tiles.append((j, s, nb, True))
                s += nb
            while s < S:
                nb = min(4, S - s)
                tiles.append((j, s, nb, False))
                s += nb
        tiles = [t for t in tiles if not t[3]] + [t for t in tiles if t[3]]
        if DEBUG.get("max_tiles"):
            tiles = tiles[:DEBUG["max_tiles"]]

        def prep_tile(ti):
            j, s0, nb, near = tiles[ti]
            xi_list = [load_x(j, s0 + b) for b in range(nb)]
            for b in range(nb):
                norm_block(xi_list[b], ti % 2, b)

        def prep_tile_interleaved(ti):
            j, s0, nb, near = tiles[ti]
            pend = []
            for b in range(nb):
                pend.append(load_x(j, s0 + b))
                if len(pend) == 2:
                    norm_block(pend.pop(0), ti % 2, b - 1)
            bb = nb - len(pend)
            for xi in pend:
                norm_block(xi, ti % 2, bb)
                bb += 1

        prep_tile_interleaved(0)
        for ti in range(len(tiles)):
            j, s0, nb, near = tiles[ti]
            pf = (lambda ti=ti: prep_tile_interleaved(ti + 1)) if ti + 1 < len(tiles) else None
            do_tile(j, s0, nb, near, ti % 2, pf)

        P.emit_phase()


def phase_b(k):
    nc, P = k.nc, k.P
    moff, mcols = mask_layout()
    with ExitStack() as es:
        def sb(name, shape, dt=F32):
            return es.enter_context(nc.sbuf_tensor("B_" + name, list(shape), dt))

        def ps(name, shape, dt=F32):
            return es.enter_context(nc.psum_tensor("B_" + name, list(shape), dt))

        R = P.res
        SMAX = JOBS[0][1]
        ident = sb("ident", (128, 128), BF16)
        tablr = sb("tablr", (128, 2, 8))
        lamv = sb("lamv", (128, 4, 64))
        lprod = sb("lprod", (128, 2, 64))
        lsum = sb("lsum", (128, 2))
        lexp = sb("lexp", (128, 2))
        nlam = sb("nlam", (128, 1))
        sg = sb("sg", (128, 128))
        epsc = sb("epsc", (128, 1))
        tabp = sb("tabp", (128, 128))
        ohup = sb("ohup", (128, 1536))
        u2s = sb("u2s", (8, 1536))
        KT = [sb("KT%d" % i, (128, SMAX * 128), BF16) for i in range(2)]
        VH = [sb("VH%d" % i, (128, SMAX, 129), BF16) for i in range(2)]
        QT = [sb("QT%d" % i, (128, 18 * 128), BF16) for i in range(2)]
        QTm = [[sb("QTm%d_%d" % (m, i), (128, 18 * 128), BF16) for i in range(2)] for m in range(2)]
        HH = [sb("HH%d" % i, (128, 1408)) for i in range(2)]
        PT = [sb("PT%d" % i, (128, 2, 512), BF16) for i in range(3)]
        PTm = sb("PTm", (128, SMAX * 4), BF16)
        Gmini = sb("Gmini", (128, 18, 2))
        osb = sb("osb", (128, 8, 129))
        rz = sb("rz", (128, 8))
        tt = sb("tt", (128, 8, 128))
        od = sb("od", (128, 4, 128))
        sqj = sb("sqj", (128, 128))
        ssq = sb("ssq", (128, 4))
        lnv = sb("lnv", (128, 4))
        rstd = sb("rstd", (128, 4))
        tmp2 = sb("tmp2", (128, 4, 128))
        onb = [sb("onb%d" % i, (128, 4, 128), BF16) for i in range(2)]
        AOh = [sb("AOh%d" % i, (128, 2050), BF16) for i in range(2)]
        Trt = [sb("Trt%d" % i, (128, 7, 2, 64)) for i in range(2)]
        TfixB = [sb("TfixB%d" % i, (128, 7, 128)) for i in range(2)]
        Tfix = [sb("Tfix%d" % i, (128, 7, 128)) for i in range(2)]
        maskt = sb("maskt", (128, mcols), BF16)
        PTn = [sb("PTn%d" % i, (128, 7, 128), BF16) for i in range(3)]
        rzn = [sb("rzn%d" % i, (128, 1)) for i in range(3)]
        onbn = [sb("onbn%d" % i, (128, 128), BF16) for i in range(3)]

        psS = [ps("psS%d" % i, (128, 2, 512)) for i in range(2)]
        acc = ps("acc", (128, 3, 512))
        pTr = ps("pTr", (128, 1024), BF16)

        r_ident, r_tablr, r_lamv, r_lprod, r_lsum, r_lexp, r_nlam = (R(n) for n in ("ident", "tablr", "lamv", "lprod", "lsum", "lexp", "nlam"))
        r_sg, r_eps, r_tabp, r_ohup, r_u2s, r_u2d = (R(n) for n in ("sg", "eps", "tabp", "ohup", "u2s", "u2d"))
        r_KT = [R("KT0"), R("KT1")]
        r_VH = [R("VH0"), R("VH1")]
        r_QT = [R("QT0"), R("QT1")]
        r_QTm = [[R("QTm%d_%d" % (m, i)) for i in range(2)] for m in range(2)]
        r_HH = [R("HH0"), R("HH1")]
        r_PT = [R("PT%d" % i) for i in range(3)]
        r_PTm, r_Gm, r_osb, r_rz, r_tt, r_od, r_sqj, r_ssq, r_lnv, r_rstd, r_tmp2 = (
            R(n) for n in ("PTm", "Gm", "osb", "rz", "tt", "od", "sqj", "ssq", "lnv", "rstd", "tmp2"))
        r_onb = [R("onb0"), R("onb1")]
        r_AOh = [R("AOh0"), R("AOh1")]
        r_Trt = [[R("Trt%d_%d" % (i, q)) for q in range(4)] for i in range(2)]
        r_TfixB = [R("TfixB0"), R("TfixB1")]
        r_Tfix = [R("Tfix0"), R("Tfix1")]
        r_mask = R("mask")
        r_PTn = [R("PTn0"), R("PTn1"), R("PTn2")]
        r_rzn = [R("rzn%d" % i) for i in range(3)]
        r_onbn = [R("onbn%d" % i) for i in range(3)]
        r_psS = [R("psS0"), R("psS1")]
        r_acc = [R("acc%d" % i) for i in range(3)]
        r_pTr = [R("pTr%d" % i) for i in range(4)]
        cnt = {"S": 0, "PT": 0, "onb": 0, "pTr": 0, "hb": 0, "PTn": 0, "accn": 0, "tb": 0}

        P.op("sp", lambda e: e.dma_start(out=ident[:], in_=k.ident.ap()), w=[r_ident], dma=r_ident)
        P.op("sp", lambda e: e.dma_start(out=tablr[:], in_=k.tablr.ap()), w=[r_tablr], dma=r_tablr)
        P.op("sp", lambda e: e.dma_start(out=lamv[:], in_=k.lamv.ap()), w=[r_lamv], dma=r_lamv)
        P.op("sp", lambda e: e.dma_start(out=sg[:], in_=k.sgb.ap()), w=[r_sg], dma=r_sg)
        P.op("dve", lambda e: e.memset(epsc[:], EPS), w=[r_eps])
        for i in range(2):
            P.op("dve", lambda e, i=i: e.memset(QTm[0][i][64:128, :], 0.0), w=[r_QTm[0][i]])
            P.op("dve", lambda e, i=i: e.memset(QTm[1][i][0:64, :], 0.0), w=[r_QTm[1][i]])
        P.op("dve", lambda e: e.memset(tabp[:], 0.0), w=[r_tabp])
        P.op("dve", lambda e: e.memset(ohup[:], 0.0), w=[r_ohup])
        P.op("sp", lambda e: e.dma_start(out=tabp[0:32, 0:8], in_=k.tab.ap()), w=[r_tabp], dma=r_tabp)
        P.op("sp", lambda e: e.dma_start(out=ohup[0:32, :], in_=k.ohu.ap()), w=[r_ohup], dma=r_ohup)
        pl = psz(lamv)
        P.op("dve", lambda e: e.tensor_tensor(out=lprod[:], in0=mkap(lamv, 0, [(pl, 128), (128, 2), (1, 64)]),
                                              in1=mkap(lamv, 64, [(pl, 128), (128, 2), (1, 64)]), op=ALU.mult),
             r=[r_lamv], w=[r_lprod])
        P.op("dve", lambda e: e.tensor_reduce(out=lsum[:], in_=lprod[:], axis=AX.X, op=ALU.add), r=[r_lprod], w=[r_lsum])
        P.op("act", lambda e: e.activation(out=lexp[:], in_=lsum[:], func=AF.Exp), r=[r_lsum], w=[r_lexp])
        P.op("dve", lambda e: e.tensor_tensor(out=nlam[:], in0=lexp[:, 1:2], in1=lexp[:, 0:1], op=ALU.subtract), r=[r_lexp], w=[r_nlam])
        P.op("dve", lambda e: e.tensor_scalar(out=nlam[:], in0=nlam[:], scalar1=-LAM_INIT, scalar2=None, op0=ALU.add), r=[r_nlam], w=[r_nlam])
        P.op("dve", lambda e: e.tensor_scalar(out=sg[:], in0=sg[:], scalar1=1.0 - LAM_INIT, scalar2=None, op0=ALU.mult), r=[r_sg], w=[r_sg])
        P.op("dve", lambda e: e.tensor_scalar(out=tabp[:], in0=tabp[:], scalar1=1.0 / SCALE_A, scalar2=None, op0=ALU.mult), r=[r_tabp], w=[r_tabp])
        for q in range(3):
            P.op("pe", lambda e, q=q: e.matmul(psS[q % 2][:, q // 2, :], lhsT=tabp[:], rhs=ohup[:, q * 512:(q + 1) * 512], start=True, stop=True),
                 r=[r_tabp, r_ohup], w=[r_psS[q % 2]])
            P.op("dve", lambda e, q=q: e.tensor_copy(out=u2s[:, q * 512:(q + 1) * 512], in_=psS[q % 2][0:8, q // 2, :]),
                 r=[r_psS[q % 2]], w=[r_u2s])
        P.op("sp", lambda e: e.dma_start(out=k.u2.ap(), in_=u2s[:]), r=[r_u2s], w=[r_u2d], dma=r_u2s)

        def acc_ap(a, rows=128, cols=129):
            return acc[0:rows, a // 3, (a % 3) * 129:(a % 3) * 129 + cols]

        def load_head(j, h, nbr):
            S = JOBS[j][1]
            hb = cnt["hb"] % 2
            cnt["hb"] += 1
            if not nbr:
                P.op("sp", lambda e: e.dma_start(out=KT[hb][:, 0:S * 128], in_=k.kaT[j][h]), w=[r_KT[hb]], dma=r_KT[hb])
                P.op("sp", lambda e: e.dma_start(out=VH[hb][:, 0:S, :], in_=k.va[j][h]), w=[r_VH[hb]], dma=r_VH[hb])
                P.op("sp", lambda e: e.dma_start(out=QTm[0][hb][0:64, :], in_=k.qaT[j][h][0:64, 2 * 128:20 * 128]), w=[r_QTm[0][hb]], dma=r_QTm[0][hb])
                P.op("sp", lambda e: e.dma_start(out=QTm[1][hb][64:128, :], in_=k.qaT[j][h][64:128, 2 * 128:20 * 128]), w=[r_QTm[1][hb]], dma=r_QTm[1][hb])
                P.op("sp", lambda e: e.dma_start(out=HH[hb][:], in_=mkap(k.u2, h * 1536, [(1, 128), (1, 1408)])),
                     r=[r_u2d], w=[r_HH[hb]], dma=r_HH[hb])
            else:
                P.op("sp", lambda e: e.dma_start(out=KT[hb][:, 0:NEAR * 128], in_=k.knT[j][h]), w=[r_KT[hb]], dma=r_KT[hb])
                P.op("sp", lambda e: e.dma_start(out=VH[hb][:, 0:NEAR, :], in_=k.vn[j][h]), w=[r_VH[hb]], dma=r_VH[hb])
                P.op("sp", lambda e: e.dma_start(out=QT[hb][:], in_=k.qnT[j][h][:, 2 * 128:20 * 128]), w=[r_QT[hb]], dma=r_QT[hb])
                tb = cnt["tb"] % 2
                cnt["tb"] += 1
                P.op("sp", lambda e: e.dma_start(out=Tfix[tb][:], in_=k.tfd[h]), w=[r_Tfix[tb]], dma=r_Tfix[tb], after=[tf_store[h]])
                return hb, tb
            return hb, None

        def finish_diff(rows, nch, ao, ecols):
            na = 2 * nch
            nbanks = (na + 2) // 3
            for b in range(nbanks):
                n_in = min(3, na - 3 * b)
                P.op("dve", lambda e, b=b, n_in=n_in: e.tensor_copy(
                    out=osb[0:rows, 3 * b:3 * b + n_in, :], in_=acc[0:rows, b, 0:n_in * 129].rearrange("p (a c) -> p a c", c=129)),
                    r=[r_acc[b]], w=[r_osb])
            po = psz(osb)
            P.op("dve", lambda e: e.reciprocal(out=rz[0:rows, 0:na], in_=mkap(osb, 128, [(po, rows), (129, na)])), r=[r_osb], w=[r_rz])
            P.op("dve", lambda e: e.tensor_tensor(out=tt[0:rows, 0:na, :], in0=osb[0:rows, 0:na, 0:128],
                                                  in1=mkap(rz, 0, [(psz(rz), rows), (1, na), (0, 128)]), op=ALU.mult),
                 r=[r_osb, r_rz], w=[r_tt])
            ptt = psz(tt)
            P.op("dve", lambda e: e.scalar_tensor_tensor(out=od[0:rows, 0:nch, :], in0=mkap(tt, 128, [(ptt, rows), (256, nch), (1, 128)]),
                                                         scalar=nlam[0:rows, :], in1=mkap(tt, 0, [(ptt, rows), (256, nch), (1, 128)]),
                                                         op0=ALU.mult, op1=ALU.add),
                 r=[r_tt, r_nlam], w=[r_od])
            P.op("dve", lambda e: e.tensor_tensor(out=tmp2[0:rows, 0:nch, :], in0=od[0:rows, 0:nch, :], in1=od[0:rows, 0:nch, :], op=ALU.mult),
                 r=[r_od], w=[r_tmp2])
            P.op("dve", lambda e: e.tensor_reduce(out=ssq[0:rows, 0:nch], in_=tmp2[0:rows, 0:nch, :], axis=AX.X, op=ALU.add),
                 r=[r_tmp2], w=[r_ssq])
            P.op("act", lambda e: e.activation(out=lnv[0:rows, 0:nch], in_=ssq[0:rows, 0:nch], func=AF.Ln, scale=1.0 / 128, bias=epsc[0:rows, :]),
                 r=[r_ssq, r_eps], w=[r_lnv])
            P.op("act", lambda e: e.activation(out=rstd[0:rows, 0:nch], in_=lnv[0:rows, 0:nch], func=AF.Exp, scale=-0.5), r=[r_lnv], w=[r_rstd])
            P.op("dve", lambda e: e.tensor_tensor(out=tmp2[0:rows, 0:nch, :], in0=od[0:rows, 0:nch, :],
                                                  in1=mkap(rstd, 0, [(psz(rstd), rows), (1, nch), (0, 128)]), op=ALU.mult),
                 r=[r_od, r_rstd], w=[r_tmp2])
            oi = cnt["onb"] % 2
            cnt["onb"] += 1
            P.op("dve", lambda e: e.tensor_tensor(out=onb[oi][0:rows, 0:nch, :], in0=tmp2[0:rows, 0:nch, :],
                                                  in1=mkap(sg, 0, [(psz(sg), rows), (0, nch), (1, 128)]), op=ALU.mult),
                 r=[r_tmp2, r_sg], w=[r_onb[oi]])
            def part2():
                for c in range(nch):
                    P.op("pe", lambda e, c=c: e.transpose(out=pTr[:, c * 128:c * 128 + rows], in_=onb[oi][0:rows, c, :], identity=ident[0:rows, 0:rows]),
                         r=[r_onb[oi], r_ident], w=[r_pTr[0]])
                if rows == 128:
                    dst = AOh[ao][:, ecols[0]:ecols[0] + 128 * nch]
                    src = pTr[:, 0:128 * nch]
                else:
                    dst = mkap(AOh[ao], 0, [(psz(AOh[ao]), 128), (2049, 2)])
                    src = pTr[:, 0:2]
                P.op("dve", lambda e, dst=dst, src=src: e.tensor_copy(out=dst, in_=src), r=[r_pTr[0]], w=[r_AOh[ao]])
            return part2

        def diff_head(j, h, hb, ao):
            S = JOBS[j][1]
            pq = psz(QT[hb])
            ph = psz(HH[hb])
            G = Gmini
            pg = psz(G)
            P.op("dve", lambda e: e.tensor_copy(out=G[:, 0:2, 0:1], in_=mkap(HH[hb], 640, [(ph, 128), (128, 2), (1, 1)])), r=[r_HH[hb]], w=[r_Gm])
            P.op("dve", lambda e: e.tensor_copy(out=G[:, 2:18, 0:1], in_=mkap(HH[hb], 896, [(ph, 128), (0, 16), (1, 1)])), r=[r_HH[hb]], w=[r_Gm])
            P.op("dve", lambda e: e.tensor_copy(out=G[:, 0:16, 1:2], in_=mkap(HH[hb], 511, [(ph, 128), (0, 16), (1, 1)])), r=[r_HH[hb]], w=[r_Gm])
            P.op("dve", lambda e: e.tensor_copy(out=G[:, 16:18, 1:2], in_=mkap(HH[hb], 639, [(ph, 128), (128, 2), (1, 1)])), r=[r_HH[hb]], w=[r_Gm])
            def issue_S(g, s):
                    qc0 = (4 * g + 1) * 128
                    ri = cnt["S"] % 2
                    cnt["S"] += 1
                    for m in range(2):
                        P.op("pe", lambda e, ri=ri, m=m, s=s: e.matmul(
                            psS[ri][:, m, :], lhsT=KT[hb][:, s * 128:(s + 1) * 128],
                            rhs=QTm[m][hb][:, qc0:qc0 + 512], start=True, stop=True),
                            r=[r_KT[hb], r_QTm[m][hb]], w=[r_psS[ri]])
                    bias = None
                    if 2 <= s <= 19:
                        d0 = (s - 3) - 4 * g
                        sw = 5 - d0
                        if sw <= 0:
                            bias = tablr[:, 1, h:h + 1]
                        elif sw >= 7:
                            bias = tablr[:, 0, h:h + 1]
                        else:
                            win = mkap(HH[hb], 1407 - 128 * sw, [(ph, 128), (0, 2), (-1, 512)])
                            P.op("dve", lambda e, ri=ri, win=win: e.tensor_tensor(out=psS[ri][:], in0=psS[ri][:], in1=win, op=ALU.add),
                                 r=[r_psS[ri], r_HH[hb]], w=[r_psS[ri]])
                    pi = cnt["PT"] % 3
                    cnt["PT"] += 1
                    if bias is None:
                        P.op("act", lambda e, ri=ri, pi=pi: e.activation(out=PT[pi][:], in_=psS[ri][:], func=AF.Exp, scale=SCALE_A),
                             r=[r_psS[ri]], w=[r_PT[pi]])
                    else:
                        P.op("act", lambda e, ri=ri, pi=pi, bias=bias: e.activation(out=PT[pi][:], in_=psS[ri][:], func=AF.Exp, scale=SCALE_A, bias=bias),
                             r=[r_psS[ri], r_tablr], w=[r_PT[pi]])
                    return pi

            def issue_PV(s, pi):
                    for c in range(4):
                        for m in range(2):
                            a = 2 * c + m
                            P.op("pe", lambda e, pi=pi, c=c, m=m, a=a, s=s: e.matmul(
                                acc_ap(a), lhsT=PT[pi][:, m, c * 128:(c + 1) * 128], rhs=VH[hb][:, s, :],
                                start=(s == 0 and a % 3 == 0), stop=(s == S - 1), skip_group_check=True),
                                r=[r_PT[pi], r_VH[hb]], w=[r_acc[a // 3]])

            parts = DEBUG.get("b_parts", ("main", "finish", "mini"))
            steps = [(g, s) for g in range(4 if "main" in parts else 0) for s in range(S)]
            pending = []
            prev = None

            def retire(prev):
                g_, s_, pi_ = prev
                issue_PV(s_, pi_)
                if s_ == S - 1 and "finish" in parts:
                    pending.append([3, finish_diff(128, 4, ao, [1 + 512 * g_ + 128 * c for c in range(4)])])

            inflight = []
            for (g, s) in steps:
                pi = issue_S(g, s)
                inflight.append((g, s, pi))
                if len(inflight) >= 3:
                    retire(inflight.pop(0))
                for pd in pending:
                    pd[0] -= 1
                while pending and pending[0][0] <= 0:
                    pending.pop(0)[1]()
            while inflight:
                retire(inflight.pop(0))
            if "mini" not in parts:
                while pending:
                    pending.pop(0)[1]()
                P.op("act", lambda e: e.dma_start(out=k.aoT[j][h], in_=AOh[ao][:]), r=[r_AOh[ao]], dma=r_AOh[ao])
                return
            ri = cnt["S"] % 2
            cnt["S"] += 1
            for s in range(S):
                for m in range(2):
                    P.op("pe", lambda e, ri=ri, m=m, s=s: e.matmul(
                        psS[ri][:, 0, s * 4 + 2 * m:s * 4 + 2 * m + 2], lhsT=KT[hb][:, s * 128:(s + 1) * 128],
                        rhs=mkap(QTm[m][hb], 127, [(pq, 128), (2049, 2)]), start=True, stop=True),
                        r=[r_KT[hb], r_QTm[m][hb]], w=[r_psS[ri]])
            pps = psz(psS[ri])
            reg = mkap(psS[ri], 8, [(pps, 128), (4, 18), (2, 2), (1, 2)])
            P.op("dve", lambda e, reg=reg: e.tensor_tensor(out=reg, in0=reg, in1=mkap(G, 0, [(pg, 128), (2, 18), (0, 2), (1, 2)]), op=ALU.add),
                 r=[r_psS[ri], r_Gm], w=[r_psS[ri]])
            P.op("act", lambda e, ri=ri: e.activation(out=PTm[:, 0:4 * S], in_=psS[ri][:, 0, 0:4 * S], func=AF.Exp, scale=SCALE_A),
                 r=[r_psS[ri]], w=[r_PTm])
            while pending:
                pending.pop(0)[1]()
            for s in range(S):
                for m in range(2):
                    P.op("pe", lambda e, m=m, s=s: e.matmul(
                        acc_ap(m, rows=2), lhsT=PTm[:, s * 4 + 2 * m:s * 4 + 2 * m + 2], rhs=VH[hb][:, s, :],
                        start=(s == 0 and m == 0), stop=(s == S - 1), skip_group_check=True),
                        r=[r_PTm, r_VH[hb]], w=[r_acc[0]])
            finish_diff(2, 1, ao, None)()
            P.op("act", lambda e: e.dma_start(out=k.aoT[j][h], in_=AOh[ao][:]), r=[r_AOh[ao]], dma=r_AOh[ao])

        def nbr_head(j, h, hb, tb, ao):
            pq = psz(QT[hb])
            pm = psz(maskt)
            def stageA(i, dl):
                nd = len(dl)
                if i == -1:
                    nq, qoff, key = 1, 127, -1
                elif i == 16:
                    nq, qoff, key = 1, 17 * 128, 16
                else:
                    nq, qoff = 128, (i + 1) * 128
                    key = i if i in (0, 1, 14, 15) else "int"
                ri = cnt["S"] % 2
                cnt["S"] += 1
                flat = psS[ri][:].rearrange("p a b -> p (a b)")
                for di, d_ in enumerate(dl):
                    nk = i + d_ + 3
                    P.op("pe", lambda e, di=di, nk=nk: e.matmul(flat[:, di * 128:di * 128 + nq], lhsT=KT[hb][:, nk * 128:(nk + 1) * 128],
                                                                rhs=QT[hb][:, qoff:qoff + nq], start=True, stop=False),
                         r=[r_KT[hb], r_QT[hb]], w=[r_psS[ri]])
                    mo = moff[key] + di * nq
                    P.op("pe", lambda e, di=di, mo=mo: e.matmul(flat[:, di * 128:di * 128 + nq], lhsT=ident[:], rhs=maskt[:, mo:mo + nq],
                                                                start=False, stop=True),
                         r=[r_ident, r_mask], w=[r_psS[ri]])
                pps = psz(psS[ri])
                reg = mkap(psS[ri], 0, [(pps, 128), (128, nd), (1, nq)])
                tfx = Tfix[tb][:, dl[0] + 3:dl[0] + 3 + nd, qoff % 128:qoff % 128 + nq]
                P.op("dve", lambda e, reg=reg, tfx=tfx: e.tensor_tensor(out=reg, in0=reg, in1=tfx, op=ALU.add),
                     r=[r_psS[ri], r_Tfix[tb]], w=[r_psS[ri]])
                pn = cnt["PTn"] % 3
                cnt["PTn"] += 1
                P.op("act", lambda e, reg=reg, pn=pn: e.activation(out=PTn[pn][:, 0:nd, 0:nq], in_=reg, func=AF.Exp, scale=SCALE_N),
                     r=[r_psS[ri]], w=[r_PTn[pn]])
                return (i, dl, nq, pn)

            def stageB(st):
                i, dl, nq, pn = st
                nd = len(dl)
                ai = cnt["accn"] % 3
                cnt["accn"] += 1
                for di, d_ in enumerate(dl):
                    nk = i + d_ + 3
                    P.op("pe", lambda e, di=di, nk=nk, ai=ai, pn=pn: e.matmul(acc[0:nq, ai, 0:129], lhsT=PTn[pn][:, di, 0:nq], rhs=VH[hb][:, nk, :],
                                                                        start=(di == 0), stop=(di == nd - 1)),
                         r=[r_PTn[pn], r_VH[hb]], w=[r_acc[ai]])
                P.op("dve", lambda e, ai=ai: e.reciprocal(out=rzn[ai][0:nq, :], in_=acc[0:nq, ai, 128:129]), r=[r_acc[ai]], w=[r_rzn[ai]])
                P.op("dve", lambda e, ai=ai: e.tensor_scalar(out=onbn[ai][0:nq, :], in0=acc[0:nq, ai, 0:128], scalar1=rzn[ai][0:nq, :], scalar2=None, op0=ALU.mult),
                     r=[r_acc[ai], r_rzn[ai]], w=[r_onbn[ai]])
                return (i, nq, ai)

            def stageC(st):
                i, nq, ai = st
                P.op("pe", lambda e, ai=ai: e.transpose(out=pTr[:, 0:nq], in_=onbn[ai][0:nq, :], identity=ident[0:nq, 0:nq]),
                     r=[r_onbn[ai], r_ident], w=[r_pTr[0]])
                if nq == 128:
                    dst = AOh[ao][:, 1 + 128 * i:1 + 128 * i + 128]
                else:
                    ecol = 0 if i == -1 else 2049
                    dst = AOh[ao][:, ecol:ecol + 1]
                P.op("dve", lambda e, dst=dst: e.tensor_copy(out=dst, in_=pTr[:, 0:nq]), r=[r_pTr[0]], w=[r_AOh[ao]])

            units = nbr_units()
            nu = len(units)
            sa, sbq = [], []
            for u in range(nu + 4):
                if u < nu:
                    sa.append(stageA(*units[u]))
                if 2 <= u < nu + 2:
                    sbq.append(stageB(sa[u - 2]))
                if 4 <= u:
                    stageC(sbq[u - 4])

            P.op("act", lambda e: e.dma_start(out=k.aoT[j][8 + h], in_=AOh[ao][:]), r=[r_AOh[ao]], dma=r_AOh[ao])

        tf_store = {}
        for h in range(HN):
            tb_ = h % 2
            for rk in range(2):
                for rq in range(2):
                    for dl in range(-3, 4):
                        dr = 2 * dl + rk - rq + 7
                        P.op("pool", lambda e, tb_=tb_, dl=dl, rk=rk, rq=rq, dr=dr, h=h: e.dma_start(
                            out=Trt[tb_][rk * 64:(rk + 1) * 64, dl + 3, rq, :],
                            in_=mkap(k.rpbp, (h * 15 + dr) * 128, [(1, 64), (1, 64)])), w=[r_Trt[tb_][2 * rk + rq]], dma=r_Trt[tb_][2 * rk + rq])
            pt_ = psz(Trt[tb_])
            for rq in range(2):
                P.op("pool", lambda e, tb_=tb_, rq=rq, pt_=pt_: e.tensor_scalar(
                    out=TfixB[tb_][:, :, rq * 64:(rq + 1) * 64],
                    in0=mkap(Trt[tb_], rq * 64 + 63, [(pt_, 128), (128, 7), (-1, 64)]),
                    scalar1=1.0 / SCALE_N, scalar2=1.0, op0=ALU.mult, op1=ALU.mult),
                    r=[r_Trt[tb_][rq], r_Trt[tb_][2 + rq]], w=[r_TfixB[tb_]])
            tf_store[h] = P.op("pool", lambda e, tb_=tb_, h=h: e.dma_start(out=k.tfd[h], in_=TfixB[tb_][:]), r=[r_TfixB[tb_]], dma=r_TfixB[tb_])

        g12c = sb("g12c", (128, 2, CK))
        cin = [sb("cin%d" % i, (128, 1024)) for i in range(2)]
        cout = [sb("cout%d" % i, (128, 1024), BF16) for i in range(2)]
        r_g = R("g12c")
        r_cin = [R("cin0"), R("cin1")]
        r_cout = [R("cout0"), R("cout1")]
        P.op("sp", lambda e: e.dma_start(out=g12c[:], in_=k.g12c.ap()), w=[r_g], dma=r_g)
        conv_state = {"n": 0}

        def convert(src, dst, nrows, ncols, tw, gidx):
            for rb in range(nrows // 128):
                for c0 in range(0, ncols, tw):
                    i = conv_state["n"] % 2
                    conv_state["n"] += 1
                    sv = src[rb * 128:(rb + 1) * 128, c0:c0 + tw]
                    dv = dst[rb * 128:(rb + 1) * 128, c0:c0 + tw]
                    P.op("pool", lambda e, i=i, sv=sv, tw=tw: e.dma_start(out=cin[i][:, 0:tw], in_=sv), w=[r_cin[i]], dma=r_cin[i])
                    if gidx is None:
                        P.op("pool", lambda e, i=i, tw=tw: e.tensor_copy(out=cout[i][:, 0:tw], in_=cin[i][:, 0:tw]),
                             r=[r_cin[i]], w=[r_cout[i]])
                    else:
                        P.op("pool", lambda e, i=i, tw=tw, rb=rb, gidx=gidx: e.tensor_scalar(
                            out=cout[i][:, 0:tw], in0=cin[i][:, 0:tw], scalar1=g12c[:, gidx, rb:rb + 1], scalar2=1.0, op0=ALU.mult, op1=ALU.mult),
                            r=[r_cin[i], r_g], w=[r_cout[i]])
                    P.op("pool", lambda e, i=i, dv=dv, tw=tw: e.dma_start(out=dv, in_=cout[i][:, 0:tw]), r=[r_cout[i]], dma=r_cout[i])

        if not DEBUG.get("skip_a") and not DEBUG.get("no_conv_b"):
            convert(k.w_out, k.wb_out, D, D, 1024, None)
            convert(k.w_up, k.wb_up, D, 2 * DFF, 688, 1)
            convert(k.w_down, k.wb_down, DFF, D, 1024, None)

        work = []
        for j in DEBUG.get("jobs", (0, 1)):
            for h in DEBUG.get("heads_a", range(HA)):
                work.append((j, h, False))
            for h in DEBUG.get("heads_n", range(HN)):
                work.append((j, h, True))
        cur_job = None
        if not work:
            P.emit_phase()
            return
        pre = load_head(*work[0])
        for wi, (j, h, nbr) in enumerate(work):
            hb, tb = pre
            if nbr and cur_job != j:
                cur_job = j
                P.op("sp", lambda e, j=j: e.dma_start(out=maskt[:], in_=k.maskd[j].ap()), w=[r_mask], dma=r_mask)
            if wi + 1 < len(work):
                pre = load_head(*work[wi + 1])
            ao = wi % 2
            if nbr:
                nbr_head(j, h, hb, tb, ao)
            else:
                diff_head(j, h, hb, ao)
        P.emit_phase()


def phase_c(k):
    nc, P = k.nc, k.P
    with ExitStack() as es:
        def sb(name, shape, dt=F32):
            return es.enter_context(nc.sbuf_tensor("C_" + name, list(shape), dt))

        R = P.res
        NW = 6
        ident = sb("ident", (128, 128), BF16)
        epsc = sb("epsc", (128, 1))
        gfb = sb("gfb", (128, D))
        convc = sb("convc", (128, FCH, 4))
        hflag = sb("hflag", (128, 4))
        cwe = sb("cwe", (128, 4, FCH))
        axT = sb("axT", (128, CK, 514), BF16)
        xmid = sb("xmid", (128, 4, D))
        xmh = sb("xmh", (2, D))
        hT = sb("hT", (128, FCH, 512), BF16)
        wr = [sb("wr%d" % i, (128, 4096), BF16) for i in range(NW)]
        xn2s = [sb("xn2_%d" % i, (128, D), BF16) for i in range(2)]
        junk = sb("junk", (128, D), BF16)
        tb = [sb("tb%d" % i, (128, 2, 256)) for i in range(2)]
        ssqs = [sb("ssq%d" % i, (128, 1)) for i in range(2)]
        lnvs = [sb("lnv%d" % i, (128, 1)) for i in range(2)]
        rstds = [sb("rstd%d" % i, (128, 1)) for i in range(2)]
        pb = es.enter_context(nc.psum_tensor("C_pb", [128, 8, 512], F32))

        r_ident, r_eps, r_gfb, r_convc, r_hflag, r_cwe = (R(n) for n in ("ident", "eps", "gfb", "convc", "hflag", "cwe"))
        r_axT = R("axT")
        r_xmid = [R("xmid%d" % i) for i in range(4)]
        r_xmh = R("xmh")
        r_hT = R("hT")
        r_wr = [R("wr%d" % i) for i in range(NW)]
        r_junk = R("junk")
        r_xn2s = [R("xn2_0"), R("xn2_1")]
        r_ssqs = [R("ssq0"), R("ssq1")]
        r_lnvs = [R("lnv0"), R("lnv1")]
        r_rstds = [R("rstd0"), R("rstd1")]
        r_tb = [R("tb0"), R("tb1")]
        r_pb = [R("pb%d" % i) for i in range(8)]
        cnt = {"w": 0, "pb": 0, "tb": 0, "u": 0, "rms": 0}

        P.op("sp", lambda e: e.dma_start(out=ident[:], in_=k.ident.ap()), w=[r_ident], dma=r_ident)
        P.op("sp", lambda e: e.dma_start(out=gfb[:], in_=k.gfb.ap()), w=[r_gfb], dma=r_gfb)
        P.op("sp", lambda e: e.dma_start(out=convc[:], in_=k.convc.ap()), w=[r_convc], dma=r_convc)
        P.op("sp", lambda e: e.dma_start(out=hflag[:], in_=k.hflag.ap()), w=[r_hflag], dma=r_hflag)
        P.op("dve", lambda e: e.memset(epsc[:], EPS), w=[r_eps])
        pc = psz(convc)
        for q in range(4):
            ci = 0 if q % 2 == 0 else 2
            P.op("dve", lambda e, q=q, ci=ci: e.tensor_scalar(out=cwe[:, q, :], in0=mkap(convc, ci, [(pc, 128), (4, FCH)]),
                                                             scalar1=hflag[:, q:q + 1], scalar2=None, op0=ALU.mult),
                 r=[r_convc, r_hflag], w=[r_cwe])

        def wslot():
            i = cnt["w"] % NW
            cnt["w"] += 1
            return i

        def load_w_cols(src_dram, c0, ncols):
            i = wslot()
            src = src_dram[:, c0:c0 + ncols].rearrange("(ck p) f -> p ck f", p=128)
            dst = wr[i][:, 0:CK * ncols].rearrange("p (ck f) -> p ck f", f=ncols)
            P.op("sp", lambda e: e.dma_start(out=dst, in_=src), w=[r_wr[i]], dma=r_wr[i])
            return i

        def load_w_rows(src_dram, r0, nr):
            i = wslot()
            src = src_dram[r0 * 128:(r0 + nr) * 128, :].rearrange("(f p) n -> p f n", p=128)
            dst = wr[i][:, 0:nr * D].rearrange("p (f n) -> p f n", n=D)
            P.op("sp", lambda e: e.dma_start(out=dst, in_=src), w=[r_wr[i]], dma=r_wr[i])
            return i

        def wview_cols(i, ncols):
            return wr[i][:, 0:CK * ncols].rearrange("p (ck f) -> p ck f", f=ncols)

        def wview_rows(i, nr):
            return wr[i][:, 0:nr * D].rearrange("p (f n) -> p f n", n=D)

        def bank():
            b = cnt["pb"] % 6
            cnt["pb"] += 1
            return b

        pax = psz(axT)

        def rms_rows(rows, src_ap, r_src, want_xn):
            q = cnt["rms"] % 2
            cnt["rms"] += 1
            ssq, lnv, rstd, xn2 = ssqs[q], lnvs[q], rstds[q], xn2s[q]
            P.op("act", lambda e: e.activation(out=junk[0:rows, :], in_=src_ap, func=AF.Square, accum_out=ssq[0:rows, :]),
                 r=[r_src], w=[r_junk, r_ssqs[q]])
            P.op("act", lambda e: e.activation(out=lnv[0:rows, :], in_=ssq[0:rows, :], func=AF.Ln, scale=1.0 / D, bias=epsc[0:rows, :]),
                 r=[r_ssqs[q], r_eps], w=[r_lnvs[q]])
            P.op("act", lambda e: e.activation(out=rstd[0:rows, :], in_=lnv[0:rows, :], func=AF.Exp, scale=-0.5), r=[r_lnvs[q]], w=[r_rstds[q]])
            if want_xn:
                P.op("dve", lambda e: e.tensor_scalar(out=xn2[0:rows, :], in0=src_ap, scalar1=rstd[0:rows, :], scalar2=None, op0=ALU.mult),
                     r=[r_src, r_rstds[q]], w=[r_xn2s[q]])
            return q

        def tile(j, T):
            e0 = 512 * T
            xs_flat = k.xs[j].ap().rearrange("s p d -> (s p) d")
            src = k.aoT[j][:, :, e0:e0 + 514].rearrange("c p n -> p c n")
            P.op("sp", lambda e: e.dma_start(out=axT[:], in_=src), w=[r_axT], dma=r_axT)
            for tc in range(4):
                r0 = 383 + e0 + 1 + 128 * tc
                P.op("sp", lambda e, tc=tc, r0=r0: e.dma_start(out=xmid[:, tc, :], in_=xs_flat[r0:r0 + 128, :]), w=[r_xmid[tc]], dma=r_xmid[tc])
            hsrc = mkap(k.xs[j], (383 + e0) * D, [(513 * D, 2), (1, D)])
            P.op("sp", lambda e: e.dma_start(out=xmh[:], in_=hsrc), w=[r_xmh], dma=r_xmh)
            NG = 256
            nxt = load_w_cols(k.wb_out, 0, NG)
            for cg in range(D // NG):
                wi = nxt
                if cg + 1 < D // NG:
                    nxt = load_w_cols(k.wb_out, (cg + 1) * NG, NG)
                wv = wview_cols(wi, NG)
                for tc in range(5):
                    b = bank()
                    if tc < 4:
                        rows = 128
                        def lhs(ck, tc=tc):
                            return axT[:, ck, 1 + 128 * tc:129 + 128 * tc]
                        dst = xmid[:, tc, cg * NG:(cg + 1) * NG]
                        rd = r_xmid[tc]
                    else:
                        rows = 2
                        def lhs(ck):
                            return mkap(axT, ck * 514, [(pax, 128), (513, 2)])
                        dst = xmh[0:2, cg * NG:(cg + 1) * NG]
                        rd = r_xmh
                    for ck in range(CK):
                        P.op("pe", lambda e, b=b, ck=ck, lhs=lhs, rows=rows, wv=wv: e.matmul(
                            pb[0:rows, b, 0:NG], lhsT=lhs(ck), rhs=wv[:, ck, :], start=(ck == 0), stop=(ck == CK - 1)),
                            r=[r_axT, r_wr[wi]], w=[r_pb[b]])
                    P.op("dve", lambda e, b=b, dst=dst, rows=rows: e.tensor_tensor(out=dst, in0=pb[0:rows, b, 0:NG], in1=dst, op=ALU.add),
                         r=[r_pb[b], rd], w=[rd])
            for tc in range(5):
                rows = 128 if tc < 4 else 2
                srcx = xmid[:, tc, :] if tc < 4 else xmh[0:2, :]
                rsrc = r_xmid[tc] if tc < 4 else r_xmh
                q_ = rms_rows(rows, srcx, rsrc, True)
                xn2, r_xn2 = xn2s[q_], r_xn2s[q_]
                if tc < 4:
                    for half in range(2):
                        pt = pb[:, 6 + half, :].bitcast(BF16)
                        for q in range(8):
                            ck = half * 8 + q
                            P.op("pe", lambda e, pt=pt, q=q, ck=ck, xn2=xn2: e.transpose(out=pt[:, q * 128:(q + 1) * 128], in_=xn2[:, ck * 128:(ck + 1) * 128], identity=ident[:]),
                                 r=[r_xn2, r_ident], w=[r_pb[6 + half]])
                        dst = axT[:, half * 8:(half + 1) * 8, 1 + 128 * tc:129 + 128 * tc]
                        srcp = pt[:, 0:1024].rearrange("p (a b) -> p a b", b=128)
                        P.op("dve", lambda e, dst=dst, srcp=srcp: e.tensor_copy(out=dst, in_=srcp), r=[r_pb[6 + half]], w=[r_axT])
                else:
                    pt = pb[:, 6, :].bitcast(BF16)
                    for ck in range(CK):
                        P.op("pe", lambda e, pt=pt, ck=ck, xn2=xn2: e.transpose(out=pt[:, 2 * ck:2 * ck + 2], in_=xn2[0:2, ck * 128:(ck + 1) * 128], identity=ident[0:2, 0:2]),
                             r=[r_xn2, r_ident], w=[r_pb[6]])
                    dst = mkap(axT, 0, [(pax, 128), (514, CK), (513, 2)])
                    srcp = pt[:, 0:2 * CK].rearrange("p (a b) -> p a b", b=2)
                    P.op("dve", lambda e, dst=dst, srcp=srcp: e.tensor_copy(out=dst, in_=srcp), r=[r_pb[6]], w=[r_axT])
            groups = [(f0, min(2, FCH - f0)) for f0 in range(0, FCH, 2)]

            def load_group(gi):
                f0, nf = groups[gi]
                return (load_w_cols(k.wb_up, f0 * 128, nf * 128), load_w_cols(k.wb_up, DFF + f0 * 128, nf * 128))

            pend = [load_group(0), load_group(1)]
            for gi, (f0, nf) in enumerate(groups):
                wa_i, wg_i = pend.pop(0)
                if gi + 2 < len(groups):
                    pend.append(load_group(gi + 2))
                wa = wview_cols(wa_i, nf * 128)
                wg = wview_cols(wg_i, nf * 128)
                for fl in range(nf):
                    fc = f0 + fl
                    u = cnt["u"] % 2
                    cnt["u"] += 1
                    ba = 3 * u
                    for ck in range(CK):
                        P.op("pe", lambda e, ba=ba, ck=ck, wa=wa, fl=fl: e.matmul(pb[:, ba, 0:258], lhsT=wa[:, ck, fl * 128:(fl + 1) * 128],
                                                                             rhs=axT[:, ck, 0:258], start=(ck == 0), stop=(ck == CK - 1)),
                             r=[r_axT, r_wr[wa_i]], w=[r_pb[ba]])
                    for ck in range(CK):
                        P.op("pe", lambda e, ba=ba, ck=ck, wa=wa, fl=fl: e.matmul(pb[:, ba + 1, 0:258], lhsT=wa[:, ck, fl * 128:(fl + 1) * 128],
                                                                             rhs=axT[:, ck, 256:514], start=(ck == 0), stop=(ck == CK - 1)),
                             r=[r_axT, r_wr[wa_i]], w=[r_pb[ba + 1]])
                    for ck in range(CK):
                        P.op("pe", lambda e, ba=ba, ck=ck, wg=wg, fl=fl: e.matmul(pb[:, ba + 2, :], lhsT=wg[:, ck, fl * 128:(fl + 1) * 128],
                                                                             rhs=axT[:, ck, 1:513], start=(ck == 0), stop=(ck == CK - 1)),
                             r=[r_axT, r_wr[wg_i]], w=[r_pb[ba + 2]])
                    ti = cnt["tb"] % 2
                    cnt["tb"] += 1
                    t_ = tb[ti]
                    rt = r_tb[ti]
                    ra = [r_pb[ba], r_pb[ba + 1]]
                    P.op("dve", lambda e, ba=ba, t_=t_, fc=fc: e.tensor_scalar(out=t_[:], in0=pb[:, ba:ba + 2, 0:256], scalar1=convc[:, fc, 0:1],
                                                                           scalar2=convc[:, fc, 3:4], op0=ALU.mult, op1=ALU.add),
                         r=ra + [r_convc], w=[rt])
                    if T == 0:
                        P.op("dve", lambda e, ba=ba, t_=t_, fc=fc: e.tensor_scalar(out=t_[:, 0, 0:1], in0=pb[:, ba, 0:1], scalar1=cwe[:, 2 * j, fc:fc + 1],
                                                                               scalar2=convc[:, fc, 3:4], op0=ALU.mult, op1=ALU.add),
                             r=ra + [r_convc, r_cwe], w=[rt])
                    P.op("dve", lambda e, ba=ba, t_=t_, fc=fc: e.scalar_tensor_tensor(out=t_[:], in0=pb[:, ba:ba + 2, 1:257], scalar=convc[:, fc, 1:2],
                                                                                  in1=t_[:], op0=ALU.mult, op1=ALU.add),
                         r=ra + [r_convc, rt], w=[rt])
                    if T == 3:
                        P.op("dve", lambda e, ba=ba, t_=t_, fc=fc: e.scalar_tensor_tensor(out=t_[:, 1, 255:256], in0=pb[:, ba + 1, 257:258], scalar=cwe[:, 2 * j + 1, fc:fc + 1],
                                                                                      in1=t_[:, 1, 255:256], op0=ALU.mult, op1=ALU.add),
                             r=ra + [r_cwe, rt], w=[rt])
                        P.op("dve", lambda e, ba=ba, t_=t_, fc=fc: e.scalar_tensor_tensor(out=t_[:, :, 0:255], in0=pb[:, ba:ba + 2, 2:257], scalar=convc[:, fc, 2:3],
                                                                                      in1=t_[:, :, 0:255], op0=ALU.mult, op1=ALU.add),
                             r=ra + [r_convc, rt], w=[rt])
                        P.op("dve", lambda e, ba=ba, t_=t_, fc=fc: e.scalar_tensor_tensor(out=t_[:, 0, 255:256], in0=pb[:, ba, 257:258], scalar=convc[:, fc, 2:3],
                                                                                      in1=t_[:, 0, 255:256], op0=ALU.mult, op1=ALU.add),
                             r=ra + [r_convc, rt], w=[rt])
                    else:
                        P.op("dve", lambda e, ba=ba, t_=t_, fc=fc: e.scalar_tensor_tensor(out=t_[:], in0=pb[:, ba:ba + 2, 2:258], scalar=convc[:, fc, 2:3],
                                                                                      in1=t_[:], op0=ALU.mult, op1=ALU.add),
                             r=ra + [r_convc, rt], w=[rt])
                    P.op("act", lambda e, t_=t_: e.activation(out=t_[:], in_=t_[:], func=AF.Gelu_apprx_tanh), r=[rt], w=[rt])
                    P.op("dve", lambda e, ba=ba, t_=t_, fc=fc: e.tensor_tensor(out=hT[:, fc, :], in0=t_[:].rearrange("p a b -> p (a b)"), in1=pb[:, ba + 2, :], op=ALU.mult),
                         r=[rt, r_pb[ba + 2]], w=[r_hT])
            dgroups = [(f0, min(2, FCH - f0)) for f0 in range(0, FCH, 2)]
            for hh in range(2):
                pend = [load_w_rows(k.wb_down, dgroups[0][0], dgroups[0][1]), load_w_rows(k.wb_down, dgroups[1][0], dgroups[1][1])]
                for gi, (f0, nf) in enumerate(dgroups):
                    wi = pend.pop(0)
                    if gi + 2 < len(dgroups):
                        pend.append(load_w_rows(k.wb_down, dgroups[gi + 2][0], dgroups[gi + 2][1]))
                    wv = wview_rows(wi, nf)
                    for fl in range(nf):
                        fc = f0 + fl
                        for tl in range(2):
                            tc = 2 * hh + tl
                            for cg in range(4):
                                b = tl * 4 + cg
                                P.op("pe", lambda e, b=b, fc=fc, tc=tc, fl=fl, cg=cg, wv=wv: e.matmul(
                                    pb[:, b, :], lhsT=hT[:, fc, tc * 128:(tc + 1) * 128], rhs=wv[:, fl, cg * 512:(cg + 1) * 512],
                                    start=(fc == 0), stop=(fc == FCH - 1)),
                                    r=[r_hT, r_wr[wi]], w=[r_pb[b]])
                for tl in range(2):
                    tc = 2 * hh + tl
                    for cg in range(4):
                        b = tl * 4 + cg
                        dst = xmid[:, tc, cg * 512:(cg + 1) * 512]
                        P.op("dve", lambda e, b=b, dst=dst: e.tensor_tensor(out=dst, in0=pb[:, b, :], in1=dst, op=ALU.add),
                             r=[r_pb[b], r_xmid[tc]], w=[r_xmid[tc]])
                    q_ = rms_rows(128, xmid[:, tc, :], r_xmid[tc], False)
                    P.op("dve", lambda e, tc=tc, q_=q_: e.scalar_tensor_tensor(out=xmid[:, tc, :], in0=xmid[:, tc, :], scalar=rstds[q_][:, :], in1=gfb[:],
                                                                       op0=ALU.mult, op1=ALU.mult),
                         r=[r_xmid[tc], r_rstds[q_], r_gfb], w=[r_xmid[tc]])
                    row0 = 512 * T + 128 * tc
                    P.op("sp", lambda e, tc=tc, row0=row0: e.dma_start(out=k.y[j][row0:row0 + 128, :], in_=xmid[:, tc, :]), r=[r_xmid[tc]], dma=r_xmid[tc])

        for j in DEBUG.get("jobs", (0, 1)):
            for T in DEBUG.get("tiles_c", range(4)):
                tile(j, T)
        P.emit_phase()


def prepare_inputs(inp):
    f32 = np.float32
    x_prompt = np.asarray(inp["x_prompt"], f32)
    x_sample = np.asarray(inp["x_sample"], f32)
    shared = {}
    shared["w_in"] = np.ascontiguousarray(np.asarray(inp["w_in"], f32)[0])
    shared["w_out"] = np.ascontiguousarray(np.asarray(inp["w_out"], f32)[0])
    shared["w_up"] = np.ascontiguousarray(np.asarray(inp["w_up"], f32)[0])
    shared["w_down"] = np.ascontiguousarray(np.asarray(inp["w_down"], f32)[0])
    g1 = np.asarray(inp["norm1_g"], f32)[0].reshape(CK, 128).T
    g2 = np.asarray(inp["norm2_g"], f32)[0].reshape(CK, 128).T
    shared["g12c"] = np.ascontiguousarray(np.stack([g1, g2], axis=1))
    shared["gfb"] = bcast128(inp["final_g"])
    shared["sgb"] = bcast128(np.asarray(inp["subln_g"], f32)[0])
    lam = np.concatenate([np.asarray(inp[n], f32)[0] for n in ("lambda_q1", "lambda_k1", "lambda_q2", "lambda_k2")])
    shared["lamv"] = bcast128(lam).reshape(128, 4, 64)
    tab = np.asarray(inp["rel_bias_table"], f32)
    shared["tab"] = np.ascontiguousarray(tab)
    shared["tablr"] = bcast128(np.concatenate([tab[15], tab[31]])).reshape(128, 2, 8)
    rel = np.arange(1536) - 767
    bk = t5_bucket_np(rel)
    ohu = np.zeros((32, 1536), f32)
    ohu[bk, np.arange(1536)] = 1.0
    shared["ohu"] = ohu
    rpbp = np.zeros((8, 15, 128), f32)
    rpbp[:, :, 48:79] = np.asarray(inp["na_rpb"], f32)[0]
    shared["rpbp"] = rpbp
    cw = np.asarray(inp["conv_w"], f32)[0]
    cb = np.asarray(inp["conv_b"], f32)[0]
    cc = np.stack([cw[0], cw[1], cw[2], cb], axis=1)
    shared["convc"] = np.ascontiguousarray(cc.reshape(FCH, 128, 4).transpose(1, 0, 2))
    shared["ident"] = np.eye(128, dtype=f32).astype(ml_dtypes.bfloat16)

    in_maps = []
    for c in range(NCORES):
        m = dict(shared)
        hflag = np.zeros((128, 4), f32)
        for j in range(2):
            nblk, S = JOBS[j]
            if j == 0:
                seq, t = x_prompt[c // 4], c % 4
            else:
                seq, t = x_sample[c // 2], c % 2
            o, blocks, near_true = job_geometry(nblk, S, t)
            xs = np.zeros((S, 128, D), f32)
            ws = np.zeros((3, S), f32)
            for s, gb in enumerate(blocks):
                if gb < 0:
                    continue
                xs[s] = seq[gb * 128:(gb + 1) * 128]
                if 2 <= s <= 19:
                    ws[0, s] = 1.0
                elif gb < o:
                    ws[1, s] = 1.0
                else:
                    ws[2, s] = 1.0
            m["xs%d" % j] = xs
            m["wsel%d" % j] = np.ascontiguousarray(np.broadcast_to(ws[None], (128, 3, S)))
            m["maskd%d" % j] = build_masks(nblk, t, o, near_true)
            hflag[:, 2 * j] = 1.0 if o > 0 else 0.0
            hflag[:, 2 * j + 1] = 1.0 if o + 16 < nblk else 0.0
        m["hflag"] = hflag
        in_maps.append(m)
    return in_maps


def kernel(**inputs):
    in_maps = prepare_inputs(inputs)
    nc = build_program()
    res = run_bass_kernel_spmd(nc, in_maps, core_ids=list(range(NCORES)))
    yp = np.zeros((2, 8192, D), np.float32)
    ysm = np.zeros((4, 4096, D), np.float32)
    for c in range(NCORES):
        r = res.results[c]
        yp[c // 4, (c % 4) * 2048:(c % 4 + 1) * 2048] = r["y0"]
        ysm[c // 2, (c % 2) * 2048:(c % 2 + 1) * 2048] = r["y1"]
    return (yp, ysm)
```

```python
import math
from contextlib import ExitStack

import numpy as np
import ml_dtypes

import concourse.bass as bass
import concourse.mybir as mybir
from concourse.bass_utils import run_bass_kernel_spmd

F32 = mybir.dt.float32
BF16 = mybir.dt.bfloat16
AF = mybir.ActivationFunctionType
ALU = mybir.AluOpType
AX = mybir.AxisListType

D = 2048
CK = 16
HA = 8
HN = 8
INC = 6144
DFF = 5504
FCH = 43
EPS = 1e-6
NCORES = 8
NEAR = 22
JOBS = ((64, 65), (32, 33))
SCALE_A = 0.125
SCALE_N = 128 ** -0.5
NEG = -3.0e5
LAM_INIT = 0.8 - 0.6 * math.exp(-0.3 * 0)

DEBUG = {"stop_after": None, "ext": False}


class Sem:
    def __init__(self, h, name):
        self.h = h
        self.v = 0
        self.name = name


class Res:
    __slots__ = ("name", "wr", "rd", "dsem")

    def __init__(self, name):
        self.name = name
        self.wr = None
        self.rd = []
        self.dsem = None


class Op:
    __slots__ = ("eng", "fn", "deps", "sig", "ev", "dma")


ENGS = ("pe", "act", "dve", "pool", "sp")


class Prog:
    def __init__(self, nc):
        self.nc = nc
        self.esem = {e: Sem(nc.alloc_semaphore(name="es_" + e), e) for e in ("pe", "act", "dve", "pool")}
        self.bar = Sem(nc.alloc_semaphore(name="bar"), "bar")
        self.free_dsems = []
        self.free_dsems_sw = []
        self.ndsem = 0
        self.reset()
        self.waited = {e: {} for e in ENGS}
        self.nphase = 0
        self.all_res = []

    def reset(self):
        self.ops = {e: [] for e in ENGS}
        self.order = []

    def res(self, name):
        r = Res(name)
        self.all_res.append(r)
        return r

    def _dsem(self, r, eng):
        if r.dsem is None:
            sw = eng == "pool"
            free = self.free_dsems_sw if sw else self.free_dsems
            if free:
                r.dsem = free.pop()
            else:
                r.dsem = Sem(self.nc.alloc_semaphore(name="ds%d" % self.ndsem), "ds%d" % self.ndsem)
                r.dsem.sw = sw
                self.ndsem += 1
        return r.dsem

    def op(self, eng, fn, r=(), w=(), dma=None, after=()):
        o = Op()
        o.eng = eng
        o.fn = fn
        o.dma = dma
        o.sig = False
        o.ev = None
        deps = []
        seen = set()

        def add(d):
            if d is None or id(d) in seen:
                return
            seen.add(id(d))
            deps.append(d)

        for d in after:
            add(d)
        for x in r:
            add(x.wr)
        for x in w:
            add(x.wr)
            for d in x.rd:
                add(d)
        o.deps = [d for d in deps if not (d.eng == "pe" and eng == "pe" and d.dma is None)]
        for d in o.deps:
            d.sig = True
        for x in w:
            x.wr = o
            x.rd = []
        for x in r:
            x.rd.append(o)
        self.ops[eng].append(o)
        self.order.append(o)
        return o

    def emit_phase(self):
        nc = self.nc
        for o in self.order:
            if o.dma is not None:
                s = self._dsem(o.dma, o.eng)
                s.v += 16
                o.ev = (s, s.v)
        for e in ("pe", "act", "dve", "pool"):
            ops = self.ops[e]
            lo = [o for o in ops if o.dma is None]
            if lo:
                lo[-1].sig = True
            for o in ops:
                if o.dma is None and o.sig:
                    s = self.esem[e]
                    s.v += 1
                    o.ev = (s, s.v)
        self.nphase += 1
        bar_target = self.nphase * len(ENGS)
        prog = self

        def run(e, eng):
            waited = prog.waited[e]

            def wait(ev):
                s, v = ev
                if waited.get(id(s), 0) < v:
                    eng.wait_ge(s.h, v)
                    waited[id(s)] = v

            last_dma = {}
            for o in prog.ops[e]:
                need = {}
                for d in o.deps:
                    sm, v = d.ev
                    if need.get(id(sm), (None, 0))[1] < v:
                        need[id(sm)] = (sm, v)
                for ev in need.values():
                    wait(ev)
                ins = o.fn(eng)
                if o.dma is not None:
                    ins.then_inc(o.ev[0].h, 16)
                    last_dma[id(o.ev[0])] = o.ev
                elif o.sig:
                    ins.then_inc(o.ev[0].h, 1)
            for ev in last_dma.values():
                wait(ev)
            if e in prog.esem and prog.ops[e]:
                lo = [o for o in prog.ops[e] if o.dma is None]
                if lo:
                    wait(lo[-1].ev)
            eng.sem_inc(prog.bar.h, 1)
            eng.wait_ge(prog.bar.h, bar_target)

        with nc.Block() as block:
            @block.tensor
            def _(eng):
                run("pe", eng)

            @block.scalar
            def _(eng):
                run("act", eng)

            @block.vector
            def _(eng):
                run("dve", eng)

            @block.gpsimd
            def _(eng):
                run("pool", eng)

            @block.sync
            def _(eng):
                run("sp", eng)

        for r in self.all_res:
            r.wr = None
            r.rd = []
            if r.dsem is not None:
                (self.free_dsems_sw if getattr(r.dsem, "sw", False) else self.free_dsems).append(r.dsem)
                r.dsem = None
        self.all_res = []
        self.reset()


def mkap(t, off, dims):
    return bass.AP(t, off, [list(d) for d in dims])


def psz(t):
    return t[:].ap[0][0]


def t5_bucket_np(rel):
    nb = 16
    me = 8
    ret = np.where(rel > 0, nb, 0)
    n = np.abs(rel)
    nf = np.maximum(n, 1).astype(np.float32)
    large = me + (np.log(nf / np.float32(me)) / np.float32(math.log(128 / 8)) * np.float32(nb - me)).astype(np.int32)
    large = np.minimum(large, nb - 1)
    return ret + np.where(n < me, n, large)


def job_geometry(nblk, nslots, t):
    o = 16 * t
    blocks = [-1] * nslots
    near_true = [False] * NEAR
    used = set()
    for n in range(NEAR):
        gb = o + n - 3
        if 0 <= gb < nblk:
            blocks[n] = gb
            near_true[n] = True
            used.add(gb)
    rest = [b for b in range(nblk) if b not in used]
    for n in range(NEAR):
        if blocks[n] == -1 and n not in (2, 19) and rest:
            blocks[n] = rest.pop(0)
    for s in range(NEAR, nslots):
        if rest:
            blocks[s] = rest.pop(0)
    assert not rest
    return o, blocks, near_true


def nbr_units():
    units = []
    for i in range(16):
        if i == 0:
            dl = list(range(-2, 4))
        elif i == 15:
            dl = list(range(-3, 3))
        else:
            dl = list(range(-2, 3))
        units.append((i, dl))
    units.append((-1, list(range(-2, 3))))
    units.append((16, list(range(-2, 3))))
    return units


def mask_layout():
    off = {}
    col = 0
    for key, nd, w in (("int", 5, 128), (0, 6, 128), (1, 5, 128), (14, 5, 128), (15, 6, 128), (-1, 5, 1), (16, 5, 1)):
        off[key] = col
        col += nd * w
    return off, col


def build_masks(nblk, t, o, near_true):
    R = nblk * 2
    L = nblk * 128
    off, ncol = mask_layout()
    out = np.zeros((128, ncol), np.float32)

    def tile(tq, valid_q, nk):
        if not valid_q:
            return np.zeros((128, len(tq)), np.float32)
        if not near_true[nk]:
            return np.full((128, len(tq)), NEG, np.float32)
        tk = (o + nk - 3) * 128 + np.arange(128)
        r = tq // 64
        c = tq % 64
        rs = np.clip(r - 4, 0, R - 8)
        cs = np.clip(c - 8, 0, 64 - 16)
        rk = (tk // 64)[:, None]
        ckk = (tk % 64)[:, None]
        ok = (rk >= rs[None]) & (rk < rs[None] + 8) & (ckk >= cs[None]) & (ckk < cs[None] + 16)
        return np.where(ok, 0.0, NEG).astype(np.float32)

    units = dict(nbr_units())
    per_unit = {}
    for i, dl in units.items():
        if i == -1:
            tq = np.array([o * 128 - 1])
            vq = o > 0
        elif i == 16:
            tq = np.array([(o + 16) * 128])
            vq = (o + 16) < nblk
        else:
            tq = (o + i) * 128 + np.arange(128)
            vq = True
        per_unit[i] = [tile(tq, vq, i + dl_ + 3) for dl_ in dl]
    for i in range(2, 14):
        for a, b in zip(per_unit[i], per_unit[7]):
            assert np.array_equal(a, b)
    def put(key, tiles):
        c0 = off[key]
        for k, tl in enumerate(tiles):
            w = tl.shape[1]
            out[:, c0 + k * w:c0 + (k + 1) * w] = tl
    put("int", per_unit[7])
    for key in (0, 1, 14, 15, -1, 16):
        put(key, per_unit[key])
    return out.astype(ml_dtypes.bfloat16)


def bcast128(a):
    a = np.asarray(a, np.float32)
    return np.ascontiguousarray(np.broadcast_to(a.reshape(1, -1), (128, a.size)))


class K:
    pass


def build_program():
    nc = bass.Bass("TRN2", target_bir_lowering=False)
    P = Prog(nc)
    k = K()
    k.nc = nc
    k.P = P
    ext = DEBUG["ext"]

    def din(name, shape, dt=F32):
        return nc.dram_tensor(name, list(shape), dt, kind="ExternalInput")

    def dscr(name, shape, dt=BF16):
        return nc.dram_tensor(name, list(shape), dt, kind=("ExternalOutput" if (ext and name in ext) else "Internal"))

    k.xs = [din("xs%d" % j, (JOBS[j][1], 128, D)) for j in range(2)]
    k.w_in = din("w_in", (D, INC))
    k.w_out = din("w_out", (D, D))
    k.w_up = din("w_up", (D, 2 * DFF))
    k.w_down = din("w_down", (DFF, D))
    k.g12c = din("g12c", (128, 2, CK))
    k.gfb = din("gfb", (128, D))
    k.sgb = din("sgb", (128, 128))
    k.lamv = din("lamv", (128, 4, 64))
    k.tab = din("tab", (32, 8))
    k.tablr = din("tablr", (128, 2, 8))
    k.ohu = din("ohu", (32, 1536))
    k.rpbp = din("rpbp", (8, 15, 128))
    k.convc = din("convc", (128, FCH, 4))
    k.ident = din("ident", (128, 128), BF16)
    k.wsel = [din("wsel%d" % j, (128, 3, JOBS[j][1])) for j in range(2)]
    _, mcols = mask_layout()
    k.maskd = [din("maskd%d" % j, (128, mcols), BF16) for j in range(2)]
    k.hflag = din("hflag", (128, 4))
    k.y = [nc.dram_tensor("y%d" % j, [2048, D], F32, kind="ExternalOutput") for j in range(2)]
    k.wb_in = dscr("wb_in", (D, INC))
    k.wb_out = dscr("wb_out", (D, D))
    k.wb_up = dscr("wb_up", (D, 2 * DFF))
    k.wb_down = dscr("wb_down", (DFF, D))
    k.qaT = [dscr("qaT%d" % j, (HA, 128, NEAR * 128)) for j in range(2)]
    k.qnT = [dscr("qnT%d" % j, (HN, 128, NEAR * 128)) for j in range(2)]
    k.knT = [dscr("knT%d" % j, (HN, 128, NEAR * 128)) for j in range(2)]
    k.kaT = [dscr("kaT%d" % j, (HA, 128, JOBS[j][1] * 128)) for j in range(2)]
    k.va = [dscr("va%d" % j, (HA, 128, JOBS[j][1], 129)) for j in range(2)]
    k.vn = [dscr("vn%d" % j, (HN, 128, NEAR, 129)) for j in range(2)]
    k.aoT = [dscr("aoT%d" % j, (16, 128, 2050)) for j in range(2)]
    k.u2 = dscr("u2", (8, 1536), F32)
    k.tfd = dscr("tfd", (8, 128, 896), F32)

    if not DEBUG.get("skip_a"):
        phase_a(k)
    if DEBUG["stop_after"] == "A":
        return nc
    phase_b(k)
    if DEBUG["stop_after"] == "B":
        return nc
    phase_c(k)
    return nc


def phase_a(k):
    nc, P = k.nc, k.P
    with ExitStack() as es:
        def sb(name, shape, dt=F32):
            return es.enter_context(nc.sbuf_tensor("A_" + name, list(shape), dt))

        def ps(name, shape, dt=F32):
            return es.enter_context(nc.psum_tensor("A_" + name, list(shape), dt))

        ident = sb("ident", (128, 128), BF16)
        g12c = sb("g12c", (128, 2, CK))
        tablr = sb("tablr", (128, 2, 8))
        elr = sb("elr", (128, 2, 8))
        epsc = sb("epsc", (128, 1))
        wsel = [sb("wsel%d" % j, (128, 3, JOBS[j][1])) for j in range(2)]
        wfull = [sb("wfull%d" % j, (128, JOBS[j][1], 8)) for j in range(2)]
        wtmp = sb("wtmp", (128, 65, 8))
        CW = 1024
        cin = [sb("cin%d" % i, (128, 512)) for i in range(4)]
        cout = [sb("cout%d" % i, (128, 512), BF16) for i in range(4)]
        xbuf = [sb("xbuf%d" % i, (128, D)) for i in range(3)]
        junk = sb("junk", (128, D), BF16)
        ssq = [sb("ssq%d" % i, (128, 1)) for i in range(3)]
        lnv = [sb("lnv%d" % i, (128, 1)) for i in range(3)]
        rstd = [sb("rstd%d" % i, (128, 1)) for i in range(3)]
        xn = [sb("xn%d" % i, (128, D), BF16) for i in range(2)]
        xnT = [sb("xnT%d" % i, (128, CK, 512), BF16) for i in range(2)]
        wring = [sb("wring%d" % i, (128, CK, 512), BF16) for i in range(3)]
        fmst = [sb("fmst%d" % i, (128, 4, 512), BF16) for i in range(2)]
        vst = [sb("vst%d" % i, (128, 8, 4, 129), BF16) for i in range(2)]
        pT = [ps("pT%d" % i, (128, 8 * 128), BF16) for i in range(2)]
        pO = [ps("pO%d" % i, (128, 512)) for i in range(4)]

        R = P.res
        r_ident, r_g, r_tablr, r_elr, r_eps = R("ident"), R("g12c"), R("tablr"), R("elr"), R("eps")
        r_wsel = [R("wsel0"), R("wsel1")]
        r_wfull = [R("wfull0"), R("wfull1")]
        r_wtmp = R("wtmp")

        P.op("sp", lambda e: e.dma_start(out=ident[:], in_=k.ident.ap()), w=[r_ident], dma=r_ident)
        P.op("sp", lambda e: e.dma_start(out=g12c[:], in_=k.g12c.ap()), w=[r_g], dma=r_g)
        P.op("sp", lambda e: e.dma_start(out=tablr[:], in_=k.tablr.ap()), w=[r_tablr], dma=r_tablr)
        for j in range(2):
            P.op("sp", lambda e, j=j: e.dma_start(out=wsel[j][:], in_=k.wsel[j].ap()), w=[r_wsel[j]], dma=r_wsel[j])
        P.op("dve", lambda e: e.memset(epsc[:], EPS), w=[r_eps])
        P.op("act", lambda e: e.activation(out=elr[:], in_=tablr[:], func=AF.Exp), r=[r_tablr], w=[r_elr])
        for j in range(2):
            S = JOBS[j][1]
            wf, ws = wfull[j], wsel[j]
            pw, pe_, pt = psz(wf), psz(ws), psz(wtmp)
            pl = psz(elr)

            def bc_s(c, ws=ws, pe_=pe_, S=S):
                return mkap(ws, c * S, [(pe_, 128), (1, S), (0, 8)])

            def bc_h(c, S=S, pl=pl):
                return mkap(elr, c * 8, [(pl, 128), (0, S), (1, 8)])

            wt = mkap(wtmp, 0, [(pt, 128), (8, S), (1, 8)])
            P.op("dve", lambda e, wf=wf, bc_s=bc_s, bc_h=bc_h: e.tensor_tensor(out=wf[:], in0=bc_s(1), in1=bc_h(0), op=ALU.mult),
                 r=[r_wsel[j], r_elr], w=[r_wfull[j]])
            P.op("dve", lambda e, wt=wt, bc_s=bc_s, bc_h=bc_h: e.tensor_tensor(out=wt, in0=bc_s(2), in1=bc_h(1), op=ALU.mult),
                 r=[r_wsel[j], r_elr], w=[r_wtmp])
            P.op("dve", lambda e, wf=wf, wt=wt: e.tensor_tensor(out=wf[:], in0=wf[:], in1=wt, op=ALU.add),
                 r=[r_wtmp], w=[r_wfull[j]])
            P.op("dve", lambda e, wf=wf, bc_s=bc_s: e.tensor_tensor(out=wf[:], in0=wf[:], in1=bc_s(0), op=ALU.add),
                 r=[r_wsel[j]], w=[r_wfull[j]])

        r_cin = [R("cin%d" % i) for i in range(4)]
        r_cout = [R("cout%d" % i) for i in range(4)]
        conv_state = {"n": 0}
        GROUP_ORDER = [1024, 1536, 2048, 2560, 0, 512, 3072, 3584, 4096, 4608, 5120, 5632]
        st_in = {}
        for col0 in GROUP_ORDER:
            st_in[col0] = []
            for rb in range(CK):
                i = conv_state["n"] % 4
                conv_state["n"] += 1
                ci, co = cin[i], cout[i]
                rci, rco = r_cin[i], r_cout[i]
                sv = k.w_in[rb * 128:(rb + 1) * 128, col0:col0 + 512]
                dv = k.wb_in[rb * 128:(rb + 1) * 128, col0:col0 + 512]
                P.op("pool", lambda e, ci=ci, sv=sv: e.dma_start(out=ci[:, 0:512], in_=sv), w=[rci], dma=rci)
                P.op("pool", lambda e, ci=ci, co=co, rb=rb: e.tensor_scalar(
                    out=co[:, 0:512], in0=ci[:, 0:512], scalar1=g12c[:, 0, rb:rb + 1], scalar2=1.0, op0=ALU.mult, op1=ALU.mult),
                    r=[rci, r_g], w=[rco])
                st_in[col0].append(P.op("pool", lambda e, co=co, dv=dv: e.dma_start(out=dv, in_=co[:, 0:512]), r=[rco], dma=rco))
        groups_loaded = set()

        r_x = [R("xbuf%d" % i) for i in range(3)]
        r_junk = R("junk")
        r_ssq = [R("ssq%d" % i) for i in range(3)]
        r_lnv = [R("lnv%d" % i) for i in range(3)]
        r_rstd = [R("rstd%d" % i) for i in range(3)]
        r_xn = [R("xn%d" % i) for i in range(2)]
        r_xnT = [[R("xnT%d_%d" % (i, b)) for b in range(4)] for i in range(2)]
        r_w = [R("wring%d" % i) for i in range(3)]
        r_fm = [R("fmst%d" % i) for i in range(2)]
        r_vst = [R("vst%d" % i) for i in range(2)]
        r_pT = [R("pT%d" % i) for i in range(2)]
        r_pO = [R("pO%d" % i) for i in range(4)]
        cnt = {"x": 0, "xn": 0, "pT": 0, "w": 0, "pO": 0, "fm": 0, "vst": 0, "tile": 0}

        SUBS_NEAR = [("ka", 1024), ("ka", 1536), ("va", 2048), ("va", 2560), ("qa", 0), ("qa", 512),
                     ("qn", 3072), ("qn", 3584), ("kn", 4096), ("kn", 4608), ("vn", 5120), ("vn", 5632)]
        SUBS_FAR = [("ka", 1024), ("ka", 1536), ("va", 2048), ("va", 2560)]

        def load_x(j, s):
            i = cnt["x"] % 3
            cnt["x"] += 1
            P.op("sp", lambda e, i=i, j=j, s=s: e.dma_start(out=xbuf[i][:], in_=k.xs[j][s]), w=[r_x[i]], dma=r_x[i])
            return i

        def norm_block(xi, xnT_i, b):
            ni = cnt["xn"] % 2
            cnt["xn"] += 1
            P.op("act", lambda e: e.activation(out=junk[:], in_=xbuf[xi][:], func=AF.Square, accum_out=ssq[xi][:]),
                 r=[r_x[xi]], w=[r_junk, r_ssq[xi]])
            P.op("act", lambda e: e.activation(out=lnv[xi][:], in_=ssq[xi][:], func=AF.Ln, scale=1.0 / D, bias=epsc[:]),
                 r=[r_ssq[xi], r_eps], w=[r_lnv[xi]])
            P.op("act", lambda e: e.activation(out=rstd[xi][:], in_=lnv[xi][:], func=AF.Exp, scale=-0.5),
                 r=[r_lnv[xi]], w=[r_rstd[xi]])
            P.op("dve", lambda e: e.tensor_scalar(out=xn[ni][:], in0=xbuf[xi][:], scalar1=rstd[xi][:], scalar2=None, op0=ALU.mult),
                 r=[r_x[xi], r_rstd[xi]], w=[r_xn[ni]])
            for half in range(2):
                pi = cnt["pT"] % 2
                cnt["pT"] += 1
                for q in range(8):
                    ck = half * 8 + q
                    P.op("pe", lambda e, pi=pi, q=q, ck=ck: e.transpose(out=pT[pi][:, q * 128:(q + 1) * 128],
                                                                        in_=xn[ni][:, ck * 128:(ck + 1) * 128], identity=ident[:]),
                         r=[r_xn[ni], r_ident], w=[r_pT[pi]])
                dst = xnT[xnT_i][:, half * 8:(half + 1) * 8, b * 128:(b + 1) * 128]
                src = pT[pi][:].rearrange("p (a b) -> p a b", b=128)
                P.op("dve", lambda e, dst=dst, src=src: e.tensor_copy(out=dst, in_=src), r=[r_pT[pi]], w=[r_xnT[xnT_i][b]])

        def load_w(col0, first):
            i = cnt["w"] % 3
            cnt["w"] += 1
            src = k.wb_in[:, col0:col0 + 512].rearrange("(ck p) f -> p ck f", p=128)
            aft = ()
            if col0 not in groups_loaded:
                groups_loaded.add(col0)
                aft = st_in[col0]
            P.op("sp", lambda e, i=i, src=src: e.dma_start(out=wring[i][:], in_=src), w=[r_w[i]], dma=r_w[i], after=aft)
            return i

        def do_tile(j, s0, nb, near, xnT_i, prefetch):
            S = JOBS[j][1]
            N = nb * 128
            subs = SUBS_NEAR if near else SUBS_FAR
            wq = []
            state = {"first": cnt["w"] == 0}
            nxt = load_w(subs[0][1], state["first"])
            for si, (kind, col0) in enumerate(subs):
                wi = nxt
                if si + 1 < len(subs):
                    nxt = load_w(subs[si + 1][1], False)
                if si == 1 and prefetch is not None:
                    prefetch()
                hb = (col0 % 1024) // 128
                if kind in ("qa", "ka", "qn", "kn"):
                    fi = cnt["fm"] % 2
                    cnt["fm"] += 1
                    for fc in range(4):
                        oi = cnt["pO"] % 4
                        cnt["pO"] += 1
                        for ck in range(CK):
                            P.op("pe", lambda e, oi=oi, wi=wi, fc=fc, ck=ck: e.matmul(
                                pO[oi][:, 0:N], lhsT=wring[wi][:, ck, fc * 128:(fc + 1) * 128], rhs=xnT[xnT_i][:, ck, 0:N],
                                start=(ck == 0), stop=(ck == CK - 1)),
                                r=[r_w[wi]] + r_xnT[xnT_i][0:nb], w=[r_pO[oi]])
                        P.op("act", lambda e, oi=oi, fi=fi, fc=fc: e.activation(out=fmst[fi][:, fc, 0:N], in_=pO[oi][:, 0:N], func=AF.Copy),
                             r=[r_pO[oi]], w=[r_fm[fi]])
                    dstT = {"qa": k.qaT, "ka": k.kaT, "qn": k.qnT, "kn": k.knT}[kind][j]
                    dv = dstT[hb:hb + 4, :, s0 * 128:s0 * 128 + N].rearrange("h p n -> p h n")
                    P.op("act", lambda e, fi=fi, dv=dv: e.dma_start(out=dv, in_=fmst[fi][:, :, 0:N]), r=[r_fm[fi]], dma=r_fm[fi])
                else:
                    if hb == 0:
                        vi = cnt["vst"] % 2
                        cnt["vst"] += 1
                        state["vi"] = vi
                    vi = state["vi"]
                    for b in range(nb):
                        oi = cnt["pO"] % 4
                        cnt["pO"] += 1
                        for ck in range(CK):
                            P.op("pe", lambda e, oi=oi, wi=wi, b=b, ck=ck: e.matmul(
                                pO[oi][:, :], lhsT=xnT[xnT_i][:, ck, b * 128:(b + 1) * 128], rhs=wring[wi][:, ck, :],
                                start=(ck == 0), stop=(ck == CK - 1)),
                                r=[r_w[wi], r_xnT[xnT_i][b]], w=[r_pO[oi]])
                        src = pO[oi][:].rearrange("p (h e) -> p h e", e=128)
                        dst = vst[vi][:, hb:hb + 4, b, 0:128]
                        if kind == "va":
                            wf = wfull[j]
                            wb = mkap(wf, (s0 + b) * 8 + hb, [(psz(wf), 128), (1, 4), (0, 128)])
                            P.op("dve", lambda e, dst=dst, src=src, wb=wb: e.tensor_tensor(out=dst, in0=src, in1=wb, op=ALU.mult),
                                 r=[r_pO[oi], r_wfull[j]], w=[r_vst[vi]])
                            if hb == 4:
                                ones_dst = vst[vi][:, :, b, 128:129]
                                wsrc = mkap(wf, (s0 + b) * 8, [(psz(wf), 128), (1, 8), (1, 1)])
                                P.op("dve", lambda e, ones_dst=ones_dst, wsrc=wsrc: e.tensor_copy(out=ones_dst, in_=wsrc),
                                     r=[r_wfull[j]], w=[r_vst[vi]])
                        else:
                            P.op("act", lambda e, dst=dst, src=src: e.activation(out=dst, in_=src, func=AF.Copy),
                                 r=[r_pO[oi]], w=[r_vst[vi]])
                            if hb == 4:
                                ones_dst = vst[vi][:, :, b, 128:129]
                                P.op("dve", lambda e, ones_dst=ones_dst: e.memset(ones_dst, 1.0), w=[r_vst[vi]])
                    if hb == 4:
                        dstV = (k.va if kind == "va" else k.vn)[j]
                        dv = dstV[:, :, s0:s0 + nb, :].rearrange("h p s e -> p h s e")
                        P.op("act", lambda e, vi=vi, dv=dv: e.dma_start(out=dv, in_=vst[vi][:, :, 0:nb, :]), r=[r_vst[vi]], dma=r_vst[vi])

        tiles = []
        for j in range(2):
            S = JOBS[j][1]
            s = 0
            while s < NEAR:
                nb = min(4, NEAR - s)
                tiles.append((j, s, nb, True))
                s += nb
            while s < S:
                nb = min(4, S - s)
                tiles.append((j, s, nb, False))
                s += nb
        tiles = [t for t in tiles if not t[3]] + [t for t in tiles if t[3]]
        if DEBUG.get("max_tiles"):
            tiles = tiles[:DEBUG["max_tiles"]]

        def prep_tile(ti):
            j, s0, nb, near = tiles[ti]
            xi_list = [load_x(j, s0 + b) for b in range(nb)]
            for b in range(nb):
                norm_block(xi_list[b], ti % 2, b)

        def prep_tile_interleaved(ti):
            j, s0, nb, near = tiles[ti]
            pend = []
            for b in range(nb):
                pend.append(load_x(j, s0 + b))
                if len(pend) == 2:
                    norm_block(pend.pop(0), ti % 2, b - 1)
            bb = nb - len(pend)
            for xi in pend:
                norm_block(xi, ti % 2, bb)
                bb += 1

        prep_tile_interleaved(0)
        for ti in range(len(tiles)):
            j, s0, nb, near = tiles[ti]
            pf = (lambda ti=ti: prep_tile_interleaved(ti + 1)) if ti + 1 < len(tiles) else None
            do_tile(j, s0, nb, near, ti % 2, pf)

        P.emit_phase()


def phase_b(k):
    nc, P = k.nc, k.P
    moff, mcols = mask_layout()
    with ExitStack() as es:
        def sb(name, shape, dt=F32):
            return es.enter_context(nc.sbuf_tensor("B_" + name, list(shape), dt))

        def ps(name, shape, dt=F32):
            return es.enter_context(nc.psum_tensor("B_" + name, list(shape), dt))

        R = P.res
        SMAX = JOBS[0][1]
        ident = sb("ident", (128, 128), BF16)
        tablr = sb("tablr", (128, 2, 8))
        lamv = sb("lamv", (128, 4, 64))
        lprod = sb("lprod", (128, 2, 64))
        lsum = sb("lsum", (128, 2))
        lexp = sb("lexp", (128, 2))
        nlam = sb("nlam", (128, 1))
        sg = sb("sg", (128, 128))
        epsc = sb("epsc", (128, 1))
        tabp = sb("tabp", (128, 128))
        ohup = sb("ohup", (128, 1536))
        u2s = sb("u2s", (8, 1536))
        KT = [sb("KT%d" % i, (128, SMAX * 128), BF16) for i in range(2)]
        VH = [sb("VH%d" % i, (128, SMAX, 129), BF16) for i in range(2)]
        QT = [sb("QT%d" % i, (128, 18 * 128), BF16) for i in range(2)]
        QTm = [[sb("QTm%d_%d" % (m, i), (128, 18 * 128), BF16) for i in range(2)] for m in range(2)]
        HH = [sb("HH%d" % i, (128, 1408)) for i in range(2)]
        PT = [sb("PT%d" % i, (128, 2, 512), BF16) for i in range(3)]
        PTm = sb("PTm", (128, SMAX * 4), BF16)
        Gmini = sb("Gmini", (128, 18, 2))
        osb = sb("osb", (128, 8, 129))
        rz = sb("rz", (128, 8))
        tt = sb("tt", (128, 8, 128))
        od = sb("od", (128, 4, 128))
        sqj = sb("sqj", (128, 128))
        ssq = sb("ssq", (128, 4))
        lnv = sb("lnv", (128, 4))
        rstd = sb("rstd", (128, 4))
        tmp2 = sb("tmp2", (128, 4, 128))
        onb = [sb("onb%d" % i, (128, 4, 128), BF16) for i in range(2)]
        od_m = sb("od_m", (2, 1, 128))
        tmp2_m = sb("tmp2_m", (2, 1, 128))
        ssq_m = sb("ssq_m", (2, 4))
        lnv_m = sb("lnv_m", (2, 4))
        rstd_m = sb("rstd_m", (2, 4))
        AOh = [sb("AOh%d" % i, (128, 2050), BF16) for i in range(2)]
        Trt = [sb("Trt%d" % i, (128, 7, 2, 64)) for i in range(2)]
        TfixB = [sb("TfixB%d" % i, (128, 7, 128)) for i in range(2)]
        Tfix = [sb("Tfix%d" % i, (128, 7, 128)) for i in range(2)]
        maskt = sb("maskt", (128, mcols), BF16)
        PTn = [sb("PTn%d" % i, (128, 7, 128), BF16) for i in range(3)]
        rzn = [sb("rzn%d" % i, (128, 1)) for i in range(3)]
        onbn = [sb("onbn%d" % i, (128, 128), BF16) for i in range(3)]

        psS = [ps("psS%d" % i, (128, 2, 512)) for i in range(2)]
        acc = ps("acc", (128, 3, 512))
        pTr = ps("pTr", (128, 1024), BF16)

        r_ident, r_tablr, r_lamv, r_lprod, r_lsum, r_lexp, r_nlam = (R(n) for n in ("ident", "tablr", "lamv", "lprod", "lsum", "lexp", "nlam"))
        r_sg, r_eps, r_tabp, r_ohup, r_u2s, r_u2d = (R(n) for n in ("sg", "eps", "tabp", "ohup", "u2s", "u2d"))
        r_KT = [R("KT0"), R("KT1")]
        r_VH = [R("VH0"), R("VH1")]
        r_QT = [R("QT0"), R("QT1")]
        r_QTm = [[R("QTm%d_%d" % (m, i)) for i in range(2)] for m in range(2)]
        r_HH = [R("HH0"), R("HH1")]
        r_PT = [R("PT%d" % i) for i in range(3)]
        r_PTm, r_Gm, r_osb, r_rz, r_tt, r_od, r_sqj, r_ssq, r_lnv, r_rstd, r_tmp2 = (
            R(n) for n in ("PTm", "Gm", "osb", "rz", "tt", "od", "sqj", "ssq", "lnv", "rstd", "tmp2"))
        r_onb = [R("onb0"), R("onb1")]
        r_odm, r_tmp2m, r_ssqm, r_lnvm, r_rstdm = (R(n) for n in ("od_m", "tmp2_m", "ssq_m", "lnv_m", "rstd_m"))
        carry = []

        def tick_carry():
            for pd in carry:
                pd[0] -= 1
            while carry and carry[0][0] <= 0:
                carry.pop(0)[1]()

        def flush_carry():
            while carry:
                carry.pop(0)[1]()
        r_AOh = [R("AOh0"), R("AOh1")]
        r_Trt = [[R("Trt%d_%d" % (i, q)) for q in range(4)] for i in range(2)]
        r_TfixB = [R("TfixB0"), R("TfixB1")]
        r_Tfix = [R("Tfix0"), R("Tfix1")]
        r_mask = R("mask")
        r_PTn = [R("PTn0"), R("PTn1"), R("PTn2")]
        r_rzn = [R("rzn%d" % i) for i in range(3)]
        r_onbn = [R("onbn%d" % i) for i in range(3)]
        r_psS = [R("psS0"), R("psS1")]
        r_acc = [R("acc%d" % i) for i in range(3)]
        r_pTr = [R("pTr%d" % i) for i in range(4)]
        cnt = {"S": 0, "PT": 0, "onb": 0, "pTr": 0, "hb": 0, "PTn": 0, "accn": 0, "tb": 0}

        P.op("sp", lambda e: e.dma_start(out=ident[:], in_=k.ident.ap()), w=[r_ident], dma=r_ident)
        P.op("sp", lambda e: e.dma_start(out=tablr[:], in_=k.tablr.ap()), w=[r_tablr], dma=r_tablr)
        P.op("sp", lambda e: e.dma_start(out=lamv[:], in_=k.lamv.ap()), w=[r_lamv], dma=r_lamv)
        P.op("sp", lambda e: e.dma_start(out=sg[:], in_=k.sgb.ap()), w=[r_sg], dma=r_sg)
        P.op("dve", lambda e: e.memset(epsc[:], EPS), w=[r_eps])
        for i in range(2):
            P.op("dve", lambda e, i=i: e.memset(QTm[0][i][64:128, :], 0.0), w=[r_QTm[0][i]])
            P.op("dve", lambda e, i=i: e.memset(QTm[1][i][0:64, :], 0.0), w=[r_QTm[1][i]])
        P.op("dve", lambda e: e.memset(tabp[:], 0.0), w=[r_tabp])
        P.op("dve", lambda e: e.memset(ohup[:], 0.0), w=[r_ohup])
        P.op("sp", lambda e: e.dma_start(out=tabp[0:32, 0:8], in_=k.tab.ap()), w=[r_tabp], dma=r_tabp)
        P.op("sp", lambda e: e.dma_start(out=ohup[0:32, :], in_=k.ohu.ap()), w=[r_ohup], dma=r_ohup)
        pl = psz(lamv)
        P.op("dve", lambda e: e.tensor_tensor(out=lprod[:], in0=mkap(lamv, 0, [(pl, 128), (128, 2), (1, 64)]),
                                              in1=mkap(lamv, 64, [(pl, 128), (128, 2), (1, 64)]), op=ALU.mult),
             r=[r_lamv], w=[r_lprod])
        P.op("dve", lambda e: e.tensor_reduce(out=lsum[:], in_=lprod[:], axis=AX.X, op=ALU.add), r=[r_lprod], w=[r_lsum])
        P.op("act", lambda e: e.activation(out=lexp[:], in_=lsum[:], func=AF.Exp), r=[r_lsum], w=[r_lexp])
        P.op("dve", lambda e: e.tensor_tensor(out=nlam[:], in0=lexp[:, 1:2], in1=lexp[:, 0:1], op=ALU.subtract), r=[r_lexp], w=[r_nlam])
        P.op("dve", lambda e: e.tensor_scalar(out=nlam[:], in0=nlam[:], scalar1=-LAM_INIT, scalar2=None, op0=ALU.add), r=[r_nlam], w=[r_nlam])
        P.op("dve", lambda e: e.tensor_scalar(out=sg[:], in0=sg[:], scalar1=1.0 - LAM_INIT, scalar2=None, op0=ALU.mult), r=[r_sg], w=[r_sg])
        P.op("dve", lambda e: e.tensor_scalar(out=tabp[:], in0=tabp[:], scalar1=1.0 / SCALE_A, scalar2=None, op0=ALU.mult), r=[r_tabp], w=[r_tabp])
        for q in range(3):
            P.op("pe", lambda e, q=q: e.matmul(psS[q % 2][:, q // 2, :], lhsT=tabp[:], rhs=ohup[:, q * 512:(q + 1) * 512], start=True, stop=True),
                 r=[r_tabp, r_ohup], w=[r_psS[q % 2]])
            P.op("dve", lambda e, q=q: e.tensor_copy(out=u2s[:, q * 512:(q + 1) * 512], in_=psS[q % 2][0:8, q // 2, :]),
                 r=[r_psS[q % 2]], w=[r_u2s])
        P.op("sp", lambda e: e.dma_start(out=k.u2.ap(), in_=u2s[:]), r=[r_u2s], w=[r_u2d], dma=r_u2s)

        def acc_ap(a, rows=128, cols=129):
            return acc[0:rows, a // 3, (a % 3) * 129:(a % 3) * 129 + cols]

        def load_head(j, h, nbr):
            S = JOBS[j][1]
            hb = cnt["hb"] % 2
            cnt["hb"] += 1
            if not nbr:
                P.op("sp", lambda e: e.dma_start(out=KT[hb][:, 0:S * 128], in_=k.kaT[j][h]), w=[r_KT[hb]], dma=r_KT[hb])
                P.op("sp", lambda e: e.dma_start(out=VH[hb][:, 0:S, :], in_=k.va[j][h]), w=[r_VH[hb]], dma=r_VH[hb])
                P.op("sp", lambda e: e.dma_start(out=QTm[0][hb][0:64, :], in_=k.qaT[j][h][0:64, 2 * 128:20 * 128]), w=[r_QTm[0][hb]], dma=r_QTm[0][hb])
                P.op("sp", lambda e: e.dma_start(out=QTm[1][hb][64:128, :], in_=k.qaT[j][h][64:128, 2 * 128:20 * 128]), w=[r_QTm[1][hb]], dma=r_QTm[1][hb])
                P.op("sp", lambda e: e.dma_start(out=HH[hb][:], in_=mkap(k.u2, h * 1536, [(1, 128), (1, 1408)])),
                     r=[r_u2d], w=[r_HH[hb]], dma=r_HH[hb])
            else:
                P.op("sp", lambda e: e.dma_start(out=KT[hb][:, 0:NEAR * 128], in_=k.knT[j][h]), w=[r_KT[hb]], dma=r_KT[hb])
                P.op("sp", lambda e: e.dma_start(out=VH[hb][:, 0:NEAR, :], in_=k.vn[j][h]), w=[r_VH[hb]], dma=r_VH[hb])
                P.op("sp", lambda e: e.dma_start(out=QT[hb][:], in_=k.qnT[j][h][:, 2 * 128:20 * 128]), w=[r_QT[hb]], dma=r_QT[hb])
                tb = cnt["tb"] % 2
                cnt["tb"] += 1
                P.op("sp", lambda e: e.dma_start(out=Tfix[tb][:], in_=k.tfd[h]), w=[r_Tfix[tb]], dma=r_Tfix[tb], after=[tf_store[h]])
                return hb, tb
            return hb, None

        def finish_diff(rows, nch, ao, ecols):
            mini = rows != 128
            od_, tmp2_, ssq_, lnv_, rstd_ = (od_m, tmp2_m, ssq_m, lnv_m, rstd_m) if mini else (od, tmp2, ssq, lnv, rstd)
            rod, rtmp2, rssq, rlnv, rrstd = (r_odm, r_tmp2m, r_ssqm, r_lnvm, r_rstdm) if mini else (r_od, r_tmp2, r_ssq, r_lnv, r_rstd)
            na = 2 * nch
            nbanks = (na + 2) // 3
            for b in range(nbanks):
                n_in = min(3, na - 3 * b)
                P.op("dve", lambda e, b=b, n_in=n_in: e.tensor_copy(
                    out=osb[0:rows, 3 * b:3 * b + n_in, :], in_=acc[0:rows, b, 0:n_in * 129].rearrange("p (a c) -> p a c", c=129)),
                    r=[r_acc[b]], w=[r_osb])
            po = psz(osb)
            P.op("dve", lambda e: e.reciprocal(out=rz[0:rows, 0:na], in_=mkap(osb, 128, [(po, rows), (129, na)])), r=[r_osb], w=[r_rz])
            P.op("dve", lambda e: e.tensor_tensor(out=tt[0:rows, 0:na, :], in0=osb[0:rows, 0:na, 0:128],
                                                  in1=mkap(rz, 0, [(psz(rz), rows), (1, na), (0, 128)]), op=ALU.mult),
                 r=[r_osb, r_rz], w=[r_tt])
            ptt = psz(tt)
            P.op("dve", lambda e: e.scalar_tensor_tensor(out=od_[0:rows, 0:nch, :], in0=mkap(tt, 128, [(ptt, rows), (256, nch), (1, 128)]),
                                                         scalar=nlam[0:rows, :], in1=mkap(tt, 0, [(ptt, rows), (256, nch), (1, 128)]),
                                                         op0=ALU.mult, op1=ALU.add),
                 r=[r_tt, r_nlam], w=[rod])
            P.op("dve", lambda e: e.tensor_tensor(out=tmp2_[0:rows, 0:nch, :], in0=od_[0:rows, 0:nch, :], in1=od_[0:rows, 0:nch, :], op=ALU.mult),
                 r=[rod], w=[rtmp2])
            P.op("dve", lambda e: e.tensor_reduce(out=ssq_[0:rows, 0:nch], in_=tmp2_[0:rows, 0:nch, :], axis=AX.X, op=ALU.add),
                 r=[rtmp2], w=[rssq])
            oi = cnt["onb"] % 2
            cnt["onb"] += 1

            def stage2():
                P.op("act", lambda e: e.activation(out=lnv_[0:rows, 0:nch], in_=ssq_[0:rows, 0:nch], func=AF.Ln, scale=1.0 / 128, bias=epsc[0:rows, :]),
                     r=[rssq, r_eps], w=[rlnv])
                P.op("act", lambda e: e.activation(out=rstd_[0:rows, 0:nch], in_=lnv_[0:rows, 0:nch], func=AF.Exp, scale=-0.5), r=[rlnv], w=[rrstd])
                P.op("dve", lambda e: e.tensor_tensor(out=tmp2_[0:rows, 0:nch, :], in0=od_[0:rows, 0:nch, :],
                                                      in1=mkap(rstd_, 0, [(psz(rstd_), rows), (1, nch), (0, 128)]), op=ALU.mult),
                     r=[rod, rrstd], w=[rtmp2])
                P.op("dve", lambda e: e.tensor_tensor(out=onb[oi][0:rows, 0:nch, :], in0=tmp2_[0:rows, 0:nch, :],
                                                      in1=mkap(sg, 0, [(psz(sg), rows), (0, nch), (1, 128)]), op=ALU.mult),
                     r=[rtmp2, r_sg], w=[r_onb[oi]])

            def stage3():
                for c in range(nch):
                    P.op("pe", lambda e, c=c: e.transpose(out=pTr[:, c * 128:c * 128 + rows], in_=onb[oi][0:rows, c, :], identity=ident[0:rows, 0:rows]),
                         r=[r_onb[oi], r_ident], w=[r_pTr[0]])
                if rows == 128:
                    dst = AOh[ao][:, ecols[0]:ecols[0] + 128 * nch]
                    src = pTr[:, 0:128 * nch]
                else:
                    dst = mkap(AOh[ao], 0, [(psz(AOh[ao]), 128), (2049, 2)])
                    src = pTr[:, 0:2]
                P.op("dve", lambda e, dst=dst, src=src: e.tensor_copy(out=dst, in_=src), r=[r_pTr[0]], w=[r_AOh[ao]])
            return [stage2, stage3]

        def diff_head(j, h, hb, ao):
            S = JOBS[j][1]
            pq = psz(QT[hb])
            ph = psz(HH[hb])
            G = Gmini
            pg = psz(G)
            P.op("dve", lambda e: e.tensor_copy(out=G[:, 0:2, 0:1], in_=mkap(HH[hb], 640, [(ph, 128), (128, 2), (1, 1)])), r=[r_HH[hb]], w=[r_Gm])
            P.op("dve", lambda e: e.tensor_copy(out=G[:, 2:18, 0:1], in_=mkap(HH[hb], 896, [(ph, 128), (0, 16), (1, 1)])), r=[r_HH[hb]], w=[r_Gm])
            P.op("dve", lambda e: e.tensor_copy(out=G[:, 0:16, 1:2], in_=mkap(HH[hb], 511, [(ph, 128), (0, 16), (1, 1)])), r=[r_HH[hb]], w=[r_Gm])
            P.op("dve", lambda e: e.tensor_copy(out=G[:, 16:18, 1:2], in_=mkap(HH[hb], 639, [(ph, 128), (128, 2), (1, 1)])), r=[r_HH[hb]], w=[r_Gm])
            def issue_S(g, s):
                    qc0 = (4 * g + 1) * 128
                    ri = cnt["S"] % 2
                    cnt["S"] += 1
                    for m in range(2):
                        P.op("pe", lambda e, ri=ri, m=m, s=s: e.matmul(
                            psS[ri][:, m, :], lhsT=KT[hb][:, s * 128:(s + 1) * 128],
                            rhs=QTm[m][hb][:, qc0:qc0 + 512], start=True, stop=True),
                            r=[r_KT[hb], r_QTm[m][hb]], w=[r_psS[ri]])
                    bias = None
                    if 2 <= s <= 19:
                        d0 = (s - 3) - 4 * g
                        sw = 5 - d0
                        if sw <= 0:
                            bias = tablr[:, 1, h:h + 1]
                        elif sw >= 7:
                            bias = tablr[:, 0, h:h + 1]
                        else:
                            win = mkap(HH[hb], 1407 - 128 * sw, [(ph, 128), (0, 2), (-1, 512)])
                            P.op("dve", lambda e, ri=ri, win=win: e.tensor_tensor(out=psS[ri][:], in0=psS[ri][:], in1=win, op=ALU.add),
                                 r=[r_psS[ri], r_HH[hb]], w=[r_psS[ri]])
                    pi = cnt["PT"] % 3
                    cnt["PT"] += 1
                    if bias is None:
                        P.op("act", lambda e, ri=ri, pi=pi: e.activation(out=PT[pi][:], in_=psS[ri][:], func=AF.Exp, scale=SCALE_A),
                             r=[r_psS[ri]], w=[r_PT[pi]])
                    else:
                        P.op("act", lambda e, ri=ri, pi=pi, bias=bias: e.activation(out=PT[pi][:], in_=psS[ri][:], func=AF.Exp, scale=SCALE_A, bias=bias),
                             r=[r_psS[ri], r_tablr], w=[r_PT[pi]])
                    return pi

            def issue_PV(s, pi):
                    for c in range(4):
                        for m in range(2):
                            a = 2 * c + m
                            P.op("pe", lambda e, pi=pi, c=c, m=m, a=a, s=s: e.matmul(
                                acc_ap(a), lhsT=PT[pi][:, m, c * 128:(c + 1) * 128], rhs=VH[hb][:, s, :],
                                start=(s == 0 and a % 3 == 0), stop=(s == S - 1), skip_group_check=True),
                                r=[r_PT[pi], r_VH[hb]], w=[r_acc[a // 3]])

            steps = [(g, s) for g in range(4) for s in range(S)]

            def retire(prev):
                g_, s_, pi_ = prev
                issue_PV(s_, pi_)
                if s_ == S - 1:
                    st = finish_diff(128, 4, ao, [1 + 512 * g_ + 128 * c for c in range(4)])
                    carry.append([8, st[0]])
                    carry.append([12, st[1]])

            inflight = []
            for (g, s) in steps:
                pi = issue_S(g, s)
                inflight.append((g, s, pi))
                if len(inflight) >= 3:
                    retire(inflight.pop(0))
                tick_carry()
            ri = cnt["S"] % 2
            cnt["S"] += 1
            for s in range(S):
                for m in range(2):
                    P.op("pe", lambda e, ri=ri, m=m, s=s: e.matmul(
                        psS[ri][:, 0, s * 4 + 2 * m:s * 4 + 2 * m + 2], lhsT=KT[hb][:, s * 128:(s + 1) * 128],
                        rhs=mkap(QTm[m][hb], 127, [(pq, 128), (2049, 2)]), start=True, stop=True),
                        r=[r_KT[hb], r_QTm[m][hb]], w=[r_psS[ri]])
            pps = psz(psS[ri])
            reg = mkap(psS[ri], 8, [(pps, 128), (4, 18), (2, 2), (1, 2)])
            P.op("dve", lambda e, reg=reg: e.tensor_tensor(out=reg, in0=reg, in1=mkap(G, 0, [(pg, 128), (2, 18), (0, 2), (1, 2)]), op=ALU.add),
                 r=[r_psS[ri], r_Gm], w=[r_psS[ri]])
            P.op("act", lambda e, ri=ri: e.activation(out=PTm[:, 0:4 * S], in_=psS[ri][:, 0, 0:4 * S], func=AF.Exp, scale=SCALE_A),
                 r=[r_psS[ri]], w=[r_PTm])
            while inflight:
                retire(inflight.pop(0))
                tick_carry()
            for s in range(S):
                for m in range(2):
                    P.op("pe", lambda e, m=m, s=s: e.matmul(
                        acc_ap(m, rows=2), lhsT=PTm[:, s * 4 + 2 * m:s * 4 + 2 * m + 2], rhs=VH[hb][:, s, :],
                        start=(s == 0 and m == 0), stop=(s == S - 1), skip_group_check=True),
                        r=[r_PTm, r_VH[hb]], w=[r_acc[0]])
            st = finish_diff(2, 1, ao, None)
            carry.append([9, st[0]])
            carry.append([13, st[1]])
            carry.append([14, lambda: P.op("act", lambda e: e.dma_start(out=k.aoT[j][h], in_=AOh[ao][:]), r=[r_AOh[ao]], dma=r_AOh[ao])])

        def nbr_head(j, h, hb, tb, ao):
            flush_carry()
            pq = psz(QT[hb])
            pm = psz(maskt)
            def stageA(i, dl):
                nd = len(dl)
                if i == -1:
                    nq, qoff, key = 1, 127, -1
                elif i == 16:
                    nq, qoff, key = 1, 17 * 128, 16
                else:
                    nq, qoff = 128, (i + 1) * 128
                    key = i if i in (0, 1, 14, 15) else "int"
                ri = cnt["S"] % 2
                cnt["S"] += 1
                flat = psS[ri][:].rearrange("p a b -> p (a b)")
                for di, d_ in enumerate(dl):
                    nk = i + d_ + 3
                    P.op("pe", lambda e, di=di, nk=nk: e.matmul(flat[:, di * 128:di * 128 + nq], lhsT=KT[hb][:, nk * 128:(nk + 1) * 128],
                                                                rhs=QT[hb][:, qoff:qoff + nq], start=True, stop=False),
                         r=[r_KT[hb], r_QT[hb]], w=[r_psS[ri]])
                    mo = moff[key] + di * nq
                    P.op("pe", lambda e, di=di, mo=mo: e.matmul(flat[:, di * 128:di * 128 + nq], lhsT=ident[:], rhs=maskt[:, mo:mo + nq],
                                                                start=False, stop=True),
                         r=[r_ident, r_mask], w=[r_psS[ri]])
                pps = psz(psS[ri])
                reg = mkap(psS[ri], 0, [(pps, 128), (128, nd), (1, nq)])
                tfx = Tfix[tb][:, dl[0] + 3:dl[0] + 3 + nd, qoff % 128:qoff % 128 + nq]
                P.op("dve", lambda e, reg=reg, tfx=tfx: e.tensor_tensor(out=reg, in0=reg, in1=tfx, op=ALU.add),
                     r=[r_psS[ri], r_Tfix[tb]], w=[r_psS[ri]])
                pn = cnt["PTn"] % 3
                cnt["PTn"] += 1
                P.op("act", lambda e, reg=reg, pn=pn: e.activation(out=PTn[pn][:, 0:nd, 0:nq], in_=reg, func=AF.Exp, scale=SCALE_N),
                     r=[r_psS[ri]], w=[r_PTn[pn]])
                return (i, dl, nq, pn)

            def stageB(st):
                i, dl, nq, pn = st
                nd = len(dl)
                ai = cnt["accn"] % 3
                cnt["accn"] += 1
                for di, d_ in enumerate(dl):
                    nk = i + d_ + 3
                    P.op("pe", lambda e, di=di, nk=nk, ai=ai, pn=pn: e.matmul(acc[0:nq, ai, 0:129], lhsT=PTn[pn][:, di, 0:nq], rhs=VH[hb][:, nk, :],
                                                                        start=(di == 0), stop=(di == nd - 1)),
                         r=[r_PTn[pn], r_VH[hb]], w=[r_acc[ai]])
                P.op("dve", lambda e, ai=ai: e.reciprocal(out=rzn[ai][0:nq, :], in_=acc[0:nq, ai, 128:129]), r=[r_acc[ai]], w=[r_rzn[ai]])
                P.op("dve", lambda e, ai=ai: e.tensor_scalar(out=onbn[ai][0:nq, :], in0=acc[0:nq, ai, 0:128], scalar1=rzn[ai][0:nq, :], scalar2=None, op0=ALU.mult),
                     r=[r_acc[ai], r_rzn[ai]], w=[r_onbn[ai]])
                return (i, nq, ai)

            def stageC(st):
                i, nq, ai = st
                P.op("pe", lambda e, ai=ai: e.transpose(out=pTr[:, 0:nq], in_=onbn[ai][0:nq, :], identity=ident[0:nq, 0:nq]),
                     r=[r_onbn[ai], r_ident], w=[r_pTr[0]])
                if nq == 128:
                    dst = AOh[ao][:, 1 + 128 * i:1 + 128 * i + 128]
                else:
                    ecol = 0 if i == -1 else 2049
                    dst = AOh[ao][:, ecol:ecol + 1]
                P.op("dve", lambda e, dst=dst: e.tensor_copy(out=dst, in_=pTr[:, 0:nq]), r=[r_pTr[0]], w=[r_AOh[ao]])

            units = nbr_units()
            nu = len(units)
            sa, sbq = [], []
            for u in range(nu + 4):
                if u < nu:
                    sa.append(stageA(*units[u]))
                if 2 <= u < nu + 2:
                    sbq.append(stageB(sa[u - 2]))
                if 4 <= u:
                    stageC(sbq[u - 4])

            P.op("act", lambda e: e.dma_start(out=k.aoT[j][8 + h], in_=AOh[ao][:]), r=[r_AOh[ao]], dma=r_AOh[ao])

        tf_store = {}
        for h in range(HN):
            tb_ = h % 2
            for rk in range(2):
                for rq in range(2):
                    for dl in range(-3, 4):
                        dr = 2 * dl + rk - rq + 7
                        P.op("pool", lambda e, tb_=tb_, dl=dl, rk=rk, rq=rq, dr=dr, h=h: e.dma_start(
                            out=Trt[tb_][rk * 64:(rk + 1) * 64, dl + 3, rq, :],
                            in_=mkap(k.rpbp, (h * 15 + dr) * 128, [(1, 64), (1, 64)])), w=[r_Trt[tb_][2 * rk + rq]], dma=r_Trt[tb_][2 * rk + rq])
            pt_ = psz(Trt[tb_])
            for rq in range(2):
                P.op("pool", lambda e, tb_=tb_, rq=rq, pt_=pt_: e.tensor_scalar(
                    out=TfixB[tb_][:, :, rq * 64:(rq + 1) * 64],
                    in0=mkap(Trt[tb_], rq * 64 + 63, [(pt_, 128), (128, 7), (-1, 64)]),
                    scalar1=1.0 / SCALE_N, scalar2=1.0, op0=ALU.mult, op1=ALU.mult),
                    r=[r_Trt[tb_][rq], r_Trt[tb_][2 + rq]], w=[r_TfixB[tb_]])
            tf_store[h] = P.op("pool", lambda e, tb_=tb_, h=h: e.dma_start(out=k.tfd[h], in_=TfixB[tb_][:]), r=[r_TfixB[tb_]], dma=r_TfixB[tb_])

        g12c = sb("g12c", (128, 2, CK))
        cin = [sb("cin%d" % i, (128, 1024)) for i in range(2)]
        cout = [sb("cout%d" % i, (128, 1024), BF16) for i in range(2)]
        r_g = R("g12c")
        r_cin = [R("cin0"), R("cin1")]
        r_cout = [R("cout0"), R("cout1")]
        P.op("sp", lambda e: e.dma_start(out=g12c[:], in_=k.g12c.ap()), w=[r_g], dma=r_g)
        conv_state = {"n": 0}

        def convert(src, dst, nrows, ncols, tw, gidx):
            for rb in range(nrows // 128):
                for c0 in range(0, ncols, tw):
                    i = conv_state["n"] % 2
                    conv_state["n"] += 1
                    sv = src[rb * 128:(rb + 1) * 128, c0:c0 + tw]
                    dv = dst[rb * 128:(rb + 1) * 128, c0:c0 + tw]
                    P.op("pool", lambda e, i=i, sv=sv, tw=tw: e.dma_start(out=cin[i][:, 0:tw], in_=sv), w=[r_cin[i]], dma=r_cin[i])
                    if gidx is None:
                        P.op("pool", lambda e, i=i, tw=tw: e.tensor_copy(out=cout[i][:, 0:tw], in_=cin[i][:, 0:tw]),
                             r=[r_cin[i]], w=[r_cout[i]])
                    else:
                        P.op("pool", lambda e, i=i, tw=tw, rb=rb, gidx=gidx: e.tensor_scalar(
                            out=cout[i][:, 0:tw], in0=cin[i][:, 0:tw], scalar1=g12c[:, gidx, rb:rb + 1], scalar2=1.0, op0=ALU.mult, op1=ALU.mult),
                            r=[r_cin[i], r_g], w=[r_cout[i]])
                    P.op("pool", lambda e, i=i, dv=dv, tw=tw: e.dma_start(out=dv, in_=cout[i][:, 0:tw]), r=[r_cout[i]], dma=r_cout[i])

        if not DEBUG.get("skip_a") and not DEBUG.get("no_conv_b"):
            convert(k.w_out, k.wb_out, D, D, 1024, None)
            convert(k.w_up, k.wb_up, D, 2 * DFF, 688, 1)
            convert(k.w_down, k.wb_down, DFF, D, 1024, None)

        work = []
        for j in DEBUG.get("jobs", (0, 1)):
            for h in DEBUG.get("heads_a", range(HA)):
                work.append((j, h, False))
            for h in DEBUG.get("heads_n", range(HN)):
                work.append((j, h, True))
        cur_job = None
        if not work:
            P.emit_phase()
            return
        pre = load_head(*work[0])
        for wi, (j, h, nbr) in enumerate(work):
            hb, tb = pre
            if nbr and cur_job != j:
                cur_job = j
                P.op("sp", lambda e, j=j: e.dma_start(out=maskt[:], in_=k.maskd[j].ap()), w=[r_mask], dma=r_mask)
            if wi + 1 < len(work):
                pre = load_head(*work[wi + 1])
            ao = wi % 2
            if nbr:
                nbr_head(j, h, hb, tb, ao)
            else:
                diff_head(j, h, hb, ao)
        flush_carry()
        P.emit_phase()


def phase_c(k):
    nc, P = k.nc, k.P
    with ExitStack() as es:
        def sb(name, shape, dt=F32):
            return es.enter_context(nc.sbuf_tensor("C_" + name, list(shape), dt))

        R = P.res
        NW = 6
        ident = sb("ident", (128, 128), BF16)
        epsc = sb("epsc", (128, 1))
        gfb = sb("gfb", (128, D))
        convc = sb("convc", (128, FCH, 4))
        hflag = sb("hflag", (128, 4))
        cwe = sb("cwe", (128, 4, FCH))
        axT = sb("axT", (128, CK, 514), BF16)
        xmid = sb("xmid", (128, 4, D))
        xmh = sb("xmh", (2, D))
        hT = sb("hT", (128, FCH, 512), BF16)
        wr = [sb("wr%d" % i, (128, 4096), BF16) for i in range(NW)]
        xn2s = [sb("xn2_%d" % i, (128, D), BF16) for i in range(2)]
        junk = sb("junk", (128, D), BF16)
        tb = [sb("tb%d" % i, (128, 2, 256)) for i in range(2)]
        ssqs = [sb("ssq%d" % i, (128, 1)) for i in range(2)]
        lnvs = [sb("lnv%d" % i, (128, 1)) for i in range(2)]
        rstds = [sb("rstd%d" % i, (128, 1)) for i in range(2)]
        pb = es.enter_context(nc.psum_tensor("C_pb", [128, 8, 512], F32))

        r_ident, r_eps, r_gfb, r_convc, r_hflag, r_cwe = (R(n) for n in ("ident", "eps", "gfb", "convc", "hflag", "cwe"))
        r_axT = R("axT")
        r_xmid = [R("xmid%d" % i) for i in range(4)]
        r_xmh = R("xmh")
        r_hT = R("hT")
        r_wr = [R("wr%d" % i) for i in range(NW)]
        r_junk = R("junk")
        r_xn2s = [R("xn2_0"), R("xn2_1")]
        r_ssqs = [R("ssq0"), R("ssq1")]
        r_lnvs = [R("lnv0"), R("lnv1")]
        r_rstds = [R("rstd0"), R("rstd1")]
        r_tb = [R("tb0"), R("tb1")]
        r_pb = [R("pb%d" % i) for i in range(8)]
        cnt = {"w": 0, "pb": 0, "tb": 0, "u": 0, "rms": 0}

        P.op("sp", lambda e: e.dma_start(out=ident[:], in_=k.ident.ap()), w=[r_ident], dma=r_ident)
        P.op("sp", lambda e: e.dma_start(out=gfb[:], in_=k.gfb.ap()), w=[r_gfb], dma=r_gfb)
        P.op("sp", lambda e: e.dma_start(out=convc[:], in_=k.convc.ap()), w=[r_convc], dma=r_convc)
        P.op("sp", lambda e: e.dma_start(out=hflag[:], in_=k.hflag.ap()), w=[r_hflag], dma=r_hflag)
        P.op("dve", lambda e: e.memset(epsc[:], EPS), w=[r_eps])
        pc = psz(convc)
        for q in range(4):
            ci = 0 if q % 2 == 0 else 2
            P.op("dve", lambda e, q=q, ci=ci: e.tensor_scalar(out=cwe[:, q, :], in0=mkap(convc, ci, [(pc, 128), (4, FCH)]),
                                                             scalar1=hflag[:, q:q + 1], scalar2=None, op0=ALU.mult),
                 r=[r_convc, r_hflag], w=[r_cwe])

        def wslot():
            i = cnt["w"] % NW
            cnt["w"] += 1
            return i

        def load_w_cols(src_dram, c0, ncols):
            i = wslot()
            src = src_dram[:, c0:c0 + ncols].rearrange("(ck p) f -> p ck f", p=128)
            dst = wr[i][:, 0:CK * ncols].rearrange("p (ck f) -> p ck f", f=ncols)
            P.op("sp", lambda e: e.dma_start(out=dst, in_=src), w=[r_wr[i]], dma=r_wr[i])
            return i

        def load_w_rows(src_dram, r0, nr):
            i = wslot()
            src = src_dram[r0 * 128:(r0 + nr) * 128, :].rearrange("(f p) n -> p f n", p=128)
            dst = wr[i][:, 0:nr * D].rearrange("p (f n) -> p f n", n=D)
            P.op("sp", lambda e: e.dma_start(out=dst, in_=src), w=[r_wr[i]], dma=r_wr[i])
            return i

        def wview_cols(i, ncols):
            return wr[i][:, 0:CK * ncols].rearrange("p (ck f) -> p ck f", f=ncols)

        def wview_rows(i, nr):
            return wr[i][:, 0:nr * D].rearrange("p (f n) -> p f n", n=D)

        def bank():
            b = cnt["pb"] % 6
            cnt["pb"] += 1
            return b

        pax = psz(axT)

        def rms_rows(rows, src_ap, r_src, want_xn):
            q = cnt["rms"] % 2
            cnt["rms"] += 1
            ssq, lnv, rstd, xn2 = ssqs[q], lnvs[q], rstds[q], xn2s[q]
            P.op("act", lambda e: e.activation(out=junk[0:rows, :], in_=src_ap, func=AF.Square, accum_out=ssq[0:rows, :]),
                 r=[r_src], w=[r_junk, r_ssqs[q]])
            P.op("act", lambda e: e.activation(out=lnv[0:rows, :], in_=ssq[0:rows, :], func=AF.Ln, scale=1.0 / D, bias=epsc[0:rows, :]),
                 r=[r_ssqs[q], r_eps], w=[r_lnvs[q]])
            P.op("act", lambda e: e.activation(out=rstd[0:rows, :], in_=lnv[0:rows, :], func=AF.Exp, scale=-0.5), r=[r_lnvs[q]], w=[r_rstds[q]])
            if want_xn:
                P.op("dve", lambda e: e.tensor_scalar(out=xn2[0:rows, :], in0=src_ap, scalar1=rstd[0:rows, :], scalar2=None, op0=ALU.mult),
                     r=[r_src, r_rstds[q]], w=[r_xn2s[q]])
            return q

        def tile(j, T):
            e0 = 512 * T
            xs_flat = k.xs[j].ap().rearrange("s p d -> (s p) d")
            src = k.aoT[j][:, :, e0:e0 + 514].rearrange("c p n -> p c n")
            P.op("sp", lambda e: e.dma_start(out=axT[:], in_=src), w=[r_axT], dma=r_axT)
            for tc in range(4):
                r0 = 383 + e0 + 1 + 128 * tc
                P.op("sp", lambda e, tc=tc, r0=r0: e.dma_start(out=xmid[:, tc, :], in_=xs_flat[r0:r0 + 128, :]), w=[r_xmid[tc]], dma=r_xmid[tc])
            hsrc = mkap(k.xs[j], (383 + e0) * D, [(513 * D, 2), (1, D)])
            P.op("sp", lambda e: e.dma_start(out=xmh[:], in_=hsrc), w=[r_xmh], dma=r_xmh)
            NG = 256
            nxt = load_w_cols(k.wb_out, 0, NG)
            for cg in range(D // NG):
                wi = nxt
                if cg + 1 < D // NG:
                    nxt = load_w_cols(k.wb_out, (cg + 1) * NG, NG)
                wv = wview_cols(wi, NG)
                for tc in range(5):
                    b = bank()
                    if tc < 4:
                        rows = 128
                        def lhs(ck, tc=tc):
                            return axT[:, ck, 1 + 128 * tc:129 + 128 * tc]
                        dst = xmid[:, tc, cg * NG:(cg + 1) * NG]
                        rd = r_xmid[tc]
                    else:
                        rows = 2
                        def lhs(ck):
                            return mkap(axT, ck * 514, [(pax, 128), (513, 2)])
                        dst = xmh[0:2, cg * NG:(cg + 1) * NG]
                        rd = r_xmh
                    for ck in range(CK):
                        P.op("pe", lambda e, b=b, ck=ck, lhs=lhs, rows=rows, wv=wv: e.matmul(
                            pb[0:rows, b, 0:NG], lhsT=lhs(ck), rhs=wv[:, ck, :], start=(ck == 0), stop=(ck == CK - 1)),
                            r=[r_axT, r_wr[wi]], w=[r_pb[b]])
                    P.op("dve", lambda e, b=b, dst=dst, rows=rows: e.tensor_tensor(out=dst, in0=pb[0:rows, b, 0:NG], in1=dst, op=ALU.add),
                         r=[r_pb[b], rd], w=[rd])
            for tc in range(5):
                rows = 128 if tc < 4 else 2
                srcx = xmid[:, tc, :] if tc < 4 else xmh[0:2, :]
                rsrc = r_xmid[tc] if tc < 4 else r_xmh
                q_ = rms_rows(rows, srcx, rsrc, True)
                xn2, r_xn2 = xn2s[q_], r_xn2s[q_]
                if tc < 4:
                    for half in range(2):
                        pt = pb[:, 6 + half, :].bitcast(BF16)
                        for q in range(8):
                            ck = half * 8 + q
                            P.op("pe", lambda e, pt=pt, q=q, ck=ck, xn2=xn2: e.transpose(out=pt[:, q * 128:(q + 1) * 128], in_=xn2[:, ck * 128:(ck + 1) * 128], identity=ident[:]),
                                 r=[r_xn2, r_ident], w=[r_pb[6 + half]])
                        dst = axT[:, half * 8:(half + 1) * 8, 1 + 128 * tc:129 + 128 * tc]
                        srcp = pt[:, 0:1024].rearrange("p (a b) -> p a b", b=128)
                        P.op("dve", lambda e, dst=dst, srcp=srcp: e.tensor_copy(out=dst, in_=srcp), r=[r_pb[6 + half]], w=[r_axT])
                else:
                    pt = pb[:, 6, :].bitcast(BF16)
                    for ck in range(CK):
                        P.op("pe", lambda e, pt=pt, ck=ck, xn2=xn2: e.transpose(out=pt[:, 2 * ck:2 * ck + 2], in_=xn2[0:2, ck * 128:(ck + 1) * 128], identity=ident[0:2, 0:2]),
                             r=[r_xn2, r_ident], w=[r_pb[6]])
                    dst = mkap(axT, 0, [(pax, 128), (514, CK), (513, 2)])
                    srcp = pt[:, 0:2 * CK].rearrange("p (a b) -> p a b", b=2)
                    P.op("dve", lambda e, dst=dst, srcp=srcp: e.tensor_copy(out=dst, in_=srcp), r=[r_pb[6]], w=[r_axT])
            groups = [(f0, min(2, FCH - f0)) for f0 in range(0, FCH, 2)]

            def load_group(gi):
                f0, nf = groups[gi]
                return (load_w_cols(k.wb_up, f0 * 128, nf * 128), load_w_cols(k.wb_up, DFF + f0 * 128, nf * 128))

            pend = [load_group(0), load_group(1)]
            for gi, (f0, nf) in enumerate(groups):
                wa_i, wg_i = pend.pop(0)
                if gi + 2 < len(groups):
                    pend.append(load_group(gi + 2))
                wa = wview_cols(wa_i, nf * 128)
                wg = wview_cols(wg_i, nf * 128)
                for fl in range(nf):
                    fc = f0 + fl
                    u = cnt["u"] % 2
                    cnt["u"] += 1
                    ba = 3 * u
                    for ck in range(CK):
                        P.op("pe", lambda e, ba=ba, ck=ck, wa=wa, fl=fl: e.matmul(pb[:, ba, 0:258], lhsT=wa[:, ck, fl * 128:(fl + 1) * 128],
                                                                             rhs=axT[:, ck, 0:258], start=(ck == 0), stop=(ck == CK - 1)),
                             r=[r_axT, r_wr[wa_i]], w=[r_pb[ba]])
                    for ck in range(CK):
                        P.op("pe", lambda e, ba=ba, ck=ck, wa=wa, fl=fl: e.matmul(pb[:, ba + 1, 0:258], lhsT=wa[:, ck, fl * 128:(fl + 1) * 128],
                                                                             rhs=axT[:, ck, 256:514], start=(ck == 0), stop=(ck == CK - 1)),
                             r=[r_axT, r_wr[wa_i]], w=[r_pb[ba + 1]])
                    for ck in range(CK):
                        P.op("pe", lambda e, ba=ba, ck=ck, wg=wg, fl=fl: e.matmul(pb[:, ba + 2, :], lhsT=wg[:, ck, fl * 128:(fl + 1) * 128],
                                                                             rhs=axT[:, ck, 1:513], start=(ck == 0), stop=(ck == CK - 1)),
                             r=[r_axT, r_wr[wg_i]], w=[r_pb[ba + 2]])
                    ti = cnt["tb"] % 2
                    cnt["tb"] += 1
                    t_ = tb[ti]
                    rt = r_tb[ti]
                    ra = [r_pb[ba], r_pb[ba + 1]]
                    P.op("dve", lambda e, ba=ba, t_=t_, fc=fc: e.tensor_scalar(out=t_[:], in0=pb[:, ba:ba + 2, 0:256], scalar1=convc[:, fc, 0:1],
                                                                           scalar2=convc[:, fc, 3:4], op0=ALU.mult, op1=ALU.add),
                         r=ra + [r_convc], w=[rt])
                    if T == 0:
                        P.op("dve", lambda e, ba=ba, t_=t_, fc=fc: e.tensor_scalar(out=t_[:, 0, 0:1], in0=pb[:, ba, 0:1], scalar1=cwe[:, 2 * j, fc:fc + 1],
                                                                               scalar2=convc[:, fc, 3:4], op0=ALU.mult, op1=ALU.add),
                             r=ra + [r_convc, r_cwe], w=[rt])
                    P.op("dve", lambda e, ba=ba, t_=t_, fc=fc: e.scalar_tensor_tensor(out=t_[:], in0=pb[:, ba:ba + 2, 1:257], scalar=convc[:, fc, 1:2],
                                                                                  in1=t_[:], op0=ALU.mult, op1=ALU.add),
                         r=ra + [r_convc, rt], w=[rt])
                    if T == 3:
                        P.op("dve", lambda e, ba=ba, t_=t_, fc=fc: e.scalar_tensor_tensor(out=t_[:, 1, 255:256], in0=pb[:, ba + 1, 257:258], scalar=cwe[:, 2 * j + 1, fc:fc + 1],
                                                                                      in1=t_[:, 1, 255:256], op0=ALU.mult, op1=ALU.add),
                             r=ra + [r_cwe, rt], w=[rt])
                        P.op("dve", lambda e, ba=ba, t_=t_, fc=fc: e.scalar_tensor_tensor(out=t_[:, :, 0:255], in0=pb[:, ba:ba + 2, 2:257], scalar=convc[:, fc, 2:3],
                                                                                      in1=t_[:, :, 0:255], op0=ALU.mult, op1=ALU.add),
                             r=ra + [r_convc, rt], w=[rt])
                        P.op("dve", lambda e, ba=ba, t_=t_, fc=fc: e.scalar_tensor_tensor(out=t_[:, 0, 255:256], in0=pb[:, ba, 257:258], scalar=convc[:, fc, 2:3],
                                                                                      in1=t_[:, 0, 255:256], op0=ALU.mult, op1=ALU.add),
                             r=ra + [r_convc, rt], w=[rt])
                    else:
                        P.op("dve", lambda e, ba=ba, t_=t_, fc=fc: e.scalar_tensor_tensor(out=t_[:], in0=pb[:, ba:ba + 2, 2:258], scalar=convc[:, fc, 2:3],
                                                                                      in1=t_[:], op0=ALU.mult, op1=ALU.add),
                             r=ra + [r_convc, rt], w=[rt])
                    P.op("act", lambda e, t_=t_: e.activation(out=t_[:], in_=t_[:], func=AF.Gelu_apprx_tanh), r=[rt], w=[rt])
                    P.op("dve", lambda e, ba=ba, t_=t_, fc=fc: e.tensor_tensor(out=hT[:, fc, :], in0=t_[:].rearrange("p a b -> p (a b)"), in1=pb[:, ba + 2, :], op=ALU.mult),
                         r=[rt, r_pb[ba + 2]], w=[r_hT])
            dgroups = [(f0, min(2, FCH - f0)) for f0 in range(0, FCH, 2)]
            for hh in range(2):
                pend = [load_w_rows(k.wb_down, dgroups[0][0], dgroups[0][1]), load_w_rows(k.wb_down, dgroups[1][0], dgroups[1][1])]
                for gi, (f0, nf) in enumerate(dgroups):
                    wi = pend.pop(0)
                    if gi + 2 < len(dgroups):
                        pend.append(load_w_rows(k.wb_down, dgroups[gi + 2][0], dgroups[gi + 2][1]))
                    wv = wview_rows(wi, nf)
                    for fl in range(nf):
                        fc = f0 + fl
                        for tl in range(2):
                            tc = 2 * hh + tl
                            for cg in range(4):
                                b = tl * 4 + cg
                                P.op("pe", lambda e, b=b, fc=fc, tc=tc, fl=fl, cg=cg, wv=wv: e.matmul(
                                    pb[:, b, :], lhsT=hT[:, fc, tc * 128:(tc + 1) * 128], rhs=wv[:, fl, cg * 512:(cg + 1) * 512],
                                    start=(fc == 0), stop=(fc == FCH - 1)),
                                    r=[r_hT, r_wr[wi]], w=[r_pb[b]])
                for tl in range(2):
                    tc = 2 * hh + tl
                    for cg in range(4):
                        b = tl * 4 + cg
                        dst = xmid[:, tc, cg * 512:(cg + 1) * 512]
                        P.op("dve", lambda e, b=b, dst=dst: e.tensor_tensor(out=dst, in0=pb[:, b, :], in1=dst, op=ALU.add),
                             r=[r_pb[b], r_xmid[tc]], w=[r_xmid[tc]])
                    q_ = rms_rows(128, xmid[:, tc, :], r_xmid[tc], False)
                    P.op("dve", lambda e, tc=tc, q_=q_: e.scalar_tensor_tensor(out=xmid[:, tc, :], in0=xmid[:, tc, :], scalar=rstds[q_][:, :], in1=gfb[:],
                                                                       op0=ALU.mult, op1=ALU.mult),
                         r=[r_xmid[tc], r_rstds[q_], r_gfb], w=[r_xmid[tc]])
                    row0 = 512 * T + 128 * tc
                    P.op("sp", lambda e, tc=tc, row0=row0: e.dma_start(out=k.y[j][row0:row0 + 128, :], in_=xmid[:, tc, :]), r=[r_xmid[tc]], dma=r_xmid[tc])

        for j in DEBUG.get("jobs", (0, 1)):
            for T in DEBUG.get("tiles_c", range(4)):
                tile(j, T)
        P.emit_phase()


def prepare_inputs(inp):
    f32 = np.float32
    x_prompt = np.asarray(inp["x_prompt"], f32)
    x_sample = np.asarray(inp["x_sample"], f32)
    shared = {}
    shared["w_in"] = np.ascontiguousarray(np.asarray(inp["w_in"], f32)[0])
    shared["w_out"] = np.ascontiguousarray(np.asarray(inp["w_out"], f32)[0])
    shared["w_up"] = np.ascontiguousarray(np.asarray(inp["w_up"], f32)[0])
    shared["w_down"] = np.ascontiguousarray(np.asarray(inp["w_down"], f32)[0])
    g1 = np.asarray(inp["norm1_g"], f32)[0].reshape(CK, 128).T
    g2 = np.asarray(inp["norm2_g"], f32)[0].reshape(CK, 128).T
    shared["g12c"] = np.ascontiguousarray(np.stack([g1, g2], axis=1))
    shared["gfb"] = bcast128(inp["final_g"])
    shared["sgb"] = bcast128(np.asarray(inp["subln_g"], f32)[0])
    lam = np.concatenate([np.asarray(inp[n], f32)[0] for n in ("lambda_q1", "lambda_k1", "lambda_q2", "lambda_k2")])
    shared["lamv"] = bcast128(lam).reshape(128, 4, 64)
    tab = np.asarray(inp["rel_bias_table"], f32)
    shared["tab"] = np.ascontiguousarray(tab)
    shared["tablr"] = bcast128(np.concatenate([tab[15], tab[31]])).reshape(128, 2, 8)
    rel = np.arange(1536) - 767
    bk = t5_bucket_np(rel)
    ohu = np.zeros((32, 1536), f32)
    ohu[bk, np.arange(1536)] = 1.0
    shared["ohu"] = ohu
    rpbp = np.zeros((8, 15, 128), f32)
    rpbp[:, :, 48:79] = np.asarray(inp["na_rpb"], f32)[0]
    shared["rpbp"] = rpbp
    cw = np.asarray(inp["conv_w"], f32)[0]
    cb = np.asarray(inp["conv_b"], f32)[0]
    cc = np.stack([cw[0], cw[1], cw[2], cb], axis=1)
    shared["convc"] = np.ascontiguousarray(cc.reshape(FCH, 128, 4).transpose(1, 0, 2))
    shared["ident"] = np.eye(128, dtype=f32).astype(ml_dtypes.bfloat16)

    in_maps = []
    for c in range(NCORES):
        m = dict(shared)
        hflag = np.zeros((128, 4), f32)
        for j in range(2):
            nblk, S = JOBS[j]
            if j == 0:
                seq, t = x_prompt[c // 4], c % 4
            else:
                seq, t = x_sample[c // 2], c % 2
            o, blocks, near_true = job_geometry(nblk, S, t)
            xs = np.zeros((S, 128, D), f32)
            ws = np.zeros((3, S), f32)
            for s, gb in enumerate(blocks):
                if gb < 0:
                    continue
                xs[s] = seq[gb * 128:(gb + 1) * 128]
                if 2 <= s <= 19:
                    ws[0, s] = 1.0
                elif gb < o:
                    ws[1, s] = 1.0
                else:
                    ws[2, s] = 1.0
            m["xs%d" % j] = xs
            m["wsel%d" % j] = np.ascontiguousarray(np.broadcast_to(ws[None], (128, 3, S)))
            m["maskd%d" % j] = build_masks(nblk, t, o, near_true)
            hflag[:, 2 * j] = 1.0 if o > 0 else 0.0
            hflag[:, 2 * j + 1] = 1.0 if o + 16 < nblk else 0.0
        m["hflag"] = hflag
        in_maps.append(m)
    return in_maps


def kernel(**inputs):
    in_maps = prepare_inputs(inputs)
    nc = build_program()
    res = run_bass_kernel_spmd(nc, in_maps, core_ids=list(range(NCORES)))
    yp = np.zeros((2, 8192, D), np.float32)
    ysm = np.zeros((4, 4096, D), np.float32)
    for c in range(NCORES):
        r = res.results[c]
        yp[c // 4, (c % 4) * 2048:(c % 4 + 1) * 2048] = r["y0"]
        ysm[c // 2, (c % 2) * 2048:(c % 2 + 1) * 2048] = r["y1"]
    return (yp, ysm)
```
